# Optimizing a Trainium2 kernel written in Bass

```python
import jax, jax.numpy as jnp
from jax import lax
import numpy as np

D_MODEL = 1024
BATCH = 8
SEQ = 4096
DEPTH = 2

HEAD_DIM = 64
POOL_WIDTH = D_MODEL // 4
POOL_WINDOWS = (2, 4, 8, 16)
POOL_GROUPS = len(POOL_WINDOWS)
POOL_GROUP_WIDTH = POOL_WIDTH // POOL_GROUPS
RET_WIDTH = 3 * D_MODEL // 8
RET_HEADS = RET_WIDTH // HEAD_DIM
RET_CHUNK = 128
NSA_WIDTH = D_MODEL - POOL_WIDTH - RET_WIDTH
NSA_HEADS = NSA_WIDTH // HEAD_DIM
NSA_KV_HEADS = 2
NSA_KV_WIDTH = NSA_KV_HEADS * HEAD_DIM
CMP_BLOCK = 32
CMP_STRIDE = 16
CMP_HIDDEN = 128
SLC_BLOCK = 64
SLC_TOP = 16
SLC_QCHUNK = 64
WINDOW = 512
WIN_QBLOCK = 128
N_BRANCH = 3
FORCE_SCORE = 1e6
MIX_WIDTH = POOL_WIDTH + RET_WIDTH + NSA_WIDTH
D_FF = 2816
CONV_WIDTH = 3
ROPE_THETA = 10000.0
LN_EPS = 1e-5
GN_EPS = 1e-5
DEEPNORM_ALPHA = (2 * DEPTH) ** 0.25
DEEPNORM_BETA = (8 * DEPTH) ** -0.25
IN_SPLITS = (POOL_WIDTH, RET_WIDTH, RET_WIDTH, RET_WIDTH, RET_WIDTH, NSA_WIDTH,
             NSA_KV_WIDTH, NSA_KV_WIDTH, NSA_KV_WIDTH, NSA_KV_WIDTH, NSA_KV_WIDTH, NSA_KV_WIDTH,
             NSA_HEADS * N_BRANCH)
IN_IS_VALUE = (True, False, False, True, False, False, False, True, False, True, False, True, False)
IN_WIDTH = sum(IN_SPLITS)

kernel_name = 'hybrid_pool_retention_nsa_deepnorm'


def _layernorm(x, g, b):
    xf = x.astype(jnp.float32)
    mu = jnp.mean(xf, axis=-1, keepdims=True)
    var = jnp.mean(jnp.square(xf - mu), axis=-1, keepdims=True)
    return ((xf - mu) * lax.rsqrt(var + LN_EPS) * g + b).astype(x.dtype)


def _rope(x, pos):
    dh = x.shape[-1]
    inv = ROPE_THETA ** (-jnp.arange(0, dh, 2, dtype=jnp.float32) / dh)
    ang = pos.astype(jnp.float32)[..., None] * inv
    cos = jnp.cos(ang)[:, :, None, :]
    sin = jnp.sin(ang)[:, :, None, :]
    xf = x.astype(jnp.float32)
    x1, x2 = xf[..., : dh // 2], xf[..., dh // 2:]
    return jnp.concatenate([x1 * cos - x2 * sin, x2 * cos + x1 * sin], axis=-1).astype(x.dtype)


def _masked_softmax(s, mask):
    s = jnp.where(mask, s, -1e30)
    p = jax.nn.softmax(s, axis=-1)
    return jnp.where(mask, p, 0.0)


def _pool_mixer(v, pool_w, pool_scale):
    b, t, _ = v.shape
    vg = v.reshape(b, t, POOL_GROUPS, POOL_GROUP_WIDTH).astype(jnp.float32)
    cs = jnp.pad(jnp.cumsum(vg, axis=1), ((0, 0), (1, 0), (0, 0), (0, 0)))
    tpos = jnp.arange(t)
    pooled = []
    for gi, w in enumerate(POOL_WINDOWS):
        lo = jnp.maximum(tpos + 1 - w, 0)
        cnt = (tpos + 1 - lo).astype(jnp.float32)[None, :, None]
        pooled.append((cs[:, 1:, gi] - cs[:, lo, gi]) / cnt)
    mixed = (jnp.stack(pooled, axis=2) - vg).astype(v.dtype)
    y = jnp.einsum('btgc,gcd->btgd', mixed, pool_w).reshape(b, t, POOL_WIDTH)
    return y * pool_scale


def _retention(q, k, v, g, gn_g, pos):
    b, t, _ = q.shape
    h, dh, c = RET_HEADS, HEAD_DIM, RET_CHUNK
    n = t // c
    f32 = jnp.float32
    qr = _rope(q.reshape(b, t, h, dh), pos).astype(f32)
    kr = _rope(k.reshape(b, t, h, dh), pos).astype(f32) * (dh ** -0.5)
    qc = qr.reshape(b, n, c, h, dh)
    kc = kr.reshape(b, n, c, h, dh)
    vc = v.reshape(b, n, c, h, dh).astype(f32)
    log_gamma = jnp.log1p(-jnp.power(2.0, -5.0 - jnp.arange(h, dtype=f32)))
    i = jnp.arange(c, dtype=f32)
    diff = i[:, None] - i[None, :]
    dmask = jnp.where(diff >= 0, jnp.exp(log_gamma[:, None, None] * jnp.maximum(diff, 0.0)), 0.0)
    xi = jnp.exp(log_gamma[:, None] * (i + 1.0))
    zeta = jnp.exp(log_gamma[:, None] * (c - 1.0 - i))
    gamma_c = jnp.exp(log_gamma * c)
    s = jnp.einsum('bnihd,bnjhd->bnhij', qc, kc) * dmask
    inner = jnp.einsum('bnhij,bnjhd->bnihd', s, vc)
    kv = jnp.einsum('bnjhd,bnjhe,hj->bnhde', kc, vc, zeta)

    def step(state, kv_i):
        return gamma_c[None, :, None, None] * state + kv_i, state

    _, r_prev = lax.scan(step, jnp.zeros((b, h, dh, dh), f32), jnp.moveaxis(kv, 1, 0))
    cross = jnp.einsum('bnihd,nbhde,hi->bnihe', qc, r_prev, xi)
    o = (inner + cross).reshape(b, t, h, dh)
    mu = jnp.mean(o, axis=-1, keepdims=True)
    var = jnp.mean(jnp.square(o - mu), axis=-1, keepdims=True)
    on = ((o - mu) * lax.rsqrt(var + GN_EPS)).reshape(b, t, RET_WIDTH) * gn_g
    return (jax.nn.silu(g.astype(f32)) * on).astype(q.dtype)


def _compress(kv, pos_emb, w1, b1, w2):
    b, t, _ = kv.shape
    n_cmp = (t - CMP_BLOCK) // CMP_STRIDE + 1
    blk_idx = (jnp.arange(n_cmp) * CMP_STRIDE)[:, None] + jnp.arange(CMP_BLOCK)[None, :]
    blocks = kv.reshape(b, t, NSA_KV_HEADS, HEAD_DIM)[:, blk_idx] + pos_emb[None, None, :, None, :]
    flat = jnp.moveaxis(blocks, 3, 2).reshape(b, n_cmp, NSA_KV_HEADS, CMP_BLOCK * HEAD_DIM)
    return jax.nn.gelu(flat @ w1 + b1) @ w2


def _nsa(q, k_cmp, v_cmp, k_slc, v_slc, k_win, v_win, gate, pos,
         cmp_pos_k, cmp_w1_k, cmp_b1_k, cmp_w2_k, cmp_pos_v, cmp_w1_v, cmp_b1_v, cmp_w2_v):
    b, t, _ = q.shape
    hq, gk, dh = NSA_HEADS, NSA_KV_HEADS, HEAD_DIM
    rep = hq // gk
    f32 = jnp.float32
    tq = jnp.arange(t)
    qg = (_rope(q.reshape(b, t, hq, dh), pos).astype(f32) * (dh ** -0.5)).reshape(b, t, gk, rep, dh)

    n_cmp = (t - CMP_BLOCK) // CMP_STRIDE + 1
    ends = jnp.arange(n_cmp) * CMP_STRIDE + CMP_BLOCK - 1
    kc = _rope(_compress(k_cmp, cmp_pos_k, cmp_w1_k, cmp_b1_k, cmp_w2_k), pos[:, ends]).astype(f32)
    vc = _compress(v_cmp, cmp_pos_v, cmp_w1_v, cmp_b1_v, cmp_w2_v).astype(f32)
    s_c = jnp.einsum('btgrd,bngd->bgrtn', qg, kc)
    p_c = _masked_softmax(s_c, ends[None, :] <= tq[:, None])
    o_cmp = jnp.einsum('bgrtn,bngd->btgrd', p_c, vc)

    n_slc = t // SLC_BLOCK
    ci = jnp.arange(n_cmp)[:, None]
    sj = jnp.arange(n_slc)[None, :]
    overlap = jnp.clip(jnp.minimum(ci * CMP_STRIDE + CMP_BLOCK, (sj + 1) * SLC_BLOCK)
                       - jnp.maximum(ci * CMP_STRIDE, sj * SLC_BLOCK), 0, None).astype(f32) / CMP_STRIDE
    imp = jnp.sum(p_c, axis=2) @ overlap
    blk = jnp.arange(n_slc)[None, :]
    cur = (tq // SLC_BLOCK)[:, None]
    forced = (blk == 0) | (blk == cur) | (blk == cur - 1)
    valid = blk * SLC_BLOCK <= tq[:, None]
    score = jnp.where(valid, jnp.where(forced, FORCE_SCORE, imp), -1.0)
    _, sel = lax.top_k(score, min(SLC_TOP, n_slc))

    ks = _rope(k_slc.reshape(b, t, gk, dh), pos)
    ksb = jnp.moveaxis(ks.reshape(b, n_slc, SLC_BLOCK, gk, dh), 3, 1)
    vsb = jnp.moveaxis(v_slc.reshape(b, n_slc, SLC_BLOCK, gk, dh), 3, 1)
    nq = t // SLC_QCHUNK
    q_ch = jnp.moveaxis(qg.reshape(b, nq, SLC_QCHUNK, gk, rep, dh), 1, 0)
    sel_ch = jnp.moveaxis(jnp.transpose(sel, (0, 2, 1, 3)).reshape(b, nq, SLC_QCHUNK, gk, -1), 1, 0)
    t_ch = tq.reshape(nq, SLC_QCHUNK)
    bi = jnp.arange(b)[:, None, None, None]
    gi = jnp.arange(gk)[None, None, :, None]

    def slc_chunk(args):
        qx, sx, tx = args
        kx = ksb[bi, gi, sx].astype(f32)
        vx = vsb[bi, gi, sx].astype(f32)
        qc_ = qx.shape[1]
        kpos = sx[..., None] * SLC_BLOCK + jnp.arange(SLC_BLOCK)
        mask = jnp.transpose((kpos <= tx[None, :, None, None, None]).reshape(b, qc_, gk, -1), (0, 2, 1, 3))[:, :, None]
        s = jnp.einsum('bqgrd,bqgkld->bgrqkl', qx, kx).reshape(b, gk, rep, qc_, -1)
        p = _masked_softmax(s, mask)
        return jnp.einsum('bgrqm,bqgmd->bqgrd', p, vx.reshape(b, qc_, gk, -1, dh))

    o_slc = jnp.moveaxis(lax.map(slc_chunk, (q_ch, sel_ch, t_ch)), 0, 1).reshape(b, t, gk, rep, dh)

    nw = t // WIN_QBLOCK
    span = WINDOW + WIN_QBLOCK
    kw = _rope(k_win.reshape(b, t, gk, dh), pos)
    kp = jnp.pad(kw, ((0, 0), (WINDOW, 0), (0, 0), (0, 0)))
    vp = jnp.pad(v_win.reshape(b, t, gk, dh), ((0, 0), (WINDOW, 0), (0, 0), (0, 0)))
    qw_ch = jnp.moveaxis(qg.reshape(b, nw, WIN_QBLOCK, gk, rep, dh), 1, 0)

    def win_block(args):
        i, qx = args
        kx = lax.dynamic_slice_in_dim(kp, i * WIN_QBLOCK, span, axis=1).astype(f32)
        vx = lax.dynamic_slice_in_dim(vp, i * WIN_QBLOCK, span, axis=1).astype(f32)
        qpos = i * WIN_QBLOCK + jnp.arange(WIN_QBLOCK)
        kpos = i * WIN_QBLOCK - WINDOW + jnp.arange(span)
        d = qpos[:, None] - kpos[None, :]
        mask = (d >= 0) & (d < WINDOW) & (kpos[None, :] >= 0)
        s = jnp.einsum('bqgrd,bkgd->bgrqk', qx, kx)
        p = _masked_softmax(s, mask)
        return jnp.einsum('bgrqk,bkgd->bqgrd', p, vx)

    o_win = jnp.moveaxis(lax.map(win_block, (jnp.arange(nw), qw_ch)), 0, 1).reshape(b, t, gk, rep, dh)

    gts = jax.nn.sigmoid(gate.astype(f32)).reshape(b, t, gk, rep, N_BRANCH)
    o = gts[..., 0:1] * o_cmp + gts[..., 1:2] * o_slc + gts[..., 2:3] * o_win
    return o.reshape(b, t, NSA_WIDTH).astype(q.dtype)


def _conv_ffn(x, w_gate, w_up, conv_w, conv_b, w_down):
    t = x.shape[1]
    hg = x @ w_gate
    hp = jnp.pad(hg, ((0, 0), (CONV_WIDTH - 1, 0), (0, 0)))
    hc = conv_b
    for kk in range(CONV_WIDTH):
        hc = hc + hp[:, kk:kk + t] * conv_w[kk]
    return (jax.nn.gelu(hc) * (x @ w_up)) @ w_down


def setup_inputs(seed: int = 0) -> dict:
    key = jax.random.key(seed)
    ks = jax.random.split(key, 26)
    nrm = jax.random.normal
    col_scale = jnp.asarray(np.concatenate(
        [np.full((s,), DEEPNORM_BETA if isv else 1.0, np.float32) for s, isv in zip(IN_SPLITS, IN_IS_VALUE)]))
    cf = CMP_BLOCK * HEAD_DIM
    return {
        'x': nrm(ks[0], (BATCH, SEQ, D_MODEL), jnp.float32),
        'positions': (jnp.arange(SEQ, dtype=jnp.int32)[None, :]
                      + jax.random.randint(ks[1], (BATCH, 1), 0, 512, dtype=jnp.int32)),
        'w_in': nrm(ks[2], (DEPTH, D_MODEL, IN_WIDTH), jnp.float32) * (D_MODEL ** -0.5) * col_scale,
        'w_out': nrm(ks[3], (DEPTH, MIX_WIDTH, D_MODEL), jnp.float32) * (MIX_WIDTH ** -0.5) * DEEPNORM_BETA,
        'pool_w': nrm(ks[4], (DEPTH, POOL_GROUPS, POOL_GROUP_WIDTH, POOL_GROUP_WIDTH), jnp.float32) * (POOL_GROUP_WIDTH ** -0.5),
        'pool_scale': 1.0 + 0.1 * nrm(ks[5], (DEPTH, POOL_WIDTH), jnp.float32),
        'ret_gn_g': 1.0 + 0.1 * nrm(ks[6], (DEPTH, RET_WIDTH), jnp.float32),
        'cmp_pos_k': 0.1 * nrm(ks[7], (DEPTH, CMP_BLOCK, HEAD_DIM), jnp.float32),
        'cmp_w1_k': nrm(ks[8], (DEPTH, cf, CMP_HIDDEN), jnp.float32) * (cf ** -0.5),
        'cmp_b1_k': 0.01 * nrm(ks[9], (DEPTH, CMP_HIDDEN), jnp.float32),
        'cmp_w2_k': nrm(ks[10], (DEPTH, CMP_HIDDEN, HEAD_DIM), jnp.float32) * (CMP_HIDDEN ** -0.5),
        'cmp_pos_v': 0.1 * nrm(ks[11], (DEPTH, CMP_BLOCK, HEAD_DIM), jnp.float32),
        'cmp_w1_v': nrm(ks[12], (DEPTH, cf, CMP_HIDDEN), jnp.float32) * (cf ** -0.5),
        'cmp_b1_v': 0.01 * nrm(ks[13], (DEPTH, CMP_HIDDEN), jnp.float32),
        'cmp_w2_v': nrm(ks[14], (DEPTH, CMP_HIDDEN, HEAD_DIM), jnp.float32) * (CMP_HIDDEN ** -0.5),
        'ffn_w_gate': nrm(ks[15], (DEPTH, D_MODEL, D_FF), jnp.float32) * (D_MODEL ** -0.5),
        'ffn_w_up': nrm(ks[16], (DEPTH, D_MODEL, D_FF), jnp.float32) * (D_MODEL ** -0.5),
        'ffn_conv_w': nrm(ks[17], (DEPTH, CONV_WIDTH, D_FF), jnp.float32) * (CONV_WIDTH ** -0.5),
        'ffn_conv_b': 0.01 * nrm(ks[18], (DEPTH, D_FF), jnp.float32),
        'ffn_w_down': nrm(ks[19], (DEPTH, D_FF, D_MODEL), jnp.float32) * (D_FF ** -0.5) * DEEPNORM_BETA,
        'ln1_g': 1.0 + 0.05 * nrm(ks[20], (DEPTH, D_MODEL), jnp.float32),
        'ln1_b': 0.01 * nrm(ks[21], (DEPTH, D_MODEL), jnp.float32),
        'ln2_g': 1.0 + 0.05 * nrm(ks[22], (DEPTH, D_MODEL), jnp.float32),
        'ln2_b': 0.01 * nrm(ks[23], (DEPTH, D_MODEL), jnp.float32),
    }


def reference(x, positions, w_in, w_out, pool_w, pool_scale, ret_gn_g,
              cmp_pos_k, cmp_w1_k, cmp_b1_k, cmp_w2_k, cmp_pos_v, cmp_w1_v, cmp_b1_v, cmp_w2_v,
              ffn_w_gate, ffn_w_up, ffn_conv_w, ffn_conv_b, ffn_w_down,
              ln1_g, ln1_b, ln2_g, ln2_b):
    split_at = [int(s) for s in np.cumsum(IN_SPLITS)[:-1]]
    for l in range(DEPTH):
        h = x @ w_in[l]
        (v_pool, q_ret, k_ret, v_ret, g_ret, q_nsa, k_cmp, v_cmp,
         k_slc, v_slc, k_win, v_win, gate_nsa) = jnp.split(h, split_at, axis=-1)
        y_a = _pool_mixer(v_pool, pool_w[l], pool_scale[l])
        y_b = _retention(q_ret, k_ret, v_ret, g_ret, ret_gn_g[l], positions)
        y_c = _nsa(q_nsa, k_cmp, v_cmp, k_slc, v_slc, k_win, v_win, gate_nsa, positions,
                   cmp_pos_k[l], cmp_w1_k[l], cmp_b1_k[l], cmp_w2_k[l],
                   cmp_pos_v[l], cmp_w1_v[l], cmp_b1_v[l], cmp_w2_v[l])
        mix = jnp.concatenate([y_a, y_b, y_c], axis=-1) @ w_out[l]
        x = _layernorm(DEEPNORM_ALPHA * x + mix, ln1_g[l], ln1_b[l])
        f = _conv_ffn(x, ffn_w_gate[l], ffn_w_up[l], ffn_conv_w[l], ffn_conv_b[l], ffn_w_down[l])
        x = _layernorm(DEEPNORM_ALPHA * x + f, ln2_g[l], ln2_b[l])
    return x
```

```python
import contextlib
import math
import numpy as np
import concourse.bass as bass
import concourse.mybir as mybir
from concourse.bass_utils import run_bass_kernel_spmd

F32 = mybir.dt.float32
BF16 = mybir.dt.bfloat16
I32 = mybir.dt.int32
AF = mybir.ActivationFunctionType
ALU = mybir.AluOpType
AX = mybir.AxisListType

T = 4096
D = 1024
DEPTH = 2
NT = T // 128
DFF = 2816
NF = DFF // 128
ALPHA = (2 * DEPTH) ** 0.25
NEG = -10000.0
DBG = {}

ENGS = ("pe", "act", "dve", "pool", "sp")


class Buf:
    __slots__ = ("name", "writers", "readers", "dsem", "dcount", "is_dram", "vsem")

    def __init__(self, name, is_dram=False):
        self.name = name
        self.is_dram = is_dram
        self.writers = []
        self.readers = []
        self.dsem = None
        self.dcount = 0
        self.vsem = None


class Op:
    __slots__ = ("eng", "emit", "waits", "is_dma", "dbuf", "dval", "needs_inc", "val", "vsem")

    def __init__(self, eng, emit, is_dma=False):
        self.eng = eng
        self.emit = emit
        self.waits = []
        self.is_dma = is_dma
        self.dbuf = None
        self.dval = 0
        self.needs_inc = False
        self.val = 0
        self.vsem = None


class Prog:
    def __init__(self, nc):
        self.nc = nc
        self.ops = {e: [] for e in ENGS}
        self.last = {e: None for e in ENGS}
        self.dma_bufs = {}
        self.pending_bar = {e: [] for e in ENGS}
        self.seq = 0
        self.bar_seq = 0
        self.vfree = {True: [], False: []}
        self.vkind = []
        self.vcount = []
        self.rsem = []

    def _dep(self, op, prod, force=False):
        if prod is op:
            return
        if (not force) and prod.val < self.bar_seq:
            return
        if not prod.is_dma and prod.eng == op.eng:
            if op.eng in ("pe", "sp"):
                return
        op.waits.append(prod)
        if not prod.is_dma:
            prod.needs_inc = True

    @staticmethod
    def _prune(lst):
        last = {}
        for r in lst:
            last[(r.eng, r.is_dma, r.vsem)] = r
        return list(last.values())

    def op(self, eng, emit, reads=(), writes=(), dma=False):
        o = Op(eng, emit, is_dma=dma)
        self.seq += 1
        o.val = self.seq
        if self.pending_bar[eng]:
            for p in self.pending_bar[eng]:
                self._dep(o, p, force=True)
            self.pending_bar[eng] = []
        for b in reads:
            for w in b.writers:
                self._dep(o, w)
        for b in writes:
            for r in b.readers:
                if (not r.is_dma) and r.eng == eng:
                    continue
                self._dep(o, r)
            if not b.readers:
                for w in b.writers:
                    if w.is_dma and dma:
                        continue
                    if (not w.is_dma) and w.eng == eng:
                        continue
                    self._dep(o, w)
        if dma:
            assert len(writes) == 1
            b = writes[0]
            if b.is_dram:
                b = [r for r in reads if not r.is_dram][0]
            if b.vsem is None:
                sw = (eng == "pool")
                if self.vfree[sw]:
                    b.vsem = self.vfree[sw].pop()
                else:
                    b.vsem = len(self.vcount)
                    self.vcount.append(0)
                    self.vkind.append(sw)
                b.dcount = self.vcount[b.vsem]
            b.dcount += 16
            self.vcount[b.vsem] = b.dcount
            o.dbuf = b
            o.dval = b.dcount
            o.vsem = b.vsem
            self.dma_bufs[id(b)] = b
        for b in writes:
            if b.readers:
                b.writers = [o]
                b.readers = []
            else:
                b.writers.append(o)
                if len(b.writers) > 6:
                    b.writers = self._prune(b.writers)
        for b in reads:
            b.readers.append(o)
            if len(b.readers) > 6:
                b.readers = self._prune(b.readers)
        self.ops[eng].append(o)
        if not dma:
            self.last[eng] = o
        return o

    def dma(self, eng, out, in_, reads, writes):
        return self.op(eng, lambda e: e.dma_start(out=out, in_=in_), reads, writes, dma=True)

    def barrier(self):
        targets = []
        for e in ENGS:
            if self.last[e] is not None:
                targets.append(self.last[e])
        for b in self.dma_bufs.values():
            if b.vsem is not None:
                p = Op("sp", None, is_dma=True)
                p.dbuf = b
                p.dval = b.dcount
                p.vsem = b.vsem
                targets.append(p)
                self.vfree[self.vkind[b.vsem]].append(b.vsem)
                b.vsem = None
        self.dma_bufs = {}
        self.seq += 1
        self.bar_seq = self.seq
        for e in ENGS:
            self.pending_bar[e] = self._prune(self.pending_bar[e] + targets)

    def emit_all(self, final_bufs=()):
        nc = self.nc
        esem = {e: nc.alloc_semaphore(name=f"es_{e}") for e in ENGS}
        self.rsem = [nc.alloc_semaphore(name=f"ds_{i}") for i in range(len(self.vcount))]
        for e in ENGS:
            c = 0
            for o in self.ops[e]:
                if (not o.is_dma) and o.needs_inc:
                    c += 1
                    o.val = c
        engobj = {"pe": "tensor", "act": "scalar", "dve": "vector", "pool": "gpsimd", "sp": "sync"}
        with nc.Block() as block:
            for e in ENGS:
                def body(eng, ops=self.ops[e], e=e):
                    waited = {}
                    for o in ops:
                        need = {}
                        for p in o.waits:
                            if p.is_dma:
                                sem, val = self.rsem[p.vsem], p.dval
                            else:
                                sem, val = esem[p.eng], p.val
                            if sem is None:
                                continue
                            if need.get(sem.num, (None, 0))[1] < val:
                                need[sem.num] = (sem, val)
                        for k, (sem, val) in need.items():
                            if waited.get(k, 0) >= val:
                                continue
                            waited[k] = val
                            eng.wait_ge(sem, val)
                        ins = o.emit(eng)
                        if o.is_dma:
                            ins.then_inc(self.rsem[o.vsem], 16)
                        elif o.needs_inc:
                            ins.then_inc(esem[e], 1)
                    if e == "sp":
                        for i, sem in enumerate(self.rsem):
                            if waited.get(sem.num, 0) < self.vcount[i]:
                                eng.wait_ge(sem, self.vcount[i])

                getattr(block, engobj[e])(body)


OFF = {}
_o = 0
for _n, _w in (("v_pool", 256), ("q_ret", 384), ("k_ret", 384), ("v_ret", 384), ("g_ret", 384),
               ("q_nsa", 384), ("k_cmp", 128), ("v_cmp", 128), ("k_slc", 128), ("v_slc", 128),
               ("k_win", 128), ("v_win", 128), ("gate", 18)):
    OFF[_n] = _o
    _o += _w


def _swap_cols(cols):
    cols = np.asarray(cols).reshape(-1, 64)
    return np.concatenate([cols[:, 32:], cols[:, :32]], axis=1).reshape(-1)


def _fm_cols():
    ch = []
    for c in range(2):
        ch.append(np.arange(OFF["v_pool"] + 128 * c, OFF["v_pool"] + 128 * (c + 1)))
    for name in ("q_ret", "k_ret", "q_nsa"):
        for c in range(3):
            cols = np.arange(OFF[name] + 128 * c, OFF[name] + 128 * (c + 1))
            ch.append(cols)
            ch.append(_swap_cols(cols))
    for name in ("k_slc", "k_win"):
        cols = np.arange(OFF[name], OFF[name] + 128)
        ch.append(cols)
        ch.append(_swap_cols(cols))
    for name in ("k_cmp", "v_cmp"):
        ch.append(np.arange(OFF[name], OFF[name] + 128))
    return ch


FM_COLS = _fm_cols()
NFM = len(FM_COLS)
TM_COLS = np.concatenate([np.arange(OFF["v_ret"], OFF["v_ret"] + 384),
                          np.arange(OFF["v_slc"], OFF["v_slc"] + 128),
                          np.arange(OFF["v_win"], OFF["v_win"] + 128),
                          np.arange(OFF["g_ret"], OFF["g_ret"] + 384),
                          np.arange(OFF["gate"], OFF["gate"] + 18)])
NTM = len(TM_COLS)


def _const_tables():
    c = {}
    p = np.arange(128)
    inv = (10000.0 ** (-np.arange(0, 64, 2, dtype=np.float32) / 64)).astype(np.float32)
    c["c_inv"] = inv[p % 32].reshape(128, 1).astype(np.float32)
    c["c_sgn"] = np.where((p % 64) < 32, -1.0, 1.0).reshape(128, 1).astype(np.float32)
    h = np.arange(6, dtype=np.float64)
    lg = np.log1p(-np.power(2.0, -5.0 - h))
    i = np.arange(128, dtype=np.float64)
    dm = np.zeros((128, 6, 128), np.float32)
    for hh in range(6):
        diff = i[None, :] - i[:, None]
        dm[:, hh, :] = np.where(diff >= 0, 0.125 * np.exp(lg[hh] * np.maximum(diff, 0)), 0.0)
    c["c_dm"] = dm
    xi = np.exp(lg[:, None] * (i[None, :] + 1.0))
    zeta = 0.125 * np.exp(lg[:, None] * (127.0 - i[None, :]))
    gam = np.exp(lg * 128.0)
    xir = np.zeros((128, 3, 128), np.float32)
    zt = np.zeros((128, 3, 128), np.float32)
    gc = np.zeros((128, 3), np.float32)
    for ck in range(3):
        for hh in range(2):
            xir[hh * 64:(hh + 1) * 64, ck, :] = xi[2 * ck + hh][None, :]
            zt[:, ck, hh * 64:(hh + 1) * 64] = zeta[2 * ck + hh][:, None]
            gc[hh * 64:(hh + 1) * 64, ck] = gam[2 * ck + hh]
    c["c_xir"] = xir
    c["c_zt"] = zt
    c["c_gc"] = gc
    win = np.zeros((128, 2), np.float32)
    rc = np.zeros((128, 2, 16), np.float32)
    for ck in range(2):
        for hh in range(2):
            w = (2, 4, 8, 16)[2 * ck + hh]
            win[hh * 64:(hh + 1) * 64, ck] = 1.0 / w
            rc[hh * 64:(hh + 1) * 64, ck, :] = 1.0 / np.minimum(np.arange(16) + 1, w)
    c["c_pw"] = win
    c["c_prc"] = rc
    kk = np.arange(128)[:, None]
    qq = np.arange(128)[None, :]
    c["c_caus"] = np.where(kk > qq, NEG, 0.0).astype(np.float32)
    c["c_upper"] = np.where(kk <= qq, NEG, 0.0).astype(np.float32)
    c["c_ident"] = np.eye(128, dtype=np.float32)
    ex = np.zeros((64, T), np.float32)
    ex[np.arange(T) // 64, np.arange(T)] = 1.0
    c["c_expand"] = ex
    n = np.arange(256)
    ends = 16 * n + 31
    cm = np.where(ends[:, None] > np.arange(T)[None, :], NEG, 0.0).astype(np.float32)
    c["c_cmn"] = np.ascontiguousarray(cm.reshape(2, 128, T).transpose(1, 0, 2))
    ci = np.arange(256)[:, None]
    sj = np.arange(64)[None, :]
    ov = np.clip(np.minimum(ci * 16 + 32, (sj + 1) * 64) - np.maximum(ci * 16, sj * 64), 0, None) / 16.0
    ov[255, :] = 0.0
    c["c_ovl"] = np.ascontiguousarray(ov.astype(np.float32).reshape(2, 128, 64).transpose(1, 0, 2))
    tq = np.arange(T)
    cur = tq // 64
    blk = np.arange(64)[None, :]
    forced = (blk == 0) | (blk == cur[:, None]) | (blk == cur[:, None] - 1)
    valid = blk * 64 <= tq[:, None]
    bias = np.where(valid, np.where(forced, 1e6, 0.0), -100.0).astype(np.float32)
    c["c_sbias"] = np.ascontiguousarray(bias.reshape(32, 128, 64).transpose(1, 0, 2))
    return c


CONSTS = _const_tables()


def _layer_arrays(inp, l):
    a = {}
    w_in = inp["w_in"][l]
    wk = w_in.reshape(8, 128, -1)
    a["wfm"] = np.ascontiguousarray(
        np.stack([wk[:, :, cols].transpose(1, 0, 2) for cols in FM_COLS], 0))
    a["wtm"] = np.ascontiguousarray(wk[:, :, TM_COLS].transpose(1, 0, 2))
    a["wout"] = np.ascontiguousarray(inp["w_out"][l].reshape(8, 128, D).transpose(1, 0, 2))
    pw = inp["pool_w"][l]
    bd = np.zeros((2, 128, 128), np.float32)
    for ck in range(2):
        for hh in range(2):
            bd[ck, hh * 64:(hh + 1) * 64, hh * 64:(hh + 1) * 64] = pw[2 * ck + hh]
    a["bd"] = bd
    a["psc"] = np.ascontiguousarray(inp["pool_scale"][l].reshape(2, 128).T)
    a["gng"] = np.ascontiguousarray(inp["ret_gn_g"][l].reshape(1, 384))
    for kv in ("k", "v"):
        w1 = inp[f"cmp_w1_{kv}"][l].reshape(32, 64, 128)
        w1d = np.concatenate([w1, w1], axis=1).transpose(1, 0, 2)
        a[f"w1{kv}"] = np.ascontiguousarray(w1d)
        a[f"b1{kv}"] = np.ascontiguousarray(inp[f"cmp_b1_{kv}"][l].reshape(128, 1))
        a[f"pos{kv}"] = np.ascontiguousarray(inp[f"cmp_pos_{kv}"][l].T)
        a[f"w2{kv}"] = np.ascontiguousarray(inp[f"cmp_w2_{kv}"][l])
    a["w2ks"] = np.ascontiguousarray(inp["cmp_w2_k"][l][:, _swap_cols(np.arange(64))])
    a["wg"] = np.ascontiguousarray(inp["ffn_w_gate"][l].reshape(8, 128, NF, 128).transpose(2, 1, 0, 3))
    a["wu"] = np.ascontiguousarray(inp["ffn_w_up"][l].reshape(8, 128, NF, 128).transpose(2, 1, 0, 3))
    a["wd"] = np.ascontiguousarray(inp["ffn_w_down"][l].reshape(NF, 128, D).transpose(1, 0, 2))
    a["cw"] = np.ascontiguousarray(inp["ffn_conv_w"][l].reshape(3, NF, 128).transpose(2, 1, 0))
    a["cb"] = np.ascontiguousarray(inp["ffn_conv_b"][l].reshape(NF, 128).T)
    for nme in ("ln1_g", "ln1_b", "ln2_g", "ln2_b"):
        a[nme] = np.ascontiguousarray(inp[nme][l].reshape(1, D))
    return a


LAYER_SHAPES = {
    "wfm": [NFM, 128, 8, 128], "wtm": [128, 8, NTM], "wout": [128, 8, D], "bd": [2, 128, 128],
    "psc": [128, 2], "gng": [1, 384],
    "w1k": [128, 32, 128], "b1k": [128, 1], "posk": [64, 32], "w2k": [128, 64],
    "w1v": [128, 32, 128], "b1v": [128, 1], "posv": [64, 32], "w2v": [128, 64], "w2ks": [128, 64],
    "wg": [NF, 128, 8, 128], "wu": [NF, 128, 8, 128], "wd": [128, NF, D], "cw": [128, NF, 3], "cb": [128, NF],
    "ln1_g": [1, D], "ln1_b": [1, D], "ln2_g": [1, D], "ln2_b": [1, D],
}


class Tile:
    __slots__ = ("t", "b")

    def __init__(self, t, b):
        self.t = t
        self.b = b


class Ctx:
    def __init__(self, nc, ext_in=(), ext_out=()):
        self.nc = nc
        self.P = Prog(nc)
        self.ext_in = set(ext_in)
        self.ext_out = set(ext_out)
        self.dram = {}
        self.stack = None
        self.uid = 0

    def dr(self, name, shape, dt, kind=None):
        if kind is None:
            kind = "ExternalInput" if name in self.ext_in else ("ExternalOutput" if name in self.ext_out else "Internal")
        t = self.nc.dram_tensor(name, list(shape), dt, kind=kind).ap()
        tl = Tile(t, Buf(name, is_dram=True))
        self.dram[name] = tl
        return tl

    def sb(self, name, shape, dt, es=None):
        self.uid += 1
        t = (es or self.stack).enter_context(self.nc.sbuf_tensor(f"{name}_{self.uid}", list(shape), dt))
        return Tile(t, Buf(name))

    def ps(self, name, shape, dt, es=None):
        self.uid += 1
        t = (es or self.stack).enter_context(self.nc.psum_tensor(f"{name}_{self.uid}", list(shape), dt))
        return Tile(t, Buf(name))

    @contextlib.contextmanager
    def phase(self):
        old = self.stack
        with contextlib.ExitStack() as es:
            self.stack = es
            yield es
            self.P.barrier()
        self.stack = old

    def dma(self, eng, out, in_, reads, writes):
        self.P.dma(eng, out, in_, [x.b for x in reads], [x.b for x in writes])

    def op(self, eng, fn, reads, writes):
        self.P.op(eng, fn, [x.b for x in reads], [x.b for x in writes])

    def mm(self, out, lhsT, rhs, start, stop, reads, writes, skip=False):
        kw = dict(start=start, stop=stop)
        if skip:
            kw["skip_group_check"] = True
        self.op("pe", lambda e: e.matmul(out, lhsT=lhsT, rhs=rhs, **kw), reads, writes)

    def tr(self, out, in_, ident, reads, writes):
        self.op("pe", lambda e: e.transpose(out, in_, ident), reads, writes)

    def copy(self, eng, out, in_, reads, writes):
        if eng == "act":
            self.op("act", lambda e: e.copy(out=out, in_=in_), reads, writes)
        else:
            self.op(eng, lambda e: e.tensor_copy(out=out, in_=in_), reads, writes)

    def act(self, out, in_, func, reads, writes, **kw):
        self.op("act", lambda e: e.activation(out=out, in_=in_, func=func, **kw), reads, writes)

    def tt(self, eng, out, in0, in1, op, reads, writes):
        self.op(eng, lambda e: e.tensor_tensor(out=out, in0=in0, in1=in1, op=op), reads, writes)

    def ts(self, eng, out, in0, s1, op0, reads, writes, s2=None, op1=None):
        if op1 is None:
            self.op(eng, lambda e: e.tensor_scalar(out=out, in0=in0, scalar1=s1, scalar2=None, op0=op0), reads, writes)
        else:
            self.op(eng, lambda e: e.tensor_scalar(out=out, in0=in0, scalar1=s1, scalar2=s2, op0=op0, op1=op1), reads, writes)

    def stt(self, out, in0, scalar, in1, op0, op1, reads, writes):
        self.op("dve", lambda e: e.scalar_tensor_tensor(out=out, in0=in0, scalar=scalar, in1=in1, op0=op0, op1=op1),
                reads, writes)


def load_cast(cx, name, src_tile, shape, es=None, eng="pool", q="sp"):
    f = cx.sb(name + "_f", shape, F32, es)
    b = cx.sb(name + "_b", shape, BF16, es)
    cx.dma(q, f.t[:], src_tile.t, [src_tile], [f])
    cx.copy(eng, b.t[:], f.t[:], [f], [b])
    return b


def load_f32(cx, name, src_ap, src_tile, shape, es=None, q="sp"):
    f = cx.sb(name, shape, F32, es)
    cx.dma(q, f.t[:], src_ap, [src_tile], [f])
    return f


def build_xT(cx, xd, xT, ident, ntiles, tok0=0):
    with contextlib.ExitStack() as es:
        xs = [cx.sb(f"xs{i}", [128, D], F32, es) for i in range(2)]
        xb = [cx.sb(f"xb{i}", [128, D], BF16, es) for i in range(2)]
        pt = [cx.ps(f"xpt{i}", [128, 8, 128], BF16, es) for i in range(2)]
        for tt in range(ntiles):
            s, bb, p = xs[tt % 2], xb[tt % 2], pt[tt % 2]
            r0 = tok0 + tt * 128
            cx.dma("sp", s.t[:], xd.t[r0:r0 + 128, :], [xd], [s])
            cx.copy("pool", bb.t[:], s.t[:], [s], [bb])
            for k in range(8):
                cx.tr(p.t[:, k, :], bb.t[:, k * 128:(k + 1) * 128], ident.t[:], [bb, ident], [p])
            cx.copy("act" if tt % 2 == 0 else "dve", xT.t[:, :, tt * 128:(tt + 1) * 128], p.t[:], [p], [xT])
        cx.P.barrier()


def layer_norm_store(cx, zps, xres, g_t, b_t, outd, r0, tmp, eng_q="sp"):
    z, st, mv, rs, o = tmp
    for hf in range(2):
        cx.stt(z.t[:, hf * 512:(hf + 1) * 512], xres.t[:, hf * 512:(hf + 1) * 512], ALPHA, zps[hf].t[:],
               ALU.mult, ALU.add, [xres, zps[hf]], [z])
    for hf in range(2):
        cx.op("dve", lambda e, hf=hf: e.bn_stats(out=st.t[:, hf, :], in_=z.t[:, hf * 512:(hf + 1) * 512]), [z], [st])
    cx.op("dve", lambda e: e.bn_aggr(out=mv.t[:], in_=st.t[:]), [st], [mv])
    cx.ts("dve", rs.t[:], mv.t[:, 1:2], 1e-5, ALU.add, [mv], [rs])
    cx.act(rs.t[:], rs.t[:], AF.Sqrt, [rs], [rs])
    cx.op("dve", lambda e: e.reciprocal(out=rs.t[:], in_=rs.t[:]), [rs], [rs])
    cx.ts("dve", o.t[:], z.t[:], mv.t[:, 0:1], ALU.subtract, [z, mv, rs], [o], s2=rs.t[:, 0:1], op1=ALU.mult)
    cx.tt("pool", o.t[:], o.t[:], g_t.t[:], ALU.mult, [o, g_t], [o])
    cx.tt("pool", o.t[:], o.t[:], b_t.t[:], ALU.add, [o, b_t], [o])
    cx.dma(eng_q, outd.t[r0:r0 + 128, :], o.t[:], [o], [outd])


def ln_tmps(cx, es, n=2):
    res = []
    for i in range(n):
        res.append((cx.sb(f"lnz{i}", [128, D], F32, es), cx.sb(f"lnst{i}", [128, 2, 6], F32, es),
                    cx.sb(f"lnmv{i}", [128, 2], F32, es), cx.sb(f"lnrs{i}", [128, 1], F32, es),
                    cx.sb(f"lno{i}", [128, D], F32, es)))
    return res


def prologue_rope(cx, posd, G):
    rope = G["rope"]
    with cx.phase() as es:
        pi_ = cx.sb("posi", [128, T], I32)
        ang = cx.sb("ang", [128, T], F32)
        m = cx.sb("rm", [128, T], F32)
        o = cx.sb("ro", [128, T], F32)
        cx.dma("sp", pi_.t[:], posd.t.broadcast_to([128, T]), [posd], [pi_])
        cx.copy("dve", ang.t[:], pi_.t[:], [pi_], [ang])
        cx.ts("dve", ang.t[:], ang.t[:], G["inv"].t[:, 0:1], ALU.mult, [ang, G["inv"]], [ang])
        ki = cx.sb("rki", [128, T], I32)
        C1 = 6.28125
        C2 = 2.0 * math.pi - 6.28125
        for which, shift in ((0, 0.25), (1, 0.0)):
            cx.ts("dve", m.t[:], ang.t[:], 1.0 / (2.0 * math.pi), ALU.mult, [ang], [m], s2=shift, op1=ALU.add)
            cx.copy("dve", ki.t[:], m.t[:], [m], [ki])
            cx.copy("dve", m.t[:], ki.t[:], [ki], [m])
            cx.stt(o.t[:], m.t[:], -C1, ang.t[:], ALU.mult, ALU.add, [m, ang], [o])
            cx.stt(o.t[:], m.t[:], -C2, o.t[:], ALU.mult, ALU.add, [m, o], [o])
            if which == 0:
                cx.ts("dve", o.t[:], o.t[:], 0.5 * math.pi, ALU.add, [o], [o])
            cx.ts("dve", o.t[:], o.t[:], math.pi, ALU.min, [o], [o], s2=-math.pi, op1=ALU.max)
            cx.act(o.t[:], o.t[:], AF.Sin, [o], [o])
            if which == 1:
                cx.ts("dve", o.t[:], o.t[:], G["sgn"].t[:, 0:1], ALU.mult, [o, G["sgn"]], [o])
            cx.dma("sp", rope.t[which], o.t[:], [o], [rope])


def phase_A(cx, l, xd, W, G, S):
    ident = G["ident"]
    rope = G["rope"]
    with cx.phase():
        xT = cx.sb("xT", [128, 8, T], BF16)
        build_xT(cx, xd, xT, ident, NT)
        with cx.phase() as es:
          if DBG.get("tm", True):
              wtm = load_cast(cx, "wtm", W["wtm"], [128, 8, NTM])
              pss = [cx.ps(f"tmps{i}", [128, 512], F32) for i in range(6)]
              ob = [cx.sb(f"tmob{i}", [128, 640], BF16) for i in range(2)]
              og = [cx.sb(f"tmog{i}", [128, 384], F32) for i in range(2)]
              ogt = [cx.sb(f"tmogt{i}", [128, 18], F32) for i in range(2)]
              for tt in range(NT):
                  p0, p1, p2 = pss[(tt % 2) * 3:(tt % 2) * 3 + 3]
                  for (pp, c0, c1) in ((p0, 0, 512), (p1, 512, 1024), (p2, 1024, NTM)):
                      for k in range(8):
                          cx.mm(pp.t[:, 0:c1 - c0], xT.t[:, k, tt * 128:(tt + 1) * 128], wtm.t[:, k, c0:c1],
                                k == 0, k == 7, [xT, wtm], [pp])
                  b_, g_, t_ = ob[tt % 2], og[tt % 2], ogt[tt % 2]
                  cx.copy("dve", b_.t[:, 0:512], p0.t[:], [p0], [b_])
                  cx.copy("dve", b_.t[:, 512:640], p1.t[:, 0:128], [p1], [b_])
                  cx.act(g_.t[:], p1.t[:, 128:512], AF.Silu, [p1], [g_])
                  cx.act(t_.t[:], p2.t[:, 0:18], AF.Sigmoid, [p2], [t_])
                  r0 = tt * 128
                  cx.dma("sp", S["tmb"].t[r0:r0 + 128, :], b_.t[:], [b_], [S["tmb"]])
                  cx.dma("sp", S["gr"].t[r0:r0 + 128, :], g_.t[:], [g_], [S["gr"]])
                  cx.dma("sp", S["gt"].t[r0:r0 + 128, :], t_.t[:], [t_], [S["gt"]])
        with cx.phase() as es:
          if DBG.get("fm", True):
              C = cx.sb("ropeC", [128, T], F32)
              Sn = cx.sb("ropeS", [128, T], F32)
              cx.dma("sp", C.t[:], rope.t[0], [rope], [C])
              cx.dma("sp", Sn.t[:], rope.t[1], [rope], [Sn])
              wf = [cx.sb(f"wf{i}", [128, 2, 8, 128], F32) for i in range(2)]
              wb = [cx.sb(f"wb{i}", [128, 2, 8, 128], BF16) for i in range(2)]
              pss = [cx.ps(f"fmps{i}", [128, 512], F32) for i in range(8)]
              ost = [cx.sb(f"fmo{i}", [128, T], BF16) for i in range(2)]
              vst = [cx.sb(f"fmv{i}", [128, 512], F32) for i in range(2)]
              t1s = [cx.sb(f"fmt1{i}", [128, 512], F32) for i in range(2)]
              t2s = [cx.sb(f"fmt2{i}", [128, 512], F32) for i in range(2)]
              units = [([0], S["vp"], 0, False), ([1], S["vp"], 128, False)]
              ci = 2
              for dest in ("qr", "kr", "qn"):
                  for c in range(3):
                      units.append(([ci, ci + 1], S[dest], 128 * c, True))
                      ci += 2
              for dest in ("ks", "kw"):
                  units.append(([ci, ci + 1], S[dest], 0, True))
                  ci += 2
              units.append(([ci], S["kc"], 0, False))
              units.append(([ci + 1], S["vc"], 0, False))
              gi = 0
              for ui, (cids, dest, row0, is_rope) in enumerate(units):
                  f_, b_ = wf[ui % 2], wb[ui % 2]
                  n = len(cids)
                  for j, cid in enumerate(cids):
                      cx.dma("pool", f_.t[:, j], W["wfm"].t[cid], [W["wfm"]], [f_])
                  cx.copy("pool", b_.t[:, 0:n], f_.t[:, 0:n], [f_], [b_])
                  o_ = ost[ui % 2]
                  for tc in range(8):
                      ts_ = slice(tc * 512, (tc + 1) * 512)
                      pp = [pss[(gi % 4) * 2 + j] for j in range(n)]
                      gi += 1
                      for j in range(n):
                          for k in range(8):
                              cx.mm(pp[j].t[:], b_.t[:, j, k, :], xT.t[:, k, ts_], k == 0, k == 7, [b_, xT], [pp[j]])
                      if is_rope:
                          t1, t2 = t1s[tc % 2], t2s[tc % 2]
                          cx.tt("dve", t1.t[:], pp[0].t[:], C.t[:, ts_], ALU.mult, [pp[0], C], [t1])
                          cx.tt("dve", t2.t[:], pp[1].t[:], Sn.t[:, ts_], ALU.mult, [pp[1], Sn], [t2])
                          cx.tt("pool", o_.t[:, ts_], t1.t[:], t2.t[:], ALU.add, [t1, t2], [o_])
                      elif dest is S["vp"]:
                          v_ = vst[tc % 2]
                          cx.copy("act", v_.t[:], pp[0].t[:], [pp[0]], [v_])
                          cx.dma("sp", dest.t[row0:row0 + 128, ts_], v_.t[:], [v_], [dest])
                      else:
                          cx.copy("act", o_.t[:, ts_], pp[0].t[:], [pp[0]], [o_])
                  if dest is not S["vp"]:
                      cx.dma("sp", dest.t[row0:row0 + 128, :], o_.t[:], [o_], [dest])


def phase_B(cx, l, W, G, S):
    with cx.phase():
        psc = load_f32(cx, "psc", W["psc"].t, W["psc"], [128, 2])
        pw = G["pw"]
        prc = G["prc"]
        pss = [cx.ps(f"bps{i}", [128, 512], F32) for i in range(4)]
        for ck in range(2):
            bd = load_cast(cx, f"bd{ck}", Tile(W["bd"].t[ck], W["bd"].b), [128, 128])
            v = cx.sb(f"pv{ck}", [128, 16 + T], F32)
            s2 = cx.sb(f"ps2{ck}", [128, 16 + T], F32)
            s4 = cx.sb(f"ps4{ck}", [128, 16 + T], F32)
            mx = cx.sb(f"pmx{ck}", [128, T], BF16)
            o = cx.sb(f"pbo{ck}", [128, T], BF16)
            for t_ in (v, s2, s4):
                cx.op("pool", lambda e, t_=t_: e.memset(t_.t[:, 0:16], 0.0), [], [t_])
            cx.dma("sp", v.t[:, 16:], S["vp"].t[ck * 128:(ck + 1) * 128, :], [S["vp"]], [v])
            if ck == 0:
                cx.tt("dve", s2.t[:, 16:], v.t[:, 16:], v.t[:, 15:15 + T], ALU.add, [v], [s2])
                cx.tt("dve", s4.t[64:128, 16:], s2.t[64:128, 16:], s2.t[64:128, 14:14 + T], ALU.add, [s2], [s4])
                lo, hi = s2, s4
            else:
                cx.tt("dve", s2.t[:, 16:], v.t[:, 16:], v.t[:, 15:15 + T], ALU.add, [v], [s2])
                cx.tt("dve", s4.t[:, 16:], s2.t[:, 16:], s2.t[:, 14:14 + T], ALU.add, [s2], [s4])
                cx.tt("dve", s2.t[:, 16:], s4.t[:, 16:], s4.t[:, 12:12 + T], ALU.add, [s4], [s2])
                cx.tt("dve", s4.t[64:128, 16:], s2.t[64:128, 16:], s2.t[64:128, 8:8 + T], ALU.add, [s2], [s4])
                lo, hi = s2, s4
            for (src, r) in ((lo, slice(0, 64)), (hi, slice(64, 128))):
                cx.stt(mx.t[r, :], src.t[r, 16:], pw.t[r, ck:ck + 1], v.t[r, 16:], ALU.mult, ALU.subtract,
                       [src, pw, v], [mx])
                cx.tt("dve", src.t[r, 0:16], src.t[r, 16:32], prc.t[r, ck, :], ALU.mult, [src, prc, mx], [src])
                cx.tt("dve", mx.t[r, 0:16], src.t[r, 0:16], v.t[r, 16:32], ALU.subtract, [src, v], [mx])
            for tc in range(8):
                ts_ = slice(tc * 512, (tc + 1) * 512)
                pp = pss[tc % 4]
                cx.mm(pp.t[:], bd.t[:], mx.t[:, ts_], True, True, [bd, mx], [pp])
                cx.act(o.t[:, ts_], pp.t[:], AF.Copy, [pp, psc], [o], scale=psc.t[:, ck:ck + 1])
            cx.dma("sp", S["yt"].t[ck * 128:(ck + 1) * 128, :], o.t[:], [o], [S["yt"]])


def phase_C(cx, l, W, G, S):
    Cd = G["C"]
    ident = G["ident"]
    with cx.phase() as es:
        dm = load_f32(cx, "dm", Cd["c_dm"].t, Cd["c_dm"], [128, 6, 128])
        xir = load_f32(cx, "xir", Cd["c_xir"].t, Cd["c_xir"], [128, 3, 128])
        zt = load_f32(cx, "zt", Cd["c_zt"].t, Cd["c_zt"], [128, 3, 128])
        gc = load_f32(cx, "gc", Cd["c_gc"].t, Cd["c_gc"], [128, 3])
        gng = load_f32(cx, "gng", W["gng"].t.broadcast_to([128, 384]), W["gng"], [128, 384])
        sps = [cx.ps(f"csp{i}", [128, 128], F32) for i in range(2)]
        ops_ = [cx.ps(f"cop{i}", [128, 128], F32) for i in range(2)]
        kvp = cx.ps("ckv", [128, 128], F32)
        ktp = [cx.ps(f"cktp{i}", [128, 128], BF16) for i in range(2)]
        ytp = cx.ps("cytp", [128, 128], BF16)
        sms = [cx.sb(f"csm{i}", [128, 128], BF16) for i in range(4)]
        kzs = [cx.sb(f"ckz{i}", [128, 128], BF16) for i in range(2)]
        sts = [cx.sb(f"cst{i}", [128, 2, 6], F32) for i in range(2)]
        mvs = [cx.sb(f"cmv{i}", [128, 2, 2], F32) for i in range(2)]
        rss = [cx.sb(f"crs{i}", [128, 2], F32) for i in range(2)]
        ons = [cx.sb(f"con{i}", [128, 128], F32) for i in range(2)]
        onb = [cx.sb(f"conb{i}", [128, 128], BF16) for i in range(2)]
        R = cx.sb("cR", [128, 64], F32)
        Rb = cx.sb("cRb", [128, 64], BF16)
        for ck in range(3):
            qT = cx.sb(f"cqT{ck}", [128, T], BF16)
            kT = cx.sb(f"ckT{ck}", [128, T], BF16)
            qx = cx.sb(f"cqx{ck}", [128, T], BF16)
            v = cx.sb(f"cv{ck}", [128, NT, 128], BF16)
            sg = cx.sb(f"csg{ck}", [128, NT, 128], F32)
            yT = cx.sb(f"cyT{ck}", [128, T], BF16)
            rows = slice(ck * 128, (ck + 1) * 128)
            cx.dma("sp", qT.t[:], S["qr"].t[rows, :], [S["qr"]], [qT])
            cx.dma("sp", kT.t[:], S["kr"].t[rows, :], [S["kr"]], [kT])
            cx.dma("pool", v.t[:], S["tmb"].t[:, ck * 128:(ck + 1) * 128].rearrange("(n p) c -> p n c", p=128),
                   [S["tmb"]], [v])
            cx.dma("pool", sg.t[:], S["gr"].t[:, ck * 128:(ck + 1) * 128].rearrange("(n p) c -> p n c", p=128),
                   [S["gr"]], [sg])
            cx.tt("pool", qx.t[:].rearrange("p (n i) -> p n i", i=128), qT.t[:].rearrange("p (n i) -> p n i", i=128),
                  xir.t[:, ck:ck + 1, :].broadcast_to([128, NT, 128]), ALU.mult, [qT, xir], [qx])
            cx.op("pool", lambda e: e.memset(R.t[:], 0.0), [], [R])
            cx.op("pool", lambda e: e.memset(Rb.t[:], 0.0), [], [Rb])
            for n in range(NT):
                ns = slice(n * 128, (n + 1) * 128)
                kz, ktp_ = kzs[n % 2], ktp[n % 2]
                cx.tr(ktp_.t[:], kT.t[:, ns], ident.t[:], [kT, ident], [ktp_])
                cx.tt("dve", kz.t[:], ktp_.t[:], zt.t[:, ck, :], ALU.mult, [ktp_, zt], [kz])
                op_ = ops_[n % 2]
                for hh in range(2):
                    r = slice(hh * 64, (hh + 1) * 64)
                    sp_, sm = sps[hh], sms[(n % 2) * 2 + hh]
                    cx.mm(sp_.t[:], kT.t[r, ns], qT.t[r, ns], True, True, [kT, qT], [sp_])
                    cx.tt("dve", sm.t[:], sp_.t[:], dm.t[:, 2 * ck + hh, :], ALU.mult, [sp_, dm], [sm])
                for hh in range(2):
                    r = slice(hh * 64, (hh + 1) * 64)
                    sm = sms[(n % 2) * 2 + hh]
                    cx.mm(op_.t[:, r], sm.t[:], v.t[:, n, r], True, False, [sm, v], [op_])
                    cx.mm(op_.t[:, r], qx.t[r, ns], Rb.t[r, :], False, True, [qx, Rb], [op_])
                cx.mm(kvp.t[:], kz.t[:], v.t[:, n, :], True, True, [kz, v], [kvp])
                for hh in range(2):
                    r = slice(hh * 64, (hh + 1) * 64)
                    cx.stt(R.t[r, :], R.t[r, :], gc.t[r, ck:ck + 1], kvp.t[r, r], ALU.mult, ALU.add, [R, gc, kvp], [R])
                cx.copy("pool", Rb.t[:], R.t[:], [R], [Rb])
                st, mv, rs, on, ob = sts[n % 2], mvs[n % 2], rss[n % 2], ons[n % 2], onb[n % 2]
                for hh in range(2):
                    r = slice(hh * 64, (hh + 1) * 64)
                    cx.op("dve", lambda e, hh=hh, r=r, st=st, op_=op_: e.bn_stats(out=st.t[:, hh, :], in_=op_.t[:, r]), [op_], [st])
                    cx.op("dve", lambda e, hh=hh, st=st, mv=mv: e.bn_aggr(out=mv.t[:, hh, :], in_=st.t[:, hh, :]), [st], [mv])
                cx.ts("dve", rs.t[:], mv.t[:, :, 1], 1e-5, ALU.add, [mv], [rs])
                cx.act(rs.t[:], rs.t[:], AF.Sqrt, [rs], [rs])
                cx.op("dve", lambda e, rs=rs: e.reciprocal(out=rs.t[:], in_=rs.t[:]), [rs], [rs])
                for hh in range(2):
                    r = slice(hh * 64, (hh + 1) * 64)
                    cx.ts("dve", on.t[:, r], op_.t[:, r], mv.t[:, hh, 0:1], ALU.subtract, [op_, mv, rs], [on],
                          s2=rs.t[:, hh:hh + 1], op1=ALU.mult)
                cx.tt("pool", on.t[:], on.t[:], gng.t[:, ck * 128:(ck + 1) * 128], ALU.mult, [on, gng], [on])
                cx.tt("pool", ob.t[:], on.t[:], sg.t[:, n, :], ALU.mult, [on, sg], [ob])
                cx.tr(ytp.t[:], ob.t[:], ident.t[:], [ob, ident], [ytp])
                cx.copy("act", yT.t[:, ns], ytp.t[:], [ytp], [yT])
            cx.dma("sp", S["yt"].t[256 + ck * 128:256 + (ck + 1) * 128, :], yT.t[:], [yT], [S["yt"]])


def phase_D(cx, l, W, G, S):
    Cd = G["C"]
    ident = G["ident"]
    rope = G["rope"]
    with cx.phase() as es:
        KCT = cx.sb("KCT", [64, 2, 256], BF16)
        VCX = cx.sb("VCX", [128, 2, 2, 129], BF16)
        with cx.phase():
            Cc = cx.sb("dCc", [64, T], F32)
            Sc = cx.sb("dSc", [64, T], F32)
            cx.dma("sp", Cc.t[:], rope.t[0][0:64, :], [rope], [Cc])
            cx.dma("sp", Sc.t[:], rope.t[1][0:64, :], [rope], [Sc])
            ovl = load_f32(cx, "ovl", Cd["c_ovl"].t, Cd["c_ovl"], [128, 2, 64])
            hps = [cx.ps(f"dhp{i}", [128, 512], F32) for i in range(2)]
            cps = cx.ps("dcp", [128, 512], F32)
            kp = cx.ps("dkp", [128, 2, 256], F32)
            ksp = cx.ps("dksp", [128, 2, 256], F32)
            vps = [cx.ps(f"dvp{i}", [128, 512], F32) for i in range(2)]
            cx.op("pool", lambda e: e.memset(KCT.t[:], 0.0), [], [KCT])
            cx.op("pool", lambda e: e.memset(VCX.t[:, :, :, 64:65], 1.0), [], [VCX])
            for g in range(2):
                cx.copy("pool", VCX.t[:, g, :, 65:129], ovl.t[:], [ovl], [VCX])
            for kv in ("k", "v"):
                src = S["kc"] if kv == "k" else S["vc"]
                kvT = cx.sb(f"dkvT{kv}", [128, T], BF16)
                cx.dma("sp", kvT.t[:], src.t, [src], [kvT])
                w1 = load_cast(cx, f"w1{kv}", W[f"w1{kv}"], [128, 32, 128])
                pos = load_cast(cx, f"pos{kv}", W[f"pos{kv}"], [64, 32], eng="dve")
                b1 = load_f32(cx, f"b1{kv}", W[f"b1{kv}"].t, W[f"b1{kv}"], [128, 1])
                w2 = load_cast(cx, f"w2{kv}", W[f"w2{kv}"], [128, 64], eng="dve")
                cb = cx.sb(f"dcb{kv}", [128, 1], F32)
                h1 = cx.sb(f"dh1{kv}", [128, 2, 256], BF16)
                cx.op("pool", lambda e, h1=h1: e.memset(h1.t[:], 0.0), [], [h1])
                for i in range(32):
                    cx.mm(cps.t[:, 0:1], w1.t[0:64, i, :], pos.t[0:64, i:i + 1], i == 0, i == 31, [w1, pos], [cps])
                cx.tt("dve", cb.t[:], cps.t[:, 0:1], b1.t[:], ALU.add, [cps, b1], [cb])
                for g in range(2):
                    r = slice(g * 64, (g + 1) * 64)
                    for i in range(32):
                        cx.mm(hps[g].t[:, 0:255], w1.t[r, i, :], kvT.t[r, i:i + 16 * 254 + 1:16], i == 0, i == 31,
                              [w1, kvT], [hps[g]])
                    cx.act(h1.t[:, g, 0:255], hps[g].t[:, 0:255], AF.Gelu_apprx_tanh, [hps[g], cb], [h1],
                           bias=cb.t[:, 0:1])
                if kv == "k":
                    w2s = load_cast(cx, "w2ks", W["w2ks"], [128, 64], eng="dve")
                    cx.mm(kp.t[0:64], w2.t[:], h1.t[:], True, True, [w2, h1], [kp])
                    cx.mm(ksp.t[0:64], w2s.t[:], h1.t[:], True, True, [w2s, h1], [ksp])
                    t1 = cx.sb("dkt1", [64, 2, 255], F32)
                    t2 = cx.sb("dkt2", [64, 2, 255], F32)
                    cview = Cc.t[:, 31::16].unsqueeze(1).broadcast_to([64, 2, 255])
                    sview = Sc.t[:, 31::16].unsqueeze(1).broadcast_to([64, 2, 255])
                    cx.tt("dve", t1.t[:], kp.t[0:64, :, 0:255], cview, ALU.mult, [kp, Cc], [t1])
                    cx.tt("dve", t2.t[:], ksp.t[0:64, :, 0:255], sview, ALU.mult, [ksp, Sc], [t2])
                    cx.tt("dve", KCT.t[:, :, 0:255], t1.t[:], t2.t[:], ALU.add, [t1, t2], [KCT])
                else:
                    for g in range(2):
                        for nt in range(2):
                            vp_ = vps[(g * 2 + nt) % 2]
                            cx.mm(vp_.t[:, 0:64], h1.t[:, g, nt * 128:(nt + 1) * 128], w2.t[:], True, True, [h1, w2], [vp_])
                            cx.copy("act", VCX.t[:, g, nt, 0:64], vp_.t[:, 0:64], [vp_], [VCX])
        dstop = DBG.get("d_stop", 9)
        if dstop <= 1:
            return
        identb = ident
        QA = [cx.sb(f"QA{h}", [128, T], BF16) for h in range(6)]
        KSA = [cx.sb(f"KSA{g}", [128, T], BF16) for g in range(2)]
        KW = [cx.sb(f"KW{g}", [64, T], BF16) for g in range(2)]
        VSX = cx.sb("VSX", [128, NT, 2, 65], BF16)
        VWX = cx.sb("VWX", [128, NT, 2, 65], BF16)
        CMN = cx.sb("CMN", [128, 2, T], BF16)
        SB_ = load_f32(cx, "sbias", Cd["c_sbias"].t, Cd["c_sbias"], [128, NT, 64])
        GT = cx.sb("GTs", [128, NT, 18], F32)
        cx.dma("sp", GT.t[:], S["gt"].t.rearrange("(n p) c -> p n c", p=128), [S["gt"]], [GT])
        causb = cx.sb("causb", [128, 128], BF16)
        upperb = cx.sb("upperb", [128, 128], BF16)
        with contextlib.ExitStack() as ts_:
            stg = cx.sb("dstg", [128, T], F32, ts_)
            cx.dma("sp", stg.t[:, 0:128], Cd["c_caus"].t, [Cd["c_caus"]], [stg])
            cx.copy("dve", causb.t[:], stg.t[:, 0:128], [stg], [causb])
            cx.dma("sp", stg.t[:, 0:128], Cd["c_upper"].t, [Cd["c_upper"]], [stg])
            cx.copy("dve", upperb.t[:], stg.t[:, 0:128], [stg], [upperb])
            for nt in range(2):
                cx.dma("sp", stg.t[:], Cd["c_cmn"].t[:, nt, :], [Cd["c_cmn"]], [stg])
                cx.copy("dve", CMN.t[:, nt, :], stg.t[:], [stg], [CMN])
            cx.dma("sp", stg.t[64:128, :], Cd["c_expand"].t, [Cd["c_expand"]], [stg])
            for g in range(2):
                cx.copy("dve", KSA[g].t[64:128, :], stg.t[64:128, :], [stg], [KSA[g]])
            cx.P.barrier()
        for h in range(6):
            cx.dma("sp", QA[h].t[0:64, :], S["qn"].t[h * 64:(h + 1) * 64, :], [S["qn"]], [QA[h]])
            cx.op("pool", lambda e, h=h: e.memset(QA[h].t[64:128, :], 0.0), [], [QA[h]])
        for g in range(2):
            cx.dma("sp", KSA[g].t[0:64, :], S["ks"].t[g * 64:(g + 1) * 64, :], [S["ks"]], [KSA[g]])
            cx.dma("sp", KW[g].t[:], S["kw"].t[g * 64:(g + 1) * 64, :], [S["kw"]], [KW[g]])
            cx.dma("pool", VSX.t[:, :, g, 0:64],
                   S["tmb"].t[:, 384 + g * 64:384 + (g + 1) * 64].rearrange("(n p) c -> p n c", p=128), [S["tmb"]], [VSX])
            cx.dma("pool", VWX.t[:, :, g, 0:64],
                   S["tmb"].t[:, 512 + g * 64:512 + (g + 1) * 64].rearrange("(n p) c -> p n c", p=128), [S["tmb"]], [VWX])
        cx.op("pool", lambda e: e.memset(VSX.t[:, :, :, 64:65], 1.0), [], [VSX])
        cx.op("pool", lambda e: e.memset(VWX.t[:, :, :, 64:65], 1.0), [], [VWX])
        if dstop <= 2:
            return
        SP = [cx.ps(f"dS{i}", [128, 512], F32) for i in range(3)]
        OP = [cx.ps(f"dO{i}", [128, 512], F32) for i in range(3)]
        TP = [cx.ps(f"dT{i}", [128, 4, 128], BF16) for i in range(2)]
        pTs = [cx.sb(f"dpT{i}", [128, 512], BF16) for i in range(4)]
        OACC = [cx.sb(f"dOACC{i}", [128, 4, 384], F32) for i in range(2)]
        IMP = [cx.sb(f"dIMP{i}", [128, 4, 64], F32) for i in range(2)]
        recs = [cx.sb(f"drec{i}", [128, 2], F32) for i in range(4)]
        sc1 = [cx.sb(f"dsc1{i}", [128, 64], F32) for i in range(2)]
        sc2 = [cx.sb(f"dsc2{i}", [128, 64], F32) for i in range(2)]
        m8 = [cx.sb(f"dm8{i}", [128, 16], F32) for i in range(2)]
        nm = [cx.sb(f"dnm{i}", [128, 128], BF16) for i in range(2)]
        for t_ in nm:
            cx.op("pool", lambda e, t_=t_: e.memset(t_.t[:], 0.0), [], [t_])
        obf = [cx.sb(f"dobf{i}", [128, 384], BF16) for i in range(2)]
        yst = [cx.sb(f"dyst{i}", [128, 3, 512], BF16) for i in range(2)]
        cnt = {"s": 0, "o": 0, "p": 0, "r": 0, "t": 0, "n": 0}

        def nxt(key, lst):
            x = lst[cnt[key] % len(lst)]
            cnt[key] += 1
            return x

        def evac(o_view, den_view, gate_col, acc_view, first, o_buf, acc_buf):
            rc = nxt("r", recs)
            cx.ts("dve", rc.t[:, 0:1], den_view, 1e-30, ALU.add, [o_buf], [rc])
            cx.op("dve", lambda e, rc=rc: e.reciprocal(out=rc.t[:, 0:1], in_=rc.t[:, 0:1]), [rc], [rc])
            cx.tt("dve", rc.t[:, 1:2], rc.t[:, 0:1], gate_col, ALU.mult, [rc, GT], [rc])
            if first:
                cx.ts("dve", acc_view, o_view, rc.t[:, 1:2], ALU.mult, [o_buf, rc], [acc_buf])
            else:
                cx.stt(acc_view, o_view, rc.t[:, 1:2], acc_view, ALU.mult, ALU.add, [o_buf, rc, acc_buf], [acc_buf])
            return rc

        for c in range(8):
            cs = slice(c * 512, (c + 1) * 512)
            oacc = OACC[c % 2]
            imp = IMP[c % 2]
            nts = [0] + ([1] if c >= 4 else [])
            for h in range(6):
                g, rr = divmod(h, 3)
                pts = {}
                for nt in nts:
                    sp_ = nxt("s", SP)
                    need_mask = (c <= 4) if nt == 0 else True
                    cx.mm(sp_.t[:], KCT.t[0:64, g, nt * 128:(nt + 1) * 128], QA[h].t[0:64, cs], True, True,
                          [KCT, QA[h]], [sp_])
                    if need_mask:
                        cx.mm(sp_.t[:], identb.t[:], CMN.t[:, nt, cs], False, True, [identb, CMN], [sp_], skip=True)
                    pT = nxt("p", pTs)
                    cx.act(pT.t[:], sp_.t[:], AF.Exp, [sp_], [pT], scale=0.125)
                    pts[nt] = pT
                for q4 in range(4):
                    qt = 4 * c + q4
                    ob_ = nxt("o", OP)
                    for j, nt in enumerate(nts):
                        cx.mm(ob_.t[:, 0:129], pts[nt].t[:, q4 * 128:(q4 + 1) * 128], VCX.t[:, g, nt, :],
                              j == 0, j == len(nts) - 1, [pts[nt], VCX], [ob_])
                    rc = evac(ob_.t[:, 0:64], ob_.t[:, 64:65], GT.t[:, qt, 3 * h:3 * h + 1],
                              oacc.t[:, q4, h * 64:(h + 1) * 64], True, ob_, oacc)
                    if rr == 0:
                        cx.ts("dve", imp.t[:, q4, :], ob_.t[:, 65:129], rc.t[:, 0:1], ALU.mult, [ob_, rc], [imp])
                    else:
                        cx.stt(imp.t[:, q4, :], ob_.t[:, 65:129], rc.t[:, 0:1], imp.t[:, q4, :], ALU.mult, ALU.add,
                               [ob_, rc, imp], [imp])
                if rr == 2 and dstop >= 4:
                    for q4 in range(4):
                        qt = 4 * c + q4
                        s1, s2, mm8, nm_ = sc1[q4 % 2], sc2[q4 % 2], m8[q4 % 2], nm[q4 % 2]
                        cx.tt("dve", s1.t[:], imp.t[:, q4, :], SB_.t[:, qt, :], ALU.add, [imp, SB_], [s1])
                        cx.op("dve", lambda e, mm8=mm8, s1=s1: e.max(out=mm8.t[:, 0:8], in_=s1.t[:]), [s1], [mm8])
                        cx.op("dve", lambda e, mm8=mm8, s1=s1, s2=s2: e.match_replace(
                            out=s2.t[:], in_to_replace=mm8.t[:, 0:8], in_values=s1.t[:], imm_value=-1e9), [s1, mm8], [s2])
                        cx.op("dve", lambda e, mm8=mm8, s2=s2: e.max(out=mm8.t[:, 8:16], in_=s2.t[:]), [s2], [mm8])
                        cx.ts("dve", mm8.t[:, 15:16], mm8.t[:, 15:16], 0.0, ALU.max, [mm8], [mm8])
                        cx.ts("dve", nm_.t[:, 64:128], s1.t[:], mm8.t[:, 15:16], ALU.is_lt, [s1, mm8], [nm_],
                              s2=NEG, op1=ALU.mult)
                        if DBG.get("sel", 9) < 2:
                            continue
                        tp = nxt("t", TP)
                        cx.tr(tp.t[:, 0, :], nm_.t[:], ident.t[:], [nm_, ident], [tp])
                        if DBG.get("sel", 9) < 3:
                            continue
                        for r3 in range(3):
                            hh = 3 * g + r3
                            cx.copy("dve",
                                    QA[hh].t[64:128, qt * 128:(qt + 1) * 128],
                                    tp.t[64:128, 0, :], [tp], [QA[hh]])
            for branch in ((1, 2) if dstop >= 6 else ((1,) if dstop >= 5 else ())):
                for h in range(6):
                    g = h // 3
                    ob_ = nxt("o", OP)
                    ov = ob_.t[:, 0:260].rearrange("p (a b) -> p a b", b=65)
                    kts = range(0, 4 * c + 4) if branch == 1 else range(max(4 * c - 4, 0), 4 * c + 4)
                    first = True
                    for kt in kts:
                        lo = max(kt - 4 * c, 0)
                        hi = 3 if branch == 1 else min(kt + 4 - 4 * c, 3)
                        n_ = (hi - lo + 1) * 128
                        q0 = c * 512 + lo * 128
                        ks_ = slice(kt * 128, (kt + 1) * 128)
                        sp_ = nxt("s", SP)
                        if branch == 1:
                            cx.mm(sp_.t[:, 0:n_], KSA[g].t[:, ks_], QA[h].t[:, q0:q0 + n_], True, True,
                                  [KSA[g], QA[h]], [sp_])
                        else:
                            cx.mm(sp_.t[:, 0:n_], KW[g].t[0:64, ks_], QA[h].t[0:64, q0:q0 + n_], True, True,
                                  [KW[g], QA[h]], [sp_])
                        if kt >= 4 * c:
                            cx.mm(sp_.t[:, 0:128], identb.t[:], causb.t[:], False, True, [identb, causb], [sp_], skip=True)
                        if branch == 2 and kt + 4 <= 4 * c + 3 and kt + 4 >= 4 * c:
                            cx.mm(sp_.t[:, n_ - 128:n_], identb.t[:], upperb.t[:], False, True, [identb, upperb], [sp_],
                                  skip=True)
                        pT = nxt("p", pTs)
                        cx.act(pT.t[:, 0:n_], sp_.t[:, 0:n_], AF.Exp, [sp_], [pT], scale=0.125)
                        vx = VSX if branch == 1 else VWX
                        for q4 in range(lo, hi + 1):
                            cx.mm(ov[:, q4, :], pT.t[:, (q4 - lo) * 128:(q4 - lo + 1) * 128], vx.t[:, kt, g, :],
                                  first, True, [pT, vx], [ob_], skip=not first)
                            first = False
                    for q4 in range(4):
                        qt = 4 * c + q4
                        evac(ov[:, q4, 0:64], ov[:, q4, 64:65], GT.t[:, qt, 3 * h + branch:3 * h + branch + 1],
                             oacc.t[:, q4, h * 64:(h + 1) * 64], False, ob_, oacc)
            ys = yst[c % 2]
            for q4 in range(4):
                ob2 = obf[q4 % 2]
                cx.copy("pool", ob2.t[:], oacc.t[:, q4, :], [oacc], [ob2])
                tp = nxt("t", TP)
                for j in range(3):
                    cx.tr(tp.t[:, j, :], ob2.t[:, j * 128:(j + 1) * 128], ident.t[:], [ob2, ident], [tp])
                cx.copy("act", ys.t[:, :, q4 * 128:(q4 + 1) * 128], tp.t[:, 0:3, :], [tp], [ys])
            for j in range(3):
                cx.dma("sp", S["yt"].t[640 + j * 128:640 + (j + 1) * 128, cs], ys.t[:, j, :], [ys], [S["yt"]])


def phase_E(cx, l, xd, x1d, W, G, S):
    with cx.phase() as es:
        wo = load_cast(cx, "wout", W["wout"], [128, 8, D])
        g_t = load_f32(cx, "ln1g", W["ln1_g"].t.broadcast_to([128, D]), W["ln1_g"], [128, D])
        b_t = load_f32(cx, "ln1b", W["ln1_b"].t.broadcast_to([128, D]), W["ln1_b"], [128, D])
        yts = [cx.sb(f"eyt{i}", [128, 8, 512], BF16) for i in range(2)]
        xrs = [cx.sb(f"exr{i}", [128, D], F32) for i in range(2)]
        pss = [cx.ps(f"eps{i}", [128, 512], F32) for i in range(4)]
        tmps = ln_tmps(cx, es)
        for tc in range(8):
            y_ = yts[tc % 2]
            cx.dma("pool", y_.t[:], S["yt"].t[:, tc * 512:(tc + 1) * 512].rearrange("(k p) t -> p k t", p=128),
                   [S["yt"]], [y_])
            for q in range(4):
                tt = tc * 4 + q
                xr = xrs[tt % 2]
                cx.dma("sp", xr.t[:], xd.t[tt * 128:(tt + 1) * 128, :], [xd], [xr])
                zp = pss[(tt % 2) * 2:(tt % 2) * 2 + 2]
                for hf in range(2):
                    for k in range(8):
                        cx.mm(zp[hf].t[:], y_.t[:, k, q * 128:(q + 1) * 128], wo.t[:, k, hf * 512:(hf + 1) * 512],
                              k == 0, k == 7, [y_, wo], [zp[hf]])
                layer_norm_store(cx, zp, xr, g_t, b_t, x1d, tt * 128, tmps[tt % 2])


def phase_F(cx, l, x1d, x2d, W, G, S):
    TC = 1024
    ident = G["ident"]
    with cx.phase() as es:
        wd = cx.sb("wd_b", [128, NF, D], BF16)
        wdf = [cx.sb(f"wdf{i}", [128, D], F32) for i in range(2)]
        for f in range(NF):
            s_ = wdf[f % 2]
            cx.dma("pool", s_.t[:], W["wd"].t[:, f, :], [W["wd"]], [s_])
            cx.copy("pool", wd.t[:, f, :], s_.t[:], [s_], [wd])
        cw = load_f32(cx, "cw", W["cw"].t, W["cw"], [128, NF, 3])
        cb = load_f32(cx, "cb", W["cb"].t, W["cb"], [128, NF])
        g_t = load_f32(cx, "ln2g", W["ln2_g"].t.broadcast_to([128, D]), W["ln2_g"], [128, D])
        b_t = load_f32(cx, "ln2b", W["ln2_b"].t.broadcast_to([128, D]), W["ln2_b"], [128, D])
        carry = cx.sb("carry", [128, NF, 2], F32)
        cx.op("pool", lambda e: e.memset(carry.t[:], 0.0), [], [carry])
        xT = cx.sb("x1T", [128, 8, TC], BF16)
        act = cx.sb("ffact", [128, NF, TC], BF16)
        wf = [cx.sb(f"fwf{i}", [128, 2, 8, 128], F32) for i in range(2)]
        wb = [cx.sb(f"fwb{i}", [128, 2, 8, 128], BF16) for i in range(2)]
        hb = [cx.sb(f"fhb{i}", [128, 514], F32) for i in range(2)]
        hc = [cx.sb(f"fhc{i}", [128, 512], F32) for i in range(2)]
        gl = [cx.sb(f"fgl{i}", [128, 512], F32) for i in range(2)]
        xrs = [cx.sb(f"fxr{i}", [128, D], F32) for i in range(2)]
        tmps = ln_tmps(cx, es)
        pss = [cx.ps(f"fps{i}", [128, 512], F32) for i in range(6)]
        gi = 0
        for tc in range(T // TC):
            build_xT(cx, x1d, xT, ident, TC // 128, tok0=tc * TC)
            for f in range(NF):
                f_, b_ = wf[f % 2], wb[f % 2]
                cx.dma("pool", f_.t[:, 0], W["wg"].t[f], [W["wg"]], [f_])
                cx.dma("pool", f_.t[:, 1], W["wu"].t[f], [W["wu"]], [f_])
                cx.copy("pool", b_.t[:], f_.t[:], [f_], [b_])
                for hf in range(TC // 512):
                    ts_ = slice(hf * 512, (hf + 1) * 512)
                    pg, pu = pss[(gi % 2) * 2], pss[(gi % 2) * 2 + 1]
                    h_, c_, g_ = hb[gi % 2], hc[gi % 2], gl[gi % 2]
                    gi += 1
                    for k in range(8):
                        cx.mm(pg.t[:], b_.t[:, 0, k, :], xT.t[:, k, ts_], k == 0, k == 7, [b_, xT], [pg])
                    for k in range(8):
                        cx.mm(pu.t[:], b_.t[:, 1, k, :], xT.t[:, k, ts_], k == 0, k == 7, [b_, xT], [pu])
                    cx.copy("pool", h_.t[:, 0:2], carry.t[:, f, :], [carry], [h_])
                    cx.copy("act", h_.t[:, 2:514], pg.t[:], [pg], [h_])
                    cx.copy("pool", carry.t[:, f, :], h_.t[:, 512:514], [h_], [carry])
                    cx.ts("dve", c_.t[:], h_.t[:, 2:514], cw.t[:, f, 2:3], ALU.mult, [h_, cw, cb], [c_],
                          s2=cb.t[:, f:f + 1], op1=ALU.add)
                    cx.stt(c_.t[:], h_.t[:, 1:513], cw.t[:, f, 1:2], c_.t[:], ALU.mult, ALU.add, [h_, cw, c_], [c_])
                    cx.stt(c_.t[:], h_.t[:, 0:512], cw.t[:, f, 0:1], c_.t[:], ALU.mult, ALU.add, [h_, cw, c_], [c_])
                    cx.act(g_.t[:], c_.t[:], AF.Gelu_apprx_tanh, [c_], [g_])
                    cx.tt("dve", act.t[:, f, ts_], g_.t[:], pu.t[:], ALU.mult, [g_, pu], [act])
            for q in range(TC // 128):
                tt = tc * (TC // 128) + q
                xr = xrs[tt % 2]
                cx.dma("sp", xr.t[:], x1d.t[tt * 128:(tt + 1) * 128, :], [x1d], [xr])
                zp = pss[4:6] if True else None
                for hf in range(2):
                    for f in range(NF):
                        cx.mm(zp[hf].t[:], act.t[:, f, q * 128:(q + 1) * 128], wd.t[:, f, hf * 512:(hf + 1) * 512],
                              f == 0, f == NF - 1, [act, wd], [zp[hf]])
                layer_norm_store(cx, zp, xr, g_t, b_t, x2d, tt * 128, tmps[tt % 2])


SCRATCH = {
    "rope": ([2, 128, T], F32), "vp": ([256, T], F32),
    "qr": ([384, T], BF16), "kr": ([384, T], BF16), "qn": ([384, T], BF16),
    "ks": ([128, T], BF16), "kw": ([128, T], BF16), "kc": ([128, T], BF16), "vc": ([128, T], BF16),
    "tmb": ([T, 640], BF16), "gr": ([T, 384], F32), "gt": ([T, 18], F32),
    "yt": ([1024, T], BF16), "x1": ([T, D], F32), "xmid": ([T, D], F32),
}


def build_program(layers=(0, 1), phases="ABCDEF", ext_in=(), ext_out=(), prologue=True):
    nc = bass.Bass("TRN2", target_bir_lowering=False)
    cx = Ctx(nc, ext_in, ext_out)
    xd = cx.dr("x", [T, D], F32, kind="ExternalInput")
    posd = cx.dr("pos", [1, T], I32, kind="ExternalInput")
    Cd = {k: cx.dr(k, list(v.shape), F32, kind="ExternalInput") for k, v in CONSTS.items()}
    Wd = {l: {k: cx.dr(f"{k}_{l}", shp, F32, kind="ExternalInput") for k, shp in LAYER_SHAPES.items()} for l in layers}
    S = {k: cx.dr(k, shp, dt) for k, (shp, dt) in SCRATCH.items()}
    outd = cx.dr("y", [T, D], F32, kind="ExternalOutput")
    with contextlib.ExitStack() as gs:
        cx.stack = gs
        G = {"rope": S["rope"]}
        idf = cx.sb("identf", [128, 128], F32)
        G["ident"] = cx.sb("ident", [128, 128], BF16)
        cx.dma("sp", idf.t[:], Cd["c_ident"].t, [Cd["c_ident"]], [idf])
        cx.copy("dve", G["ident"].t[:], idf.t[:], [idf], [G["ident"]])
        G["inv"] = load_f32(cx, "inv", Cd["c_inv"].t, Cd["c_inv"], [128, 1])
        G["sgn"] = load_f32(cx, "sgn", Cd["c_sgn"].t, Cd["c_sgn"], [128, 1])
        G["pw"] = load_f32(cx, "pw", Cd["c_pw"].t, Cd["c_pw"], [128, 2])
        G["prc"] = load_f32(cx, "prc", Cd["c_prc"].t, Cd["c_prc"], [128, 2, 16])
        G["C"] = Cd
        if prologue:
            prologue_rope(cx, posd, G)
        cur = xd
        for li, l in enumerate(layers):
            nxt = outd if li == len(layers) - 1 else S["xmid"]
            W = Wd[l]
            if "A" in phases:
                phase_A(cx, l, cur, W, G, S)
            if "B" in phases:
                phase_B(cx, l, W, G, S)
            if "C" in phases:
                phase_C(cx, l, W, G, S)
            if "D" in phases:
                phase_D(cx, l, W, G, S)
            if "E" in phases:
                phase_E(cx, l, cur, S["x1"], W, G, S)
            if "F" in phases:
                phase_F(cx, l, S["x1"], nxt, W, G, S)
            cur = nxt
        cx.P.barrier()
        finals = [outd.b] + [cx.dram[n].b for n in cx.ext_out]
        cx.P.emit_all(final_bufs=finals)
    return nc, cx


def make_in_maps(inputs, layers=(0, 1), cores=range(8)):
    shared = dict(CONSTS)
    for l in layers:
        for k, v in _layer_arrays(inputs, l).items():
            assert list(v.shape) == LAYER_SHAPES[k], (k, v.shape)
            shared[f"{k}_{l}"] = v.astype(np.float32, copy=False)
    maps = []
    for b in cores:
        m = dict(shared)
        m["x"] = np.ascontiguousarray(inputs["x"][b])
        m["pos"] = np.ascontiguousarray(inputs["positions"][b].reshape(1, T).astype(np.int32))
        maps.append(m)
    return maps


def kernel(**inputs):
    inputs = {k: np.asarray(v) for k, v in inputs.items()}
    nc, cx = build_program()
    maps = make_in_maps(inputs)
    res = run_bass_kernel_spmd(nc, maps, core_ids=list(range(8)))
    return np.stack([np.asarray(r["y"]) for r in res.results], 0).astype(np.float32)
```

```python
import contextlib
import math
import numpy as np
import concourse.bass as bass
import concourse.mybir as mybir
from concourse.bass_utils import run_bass_kernel_spmd

F32 = mybir.dt.float32
BF16 = mybir.dt.bfloat16
I32 = mybir.dt.int32
AF = mybir.ActivationFunctionType
ALU = mybir.AluOpType
AX = mybir.AxisListType

T = 4096
D = 1024
DEPTH = 2
NT = T // 128
DFF = 2816
NF = DFF // 128
ALPHA = (2 * DEPTH) ** 0.25
NEG = -10000.0
DBG = {}

ENGS = ("pe", "act", "dve", "pool", "sp")


class Buf:
    __slots__ = ("name", "writers", "readers", "dsem", "dcount", "is_dram", "vsem")

    def __init__(self, name, is_dram=False):
        self.name = name
        self.is_dram = is_dram
        self.writers = []
        self.readers = []
        self.dsem = None
        self.dcount = 0
        self.vsem = None


class Op:
    __slots__ = ("eng", "emit", "waits", "is_dma", "dbuf", "dval", "needs_inc", "val", "vsem")

    def __init__(self, eng, emit, is_dma=False):
        self.eng = eng
        self.emit = emit
        self.waits = []
        self.is_dma = is_dma
        self.dbuf = None
        self.dval = 0
        self.needs_inc = False
        self.val = 0
        self.vsem = None


class Prog:
    def __init__(self, nc):
        self.nc = nc
        self.ops = {e: [] for e in ENGS}
        self.last = {e: None for e in ENGS}
        self.dma_bufs = {}
        self.pending_bar = {e: [] for e in ENGS}
        self.seq = 0
        self.bar_seq = 0
        self.vfree = {True: [], False: []}
        self.vkind = []
        self.vcount = []
        self.rsem = []

    def _dep(self, op, prod, force=False):
        if prod is op:
            return
        if (not force) and prod.val < self.bar_seq:
            return
        if not prod.is_dma and prod.eng == op.eng:
            if op.eng in ("pe", "sp"):
                return
        op.waits.append(prod)
        if not prod.is_dma:
            prod.needs_inc = True

    @staticmethod
    def _prune(lst):
        last = {}
        for r in lst:
            last[(r.eng, r.is_dma, r.vsem)] = r
        return list(last.values())

    def op(self, eng, emit, reads=(), writes=(), dma=False):
        o = Op(eng, emit, is_dma=dma)
        self.seq += 1
        o.val = self.seq
        if self.pending_bar[eng]:
            for p in self.pending_bar[eng]:
                self._dep(o, p, force=True)
            self.pending_bar[eng] = []
        for b in reads:
            for w in b.writers:
                self._dep(o, w)
        for b in writes:
            for r in b.readers:
                if (not r.is_dma) and r.eng == eng:
                    continue
                self._dep(o, r)
            if not b.readers:
                for w in b.writers:
                    if w.is_dma and dma:
                        continue
                    if (not w.is_dma) and w.eng == eng:
                        continue
                    self._dep(o, w)
        if dma:
            assert len(writes) == 1
            b = writes[0]
            if b.is_dram:
                b = [r for r in reads if not r.is_dram][0]
            if b.vsem is None:
                sw = (eng == "pool")
                if self.vfree[sw]:
                    b.vsem = self.vfree[sw].pop()
                else:
                    b.vsem = len(self.vcount)
                    self.vcount.append(0)
                    self.vkind.append(sw)
                b.dcount = self.vcount[b.vsem]
            b.dcount += 16
            self.vcount[b.vsem] = b.dcount
            o.dbuf = b
            o.dval = b.dcount
            o.vsem = b.vsem
            self.dma_bufs[id(b)] = b
        for b in writes:
            if b.readers:
                b.writers = [o]
                b.readers = []
            else:
                b.writers.append(o)
                if len(b.writers) > 6:
                    b.writers = self._prune(b.writers)
        for b in reads:
            b.readers.append(o)
            if len(b.readers) > 6:
                b.readers = self._prune(b.readers)
        self.ops[eng].append(o)
        if not dma:
            self.last[eng] = o
        return o

    def dma(self, eng, out, in_, reads, writes):
        return self.op(eng, lambda e: e.dma_start(out=out, in_=in_), reads, writes, dma=True)

    def barrier(self):
        targets = []
        for e in ENGS:
            if self.last[e] is not None:
                targets.append(self.last[e])
        for b in self.dma_bufs.values():
            if b.vsem is not None:
                p = Op("sp", None, is_dma=True)
                p.dbuf = b
                p.dval = b.dcount
                p.vsem = b.vsem
                targets.append(p)
                self.vfree[self.vkind[b.vsem]].append(b.vsem)
                b.vsem = None
        self.dma_bufs = {}
        self.seq += 1
        self.bar_seq = self.seq
        for e in ENGS:
            self.pending_bar[e] = self._prune(self.pending_bar[e] + targets)

    def emit_all(self, final_bufs=()):
        nc = self.nc
        esem = {e: nc.alloc_semaphore(name=f"es_{e}") for e in ENGS}
        self.rsem = [nc.alloc_semaphore(name=f"ds_{i}") for i in range(len(self.vcount))]
        for e in ENGS:
            c = 0
            for o in self.ops[e]:
                if (not o.is_dma) and o.needs_inc:
                    c += 1
                    o.val = c
        engobj = {"pe": "tensor", "act": "scalar", "dve": "vector", "pool": "gpsimd", "sp": "sync"}
        with nc.Block() as block:
            for e in ENGS:
                def body(eng, ops=self.ops[e], e=e):
                    waited = {}
                    for o in ops:
                        need = {}
                        for p in o.waits:
                            if p.is_dma:
                                sem, val = self.rsem[p.vsem], p.dval
                            else:
                                sem, val = esem[p.eng], p.val
                            if sem is None:
                                continue
                            if need.get(sem.num, (None, 0))[1] < val:
                                need[sem.num] = (sem, val)
                        for k, (sem, val) in need.items():
                            if waited.get(k, 0) >= val:
                                continue
                            waited[k] = val
                            eng.wait_ge(sem, val)
                        ins = o.emit(eng)
                        if o.is_dma:
                            ins.then_inc(self.rsem[o.vsem], 16)
                        elif o.needs_inc:
                            ins.then_inc(esem[e], 1)
                    if e == "sp":
                        for i, sem in enumerate(self.rsem):
                            if waited.get(sem.num, 0) < self.vcount[i]:
                                eng.wait_ge(sem, self.vcount[i])

                getattr(block, engobj[e])(body)


OFF = {}
_o = 0
for _n, _w in (("v_pool", 256), ("q_ret", 384), ("k_ret", 384), ("v_ret", 384), ("g_ret", 384),
               ("q_nsa", 384), ("k_cmp", 128), ("v_cmp", 128), ("k_slc", 128), ("v_slc", 128),
               ("k_win", 128), ("v_win", 128), ("gate", 18)):
    OFF[_n] = _o
    _o += _w


def _swap_cols(cols):
    cols = np.asarray(cols).reshape(-1, 64)
    return np.concatenate([cols[:, 32:], cols[:, :32]], axis=1).reshape(-1)


def _fm_cols():
    ch = []
    for c in range(2):
        ch.append(np.arange(OFF["v_pool"] + 128 * c, OFF["v_pool"] + 128 * (c + 1)))
    for name in ("q_ret", "k_ret", "q_nsa"):
        for c in range(3):
            cols = np.arange(OFF[name] + 128 * c, OFF[name] + 128 * (c + 1))
            ch.append(cols)
            ch.append(_swap_cols(cols))
    for name in ("k_slc", "k_win"):
        cols = np.arange(OFF[name], OFF[name] + 128)
        ch.append(cols)
        ch.append(_swap_cols(cols))
    for name in ("k_cmp", "v_cmp"):
        ch.append(np.arange(OFF[name], OFF[name] + 128))
    return ch


FM_COLS = _fm_cols()
NFM = len(FM_COLS)
TM_COLS = np.concatenate([np.arange(OFF["v_ret"], OFF["v_ret"] + 384),
                          np.arange(OFF["v_slc"], OFF["v_slc"] + 128),
                          np.arange(OFF["v_win"], OFF["v_win"] + 128),
                          np.arange(OFF["g_ret"], OFF["g_ret"] + 384),
                          np.arange(OFF["gate"], OFF["gate"] + 18)])
NTM = len(TM_COLS)


def _const_tables():
    c = {}
    p = np.arange(128)
    inv = (10000.0 ** (-np.arange(0, 64, 2, dtype=np.float32) / 64)).astype(np.float32)
    c["c_inv"] = inv[p % 32].reshape(128, 1).astype(np.float32)
    c["c_sgn"] = np.where((p % 64) < 32, -1.0, 1.0).reshape(128, 1).astype(np.float32)
    h = np.arange(6, dtype=np.float64)
    lg = np.log1p(-np.power(2.0, -5.0 - h))
    i = np.arange(128, dtype=np.float64)
    dm = np.zeros((128, 6, 128), np.float32)
    for hh in range(6):
        diff = i[None, :] - i[:, None]
        dm[:, hh, :] = np.where(diff >= 0, 0.125 * np.exp(lg[hh] * np.maximum(diff, 0)), 0.0)
    c["c_dm"] = dm
    xi = np.exp(lg[:, None] * (i[None, :] + 1.0))
    zeta = 0.125 * np.exp(lg[:, None] * (127.0 - i[None, :]))
    gam = np.exp(lg * 128.0)
    xir = np.zeros((128, 3, 128), np.float32)
    zt = np.zeros((128, 3, 128), np.float32)
    gc = np.zeros((128, 3), np.float32)
    for ck in range(3):
        for hh in range(2):
            xir[hh * 64:(hh + 1) * 64, ck, :] = xi[2 * ck + hh][None, :]
            zt[:, ck, hh * 64:(hh + 1) * 64] = zeta[2 * ck + hh][:, None]
            gc[hh * 64:(hh + 1) * 64, ck] = gam[2 * ck + hh]
    c["c_xir"] = xir
    c["c_zt"] = zt
    c["c_gc"] = gc
    win = np.zeros((128, 2), np.float32)
    rc = np.zeros((128, 2, 16), np.float32)
    for ck in range(2):
        for hh in range(2):
            w = (2, 4, 8, 16)[2 * ck + hh]
            win[hh * 64:(hh + 1) * 64, ck] = 1.0 / w
            rc[hh * 64:(hh + 1) * 64, ck, :] = 1.0 / np.minimum(np.arange(16) + 1, w)
    c["c_pw"] = win
    c["c_prc"] = rc
    kk = np.arange(128)[:, None]
    qq = np.arange(128)[None, :]
    c["c_caus"] = np.where(kk > qq, NEG, 0.0).astype(np.float32)
    c["c_upper"] = np.where(kk <= qq, NEG, 0.0).astype(np.float32)
    c["c_ident"] = np.eye(128, dtype=np.float32)
    ex = np.zeros((64, T), np.float32)
    ex[np.arange(T) // 64, np.arange(T)] = 1.0
    c["c_expand"] = ex
    n = np.arange(256)
    ends = 16 * n + 31
    cm = np.where(ends[:, None] > np.arange(T)[None, :], NEG, 0.0).astype(np.float32)
    c["c_cmn"] = np.ascontiguousarray(cm.reshape(2, 128, T).transpose(1, 0, 2))
    ci = np.arange(256)[:, None]
    sj = np.arange(64)[None, :]
    ov = np.clip(np.minimum(ci * 16 + 32, (sj + 1) * 64) - np.maximum(ci * 16, sj * 64), 0, None) / 16.0
    ov[255, :] = 0.0
    c["c_ovl"] = np.ascontiguousarray(ov.astype(np.float32).reshape(2, 128, 64).transpose(1, 0, 2))
    tq = np.arange(T)
    cur = tq // 64
    blk = np.arange(64)[None, :]
    forced = (blk == 0) | (blk == cur[:, None]) | (blk == cur[:, None] - 1)
    valid = blk * 64 <= tq[:, None]
    bias = np.where(valid, np.where(forced, 1e6, 0.0), -100.0).astype(np.float32)
    c["c_sbias"] = np.ascontiguousarray(bias.reshape(32, 128, 64).transpose(1, 0, 2))
    return c


CONSTS = _const_tables()


def _layer_arrays(inp, l):
    a = {}
    w_in = inp["w_in"][l]
    wk = w_in.reshape(8, 128, -1)
    a["wfm"] = np.ascontiguousarray(
        np.stack([wk[:, :, cols].transpose(1, 0, 2) for cols in FM_COLS], 0))
    a["wtm"] = np.ascontiguousarray(wk[:, :, TM_COLS].transpose(1, 0, 2))
    a["wout"] = np.ascontiguousarray(inp["w_out"][l].reshape(8, 128, D).transpose(1, 0, 2))
    pw = inp["pool_w"][l]
    bd = np.zeros((2, 128, 128), np.float32)
    for ck in range(2):
        for hh in range(2):
            bd[ck, hh * 64:(hh + 1) * 64, hh * 64:(hh + 1) * 64] = pw[2 * ck + hh]
    a["bd"] = bd
    a["psc"] = np.ascontiguousarray(inp["pool_scale"][l].reshape(2, 128).T)
    a["gng"] = np.ascontiguousarray(inp["ret_gn_g"][l].reshape(1, 384))
    for kv in ("k", "v"):
        w1 = inp[f"cmp_w1_{kv}"][l].reshape(32, 64, 128)
        w1d = np.concatenate([w1, w1], axis=1).transpose(1, 0, 2)
        a[f"w1{kv}"] = np.ascontiguousarray(w1d)
        a[f"b1{kv}"] = np.ascontiguousarray(inp[f"cmp_b1_{kv}"][l].reshape(128, 1))
        a[f"pos{kv}"] = np.ascontiguousarray(inp[f"cmp_pos_{kv}"][l].T)
        a[f"w2{kv}"] = np.ascontiguousarray(inp[f"cmp_w2_{kv}"][l])
    a["w2ks"] = np.ascontiguousarray(inp["cmp_w2_k"][l][:, _swap_cols(np.arange(64))])
    a["wg"] = np.ascontiguousarray(inp["ffn_w_gate"][l].reshape(8, 128, NF, 128).transpose(2, 1, 0, 3))
    a["wu"] = np.ascontiguousarray(inp["ffn_w_up"][l].reshape(8, 128, NF, 128).transpose(2, 1, 0, 3))
    a["wd"] = np.ascontiguousarray(inp["ffn_w_down"][l].reshape(NF, 128, D).transpose(1, 0, 2))
    a["cw"] = np.ascontiguousarray(inp["ffn_conv_w"][l].reshape(3, NF, 128).transpose(2, 1, 0))
    a["cb"] = np.ascontiguousarray(inp["ffn_conv_b"][l].reshape(NF, 128).T)
    for nme in ("ln1_g", "ln1_b", "ln2_g", "ln2_b"):
        a[nme] = np.ascontiguousarray(inp[nme][l].reshape(1, D))
    return a


LAYER_SHAPES = {
    "wfm": [NFM, 128, 8, 128], "wtm": [128, 8, NTM], "wout": [128, 8, D], "bd": [2, 128, 128],
    "psc": [128, 2], "gng": [1, 384],
    "w1k": [128, 32, 128], "b1k": [128, 1], "posk": [64, 32], "w2k": [128, 64],
    "w1v": [128, 32, 128], "b1v": [128, 1], "posv": [64, 32], "w2v": [128, 64], "w2ks": [128, 64],
    "wg": [NF, 128, 8, 128], "wu": [NF, 128, 8, 128], "wd": [128, NF, D], "cw": [128, NF, 3], "cb": [128, NF],
    "ln1_g": [1, D], "ln1_b": [1, D], "ln2_g": [1, D], "ln2_b": [1, D],
}


class Tile:
    __slots__ = ("t", "b")

    def __init__(self, t, b):
        self.t = t
        self.b = b


class Ctx:
    def __init__(self, nc, ext_in=(), ext_out=()):
        self.nc = nc
        self.P = Prog(nc)
        self.ext_in = set(ext_in)
        self.ext_out = set(ext_out)
        self.dram = {}
        self.stack = None
        self.uid = 0

    def dr(self, name, shape, dt, kind=None):
        if kind is None:
            kind = "ExternalInput" if name in self.ext_in else ("ExternalOutput" if name in self.ext_out else "Internal")
        t = self.nc.dram_tensor(name, list(shape), dt, kind=kind).ap()
        tl = Tile(t, Buf(name, is_dram=True))
        self.dram[name] = tl
        return tl

    def sb(self, name, shape, dt, es=None):
        self.uid += 1
        t = (es or self.stack).enter_context(self.nc.sbuf_tensor(f"{name}_{self.uid}", list(shape), dt))
        return Tile(t, Buf(name))

    def ps(self, name, shape, dt, es=None):
        self.uid += 1
        t = (es or self.stack).enter_context(self.nc.psum_tensor(f"{name}_{self.uid}", list(shape), dt))
        return Tile(t, Buf(name))

    @contextlib.contextmanager
    def phase(self):
        old = self.stack
        with contextlib.ExitStack() as es:
            self.stack = es
            yield es
            self.P.barrier()
        self.stack = old

    def dma(self, eng, out, in_, reads, writes):
        self.P.dma(eng, out, in_, [x.b for x in reads], [x.b for x in writes])

    def op(self, eng, fn, reads, writes):
        self.P.op(eng, fn, [x.b for x in reads], [x.b for x in writes])

    def mm(self, out, lhsT, rhs, start, stop, reads, writes, skip=False):
        kw = dict(start=start, stop=stop)
        if skip:
            kw["skip_group_check"] = True
        self.op("pe", lambda e: e.matmul(out, lhsT=lhsT, rhs=rhs, **kw), reads, writes)

    def tr(self, out, in_, ident, reads, writes):
        self.op("pe", lambda e: e.transpose(out, in_, ident), reads, writes)

    def copy(self, eng, out, in_, reads, writes):
        if eng == "act":
            self.op("act", lambda e: e.copy(out=out, in_=in_), reads, writes)
        else:
            self.op(eng, lambda e: e.tensor_copy(out=out, in_=in_), reads, writes)

    def act(self, out, in_, func, reads, writes, **kw):
        self.op("act", lambda e: e.activation(out=out, in_=in_, func=func, **kw), reads, writes)

    def tt(self, eng, out, in0, in1, op, reads, writes):
        self.op(eng, lambda e: e.tensor_tensor(out=out, in0=in0, in1=in1, op=op), reads, writes)

    def ts(self, eng, out, in0, s1, op0, reads, writes, s2=None, op1=None):
        if op1 is None:
            self.op(eng, lambda e: e.tensor_scalar(out=out, in0=in0, scalar1=s1, scalar2=None, op0=op0), reads, writes)
        else:
            self.op(eng, lambda e: e.tensor_scalar(out=out, in0=in0, scalar1=s1, scalar2=s2, op0=op0, op1=op1), reads, writes)

    def stt(self, out, in0, scalar, in1, op0, op1, reads, writes):
        self.op("dve", lambda e: e.scalar_tensor_tensor(out=out, in0=in0, scalar=scalar, in1=in1, op0=op0, op1=op1),
                reads, writes)


def load_cast(cx, name, src_tile, shape, es=None, eng=None, q=None):
    b = cx.sb(name + "_b", shape, BF16, es)
    cx.dma("pool", b.t[:], src_tile.t, [src_tile], [b])
    return b


def load_f32(cx, name, src_ap, src_tile, shape, es=None, q="sp"):
    f = cx.sb(name, shape, F32, es)
    cx.dma(q, f.t[:], src_ap, [src_tile], [f])
    return f


def build_xT(cx, xd, xT, ident, ntiles, tok0=0):
    with contextlib.ExitStack() as es:
        xb = [cx.sb(f"xb{i}", [128, D], BF16, es) for i in range(3)]
        pt = [cx.ps(f"xpt{i}", [128, 8, 128], BF16, es) for i in range(2)]
        for tt in range(ntiles):
            bb, p = xb[tt % 3], pt[tt % 2]
            r0 = tok0 + tt * 128
            cx.dma("pool", bb.t[:], xd.t[r0:r0 + 128, :], [xd], [bb])
            for k in range(8):
                cx.tr(p.t[:, k, :], bb.t[:, k * 128:(k + 1) * 128], ident.t[:], [bb, ident], [p])
            cx.copy("act" if tt % 2 == 0 else "dve", xT.t[:, :, tt * 128:(tt + 1) * 128], p.t[:], [p], [xT])
        cx.P.barrier()


def layer_norm_store(cx, zps, xres, g_t, b_t, outd, r0, tmp, eng_q="pool"):
    z, st, mv, rs, o = tmp
    for hf in range(2):
        cx.stt(z.t[:, hf * 512:(hf + 1) * 512], xres.t[:, hf * 512:(hf + 1) * 512], ALPHA, zps[hf].t[:],
               ALU.mult, ALU.add, [xres, zps[hf]], [z])
    for hf in range(2):
        cx.op("dve", lambda e, hf=hf: e.bn_stats(out=st.t[:, hf, :], in_=z.t[:, hf * 512:(hf + 1) * 512]), [z], [st])
    cx.op("dve", lambda e: e.bn_aggr(out=mv.t[:], in_=st.t[:]), [st], [mv])
    cx.ts("dve", rs.t[:], mv.t[:, 1:2], 1e-5, ALU.add, [mv], [rs])
    cx.act(rs.t[:], rs.t[:], AF.Sqrt, [rs], [rs])
    cx.op("dve", lambda e: e.reciprocal(out=rs.t[:], in_=rs.t[:]), [rs], [rs])
    cx.ts("dve", o.t[:], z.t[:], mv.t[:, 0:1], ALU.subtract, [z, mv, rs], [o], s2=rs.t[:, 0:1], op1=ALU.mult)
    cx.tt("pool", o.t[:], o.t[:], g_t.t[:], ALU.mult, [o, g_t], [o])
    cx.tt("pool", o.t[:], o.t[:], b_t.t[:], ALU.add, [o, b_t], [o])
    cx.dma(eng_q, outd.t[r0:r0 + 128, :], o.t[:], [o], [outd])


def ln_tmps(cx, es, n=2):
    res = []
    for i in range(n):
        res.append((cx.sb(f"lnz{i}", [128, D], F32, es), cx.sb(f"lnst{i}", [128, 2, 6], F32, es),
                    cx.sb(f"lnmv{i}", [128, 2], F32, es), cx.sb(f"lnrs{i}", [128, 1], F32, es),
                    cx.sb(f"lno{i}", [128, D], F32, es)))
    return res


def prologue_rope(cx, posd, G):
    rope = G["rope"]
    with cx.phase() as es:
        pi_ = cx.sb("posi", [128, T], I32)
        ang = cx.sb("ang", [128, T], F32)
        m = cx.sb("rm", [128, T], F32)
        o = cx.sb("ro", [128, T], F32)
        cx.dma("sp", pi_.t[:], posd.t.broadcast_to([128, T]), [posd], [pi_])
        cx.copy("dve", ang.t[:], pi_.t[:], [pi_], [ang])
        cx.ts("dve", ang.t[:], ang.t[:], G["inv"].t[:, 0:1], ALU.mult, [ang, G["inv"]], [ang])
        ki = cx.sb("rki", [128, T], I32)
        C1 = 6.28125
        C2 = 2.0 * math.pi - 6.28125
        for which, shift in ((0, 0.25), (1, 0.0)):
            cx.ts("dve", m.t[:], ang.t[:], 1.0 / (2.0 * math.pi), ALU.mult, [ang], [m], s2=shift, op1=ALU.add)
            cx.copy("dve", ki.t[:], m.t[:], [m], [ki])
            cx.copy("dve", m.t[:], ki.t[:], [ki], [m])
            cx.stt(o.t[:], m.t[:], -C1, ang.t[:], ALU.mult, ALU.add, [m, ang], [o])
            cx.stt(o.t[:], m.t[:], -C2, o.t[:], ALU.mult, ALU.add, [m, o], [o])
            if which == 0:
                cx.ts("dve", o.t[:], o.t[:], 0.5 * math.pi, ALU.add, [o], [o])
            cx.ts("dve", o.t[:], o.t[:], math.pi, ALU.min, [o], [o], s2=-math.pi, op1=ALU.max)
            cx.act(o.t[:], o.t[:], AF.Sin, [o], [o])
            if which == 1:
                cx.ts("dve", o.t[:], o.t[:], G["sgn"].t[:, 0:1], ALU.mult, [o, G["sgn"]], [o])
            cx.dma("sp", rope.t[which], o.t[:], [o], [rope])


def phase_A(cx, l, xd, W, G, S):
    ident = G["ident"]
    rope = G["rope"]
    with cx.phase():
        xT = cx.sb("xT", [128, 8, T], BF16)
        build_xT(cx, xd, xT, ident, NT)
        with cx.phase() as es:
          if DBG.get("tm", True):
              wtm = load_cast(cx, "wtm", W["wtm"], [128, 8, NTM])
              pss = [cx.ps(f"tmps{i}", [128, 512], F32) for i in range(6)]
              ob = [cx.sb(f"tmob{i}", [128, 640], BF16) for i in range(2)]
              og = [cx.sb(f"tmog{i}", [128, 384], F32) for i in range(2)]
              ogt = [cx.sb(f"tmogt{i}", [128, 18], F32) for i in range(2)]
              for tt in range(NT):
                  p0, p1, p2 = pss[(tt % 2) * 3:(tt % 2) * 3 + 3]
                  for (pp, c0, c1) in ((p0, 0, 512), (p1, 512, 1024), (p2, 1024, NTM)):
                      for k in range(8):
                          cx.mm(pp.t[:, 0:c1 - c0], xT.t[:, k, tt * 128:(tt + 1) * 128], wtm.t[:, k, c0:c1],
                                k == 0, k == 7, [xT, wtm], [pp])
                  b_, g_, t_ = ob[tt % 2], og[tt % 2], ogt[tt % 2]
                  cx.copy("dve", b_.t[:, 0:512], p0.t[:], [p0], [b_])
                  cx.copy("dve", b_.t[:, 512:640], p1.t[:, 0:128], [p1], [b_])
                  cx.act(g_.t[:], p1.t[:, 128:512], AF.Silu, [p1], [g_])
                  cx.act(t_.t[:], p2.t[:, 0:18], AF.Sigmoid, [p2], [t_])
                  r0 = tt * 128
                  cx.dma("sp", S["tmb"].t[r0:r0 + 128, :], b_.t[:], [b_], [S["tmb"]])
                  cx.dma("sp", S["gr"].t[r0:r0 + 128, :], g_.t[:], [g_], [S["gr"]])
                  cx.dma("sp", S["gt"].t[r0:r0 + 128, :], t_.t[:], [t_], [S["gt"]])
        with cx.phase() as es:
          if DBG.get("fm", True):
              C = cx.sb("ropeC", [128, T], F32)
              Sn = cx.sb("ropeS", [128, T], F32)
              cx.dma("sp", C.t[:], rope.t[0], [rope], [C])
              cx.dma("sp", Sn.t[:], rope.t[1], [rope], [Sn])
              wb = [cx.sb(f"wb{i}", [128, 2, 8, 128], BF16) for i in range(2)]
              pss = [cx.ps(f"fmps{i}", [128, 512], F32) for i in range(8)]
              ost = [cx.sb(f"fmo{i}", [128, T], BF16) for i in range(2)]
              vst = [cx.sb(f"fmv{i}", [128, 512], F32) for i in range(2)]
              t1s = [cx.sb(f"fmt1{i}", [128, 512], F32) for i in range(2)]
              t2s = [cx.sb(f"fmt2{i}", [128, 512], F32) for i in range(2)]
              units = [([0], S["vp"], 0, False), ([1], S["vp"], 128, False)]
              ci = 2
              for dest in ("qr", "kr", "qn"):
                  for c in range(3):
                      units.append(([ci, ci + 1], S[dest], 128 * c, True))
                      ci += 2
              for dest in ("ks", "kw"):
                  units.append(([ci, ci + 1], S[dest], 0, True))
                  ci += 2
              units.append(([ci], S["kc"], 0, False))
              units.append(([ci + 1], S["vc"], 0, False))
              gi = 0

              def load_w(ui):
                  for j, cid in enumerate(units[ui][0]):
                      cx.dma("pool", wb[ui % 2].t[:, j], W["wfm"].t[cid], [W["wfm"]], [wb[ui % 2]])
              load_w(0)
              for ui, (cids, dest, row0, is_rope) in enumerate(units):
                  b_ = wb[ui % 2]
                  n = len(cids)
                  if ui + 1 < len(units):
                      load_w(ui + 1)
                  o_ = ost[ui % 2]
                  for tc in range(8):
                      ts_ = slice(tc * 512, (tc + 1) * 512)
                      pp = [pss[(gi % 4) * 2 + j] for j in range(n)]
                      gi += 1
                      for j in range(n):
                          for k in range(8):
                              cx.mm(pp[j].t[:], b_.t[:, j, k, :], xT.t[:, k, ts_], k == 0, k == 7, [b_, xT], [pp[j]])
                      if is_rope:
                          t1, t2 = t1s[tc % 2], t2s[tc % 2]
                          cx.tt("dve", t1.t[:], pp[0].t[:], C.t[:, ts_], ALU.mult, [pp[0], C], [t1])
                          cx.tt("dve", t2.t[:], pp[1].t[:], Sn.t[:, ts_], ALU.mult, [pp[1], Sn], [t2])
                          cx.tt("pool", o_.t[:, ts_], t1.t[:], t2.t[:], ALU.add, [t1, t2], [o_])
                      elif dest is S["vp"]:
                          v_ = vst[tc % 2]
                          cx.copy("act", v_.t[:], pp[0].t[:], [pp[0]], [v_])
                          cx.dma("sp", dest.t[row0:row0 + 128, ts_], v_.t[:], [v_], [dest])
                      else:
                          cx.copy("act", o_.t[:, ts_], pp[0].t[:], [pp[0]], [o_])
                  if dest is not S["vp"]:
                      cx.dma("sp", dest.t[row0:row0 + 128, :], o_.t[:], [o_], [dest])


def phase_B(cx, l, W, G, S):
    with cx.phase():
        psc = load_f32(cx, "psc", W["psc"].t, W["psc"], [128, 2])
        pw = G["pw"]
        prc = G["prc"]
        pss = [cx.ps(f"bps{i}", [128, 512], F32) for i in range(4)]
        for ck in range(2):
            bd = load_cast(cx, f"bd{ck}", Tile(W["bd"].t[ck], W["bd"].b), [128, 128])
            v = cx.sb(f"pv{ck}", [128, 16 + T], F32)
            s2 = cx.sb(f"ps2{ck}", [128, 16 + T], F32)
            s4 = cx.sb(f"ps4{ck}", [128, 16 + T], F32)
            mx = cx.sb(f"pmx{ck}", [128, T], BF16)
            o = cx.sb(f"pbo{ck}", [128, T], BF16)
            for t_ in (v, s2, s4):
                cx.op("pool", lambda e, t_=t_: e.memset(t_.t[:, 0:16], 0.0), [], [t_])
            cx.dma("sp", v.t[:, 16:], S["vp"].t[ck * 128:(ck + 1) * 128, :], [S["vp"]], [v])
            if ck == 0:
                cx.tt("dve", s2.t[:, 16:], v.t[:, 16:], v.t[:, 15:15 + T], ALU.add, [v], [s2])
                cx.tt("dve", s4.t[64:128, 16:], s2.t[64:128, 16:], s2.t[64:128, 14:14 + T], ALU.add, [s2], [s4])
                lo, hi = s2, s4
            else:
                cx.tt("dve", s2.t[:, 16:], v.t[:, 16:], v.t[:, 15:15 + T], ALU.add, [v], [s2])
                cx.tt("dve", s4.t[:, 16:], s2.t[:, 16:], s2.t[:, 14:14 + T], ALU.add, [s2], [s4])
                cx.tt("dve", s2.t[:, 16:], s4.t[:, 16:], s4.t[:, 12:12 + T], ALU.add, [s4], [s2])
                cx.tt("dve", s4.t[64:128, 16:], s2.t[64:128, 16:], s2.t[64:128, 8:8 + T], ALU.add, [s2], [s4])
                lo, hi = s2, s4
            for (src, r) in ((lo, slice(0, 64)), (hi, slice(64, 128))):
                cx.stt(mx.t[r, :], src.t[r, 16:], pw.t[r, ck:ck + 1], v.t[r, 16:], ALU.mult, ALU.subtract,
                       [src, pw, v], [mx])
                cx.tt("dve", src.t[r, 0:16], src.t[r, 16:32], prc.t[r, ck, :], ALU.mult, [src, prc, mx], [src])
                cx.tt("dve", mx.t[r, 0:16], src.t[r, 0:16], v.t[r, 16:32], ALU.subtract, [src, v], [mx])
            for tc in range(8):
                ts_ = slice(tc * 512, (tc + 1) * 512)
                pp = pss[tc % 4]
                cx.mm(pp.t[:], bd.t[:], mx.t[:, ts_], True, True, [bd, mx], [pp])
                cx.act(o.t[:, ts_], pp.t[:], AF.Copy, [pp, psc], [o], scale=psc.t[:, ck:ck + 1])
            cx.dma("sp", S["yt"].t[ck * 128:(ck + 1) * 128, :], o.t[:], [o], [S["yt"]])


def phase_C(cx, l, W, G, S):
    Cd = G["C"]
    ident = G["ident"]
    with cx.phase() as es:
        dm = load_f32(cx, "dm", Cd["c_dm"].t, Cd["c_dm"], [128, 6, 128])
        xir = load_f32(cx, "xir", Cd["c_xir"].t, Cd["c_xir"], [128, 3, 128])
        zt = load_f32(cx, "zt", Cd["c_zt"].t, Cd["c_zt"], [128, 3, 128])
        gc = load_f32(cx, "gc", Cd["c_gc"].t, Cd["c_gc"], [128, 3])
        gng = load_f32(cx, "gng", W["gng"].t.broadcast_to([128, 384]), W["gng"], [128, 384])
        sps = [cx.ps(f"csp{i}", [128, 128], F32) for i in range(2)]
        ops_ = [cx.ps(f"cop{i}", [128, 128], F32) for i in range(2)]
        kvp = cx.ps("ckv", [128, 128], F32)
        ktp = [cx.ps(f"cktp{i}", [128, 128], BF16) for i in range(2)]
        ytp = cx.ps("cytp", [128, 128], BF16)
        sms = [cx.sb(f"csm{i}", [128, 128], BF16) for i in range(4)]
        kzs = [cx.sb(f"ckz{i}", [128, 128], BF16) for i in range(2)]
        sts = [cx.sb(f"cst{i}", [128, 2, 6], F32) for i in range(2)]
        mvs = [cx.sb(f"cmv{i}", [128, 2, 2], F32) for i in range(2)]
        rss = [cx.sb(f"crs{i}", [128, 2], F32) for i in range(2)]
        ons = [cx.sb(f"con{i}", [128, 128], F32) for i in range(2)]
        onb = [cx.sb(f"conb{i}", [128, 128], BF16) for i in range(2)]
        R = cx.sb("cR", [128, 64], F32)
        Rb = cx.sb("cRb", [128, 64], BF16)
        for ck in range(3):
            qT = cx.sb(f"cqT{ck}", [128, T], BF16)
            kT = cx.sb(f"ckT{ck}", [128, T], BF16)
            qx = cx.sb(f"cqx{ck}", [128, T], BF16)
            v = cx.sb(f"cv{ck}", [128, NT, 128], BF16)
            sg = cx.sb(f"csg{ck}", [128, NT, 128], F32)
            yT = cx.sb(f"cyT{ck}", [128, T], BF16)
            rows = slice(ck * 128, (ck + 1) * 128)
            cx.dma("sp", qT.t[:], S["qr"].t[rows, :], [S["qr"]], [qT])
            cx.dma("sp", kT.t[:], S["kr"].t[rows, :], [S["kr"]], [kT])
            cx.dma("pool", v.t[:], S["tmb"].t[:, ck * 128:(ck + 1) * 128].rearrange("(n p) c -> p n c", p=128),
                   [S["tmb"]], [v])
            cx.dma("pool", sg.t[:], S["gr"].t[:, ck * 128:(ck + 1) * 128].rearrange("(n p) c -> p n c", p=128),
                   [S["gr"]], [sg])
            cx.tt("pool", qx.t[:].rearrange("p (n i) -> p n i", i=128), qT.t[:].rearrange("p (n i) -> p n i", i=128),
                  xir.t[:, ck:ck + 1, :].broadcast_to([128, NT, 128]), ALU.mult, [qT, xir], [qx])
            cx.op("pool", lambda e: e.memset(R.t[:], 0.0), [], [R])
            cx.op("pool", lambda e: e.memset(Rb.t[:], 0.0), [], [Rb])
            for n in range(NT):
                ns = slice(n * 128, (n + 1) * 128)
                kz, ktp_ = kzs[n % 2], ktp[n % 2]
                cx.tr(ktp_.t[:], kT.t[:, ns], ident.t[:], [kT, ident], [ktp_])
                cx.tt("dve", kz.t[:], ktp_.t[:], zt.t[:, ck, :], ALU.mult, [ktp_, zt], [kz])
                op_ = ops_[n % 2]
                for hh in range(2):
                    r = slice(hh * 64, (hh + 1) * 64)
                    sp_, sm = sps[hh], sms[(n % 2) * 2 + hh]
                    cx.mm(sp_.t[:], kT.t[r, ns], qT.t[r, ns], True, True, [kT, qT], [sp_])
                    cx.tt("dve", sm.t[:], sp_.t[:], dm.t[:, 2 * ck + hh, :], ALU.mult, [sp_, dm], [sm])
                for hh in range(2):
                    r = slice(hh * 64, (hh + 1) * 64)
                    sm = sms[(n % 2) * 2 + hh]
                    cx.mm(op_.t[:, r], sm.t[:], v.t[:, n, r], True, False, [sm, v], [op_])
                    cx.mm(op_.t[:, r], qx.t[r, ns], Rb.t[r, :], False, True, [qx, Rb], [op_])
                cx.mm(kvp.t[:], kz.t[:], v.t[:, n, :], True, True, [kz, v], [kvp])
                for hh in range(2):
                    r = slice(hh * 64, (hh + 1) * 64)
                    cx.stt(R.t[r, :], R.t[r, :], gc.t[r, ck:ck + 1], kvp.t[r, r], ALU.mult, ALU.add, [R, gc, kvp], [R])
                cx.copy("pool", Rb.t[:], R.t[:], [R], [Rb])
                st, mv, rs, on, ob = sts[n % 2], mvs[n % 2], rss[n % 2], ons[n % 2], onb[n % 2]
                for hh in range(2):
                    r = slice(hh * 64, (hh + 1) * 64)
                    cx.op("dve", lambda e, hh=hh, r=r, st=st, op_=op_: e.bn_stats(out=st.t[:, hh, :], in_=op_.t[:, r]), [op_], [st])
                    cx.op("dve", lambda e, hh=hh, st=st, mv=mv: e.bn_aggr(out=mv.t[:, hh, :], in_=st.t[:, hh, :]), [st], [mv])
                cx.ts("dve", rs.t[:], mv.t[:, :, 1], 1e-5, ALU.add, [mv], [rs])
                cx.act(rs.t[:], rs.t[:], AF.Sqrt, [rs], [rs])
                cx.op("dve", lambda e, rs=rs: e.reciprocal(out=rs.t[:], in_=rs.t[:]), [rs], [rs])
                for hh in range(2):
                    r = slice(hh * 64, (hh + 1) * 64)
                    cx.ts("dve", on.t[:, r], op_.t[:, r], mv.t[:, hh, 0:1], ALU.subtract, [op_, mv, rs], [on],
                          s2=rs.t[:, hh:hh + 1], op1=ALU.mult)
                cx.tt("pool", on.t[:], on.t[:], gng.t[:, ck * 128:(ck + 1) * 128], ALU.mult, [on, gng], [on])
                cx.tt("pool", ob.t[:], on.t[:], sg.t[:, n, :], ALU.mult, [on, sg], [ob])
                cx.tr(ytp.t[:], ob.t[:], ident.t[:], [ob, ident], [ytp])
                cx.copy("act", yT.t[:, ns], ytp.t[:], [ytp], [yT])
            cx.dma("sp", S["yt"].t[256 + ck * 128:256 + (ck + 1) * 128, :], yT.t[:], [yT], [S["yt"]])


def phase_D(cx, l, W, G, S):
    Cd = G["C"]
    ident = G["ident"]
    rope = G["rope"]
    with cx.phase() as es:
        KCT = cx.sb("KCT", [64, 2, 256], BF16)
        VCX = cx.sb("VCX", [128, 2, 2, 129], BF16)
        with cx.phase():
            Cc = cx.sb("dCc", [64, T], F32)
            Sc = cx.sb("dSc", [64, T], F32)
            cx.dma("sp", Cc.t[:], rope.t[0][0:64, :], [rope], [Cc])
            cx.dma("sp", Sc.t[:], rope.t[1][0:64, :], [rope], [Sc])
            ovl = load_f32(cx, "ovl", Cd["c_ovl"].t, Cd["c_ovl"], [128, 2, 64])
            hps = [cx.ps(f"dhp{i}", [128, 512], F32) for i in range(2)]
            cps = cx.ps("dcp", [128, 512], F32)
            kp = cx.ps("dkp", [128, 2, 256], F32)
            ksp = cx.ps("dksp", [128, 2, 256], F32)
            vps = [cx.ps(f"dvp{i}", [128, 512], F32) for i in range(2)]
            cx.op("pool", lambda e: e.memset(KCT.t[:], 0.0), [], [KCT])
            cx.op("pool", lambda e: e.memset(VCX.t[:, :, :, 64:65], 1.0), [], [VCX])
            for g in range(2):
                cx.copy("pool", VCX.t[:, g, :, 65:129], ovl.t[:], [ovl], [VCX])
            for kv in ("k", "v"):
                src = S["kc"] if kv == "k" else S["vc"]
                kvT = cx.sb(f"dkvT{kv}", [128, T], BF16)
                cx.dma("sp", kvT.t[:], src.t, [src], [kvT])
                w1 = load_cast(cx, f"w1{kv}", W[f"w1{kv}"], [128, 32, 128])
                pos = load_cast(cx, f"pos{kv}", W[f"pos{kv}"], [64, 32], eng="dve")
                b1 = load_f32(cx, f"b1{kv}", W[f"b1{kv}"].t, W[f"b1{kv}"], [128, 1])
                w2 = load_cast(cx, f"w2{kv}", W[f"w2{kv}"], [128, 64], eng="dve")
                cb = cx.sb(f"dcb{kv}", [128, 1], F32)
                h1 = cx.sb(f"dh1{kv}", [128, 2, 256], BF16)
                cx.op("pool", lambda e, h1=h1: e.memset(h1.t[:], 0.0), [], [h1])
                for i in range(32):
                    cx.mm(cps.t[:, 0:1], w1.t[0:64, i, :], pos.t[0:64, i:i + 1], i == 0, i == 31, [w1, pos], [cps])
                cx.tt("dve", cb.t[:], cps.t[:, 0:1], b1.t[:], ALU.add, [cps, b1], [cb])
                for g in range(2):
                    r = slice(g * 64, (g + 1) * 64)
                    for i in range(32):
                        cx.mm(hps[g].t[:, 0:255], w1.t[r, i, :], kvT.t[r, i:i + 16 * 254 + 1:16], i == 0, i == 31,
                              [w1, kvT], [hps[g]])
                    cx.act(h1.t[:, g, 0:255], hps[g].t[:, 0:255], AF.Gelu_apprx_tanh, [hps[g], cb], [h1],
                           bias=cb.t[:, 0:1])
                if kv == "k":
                    w2s = load_cast(cx, "w2ks", W["w2ks"], [128, 64], eng="dve")
                    cx.mm(kp.t[0:64], w2.t[:], h1.t[:], True, True, [w2, h1], [kp])
                    cx.mm(ksp.t[0:64], w2s.t[:], h1.t[:], True, True, [w2s, h1], [ksp])
                    t1 = cx.sb("dkt1", [64, 2, 255], F32)
                    t2 = cx.sb("dkt2", [64, 2, 255], F32)
                    cview = Cc.t[:, 31::16].unsqueeze(1).broadcast_to([64, 2, 255])
                    sview = Sc.t[:, 31::16].unsqueeze(1).broadcast_to([64, 2, 255])
                    cx.tt("dve", t1.t[:], kp.t[0:64, :, 0:255], cview, ALU.mult, [kp, Cc], [t1])
                    cx.tt("dve", t2.t[:], ksp.t[0:64, :, 0:255], sview, ALU.mult, [ksp, Sc], [t2])
                    cx.tt("dve", KCT.t[:, :, 0:255], t1.t[:], t2.t[:], ALU.add, [t1, t2], [KCT])
                else:
                    for g in range(2):
                        for nt in range(2):
                            vp_ = vps[(g * 2 + nt) % 2]
                            cx.mm(vp_.t[:, 0:64], h1.t[:, g, nt * 128:(nt + 1) * 128], w2.t[:], True, True, [h1, w2], [vp_])
                            cx.copy("act", VCX.t[:, g, nt, 0:64], vp_.t[:, 0:64], [vp_], [VCX])
        dstop = DBG.get("d_stop", 9)
        if dstop <= 1:
            return
        identb = ident
        QA = [cx.sb(f"QA{h}", [128, T], BF16) for h in range(6)]
        KSA = [cx.sb(f"KSA{g}", [128, T], BF16) for g in range(2)]
        KW = [cx.sb(f"KW{g}", [64, T], BF16) for g in range(2)]
        VSX = cx.sb("VSX", [128, NT, 2, 65], BF16)
        VWX = cx.sb("VWX", [128, NT, 2, 65], BF16)
        CMN = cx.sb("CMN", [128, 2, T], BF16)
        SB_ = load_f32(cx, "sbias", Cd["c_sbias"].t, Cd["c_sbias"], [128, NT, 64])
        GT = cx.sb("GTs", [128, NT, 18], F32)
        cx.dma("sp", GT.t[:], S["gt"].t.rearrange("(n p) c -> p n c", p=128), [S["gt"]], [GT])
        causb = cx.sb("causb", [128, 128], BF16)
        upperb = cx.sb("upperb", [128, 128], BF16)
        cx.dma("pool", causb.t[:], Cd["c_caus"].t, [Cd["c_caus"]], [causb])
        cx.dma("pool", upperb.t[:], Cd["c_upper"].t, [Cd["c_upper"]], [upperb])
        for nt in range(2):
            cx.dma("pool", CMN.t[:, nt, :], Cd["c_cmn"].t[:, nt, :], [Cd["c_cmn"]], [CMN])
        for g in range(2):
            cx.dma("pool", KSA[g].t[64:128, :], Cd["c_expand"].t, [Cd["c_expand"]], [KSA[g]])
        for h in range(6):
            cx.dma("sp", QA[h].t[0:64, :], S["qn"].t[h * 64:(h + 1) * 64, :], [S["qn"]], [QA[h]])
            cx.op("pool", lambda e, h=h: e.memset(QA[h].t[64:128, :], 0.0), [], [QA[h]])
        for g in range(2):
            cx.dma("pool", KSA[g].t[0:64, :], S["ks"].t[g * 64:(g + 1) * 64, :], [S["ks"]], [KSA[g]])
            cx.dma("sp", KW[g].t[:], S["kw"].t[g * 64:(g + 1) * 64, :], [S["kw"]], [KW[g]])
            cx.dma("pool", VSX.t[:, :, g, 0:64],
                   S["tmb"].t[:, 384 + g * 64:384 + (g + 1) * 64].rearrange("(n p) c -> p n c", p=128), [S["tmb"]], [VSX])
            cx.dma("pool", VWX.t[:, :, g, 0:64],
                   S["tmb"].t[:, 512 + g * 64:512 + (g + 1) * 64].rearrange("(n p) c -> p n c", p=128), [S["tmb"]], [VWX])
        cx.op("pool", lambda e: e.memset(VSX.t[:, :, :, 64:65], 1.0), [], [VSX])
        cx.op("pool", lambda e: e.memset(VWX.t[:, :, :, 64:65], 1.0), [], [VWX])
        if dstop <= 2:
            return
        SP = [cx.ps(f"dS{i}", [128, 512], F32) for i in range(4)]
        OP = [cx.ps(f"dO{i}", [128, 512], F32) for i in range(3)]
        tpt = cx.ps("dT", [128, 8, 128], BF16)
        _tb = Buf("dT")
        TP = [Tile(tpt.t[:, 4 * i:4 * i + 4, :], _tb) for i in range(2)]
        pTs = [cx.sb(f"dpT{i}", [128, 512], BF16) for i in range(4)]
        OACC = [cx.sb(f"dOACC{i}", [128, 4, 384], F32) for i in range(2)]
        IMP = [cx.sb(f"dIMP{i}", [128, 4, 64], F32) for i in range(2)]
        recs = [cx.sb(f"drec{i}", [128, 2, 4], F32) for i in range(4)]
        sc1 = [cx.sb(f"dsc1{i}", [128, 64], F32) for i in range(2)]
        sc2 = [cx.sb(f"dsc2{i}", [128, 64], F32) for i in range(2)]
        m8 = [cx.sb(f"dm8{i}", [128, 16], F32) for i in range(2)]
        nm = [cx.sb(f"dnm{i}", [128, 128], BF16) for i in range(2)]
        for t_ in nm:
            cx.op("pool", lambda e, t_=t_: e.memset(t_.t[:], 0.0), [], [t_])
        obf = [cx.sb(f"dobf{i}", [128, 384], BF16) for i in range(2)]
        yst = [cx.sb(f"dyst{i}", [128, 3, 512], BF16) for i in range(2)]
        cnt = {"s": 0, "o": 0, "p": 0, "r": 0, "t": 0}

        def nxt(key, lst):
            x = lst[cnt[key] % len(lst)]
            cnt[key] += 1
            return x

        items = []

        def cmp_item(c, h):
            g, rr = divmod(h, 3)
            cs = slice(c * 512, (c + 1) * 512)
            oacc, imp = OACC[c % 2], IMP[c % 2]
            nts = [0] + ([1] if c >= 4 else [])
            st = {}

            def S_():
                st["pts"] = {}
                for nt in nts:
                    sp_ = nxt("s", SP)
                    need_mask = (c <= 4) if nt == 0 else True
                    cx.mm(sp_.t[:], KCT.t[0:64, g, nt * 128:(nt + 1) * 128], QA[h].t[0:64, cs], True, True,
                          [KCT, QA[h]], [sp_])
                    if need_mask:
                        cx.mm(sp_.t[:], identb.t[:], CMN.t[:, nt, cs], False, True, [identb, CMN], [sp_], skip=True)
                    pT = nxt("p", pTs)
                    cx.act(pT.t[:], sp_.t[:], AF.Exp, [sp_], [pT], scale=0.125)
                    st["pts"][nt] = pT

            def PV_():
                pts = st["pts"]
                for q4 in range(4):
                    qt = 4 * c + q4
                    ob_ = nxt("o", OP)
                    for j, nt in enumerate(nts):
                        cx.mm(ob_.t[:, 0:129], pts[nt].t[:, q4 * 128:(q4 + 1) * 128], VCX.t[:, g, nt, :],
                              j == 0, j == len(nts) - 1, [pts[nt], VCX], [ob_])
                    rc = nxt("r", recs)
                    cx.ts("dve", rc.t[:, 0, 0:1], ob_.t[:, 64:65], 1e-30, ALU.add, [ob_], [rc])
                    cx.op("dve", lambda e, rc=rc: e.reciprocal(out=rc.t[:, 0, 0:1], in_=rc.t[:, 0, 0:1]), [rc], [rc])
                    cx.tt("dve", rc.t[:, 1, 0:1], rc.t[:, 0, 0:1], GT.t[:, qt, 3 * h:3 * h + 1], ALU.mult, [rc, GT], [rc])
                    cx.ts("dve", oacc.t[:, q4, h * 64:(h + 1) * 64], ob_.t[:, 0:64], rc.t[:, 1, 0:1], ALU.mult,
                          [ob_, rc], [oacc])
                    if rr == 0:
                        cx.ts("dve", imp.t[:, q4, :], ob_.t[:, 65:129], rc.t[:, 0, 0:1], ALU.mult, [ob_, rc], [imp])
                    else:
                        cx.stt(imp.t[:, q4, :], ob_.t[:, 65:129], rc.t[:, 0, 0:1], imp.t[:, q4, :], ALU.mult, ALU.add,
                               [ob_, rc, imp], [imp])
                if rr == 2:
                    for q4 in range(4):
                        qt = 4 * c + q4
                        s1, s2, mm8, nm_ = sc1[q4 % 2], sc2[q4 % 2], m8[q4 % 2], nm[q4 % 2]
                        cx.tt("dve", s1.t[:], imp.t[:, q4, :], SB_.t[:, qt, :], ALU.add, [imp, SB_], [s1])
                        cx.op("dve", lambda e, mm8=mm8, s1=s1: e.max(out=mm8.t[:, 0:8], in_=s1.t[:]), [s1], [mm8])
                        cx.op("dve", lambda e, mm8=mm8, s1=s1, s2=s2: e.match_replace(
                            out=s2.t[:], in_to_replace=mm8.t[:, 0:8], in_values=s1.t[:], imm_value=-1e9), [s1, mm8], [s2])
                        cx.op("dve", lambda e, mm8=mm8, s2=s2: e.max(out=mm8.t[:, 8:16], in_=s2.t[:]), [s2], [mm8])
                        cx.ts("dve", mm8.t[:, 15:16], mm8.t[:, 15:16], 0.0, ALU.max, [mm8], [mm8])
                        cx.ts("dve", nm_.t[:, 64:128], s1.t[:], mm8.t[:, 15:16], ALU.is_lt, [s1, mm8], [nm_],
                              s2=NEG, op1=ALU.mult)
                        tp = nxt("t", TP)
                        cx.tr(tp.t[:, 0, :], nm_.t[:], ident.t[:], [nm_, ident], [tp])
                        for r3 in range(3):
                            hh = 3 * g + r3
                            cx.copy("dve",
                                    QA[hh].t[64:128, qt * 128:(qt + 1) * 128], tp.t[64:128, 0, :], [tp], [QA[hh]])
            return S_, PV_

        def att_items(c, branch, h):
            g = h // 3
            oacc = OACC[c % 2]
            kts = list(range(0, 4 * c + 4) if branch == 1 else range(max(4 * c - 4, 0), 4 * c + 4))
            shared = {"first": True}
            res = []
            for kt in kts:
                lo = max(kt - 4 * c, 0)
                hi = 3 if branch == 1 else min(kt + 4 - 4 * c, 3)
                n_ = (hi - lo + 1) * 128
                q0 = c * 512 + lo * 128
                ks_ = slice(kt * 128, (kt + 1) * 128)
                st = {}

                def S_(kt=kt, lo=lo, hi=hi, n_=n_, q0=q0, ks_=ks_, st=st):
                    sp_ = nxt("s", SP)
                    if branch == 1:
                        cx.mm(sp_.t[:, 0:n_], KSA[g].t[:, ks_], QA[h].t[:, q0:q0 + n_], True, True,
                              [KSA[g], QA[h]], [sp_])
                    else:
                        cx.mm(sp_.t[:, 0:n_], KW[g].t[0:64, ks_], QA[h].t[0:64, q0:q0 + n_], True, True,
                              [KW[g], QA[h]], [sp_])
                    if kt >= 4 * c:
                        cx.mm(sp_.t[:, 0:128], identb.t[:], causb.t[:], False, True, [identb, causb], [sp_], skip=True)
                    if branch == 2 and 4 * c <= kt + 4 <= 4 * c + 3:
                        cx.mm(sp_.t[:, n_ - 128:n_], identb.t[:], upperb.t[:], False, True, [identb, upperb], [sp_],
                              skip=True)
                    pT = nxt("p", pTs)
                    cx.act(pT.t[:, 0:n_], sp_.t[:, 0:n_], AF.Exp, [sp_], [pT], scale=0.125)
                    st["pT"] = pT

                def PV_(kt=kt, lo=lo, hi=hi, st=st, last=(kt == kts[-1])):
                    if shared["first"]:
                        shared["ob"] = nxt("o", OP)
                    ob_ = shared["ob"]
                    ov = ob_.t[:, 0:260].rearrange("p (a b) -> p a b", b=65)
                    pT = st["pT"]
                    vx = VSX if branch == 1 else VWX
                    for q4 in range(lo, hi + 1):
                        cx.mm(ov[:, q4, :], pT.t[:, (q4 - lo) * 128:(q4 - lo + 1) * 128], vx.t[:, kt, g, :],
                              shared["first"], True, [pT, vx], [ob_], skip=not shared["first"])
                        shared["first"] = False
                    if last:
                        rc = nxt("r", recs)
                        cx.ts("dve", rc.t[:, 0, :], ov[:, :, 64], 1e-30, ALU.add, [ob_], [rc])
                        cx.op("dve", lambda e, rc=rc: e.reciprocal(out=rc.t[:, 0, :], in_=rc.t[:, 0, :]), [rc], [rc])
                        col = 3 * h + branch
                        cx.tt("dve", rc.t[:, 1, :], rc.t[:, 0, :], GT.t[:, 4 * c:4 * c + 4, col], ALU.mult, [rc, GT], [rc])
                        for q4 in range(4):
                            av = oacc.t[:, q4, h * 64:(h + 1) * 64]
                            cx.stt(av, ov[:, q4, 0:64], rc.t[:, 1, q4:q4 + 1], av, ALU.mult, ALU.add,
                                   [ob_, rc, oacc], [oacc])
                res.append((S_, PV_))
            return res

        def out_item(c):
            cs = slice(c * 512, (c + 1) * 512)
            oacc = OACC[c % 2]

            def PV_():
                ys = yst[c % 2]
                for q4 in range(4):
                    ob2 = obf[q4 % 2]
                    cx.copy("pool", ob2.t[:], oacc.t[:, q4, :], [oacc], [ob2])
                    tp = nxt("t", TP)
                    for j in range(3):
                        cx.tr(tp.t[:, j, :], ob2.t[:, j * 128:(j + 1) * 128], ident.t[:], [ob2, ident], [tp])
                    cx.copy("act", ys.t[:, :, q4 * 128:(q4 + 1) * 128], tp.t[:, 0:3, :], [tp], [ys])
                for j in range(3):
                    cx.dma("sp", S["yt"].t[640 + j * 128:640 + (j + 1) * 128, cs], ys.t[:, j, :], [ys], [S["yt"]])
            return (lambda: None), PV_

        for c in range(8):
            for h in range(6):
                items.append(cmp_item(c, h))
            if dstop >= 5:
                for branch in ((1, 2) if dstop >= 6 else (1,)):
                    for h in range(6):
                        items.extend(att_items(c, branch, h))
            items.append(out_item(c))
        for i in range(len(items) + 1):
            if i < len(items):
                items[i][0]()
            if i >= 1:
                items[i - 1][1]()


def phase_E(cx, l, xd, x1d, W, G, S):
    with cx.phase() as es:
        wo = load_cast(cx, "wout", W["wout"], [128, 8, D])
        g_t = load_f32(cx, "ln1g", W["ln1_g"].t.broadcast_to([128, D]), W["ln1_g"], [128, D])
        b_t = load_f32(cx, "ln1b", W["ln1_b"].t.broadcast_to([128, D]), W["ln1_b"], [128, D])
        yts = [cx.sb(f"eyt{i}", [128, 8, 512], BF16) for i in range(2)]
        xrs = [cx.sb(f"exr{i}", [128, D], F32) for i in range(2)]
        pss = [cx.ps(f"eps{i}", [128, 512], F32) for i in range(4)]
        tmps = ln_tmps(cx, es)
        def load_y(tc):
            cx.dma("sp", yts[tc % 2].t[:], S["yt"].t[:, tc * 512:(tc + 1) * 512].rearrange("(k p) t -> p k t", p=128),
                   [S["yt"]], [yts[tc % 2]])
        load_y(0)
        for tc in range(8):
            y_ = yts[tc % 2]
            if tc + 1 < 8:
                load_y(tc + 1)
            for q in range(4):
                tt = tc * 4 + q
                xr = xrs[tt % 2]
                cx.dma("sp", xr.t[:], xd.t[tt * 128:(tt + 1) * 128, :], [xd], [xr])
                zp = pss[(tt % 2) * 2:(tt % 2) * 2 + 2]
                for hf in range(2):
                    for k in range(8):
                        cx.mm(zp[hf].t[:], y_.t[:, k, q * 128:(q + 1) * 128], wo.t[:, k, hf * 512:(hf + 1) * 512],
                              k == 0, k == 7, [y_, wo], [zp[hf]])
                layer_norm_store(cx, zp, xr, g_t, b_t, x1d, tt * 128, tmps[tt % 2])


def phase_F(cx, l, x1d, x2d, W, G, S):
    TC = 1024
    ident = G["ident"]
    with cx.phase() as es:
        wd = cx.sb("wd_b", [128, NF, D], BF16)
        for f in range(NF):
            cx.dma("pool", wd.t[:, f, :], W["wd"].t[:, f, :], [W["wd"]], [wd])
        cw = load_f32(cx, "cw", W["cw"].t, W["cw"], [128, NF, 3])
        cb = load_f32(cx, "cb", W["cb"].t, W["cb"], [128, NF])
        g_t = load_f32(cx, "ln2g", W["ln2_g"].t.broadcast_to([128, D]), W["ln2_g"], [128, D])
        b_t = load_f32(cx, "ln2b", W["ln2_b"].t.broadcast_to([128, D]), W["ln2_b"], [128, D])
        carry = cx.sb("carry", [128, NF, 2], F32)
        cx.op("pool", lambda e: e.memset(carry.t[:], 0.0), [], [carry])
        xT = cx.sb("x1T", [128, 8, TC], BF16)
        act = cx.sb("ffact", [128, NF, TC], BF16)
        wb = [cx.sb(f"fwb{i}", [128, 2, 8, 128], BF16) for i in range(3)]
        hts = [cx.sb(f"fhb{i}", [128, 2 + TC], F32) for i in range(2)]
        hbA = [Tile(t.t, Buf(f"fhbA{i}")) for i, t in enumerate(hts)]
        hbB = [Tile(t.t, Buf(f"fhbB{i}")) for i, t in enumerate(hts)]
        hc = [cx.sb(f"fhc{i}", [128, 512], F32) for i in range(2)]
        gl = [cx.sb(f"fgl{i}", [128, 512], F32) for i in range(2)]
        xrs = [cx.sb(f"fxr{i}", [128, D], F32) for i in range(2)]
        tmps = ln_tmps(cx, es)
        pss = [cx.ps(f"fps{i}", [128, 512], F32) for i in range(6)]
        gi = 0
        nsteps = (T // TC) * NF

        def load_w(step):
            f = step % NF
            b_ = wb[step % 3]
            cx.dma("pool", b_.t[:, 0], W["wg"].t[f], [W["wg"]], [b_])
            cx.dma("pool", b_.t[:, 1], W["wu"].t[f], [W["wu"]], [b_])
        load_w(0)
        load_w(1)
        step = 0
        for tc in range(T // TC):
            build_xT(cx, x1d, xT, ident, TC // 128, tok0=tc * TC)
            for f in range(NF):
                b_ = wb[step % 3]
                if step + 2 < nsteps:
                    load_w(step + 2)
                step += 1
                hA, hB = hbA[f % 2], hbB[f % 2]
                ht = hts[f % 2].t
                cx.copy("act", ht[:, 0:2], carry.t[:, f, :], [carry], [hA])
                for hf in range(TC // 512):
                    ts_ = slice(hf * 512, (hf + 1) * 512)
                    pg, pu = pss[(gi % 2) * 2], pss[(gi % 2) * 2 + 1]
                    c_, g_ = hc[gi % 2], gl[gi % 2]
                    gi += 1
                    for k in range(8):
                        cx.mm(pg.t[:], b_.t[:, 0, k, :], xT.t[:, k, ts_], k == 0, k == 7, [b_, xT], [pg])
                    for k in range(8):
                        cx.mm(pu.t[:], b_.t[:, 1, k, :], xT.t[:, k, ts_], k == 0, k == 7, [b_, xT], [pu])
                    hw_ = [hA] if hf == 0 else [hB]
                    hr_ = [hA] if hf == 0 else [hA, hB]
                    o = hf * 512
                    cx.copy("act", ht[:, 2 + o:514 + o], pg.t[:], [pg], hw_)
                    cx.ts("dve", c_.t[:], ht[:, 2 + o:514 + o], cw.t[:, f, 2:3], ALU.mult, hr_ + [cw, cb], [c_],
                          s2=cb.t[:, f:f + 1], op1=ALU.add)
                    cx.stt(c_.t[:], ht[:, 1 + o:513 + o], cw.t[:, f, 1:2], c_.t[:], ALU.mult, ALU.add, hr_ + [cw, c_], [c_])
                    cx.stt(c_.t[:], ht[:, o:512 + o], cw.t[:, f, 0:1], c_.t[:], ALU.mult, ALU.add, hr_ + [cw, c_], [c_])
                    cx.act(g_.t[:], c_.t[:], AF.Gelu_apprx_tanh, [c_], [g_])
                    cx.tt("dve", act.t[:, f, ts_], g_.t[:], pu.t[:], ALU.mult, [g_, pu], [act])
                cx.copy("act", carry.t[:, f, :], ht[:, TC:TC + 2], [hB], [carry])
            for q in range(TC // 128):
                tt = tc * (TC // 128) + q
                xr = xrs[tt % 2]
                cx.dma("sp", xr.t[:], x1d.t[tt * 128:(tt + 1) * 128, :], [x1d], [xr])
                zp = pss[4:6]
                for hf in range(2):
                    for f in range(NF):
                        cx.mm(zp[hf].t[:], act.t[:, f, q * 128:(q + 1) * 128], wd.t[:, f, hf * 512:(hf + 1) * 512],
                              f == 0, f == NF - 1, [act, wd], [zp[hf]])
                layer_norm_store(cx, zp, xr, g_t, b_t, x2d, tt * 128, tmps[tt % 2])

SCRATCH = {
    "rope": ([2, 128, T], F32), "vp": ([256, T], F32),
    "qr": ([384, T], BF16), "kr": ([384, T], BF16), "qn": ([384, T], BF16),
    "ks": ([128, T], BF16), "kw": ([128, T], BF16), "kc": ([128, T], BF16), "vc": ([128, T], BF16),
    "tmb": ([T, 640], BF16), "gr": ([T, 384], F32), "gt": ([T, 18], F32),
    "yt": ([1024, T], BF16), "x1": ([T, D], F32), "xmid": ([T, D], F32),
}


def build_program(layers=(0, 1), phases="ABCDEF", ext_in=(), ext_out=(), prologue=True):
    nc = bass.Bass("TRN2", target_bir_lowering=False)
    cx = Ctx(nc, ext_in, ext_out)
    xd = cx.dr("x", [T, D], F32, kind="ExternalInput")
    posd = cx.dr("pos", [1, T], I32, kind="ExternalInput")
    Cd = {k: cx.dr(k, list(v.shape), F32, kind="ExternalInput") for k, v in CONSTS.items()}
    Wd = {l: {k: cx.dr(f"{k}_{l}", shp, F32, kind="ExternalInput") for k, shp in LAYER_SHAPES.items()} for l in layers}
    S = {k: cx.dr(k, shp, dt) for k, (shp, dt) in SCRATCH.items()}
    outd = cx.dr("y", [T, D], F32, kind="ExternalOutput")
    with contextlib.ExitStack() as gs:
        cx.stack = gs
        G = {"rope": S["rope"]}
        G["ident"] = load_cast(cx, "ident", Cd["c_ident"], [128, 128])
        G["inv"] = load_f32(cx, "inv", Cd["c_inv"].t, Cd["c_inv"], [128, 1])
        G["sgn"] = load_f32(cx, "sgn", Cd["c_sgn"].t, Cd["c_sgn"], [128, 1])
        G["pw"] = load_f32(cx, "pw", Cd["c_pw"].t, Cd["c_pw"], [128, 2])
        G["prc"] = load_f32(cx, "prc", Cd["c_prc"].t, Cd["c_prc"], [128, 2, 16])
        G["C"] = Cd
        if prologue:
            prologue_rope(cx, posd, G)
        cur = xd
        for li, l in enumerate(layers):
            nxt = outd if li == len(layers) - 1 else S["xmid"]
            W = Wd[l]
            if "A" in phases:
                phase_A(cx, l, cur, W, G, S)
            if "B" in phases:
                phase_B(cx, l, W, G, S)
            if "C" in phases:
                phase_C(cx, l, W, G, S)
            if "D" in phases:
                phase_D(cx, l, W, G, S)
            if "E" in phases:
                phase_E(cx, l, cur, S["x1"], W, G, S)
            if "F" in phases:
                phase_F(cx, l, S["x1"], nxt, W, G, S)
            cur = nxt
        cx.P.barrier()
        finals = [outd.b] + [cx.dram[n].b for n in cx.ext_out]
        cx.P.emit_all(final_bufs=finals)
    return nc, cx


def make_in_maps(inputs, layers=(0, 1), cores=range(8)):
    shared = dict(CONSTS)
    for l in layers:
        for k, v in _layer_arrays(inputs, l).items():
            assert list(v.shape) == LAYER_SHAPES[k], (k, v.shape)
            shared[f"{k}_{l}"] = v.astype(np.float32, copy=False)
    maps = []
    for b in cores:
        m = dict(shared)
        m["x"] = np.ascontiguousarray(inputs["x"][b])
        m["pos"] = np.ascontiguousarray(inputs["positions"][b].reshape(1, T).astype(np.int32))
        maps.append(m)
    return maps


def kernel(**inputs):
    inputs = {k: np.asarray(v) for k, v in inputs.items()}
    nc, cx = build_program()
    maps = make_in_maps(inputs)
    res = run_bass_kernel_spmd(nc, maps, core_ids=list(range(8)))
    return np.stack([np.asarray(r["y"]) for r in res.results], 0).astype(np.float32)
```

```python
import contextlib
import math
import numpy as np
import concourse.bass as bass
import concourse.mybir as mybir
from concourse.bass_utils import run_bass_kernel_spmd

F32 = mybir.dt.float32
BF16 = mybir.dt.bfloat16
I32 = mybir.dt.int32
AF = mybir.ActivationFunctionType
ALU = mybir.AluOpType
AX = mybir.AxisListType

T = 4096
D = 1024
DEPTH = 2
NT = T // 128
DFF = 2816
NF = DFF // 128
ALPHA = (2 * DEPTH) ** 0.25
NEG = -10000.0
DBG = {}

ENGS = ("pe", "act", "dve", "pool", "sp")


class Buf:
    __slots__ = ("name", "writers", "readers", "dsem", "dcount", "is_dram", "vsem", "excl")

    def __init__(self, name, is_dram=False, excl=False):
        self.name = name
        self.is_dram = is_dram
        self.excl = excl
        self.writers = []
        self.readers = []
        self.dsem = None
        self.dcount = 0
        self.vsem = None


class Op:
    __slots__ = ("eng", "emit", "waits", "is_dma", "dbuf", "dval", "needs_inc", "val", "vsem")

    def __init__(self, eng, emit, is_dma=False):
        self.eng = eng
        self.emit = emit
        self.waits = []
        self.is_dma = is_dma
        self.dbuf = None
        self.dval = 0
        self.needs_inc = False
        self.val = 0
        self.vsem = None


class Prog:
    def __init__(self, nc):
        self.nc = nc
        self.ops = {e: [] for e in ENGS}
        self.last = {e: None for e in ENGS}
        self.dma_bufs = {}
        self.pending_bar = {e: [] for e in ENGS}
        self.seq = 0
        self.bar_seq = 0
        self.vfree = {True: [], False: []}
        self.vkind = []
        self.vcount = []
        self.rsem = []

    def _dep(self, op, prod, force=False):
        if prod is op:
            return
        if (not force) and prod.val < self.bar_seq:
            return
        if not prod.is_dma and prod.eng == op.eng:
            if op.eng in ("pe", "sp"):
                return
        op.waits.append(prod)
        if not prod.is_dma:
            prod.needs_inc = True

    @staticmethod
    def _prune(lst):
        last = {}
        for r in lst:
            last[(r.eng, r.is_dma, r.vsem)] = r
        return list(last.values())

    def op(self, eng, emit, reads=(), writes=(), dma=False):
        o = Op(eng, emit, is_dma=dma)
        self.seq += 1
        o.val = self.seq
        if self.pending_bar[eng]:
            for p in self.pending_bar[eng]:
                self._dep(o, p, force=True)
            self.pending_bar[eng] = []
        for b in reads:
            for w in b.writers:
                self._dep(o, w)
            if b.excl:
                for r in b.readers:
                    if r.eng != eng:
                        self._dep(o, r)
        for b in writes:
            for r in b.readers:
                if (not r.is_dma) and r.eng == eng:
                    continue
                self._dep(o, r)
            if not b.readers:
                for w in b.writers:
                    if w.is_dma and dma:
                        continue
                    if (not w.is_dma) and w.eng == eng:
                        continue
                    self._dep(o, w)
        if dma:
            assert len(writes) == 1
            b = writes[0]
            if b.is_dram:
                b = [r for r in reads if not r.is_dram][0]
            if b.vsem is None:
                sw = (eng == "pool")
                if self.vfree[sw]:
                    b.vsem = self.vfree[sw].pop()
                else:
                    b.vsem = len(self.vcount)
                    self.vcount.append(0)
                    self.vkind.append(sw)
                b.dcount = self.vcount[b.vsem]
            b.dcount += 16
            self.vcount[b.vsem] = b.dcount
            o.dbuf = b
            o.dval = b.dcount
            o.vsem = b.vsem
            self.dma_bufs[id(b)] = b
        for b in writes:
            if b.readers:
                b.writers = [o]
                b.readers = []
            else:
                b.writers.append(o)
                if len(b.writers) > 6:
                    b.writers = self._prune(b.writers)
        for b in reads:
            b.readers.append(o)
            if len(b.readers) > 6:
                b.readers = self._prune(b.readers)
        self.ops[eng].append(o)
        if not dma:
            self.last[eng] = o
        return o

    def dma(self, eng, out, in_, reads, writes):
        return self.op(eng, lambda e: e.dma_start(out=out, in_=in_), reads, writes, dma=True)

    def barrier(self):
        targets = []
        for e in ENGS:
            if self.last[e] is not None:
                targets.append(self.last[e])
        for b in self.dma_bufs.values():
            if b.vsem is not None:
                p = Op("sp", None, is_dma=True)
                p.dbuf = b
                p.dval = b.dcount
                p.vsem = b.vsem
                targets.append(p)
                self.vfree[self.vkind[b.vsem]].append(b.vsem)
                b.vsem = None
        self.dma_bufs = {}
        self.seq += 1
        self.bar_seq = self.seq
        for e in ENGS:
            self.pending_bar[e] = self._prune(self.pending_bar[e] + targets)

    def emit_all(self, final_bufs=()):
        nc = self.nc
        esem = {e: nc.alloc_semaphore(name=f"es_{e}") for e in ENGS}
        self.rsem = [nc.alloc_semaphore(name=f"ds_{i}") for i in range(len(self.vcount))]
        for e in ENGS:
            c = 0
            for o in self.ops[e]:
                if (not o.is_dma) and o.needs_inc:
                    c += 1
                    o.val = c
        engobj = {"pe": "tensor", "act": "scalar", "dve": "vector", "pool": "gpsimd", "sp": "sync"}
        with nc.Block() as block:
            for e in ENGS:
                def body(eng, ops=self.ops[e], e=e):
                    waited = {}
                    for o in ops:
                        need = {}
                        for p in o.waits:
                            if p.is_dma:
                                sem, val = self.rsem[p.vsem], p.dval
                            else:
                                sem, val = esem[p.eng], p.val
                            if sem is None:
                                continue
                            if need.get(sem.num, (None, 0))[1] < val:
                                need[sem.num] = (sem, val)
                        for k, (sem, val) in need.items():
                            if waited.get(k, 0) >= val:
                                continue
                            waited[k] = val
                            eng.wait_ge(sem, val)
                        ins = o.emit(eng)
                        if o.is_dma:
                            ins.then_inc(self.rsem[o.vsem], 16)
                        elif o.needs_inc:
                            ins.then_inc(esem[e], 1)
                    if e == "sp":
                        for i, sem in enumerate(self.rsem):
                            if waited.get(sem.num, 0) < self.vcount[i]:
                                eng.wait_ge(sem, self.vcount[i])

                getattr(block, engobj[e])(body)


OFF = {}
_o = 0
for _n, _w in (("v_pool", 256), ("q_ret", 384), ("k_ret", 384), ("v_ret", 384), ("g_ret", 384),
               ("q_nsa", 384), ("k_cmp", 128), ("v_cmp", 128), ("k_slc", 128), ("v_slc", 128),
               ("k_win", 128), ("v_win", 128), ("gate", 18)):
    OFF[_n] = _o
    _o += _w


def _swap_cols(cols):
    cols = np.asarray(cols).reshape(-1, 64)
    return np.concatenate([cols[:, 32:], cols[:, :32]], axis=1).reshape(-1)


def _fm_cols():
    ch = []
    for c in range(2):
        ch.append(np.arange(OFF["v_pool"] + 128 * c, OFF["v_pool"] + 128 * (c + 1)))
    for name in ("q_ret", "k_ret", "q_nsa"):
        for c in range(3):
            cols = np.arange(OFF[name] + 128 * c, OFF[name] + 128 * (c + 1))
            ch.append(cols)
            ch.append(_swap_cols(cols))
    for name in ("k_slc", "k_win"):
        cols = np.arange(OFF[name], OFF[name] + 128)
        ch.append(cols)
        ch.append(_swap_cols(cols))
    for name in ("k_cmp", "v_cmp"):
        ch.append(np.arange(OFF[name], OFF[name] + 128))
    return ch


FM_COLS = _fm_cols()
NFM = len(FM_COLS)
TM_COLS = np.concatenate([np.arange(OFF["v_ret"], OFF["v_ret"] + 384),
                          np.arange(OFF["v_slc"], OFF["v_slc"] + 128),
                          np.arange(OFF["v_win"], OFF["v_win"] + 128),
                          np.arange(OFF["g_ret"], OFF["g_ret"] + 384),
                          np.arange(OFF["gate"], OFF["gate"] + 18)])
NTM = len(TM_COLS)


def _const_tables():
    c = {}
    p = np.arange(128)
    inv = (10000.0 ** (-np.arange(0, 64, 2, dtype=np.float32) / 64)).astype(np.float32)
    c["c_inv"] = inv[p % 32].reshape(128, 1).astype(np.float32)
    c["c_sgn"] = np.where((p % 64) < 32, -1.0, 1.0).reshape(128, 1).astype(np.float32)
    h = np.arange(6, dtype=np.float64)
    lg = np.log1p(-np.power(2.0, -5.0 - h))
    i = np.arange(128, dtype=np.float64)
    dm = np.zeros((128, 6, 128), np.float32)
    for hh in range(6):
        diff = i[None, :] - i[:, None]
        dm[:, hh, :] = np.where(diff >= 0, 0.125 * np.exp(lg[hh] * np.maximum(diff, 0)), 0.0)
    c["c_dm"] = dm
    xi = np.exp(lg[:, None] * (i[None, :] + 1.0))
    zeta = 0.125 * np.exp(lg[:, None] * (127.0 - i[None, :]))
    gam = np.exp(lg * 128.0)
    xir = np.zeros((128, 3, 128), np.float32)
    zt = np.zeros((128, 3, 128), np.float32)
    gc = np.zeros((128, 3), np.float32)
    for ck in range(3):
        for hh in range(2):
            xir[hh * 64:(hh + 1) * 64, ck, :] = xi[2 * ck + hh][None, :]
            zt[:, ck, hh * 64:(hh + 1) * 64] = zeta[2 * ck + hh][:, None]
            gc[hh * 64:(hh + 1) * 64, ck] = gam[2 * ck + hh]
    c["c_xir"] = xir
    c["c_zt"] = zt
    c["c_gc"] = gc
    win = np.zeros((128, 2), np.float32)
    rc = np.zeros((128, 2, 16), np.float32)
    for ck in range(2):
        for hh in range(2):
            w = (2, 4, 8, 16)[2 * ck + hh]
            win[hh * 64:(hh + 1) * 64, ck] = 1.0 / w
            rc[hh * 64:(hh + 1) * 64, ck, :] = 1.0 / np.minimum(np.arange(16) + 1, w)
    c["c_pw"] = win
    c["c_prc"] = rc
    kk = np.arange(128)[:, None]
    qq = np.arange(128)[None, :]
    c["c_caus"] = np.where(kk > qq, NEG, 0.0).astype(np.float32)
    c["c_upper"] = np.where(kk <= qq, NEG, 0.0).astype(np.float32)
    c["c_ident"] = np.eye(128, dtype=np.float32)
    ex = np.zeros((64, T), np.float32)
    ex[np.arange(T) // 64, np.arange(T)] = 1.0
    c["c_expand"] = ex
    n = np.arange(256)
    ends = 16 * n + 31
    cm = np.where(ends[:, None] > np.arange(T)[None, :], NEG, 0.0).astype(np.float32)
    c["c_cmn"] = np.ascontiguousarray(cm.reshape(2, 128, T).transpose(1, 0, 2))
    ci = np.arange(256)[:, None]
    sj = np.arange(64)[None, :]
    ov = np.clip(np.minimum(ci * 16 + 32, (sj + 1) * 64) - np.maximum(ci * 16, sj * 64), 0, None) / 16.0
    ov[255, :] = 0.0
    c["c_ovl"] = np.ascontiguousarray(ov.astype(np.float32).reshape(2, 128, 64).transpose(1, 0, 2))
    tq = np.arange(T)
    cur = tq // 64
    blk = np.arange(64)[None, :]
    forced = (blk == 0) | (blk == cur[:, None]) | (blk == cur[:, None] - 1)
    valid = blk * 64 <= tq[:, None]
    bias = np.where(valid, np.where(forced, 1e6, 0.0), -100.0).astype(np.float32)
    c["c_sbias"] = np.ascontiguousarray(bias.reshape(32, 128, 64).transpose(1, 0, 2))
    return c


CONSTS = _const_tables()


def _layer_arrays(inp, l):
    a = {}
    w_in = inp["w_in"][l]
    wk = w_in.reshape(8, 128, -1)
    a["wfm"] = np.ascontiguousarray(
        np.stack([wk[:, :, cols].transpose(1, 0, 2) for cols in FM_COLS], 0))
    a["wtm"] = np.ascontiguousarray(wk[:, :, TM_COLS].transpose(1, 0, 2))
    a["wout"] = np.ascontiguousarray(inp["w_out"][l].reshape(8, 128, D).transpose(1, 0, 2))
    pw = inp["pool_w"][l]
    bd = np.zeros((2, 128, 128), np.float32)
    for ck in range(2):
        for hh in range(2):
            bd[ck, hh * 64:(hh + 1) * 64, hh * 64:(hh + 1) * 64] = pw[2 * ck + hh]
    a["bd"] = bd
    a["psc"] = np.ascontiguousarray(inp["pool_scale"][l].reshape(2, 128).T)
    a["gng"] = np.ascontiguousarray(inp["ret_gn_g"][l].reshape(1, 384))
    for kv in ("k", "v"):
        w1 = inp[f"cmp_w1_{kv}"][l].reshape(32, 64, 128)
        w1d = np.concatenate([w1, w1], axis=1).transpose(1, 0, 2)
        a[f"w1{kv}"] = np.ascontiguousarray(w1d)
        a[f"b1{kv}"] = np.ascontiguousarray(inp[f"cmp_b1_{kv}"][l].reshape(128, 1))
        a[f"pos{kv}"] = np.ascontiguousarray(inp[f"cmp_pos_{kv}"][l].T)
        a[f"w2{kv}"] = np.ascontiguousarray(inp[f"cmp_w2_{kv}"][l])
    a["w2ks"] = np.ascontiguousarray(inp["cmp_w2_k"][l][:, _swap_cols(np.arange(64))])
    a["wg"] = np.ascontiguousarray(inp["ffn_w_gate"][l].reshape(8, 128, NF, 128).transpose(2, 1, 0, 3))
    a["wu"] = np.ascontiguousarray(inp["ffn_w_up"][l].reshape(8, 128, NF, 128).transpose(2, 1, 0, 3))
    a["wd"] = np.ascontiguousarray(inp["ffn_w_down"][l].reshape(NF, 128, D).transpose(1, 0, 2))
    a["cw"] = np.ascontiguousarray(inp["ffn_conv_w"][l].reshape(3, NF, 128).transpose(2, 1, 0))
    a["cb"] = np.ascontiguousarray(inp["ffn_conv_b"][l].reshape(NF, 128).T)
    for nme in ("ln1_g", "ln1_b", "ln2_g", "ln2_b"):
        a[nme] = np.ascontiguousarray(inp[nme][l].reshape(1, D))
    return a


LAYER_SHAPES = {
    "wfm": [NFM, 128, 8, 128], "wtm": [128, 8, NTM], "wout": [128, 8, D], "bd": [2, 128, 128],
    "psc": [128, 2], "gng": [1, 384],
    "w1k": [128, 32, 128], "b1k": [128, 1], "posk": [64, 32], "w2k": [128, 64],
    "w1v": [128, 32, 128], "b1v": [128, 1], "posv": [64, 32], "w2v": [128, 64], "w2ks": [128, 64],
    "wg": [NF, 128, 8, 128], "wu": [NF, 128, 8, 128], "wd": [128, NF, D], "cw": [128, NF, 3], "cb": [128, NF],
    "ln1_g": [1, D], "ln1_b": [1, D], "ln2_g": [1, D], "ln2_b": [1, D],
}


class Tile:
    __slots__ = ("t", "b")

    def __init__(self, t, b):
        self.t = t
        self.b = b


class Ctx:
    def __init__(self, nc, ext_in=(), ext_out=()):
        self.nc = nc
        self.P = Prog(nc)
        self.ext_in = set(ext_in)
        self.ext_out = set(ext_out)
        self.dram = {}
        self.stack = None
        self.uid = 0

    def dr(self, name, shape, dt, kind=None):
        if kind is None:
            kind = "ExternalInput" if name in self.ext_in else ("ExternalOutput" if name in self.ext_out else "Internal")
        t = self.nc.dram_tensor(name, list(shape), dt, kind=kind).ap()
        tl = Tile(t, Buf(name, is_dram=True))
        self.dram[name] = tl
        return tl

    def sb(self, name, shape, dt, es=None):
        self.uid += 1
        t = (es or self.stack).enter_context(self.nc.sbuf_tensor(f"{name}_{self.uid}", list(shape), dt))
        return Tile(t, Buf(name))

    def ps(self, name, shape, dt, es=None):
        self.uid += 1
        t = (es or self.stack).enter_context(self.nc.psum_tensor(f"{name}_{self.uid}", list(shape), dt))
        return Tile(t, Buf(name, excl=True))

    @contextlib.contextmanager
    def phase(self):
        old = self.stack
        with contextlib.ExitStack() as es:
            self.stack = es
            yield es
            self.P.barrier()
        self.stack = old

    def dma(self, eng, out, in_, reads, writes):
        self.P.dma(eng, out, in_, [x.b for x in reads], [x.b for x in writes])

    def op(self, eng, fn, reads, writes):
        self.P.op(eng, fn, [x.b for x in reads], [x.b for x in writes])

    def mm(self, out, lhsT, rhs, start, stop, reads, writes, skip=False):
        kw = dict(start=start, stop=stop)
        if skip:
            kw["skip_group_check"] = True
        self.op("pe", lambda e: e.matmul(out, lhsT=lhsT, rhs=rhs, **kw), reads, writes)

    def tr(self, out, in_, ident, reads, writes):
        self.op("pe", lambda e: e.transpose(out, in_, ident), reads, writes)

    def copy(self, eng, out, in_, reads, writes):
        if eng == "act":
            self.op("act", lambda e: e.copy(out=out, in_=in_), reads, writes)
        else:
            self.op(eng, lambda e: e.tensor_copy(out=out, in_=in_), reads, writes)

    def act(self, out, in_, func, reads, writes, **kw):
        self.op("act", lambda e: e.activation(out=out, in_=in_, func=func, **kw), reads, writes)

    def tt(self, eng, out, in0, in1, op, reads, writes):
        self.op(eng, lambda e: e.tensor_tensor(out=out, in0=in0, in1=in1, op=op), reads, writes)

    def ts(self, eng, out, in0, s1, op0, reads, writes, s2=None, op1=None):
        if op1 is None:
            self.op(eng, lambda e: e.tensor_scalar(out=out, in0=in0, scalar1=s1, scalar2=None, op0=op0), reads, writes)
        else:
            self.op(eng, lambda e: e.tensor_scalar(out=out, in0=in0, scalar1=s1, scalar2=s2, op0=op0, op1=op1), reads, writes)

    def stt(self, out, in0, scalar, in1, op0, op1, reads, writes):
        self.op("dve", lambda e: e.scalar_tensor_tensor(out=out, in0=in0, scalar=scalar, in1=in1, op0=op0, op1=op1),
                reads, writes)


def load_cast(cx, name, src_tile, shape, es=None, eng=None, q=None):
    b = cx.sb(name + "_b", shape, BF16, es)
    cx.dma("pool", b.t[:], src_tile.t, [src_tile], [b])
    return b


def load_f32(cx, name, src_ap, src_tile, shape, es=None, q="sp"):
    f = cx.sb(name, shape, F32, es)
    cx.dma(q, f.t[:], src_ap, [src_tile], [f])
    return f


def build_xT(cx, xd, xT, ident, ntiles, tok0=0):
    with contextlib.ExitStack() as es:
        xb = [cx.sb(f"xb{i}", [128, D], BF16, es) for i in range(3)]
        pt = [cx.ps(f"xpt{i}", [128, 8, 128], BF16, es) for i in range(2)]
        for tt in range(ntiles):
            bb, p = xb[tt % 3], pt[tt % 2]
            r0 = tok0 + tt * 128
            cx.dma("pool", bb.t[:], xd.t[r0:r0 + 128, :], [xd], [bb])
            for k in range(8):
                cx.tr(p.t[:, k, :], bb.t[:, k * 128:(k + 1) * 128], ident.t[:], [bb, ident], [p])
            cx.copy("act" if tt % 2 == 0 else "dve", xT.t[:, :, tt * 128:(tt + 1) * 128], p.t[:], [p], [xT])
        cx.P.barrier()


def layer_norm_store(cx, zps, xres, g_t, b_t, outd, r0, tmp, eng_q="pool"):
    z, st, mv, rs, o = tmp
    for hf in range(2):
        cx.stt(z.t[:, hf * 512:(hf + 1) * 512], xres.t[:, hf * 512:(hf + 1) * 512], ALPHA, zps[hf].t[:],
               ALU.mult, ALU.add, [xres, zps[hf]], [z])
    for hf in range(2):
        cx.op("dve", lambda e, hf=hf: e.bn_stats(out=st.t[:, hf, :], in_=z.t[:, hf * 512:(hf + 1) * 512]), [z], [st])
    cx.op("dve", lambda e: e.bn_aggr(out=mv.t[:], in_=st.t[:]), [st], [mv])
    cx.ts("dve", rs.t[:], mv.t[:, 1:2], 1e-5, ALU.add, [mv], [rs])
    cx.act(rs.t[:], rs.t[:], AF.Sqrt, [rs], [rs])
    cx.op("dve", lambda e: e.reciprocal(out=rs.t[:], in_=rs.t[:]), [rs], [rs])
    cx.ts("dve", o.t[:], z.t[:], mv.t[:, 0:1], ALU.subtract, [z, mv, rs], [o], s2=rs.t[:, 0:1], op1=ALU.mult)
    cx.tt("pool", o.t[:], o.t[:], g_t.t[:], ALU.mult, [o, g_t], [o])
    cx.tt("pool", o.t[:], o.t[:], b_t.t[:], ALU.add, [o, b_t], [o])
    cx.dma(eng_q, outd.t[r0:r0 + 128, :], o.t[:], [o], [outd])


def ln_tmps(cx, es, n=2):
    res = []
    for i in range(n):
        res.append((cx.sb(f"lnz{i}", [128, D], F32, es), cx.sb(f"lnst{i}", [128, 2, 6], F32, es),
                    cx.sb(f"lnmv{i}", [128, 2], F32, es), cx.sb(f"lnrs{i}", [128, 1], F32, es),
                    cx.sb(f"lno{i}", [128, D], F32, es)))
    return res


def prologue_rope(cx, posd, G):
    rope = G["rope"]
    with cx.phase() as es:
        pi_ = cx.sb("posi", [128, T], I32)
        ang = cx.sb("ang", [128, T], F32)
        m = cx.sb("rm", [128, T], F32)
        o = cx.sb("ro", [128, T], F32)
        cx.dma("sp", pi_.t[:], posd.t.broadcast_to([128, T]), [posd], [pi_])
        cx.copy("dve", ang.t[:], pi_.t[:], [pi_], [ang])
        cx.ts("dve", ang.t[:], ang.t[:], G["inv"].t[:, 0:1], ALU.mult, [ang, G["inv"]], [ang])
        ki = cx.sb("rki", [128, T], I32)
        C1 = 6.28125
        C2 = 2.0 * math.pi - 6.28125
        for which, shift in ((0, 0.25), (1, 0.0)):
            cx.ts("dve", m.t[:], ang.t[:], 1.0 / (2.0 * math.pi), ALU.mult, [ang], [m], s2=shift, op1=ALU.add)
            cx.copy("dve", ki.t[:], m.t[:], [m], [ki])
            cx.copy("dve", m.t[:], ki.t[:], [ki], [m])
            cx.stt(o.t[:], m.t[:], -C1, ang.t[:], ALU.mult, ALU.add, [m, ang], [o])
            cx.stt(o.t[:], m.t[:], -C2, o.t[:], ALU.mult, ALU.add, [m, o], [o])
            if which == 0:
                cx.ts("dve", o.t[:], o.t[:], 0.5 * math.pi, ALU.add, [o], [o])
            cx.ts("dve", o.t[:], o.t[:], math.pi, ALU.min, [o], [o], s2=-math.pi, op1=ALU.max)
            cx.act(o.t[:], o.t[:], AF.Sin, [o], [o])
            if which == 1:
                cx.ts("dve", o.t[:], o.t[:], G["sgn"].t[:, 0:1], ALU.mult, [o, G["sgn"]], [o])
            cx.dma("sp", rope.t[which], o.t[:], [o], [rope])


def phase_A(cx, l, xd, W, G, S):
    ident = G["ident"]
    rope = G["rope"]
    with cx.phase():
        xT = cx.sb("xT", [128, 8, T], BF16)
        build_xT(cx, xd, xT, ident, NT)
        with cx.phase() as es:
          if DBG.get("tm", True):
              wtm = load_cast(cx, "wtm", W["wtm"], [128, 8, NTM])
              pss = [cx.ps(f"tmps{i}", [128, 512], F32) for i in range(6)]
              ob = [cx.sb(f"tmob{i}", [128, 640], BF16) for i in range(2)]
              og = [cx.sb(f"tmog{i}", [128, 384], F32) for i in range(2)]
              ogt = [cx.sb(f"tmogt{i}", [128, 18], F32) for i in range(2)]
              for tt in range(NT):
                  p0, p1, p2 = pss[(tt % 2) * 3:(tt % 2) * 3 + 3]
                  for (pp, c0, c1) in ((p0, 0, 512), (p1, 512, 1024), (p2, 1024, NTM)):
                      for k in range(8):
                          cx.mm(pp.t[:, 0:c1 - c0], xT.t[:, k, tt * 128:(tt + 1) * 128], wtm.t[:, k, c0:c1],
                                k == 0, k == 7, [xT, wtm], [pp])
                  b_, g_, t_ = ob[tt % 2], og[tt % 2], ogt[tt % 2]
                  cx.copy("dve", b_.t[:, 0:512], p0.t[:], [p0], [b_])
                  cx.copy("dve", b_.t[:, 512:640], p1.t[:, 0:128], [p1], [b_])
                  cx.act(g_.t[:], p1.t[:, 128:512], AF.Silu, [p1], [g_])
                  cx.act(t_.t[:], p2.t[:, 0:18], AF.Sigmoid, [p2], [t_])
                  r0 = tt * 128
                  cx.dma("sp", S["tmb"].t[r0:r0 + 128, :], b_.t[:], [b_], [S["tmb"]])
                  cx.dma("sp", S["gr"].t[r0:r0 + 128, :], g_.t[:], [g_], [S["gr"]])
                  cx.dma("sp", S["gt"].t[r0:r0 + 128, :], t_.t[:], [t_], [S["gt"]])
        with cx.phase() as es:
          if DBG.get("fm", True):
              C = cx.sb("ropeC", [128, T], F32)
              Sn = cx.sb("ropeS", [128, T], F32)
              cx.dma("sp", C.t[:], rope.t[0], [rope], [C])
              cx.dma("sp", Sn.t[:], rope.t[1], [rope], [Sn])
              wb = [cx.sb(f"wb{i}", [128, 2, 8, 128], BF16) for i in range(2)]
              pss = [cx.ps(f"fmps{i}", [128, 512], F32) for i in range(8)]
              ost = [cx.sb(f"fmo{i}", [128, T], BF16) for i in range(2)]
              vst = [cx.sb(f"fmv{i}", [128, 512], F32) for i in range(2)]
              t1s = [cx.sb(f"fmt1{i}", [128, 512], F32) for i in range(2)]
              t2s = [cx.sb(f"fmt2{i}", [128, 512], F32) for i in range(2)]
              units = [([0], S["vp"], 0, False), ([1], S["vp"], 128, False)]
              ci = 2
              for dest in ("qr", "kr", "qn"):
                  for c in range(3):
                      units.append(([ci, ci + 1], S[dest], 128 * c, True))
                      ci += 2
              for dest in ("ks", "kw"):
                  units.append(([ci, ci + 1], S[dest], 0, True))
                  ci += 2
              units.append(([ci], S["kc"], 0, False))
              units.append(([ci + 1], S["vc"], 0, False))
              gi = 0

              def load_w(ui):
                  for j, cid in enumerate(units[ui][0]):
                      cx.dma("pool", wb[ui % 2].t[:, j], W["wfm"].t[cid], [W["wfm"]], [wb[ui % 2]])
              load_w(0)
              for ui, (cids, dest, row0, is_rope) in enumerate(units):
                  b_ = wb[ui % 2]
                  n = len(cids)
                  if ui + 1 < len(units):
                      load_w(ui + 1)
                  o_ = ost[ui % 2]
                  for tc in range(8):
                      ts_ = slice(tc * 512, (tc + 1) * 512)
                      pp = [pss[(gi % 4) * 2 + j] for j in range(n)]
                      gi += 1
                      for j in range(n):
                          for k in range(8):
                              cx.mm(pp[j].t[:], b_.t[:, j, k, :], xT.t[:, k, ts_], k == 0, k == 7, [b_, xT], [pp[j]])
                      if is_rope:
                          t1, t2 = t1s[tc % 2], t2s[tc % 2]
                          cx.tt("dve", t1.t[:], pp[0].t[:], C.t[:, ts_], ALU.mult, [pp[0], C], [t1])
                          cx.tt("dve", t2.t[:], pp[1].t[:], Sn.t[:, ts_], ALU.mult, [pp[1], Sn], [t2])
                          cx.tt("pool", o_.t[:, ts_], t1.t[:], t2.t[:], ALU.add, [t1, t2], [o_])
                      elif dest is S["vp"]:
                          v_ = vst[tc % 2]
                          cx.copy("act", v_.t[:], pp[0].t[:], [pp[0]], [v_])
                          cx.dma("sp", dest.t[row0:row0 + 128, ts_], v_.t[:], [v_], [dest])
                      else:
                          cx.copy("act", o_.t[:, ts_], pp[0].t[:], [pp[0]], [o_])
                  if dest is not S["vp"]:
                      cx.dma("sp", dest.t[row0:row0 + 128, :], o_.t[:], [o_], [dest])


def phase_B(cx, l, W, G, S):
    with cx.phase():
        psc = load_f32(cx, "psc", W["psc"].t, W["psc"], [128, 2])
        pw = G["pw"]
        prc = G["prc"]
        pss = [cx.ps(f"bps{i}", [128, 512], F32) for i in range(4)]
        for ck in range(2):
            bd = load_cast(cx, f"bd{ck}", Tile(W["bd"].t[ck], W["bd"].b), [128, 128])
            v = cx.sb(f"pv{ck}", [128, 16 + T], F32)
            s2 = cx.sb(f"ps2{ck}", [128, 16 + T], F32)
            s4 = cx.sb(f"ps4{ck}", [128, 16 + T], F32)
            mx = cx.sb(f"pmx{ck}", [128, T], BF16)
            o = cx.sb(f"pbo{ck}", [128, T], BF16)
            for t_ in (v, s2, s4):
                cx.op("pool", lambda e, t_=t_: e.memset(t_.t[:, 0:16], 0.0), [], [t_])
            cx.dma("sp", v.t[:, 16:], S["vp"].t[ck * 128:(ck + 1) * 128, :], [S["vp"]], [v])
            if ck == 0:
                cx.tt("dve", s2.t[:, 16:], v.t[:, 16:], v.t[:, 15:15 + T], ALU.add, [v], [s2])
                cx.tt("dve", s4.t[64:128, 16:], s2.t[64:128, 16:], s2.t[64:128, 14:14 + T], ALU.add, [s2], [s4])
                lo, hi = s2, s4
            else:
                cx.tt("dve", s2.t[:, 16:], v.t[:, 16:], v.t[:, 15:15 + T], ALU.add, [v], [s2])
                cx.tt("dve", s4.t[:, 16:], s2.t[:, 16:], s2.t[:, 14:14 + T], ALU.add, [s2], [s4])
                cx.tt("dve", s2.t[:, 16:], s4.t[:, 16:], s4.t[:, 12:12 + T], ALU.add, [s4], [s2])
                cx.tt("dve", s4.t[64:128, 16:], s2.t[64:128, 16:], s2.t[64:128, 8:8 + T], ALU.add, [s2], [s4])
                lo, hi = s2, s4
            for (src, r) in ((lo, slice(0, 64)), (hi, slice(64, 128))):
                cx.stt(mx.t[r, :], src.t[r, 16:], pw.t[r, ck:ck + 1], v.t[r, 16:], ALU.mult, ALU.subtract,
                       [src, pw, v], [mx])
                cx.tt("dve", src.t[r, 0:16], src.t[r, 16:32], prc.t[r, ck, :], ALU.mult, [src, prc, mx], [src])
                cx.tt("dve", mx.t[r, 0:16], src.t[r, 0:16], v.t[r, 16:32], ALU.subtract, [src, v], [mx])
            for tc in range(8):
                ts_ = slice(tc * 512, (tc + 1) * 512)
                pp = pss[tc % 4]
                cx.mm(pp.t[:], bd.t[:], mx.t[:, ts_], True, True, [bd, mx], [pp])
                cx.act(o.t[:, ts_], pp.t[:], AF.Copy, [pp, psc], [o], scale=psc.t[:, ck:ck + 1])
            cx.dma("sp", S["yt"].t[ck * 128:(ck + 1) * 128, :], o.t[:], [o], [S["yt"]])


def phase_C(cx, l, W, G, S):
    Cd = G["C"]
    ident = G["ident"]
    with cx.phase() as es:
        dm = load_f32(cx, "dm", Cd["c_dm"].t, Cd["c_dm"], [128, 6, 128])
        xir = load_f32(cx, "xir", Cd["c_xir"].t, Cd["c_xir"], [128, 3, 128])
        zt = load_f32(cx, "zt", Cd["c_zt"].t, Cd["c_zt"], [128, 3, 128])
        gc = load_f32(cx, "gc", Cd["c_gc"].t, Cd["c_gc"], [128, 3])
        gng = load_f32(cx, "gng", W["gng"].t.broadcast_to([128, 384]), W["gng"], [128, 384])
        sps = [cx.ps(f"csp{i}", [128, 128], F32) for i in range(2)]
        ops_ = [cx.ps(f"cop{i}", [128, 128], F32) for i in range(2)]
        kvp = cx.ps("ckv", [128, 128], F32)
        ktp = [cx.ps(f"cktp{i}", [128, 128], BF16) for i in range(2)]
        ytp = cx.ps("cytp", [128, 128], BF16)
        sms = [cx.sb(f"csm{i}", [128, 128], BF16) for i in range(4)]
        kzs = [cx.sb(f"ckz{i}", [128, 128], BF16) for i in range(2)]
        sts = [cx.sb(f"cst{i}", [128, 2, 6], F32) for i in range(2)]
        mvs = [cx.sb(f"cmv{i}", [128, 2, 2], F32) for i in range(2)]
        rss = [cx.sb(f"crs{i}", [128, 2], F32) for i in range(2)]
        ons = [cx.sb(f"con{i}", [128, 128], F32) for i in range(2)]
        onb = [cx.sb(f"conb{i}", [128, 128], BF16) for i in range(2)]
        R = cx.sb("cR", [128, 64], F32)
        Rb = cx.sb("cRb", [128, 64], BF16)
        for ck in range(3):
            qT = cx.sb(f"cqT{ck}", [128, T], BF16)
            kT = cx.sb(f"ckT{ck}", [128, T], BF16)
            qx = cx.sb(f"cqx{ck}", [128, T], BF16)
            v = cx.sb(f"cv{ck}", [128, NT, 128], BF16)
            sg = cx.sb(f"csg{ck}", [128, NT, 128], F32)
            yT = cx.sb(f"cyT{ck}", [128, T], BF16)
            rows = slice(ck * 128, (ck + 1) * 128)
            cx.dma("sp", qT.t[:], S["qr"].t[rows, :], [S["qr"]], [qT])
            cx.dma("sp", kT.t[:], S["kr"].t[rows, :], [S["kr"]], [kT])
            cx.dma("pool", v.t[:], S["tmb"].t[:, ck * 128:(ck + 1) * 128].rearrange("(n p) c -> p n c", p=128),
                   [S["tmb"]], [v])
            cx.dma("pool", sg.t[:], S["gr"].t[:, ck * 128:(ck + 1) * 128].rearrange("(n p) c -> p n c", p=128),
                   [S["gr"]], [sg])
            cx.tt("pool", qx.t[:].rearrange("p (n i) -> p n i", i=128), qT.t[:].rearrange("p (n i) -> p n i", i=128),
                  xir.t[:, ck:ck + 1, :].broadcast_to([128, NT, 128]), ALU.mult, [qT, xir], [qx])
            cx.op("pool", lambda e: e.memset(R.t[:], 0.0), [], [R])
            cx.op("pool", lambda e: e.memset(Rb.t[:], 0.0), [], [Rb])
            for n in range(NT):
                ns = slice(n * 128, (n + 1) * 128)
                kz, ktp_ = kzs[n % 2], ktp[n % 2]
                cx.tr(ktp_.t[:], kT.t[:, ns], ident.t[:], [kT, ident], [ktp_])
                cx.tt("dve", kz.t[:], ktp_.t[:], zt.t[:, ck, :], ALU.mult, [ktp_, zt], [kz])
                op_ = ops_[n % 2]
                for hh in range(2):
                    r = slice(hh * 64, (hh + 1) * 64)
                    sp_, sm = sps[hh], sms[(n % 2) * 2 + hh]
                    cx.mm(sp_.t[:], kT.t[r, ns], qT.t[r, ns], True, True, [kT, qT], [sp_])
                    cx.tt("dve", sm.t[:], sp_.t[:], dm.t[:, 2 * ck + hh, :], ALU.mult, [sp_, dm], [sm])
                for hh in range(2):
                    r = slice(hh * 64, (hh + 1) * 64)
                    sm = sms[(n % 2) * 2 + hh]
                    cx.mm(op_.t[:, r], sm.t[:], v.t[:, n, r], True, False, [sm, v], [op_])
                    cx.mm(op_.t[:, r], qx.t[r, ns], Rb.t[r, :], False, True, [qx, Rb], [op_])
                cx.mm(kvp.t[:], kz.t[:], v.t[:, n, :], True, True, [kz, v], [kvp])
                for hh in range(2):
                    r = slice(hh * 64, (hh + 1) * 64)
                    cx.stt(R.t[r, :], R.t[r, :], gc.t[r, ck:ck + 1], kvp.t[r, r], ALU.mult, ALU.add, [R, gc, kvp], [R])
                cx.copy("pool", Rb.t[:], R.t[:], [R], [Rb])
                st, mv, rs, on, ob = sts[n % 2], mvs[n % 2], rss[n % 2], ons[n % 2], onb[n % 2]
                for hh in range(2):
                    r = slice(hh * 64, (hh + 1) * 64)
                    cx.op("dve", lambda e, hh=hh, r=r, st=st, op_=op_: e.bn_stats(out=st.t[:, hh, :], in_=op_.t[:, r]), [op_], [st])
                    cx.op("dve", lambda e, hh=hh, st=st, mv=mv: e.bn_aggr(out=mv.t[:, hh, :], in_=st.t[:, hh, :]), [st], [mv])
                cx.ts("dve", rs.t[:], mv.t[:, :, 1], 1e-5, ALU.add, [mv], [rs])
                cx.act(rs.t[:], rs.t[:], AF.Sqrt, [rs], [rs])
                cx.op("dve", lambda e, rs=rs: e.reciprocal(out=rs.t[:], in_=rs.t[:]), [rs], [rs])
                for hh in range(2):
                    r = slice(hh * 64, (hh + 1) * 64)
                    cx.ts("dve", on.t[:, r], op_.t[:, r], mv.t[:, hh, 0:1], ALU.subtract, [op_, mv, rs], [on],
                          s2=rs.t[:, hh:hh + 1], op1=ALU.mult)
                cx.tt("pool", on.t[:], on.t[:], gng.t[:, ck * 128:(ck + 1) * 128], ALU.mult, [on, gng], [on])
                cx.tt("pool", ob.t[:], on.t[:], sg.t[:, n, :], ALU.mult, [on, sg], [ob])
                cx.tr(ytp.t[:], ob.t[:], ident.t[:], [ob, ident], [ytp])
                cx.copy("act", yT.t[:, ns], ytp.t[:], [ytp], [yT])
            cx.dma("sp", S["yt"].t[256 + ck * 128:256 + (ck + 1) * 128, :], yT.t[:], [yT], [S["yt"]])


def phase_D(cx, l, W, G, S):
    Cd = G["C"]
    ident = G["ident"]
    rope = G["rope"]
    with cx.phase() as es:
        KCT = cx.sb("KCT", [64, 2, 256], BF16)
        VCX = cx.sb("VCX", [128, 2, 2, 129], BF16)
        with cx.phase():
            Cc = cx.sb("dCc", [64, T], F32)
            Sc = cx.sb("dSc", [64, T], F32)
            cx.dma("sp", Cc.t[:], rope.t[0][0:64, :], [rope], [Cc])
            cx.dma("sp", Sc.t[:], rope.t[1][0:64, :], [rope], [Sc])
            ovl = load_f32(cx, "ovl", Cd["c_ovl"].t, Cd["c_ovl"], [128, 2, 64])
            hps = [cx.ps(f"dhp{i}", [128, 512], F32) for i in range(2)]
            cps = cx.ps("dcp", [128, 512], F32)
            kp = cx.ps("dkp", [128, 2, 256], F32)
            ksp = cx.ps("dksp", [128, 2, 256], F32)
            vps = [cx.ps(f"dvp{i}", [128, 512], F32) for i in range(2)]
            cx.op("pool", lambda e: e.memset(KCT.t[:], 0.0), [], [KCT])
            cx.op("pool", lambda e: e.memset(VCX.t[:, :, :, 64:65], 1.0), [], [VCX])
            for g in range(2):
                cx.copy("pool", VCX.t[:, g, :, 65:129], ovl.t[:], [ovl], [VCX])
            for kv in ("k", "v"):
                src = S["kc"] if kv == "k" else S["vc"]
                kvT = cx.sb(f"dkvT{kv}", [128, T], BF16)
                cx.dma("sp", kvT.t[:], src.t, [src], [kvT])
                w1 = load_cast(cx, f"w1{kv}", W[f"w1{kv}"], [128, 32, 128])
                pos = load_cast(cx, f"pos{kv}", W[f"pos{kv}"], [64, 32], eng="dve")
                b1 = load_f32(cx, f"b1{kv}", W[f"b1{kv}"].t, W[f"b1{kv}"], [128, 1])
                w2 = load_cast(cx, f"w2{kv}", W[f"w2{kv}"], [128, 64], eng="dve")
                cb = cx.sb(f"dcb{kv}", [128, 1], F32)
                h1 = cx.sb(f"dh1{kv}", [128, 2, 256], BF16)
                cx.op("pool", lambda e, h1=h1: e.memset(h1.t[:], 0.0), [], [h1])
                for i in range(32):
                    cx.mm(cps.t[:, 0:1], w1.t[0:64, i, :], pos.t[0:64, i:i + 1], i == 0, i == 31, [w1, pos], [cps])
                cx.tt("dve", cb.t[:], cps.t[:, 0:1], b1.t[:], ALU.add, [cps, b1], [cb])
                for g in range(2):
                    r = slice(g * 64, (g + 1) * 64)
                    for i in range(32):
                        cx.mm(hps[g].t[:, 0:255], w1.t[r, i, :], kvT.t[r, i:i + 16 * 254 + 1:16], i == 0, i == 31,
                              [w1, kvT], [hps[g]])
                    cx.act(h1.t[:, g, 0:255], hps[g].t[:, 0:255], AF.Gelu_apprx_tanh, [hps[g], cb], [h1],
                           bias=cb.t[:, 0:1])
                if kv == "k":
                    w2s = load_cast(cx, "w2ks", W["w2ks"], [128, 64], eng="dve")
                    cx.mm(kp.t[0:64], w2.t[:], h1.t[:], True, True, [w2, h1], [kp])
                    cx.mm(ksp.t[0:64], w2s.t[:], h1.t[:], True, True, [w2s, h1], [ksp])
                    t1 = cx.sb("dkt1", [64, 2, 255], F32)
                    t2 = cx.sb("dkt2", [64, 2, 255], F32)
                    cview = Cc.t[:, 31::16].unsqueeze(1).broadcast_to([64, 2, 255])
                    sview = Sc.t[:, 31::16].unsqueeze(1).broadcast_to([64, 2, 255])
                    cx.tt("dve", t1.t[:], kp.t[0:64, :, 0:255], cview, ALU.mult, [kp, Cc], [t1])
                    cx.tt("dve", t2.t[:], ksp.t[0:64, :, 0:255], sview, ALU.mult, [ksp, Sc], [t2])
                    cx.tt("dve", KCT.t[:, :, 0:255], t1.t[:], t2.t[:], ALU.add, [t1, t2], [KCT])
                else:
                    for g in range(2):
                        for nt in range(2):
                            vp_ = vps[(g * 2 + nt) % 2]
                            cx.mm(vp_.t[:, 0:64], h1.t[:, g, nt * 128:(nt + 1) * 128], w2.t[:], True, True, [h1, w2], [vp_])
                            cx.copy("act", VCX.t[:, g, nt, 0:64], vp_.t[:, 0:64], [vp_], [VCX])
        dstop = DBG.get("d_stop", 9)
        if dstop <= 1:
            return
        identb = ident
        QA = [cx.sb(f"QA{h}", [128, T], BF16) for h in range(6)]
        KSA = [cx.sb(f"KSA{g}", [128, T], BF16) for g in range(2)]
        KW = [cx.sb(f"KW{g}", [64, T], BF16) for g in range(2)]
        VSX = cx.sb("VSX", [128, NT, 2, 65], BF16)
        VWX = cx.sb("VWX", [128, NT, 2, 65], BF16)
        CMN = cx.sb("CMN", [128, 2, T], BF16)
        SB_ = load_f32(cx, "sbias", Cd["c_sbias"].t, Cd["c_sbias"], [128, NT, 64])
        GT = cx.sb("GTs", [128, NT, 18], F32)
        cx.dma("sp", GT.t[:], S["gt"].t.rearrange("(n p) c -> p n c", p=128), [S["gt"]], [GT])
        causb = cx.sb("causb", [128, 128], BF16)
        upperb = cx.sb("upperb", [128, 128], BF16)
        cx.dma("pool", causb.t[:], Cd["c_caus"].t, [Cd["c_caus"]], [causb])
        cx.dma("pool", upperb.t[:], Cd["c_upper"].t, [Cd["c_upper"]], [upperb])
        for nt in range(2):
            cx.dma("pool", CMN.t[:, nt, :], Cd["c_cmn"].t[:, nt, :], [Cd["c_cmn"]], [CMN])
        for g in range(2):
            cx.dma("pool", KSA[g].t[64:128, :], Cd["c_expand"].t, [Cd["c_expand"]], [KSA[g]])
        for h in range(6):
            cx.dma("sp", QA[h].t[0:64, :], S["qn"].t[h * 64:(h + 1) * 64, :], [S["qn"]], [QA[h]])
            cx.op("pool", lambda e, h=h: e.memset(QA[h].t[64:128, :], 0.0), [], [QA[h]])
        for g in range(2):
            cx.dma("pool", KSA[g].t[0:64, :], S["ks"].t[g * 64:(g + 1) * 64, :], [S["ks"]], [KSA[g]])
            cx.dma("sp", KW[g].t[:], S["kw"].t[g * 64:(g + 1) * 64, :], [S["kw"]], [KW[g]])
            cx.dma("pool", VSX.t[:, :, g, 0:64],
                   S["tmb"].t[:, 384 + g * 64:384 + (g + 1) * 64].rearrange("(n p) c -> p n c", p=128), [S["tmb"]], [VSX])
            cx.dma("pool", VWX.t[:, :, g, 0:64],
                   S["tmb"].t[:, 512 + g * 64:512 + (g + 1) * 64].rearrange("(n p) c -> p n c", p=128), [S["tmb"]], [VWX])
        cx.op("pool", lambda e: e.memset(VSX.t[:, :, :, 64:65], 1.0), [], [VSX])
        cx.op("pool", lambda e: e.memset(VWX.t[:, :, :, 64:65], 1.0), [], [VWX])
        if dstop <= 2:
            return
        SP = [cx.ps(f"dS{i}", [128, 512], F32) for i in range(4)]
        OP = [cx.ps(f"dO{i}", [128, 512], F32) for i in range(3)]
        tpt = cx.ps("dT", [128, 8, 128], BF16)
        _tb = Buf("dT", excl=True)
        TP = [Tile(tpt.t[:, 4 * i:4 * i + 4, :], _tb) for i in range(2)]
        pTs = [cx.sb(f"dpT{i}", [128, 512], BF16) for i in range(4)]
        OACC = [cx.sb(f"dOACC{i}", [128, 4, 384], F32) for i in range(2)]
        IMP = [cx.sb(f"dIMP{i}", [128, 4, 64], F32) for i in range(2)]
        recs = [cx.sb(f"drec{i}", [128, 2, 4], F32) for i in range(4)]
        sc1 = [cx.sb(f"dsc1{i}", [128, 64], F32) for i in range(2)]
        sc2 = [cx.sb(f"dsc2{i}", [128, 64], F32) for i in range(2)]
        m8 = [cx.sb(f"dm8{i}", [128, 16], F32) for i in range(2)]
        nm = [cx.sb(f"dnm{i}", [128, 128], BF16) for i in range(2)]
        for t_ in nm:
            cx.op("pool", lambda e, t_=t_: e.memset(t_.t[:], 0.0), [], [t_])
        obf = [cx.sb(f"dobf{i}", [128, 384], BF16) for i in range(2)]
        yst = [cx.sb(f"dyst{i}", [128, 3, 512], BF16) for i in range(2)]
        cnt = {"s": 0, "o": 0, "p": 0, "r": 0, "t": 0}

        def nxt(key, lst):
            x = lst[cnt[key] % len(lst)]
            cnt[key] += 1
            return x

        items = []

        def cmp_item(c, h):
            g, rr = divmod(h, 3)
            cs = slice(c * 512, (c + 1) * 512)
            oacc, imp = OACC[c % 2], IMP[c % 2]
            nts = [0] + ([1] if c >= 4 else [])
            st = {}

            def S_():
                st["pts"] = {}
                for nt in nts:
                    sp_ = nxt("s", SP)
                    need_mask = (c <= 4) if nt == 0 else True
                    cx.mm(sp_.t[:], KCT.t[0:64, g, nt * 128:(nt + 1) * 128], QA[h].t[0:64, cs], True, True,
                          [KCT, QA[h]], [sp_])
                    if need_mask:
                        cx.mm(sp_.t[:], identb.t[:], CMN.t[:, nt, cs], False, True, [identb, CMN], [sp_], skip=True)
                    pT = nxt("p", pTs)
                    cx.act(pT.t[:], sp_.t[:], AF.Exp, [sp_], [pT], scale=0.125)
                    st["pts"][nt] = pT

            def PV_():
                pts = st["pts"]
                for q4 in range(4):
                    qt = 4 * c + q4
                    ob_ = nxt("o", OP)
                    for j, nt in enumerate(nts):
                        cx.mm(ob_.t[:, 0:129], pts[nt].t[:, q4 * 128:(q4 + 1) * 128], VCX.t[:, g, nt, :],
                              j == 0, j == len(nts) - 1, [pts[nt], VCX], [ob_])
                    rc = nxt("r", recs)
                    cx.ts("dve", rc.t[:, 0, 0:1], ob_.t[:, 64:65], 1e-30, ALU.add, [ob_], [rc])
                    cx.op("dve", lambda e, rc=rc: e.reciprocal(out=rc.t[:, 0, 0:1], in_=rc.t[:, 0, 0:1]), [rc], [rc])
                    cx.tt("dve", rc.t[:, 1, 0:1], rc.t[:, 0, 0:1], GT.t[:, qt, 3 * h:3 * h + 1], ALU.mult, [rc, GT], [rc])
                    cx.ts("dve", oacc.t[:, q4, h * 64:(h + 1) * 64], ob_.t[:, 0:64], rc.t[:, 1, 0:1], ALU.mult,
                          [ob_, rc], [oacc])
                    if rr == 0:
                        cx.ts("dve", imp.t[:, q4, :], ob_.t[:, 65:129], rc.t[:, 0, 0:1], ALU.mult, [ob_, rc], [imp])
                    else:
                        cx.stt(imp.t[:, q4, :], ob_.t[:, 65:129], rc.t[:, 0, 0:1], imp.t[:, q4, :], ALU.mult, ALU.add,
                               [ob_, rc, imp], [imp])
                if rr == 2:
                    for q4 in range(4):
                        qt = 4 * c + q4
                        s1, s2, mm8, nm_ = sc1[q4 % 2], sc2[q4 % 2], m8[q4 % 2], nm[q4 % 2]
                        cx.tt("dve", s1.t[:], imp.t[:, q4, :], SB_.t[:, qt, :], ALU.add, [imp, SB_], [s1])
                        cx.op("dve", lambda e, mm8=mm8, s1=s1: e.max(out=mm8.t[:, 0:8], in_=s1.t[:]), [s1], [mm8])
                        cx.op("dve", lambda e, mm8=mm8, s1=s1, s2=s2: e.match_replace(
                            out=s2.t[:], in_to_replace=mm8.t[:, 0:8], in_values=s1.t[:], imm_value=-1e9), [s1, mm8], [s2])
                        cx.op("dve", lambda e, mm8=mm8, s2=s2: e.max(out=mm8.t[:, 8:16], in_=s2.t[:]), [s2], [mm8])
                        cx.ts("dve", mm8.t[:, 15:16], mm8.t[:, 15:16], 0.0, ALU.max, [mm8], [mm8])
                        cx.ts("dve", nm_.t[:, 64:128], s1.t[:], mm8.t[:, 15:16], ALU.is_lt, [s1, mm8], [nm_],
                              s2=NEG, op1=ALU.mult)
                        tp = nxt("t", TP)
                        cx.tr(tp.t[:, 0, :], nm_.t[:], ident.t[:], [nm_, ident], [tp])
                        for r3 in range(3):
                            hh = 3 * g + r3
                            cx.copy("dve",
                                    QA[hh].t[64:128, qt * 128:(qt + 1) * 128], tp.t[64:128, 0, :], [tp], [QA[hh]])
            return S_, PV_

        def att_items(c, branch, h):
            g = h // 3
            oacc = OACC[c % 2]
            kts = list(range(0, 4 * c + 4) if branch == 1 else range(max(4 * c - 4, 0), 4 * c + 4))
            shared = {"first": True}
            res = []
            for kt in kts:
                lo = max(kt - 4 * c, 0)
                hi = 3 if branch == 1 else min(kt + 4 - 4 * c, 3)
                n_ = (hi - lo + 1) * 128
                q0 = c * 512 + lo * 128
                ks_ = slice(kt * 128, (kt + 1) * 128)
                st = {}

                def S_(kt=kt, lo=lo, hi=hi, n_=n_, q0=q0, ks_=ks_, st=st):
                    sp_ = nxt("s", SP)
                    if branch == 1:
                        cx.mm(sp_.t[:, 0:n_], KSA[g].t[:, ks_], QA[h].t[:, q0:q0 + n_], True, True,
                              [KSA[g], QA[h]], [sp_])
                    else:
                        cx.mm(sp_.t[:, 0:n_], KW[g].t[0:64, ks_], QA[h].t[0:64, q0:q0 + n_], True, True,
                              [KW[g], QA[h]], [sp_])
                    if kt >= 4 * c:
                        cx.mm(sp_.t[:, 0:128], identb.t[:], causb.t[:], False, True, [identb, causb], [sp_], skip=True)
                    if branch == 2 and 4 * c <= kt + 4 <= 4 * c + 3:
                        cx.mm(sp_.t[:, n_ - 128:n_], identb.t[:], upperb.t[:], False, True, [identb, upperb], [sp_],
                              skip=True)
                    pT = nxt("p", pTs)
                    cx.act(pT.t[:, 0:n_], sp_.t[:, 0:n_], AF.Exp, [sp_], [pT], scale=0.125)
                    st["pT"] = pT

                def PV_(kt=kt, lo=lo, hi=hi, st=st, last=(kt == kts[-1])):
                    if shared["first"]:
                        shared["ob"] = nxt("o", OP)
                    ob_ = shared["ob"]
                    ov = ob_.t[:, 0:260].rearrange("p (a b) -> p a b", b=65)
                    pT = st["pT"]
                    vx = VSX if branch == 1 else VWX
                    for q4 in range(lo, hi + 1):
                        cx.mm(ov[:, q4, :], pT.t[:, (q4 - lo) * 128:(q4 - lo + 1) * 128], vx.t[:, kt, g, :],
                              shared["first"], True, [pT, vx], [ob_], skip=not shared["first"])
                        shared["first"] = False
                    if last:
                        rc = nxt("r", recs)
                        cx.ts("dve", rc.t[:, 0, :], ov[:, :, 64], 1e-30, ALU.add, [ob_], [rc])
                        cx.op("dve", lambda e, rc=rc: e.reciprocal(out=rc.t[:, 0, :], in_=rc.t[:, 0, :]), [rc], [rc])
                        col = 3 * h + branch
                        cx.tt("dve", rc.t[:, 1, :], rc.t[:, 0, :], GT.t[:, 4 * c:4 * c + 4, col], ALU.mult, [rc, GT], [rc])
                        for q4 in range(4):
                            av = oacc.t[:, q4, h * 64:(h + 1) * 64]
                            cx.stt(av, ov[:, q4, 0:64], rc.t[:, 1, q4:q4 + 1], av, ALU.mult, ALU.add,
                                   [ob_, rc, oacc], [oacc])
                res.append((S_, PV_))
            return res

        def out_item(c):
            cs = slice(c * 512, (c + 1) * 512)
            oacc = OACC[c % 2]

            def PV_():
                ys = yst[c % 2]
                for q4 in range(4):
                    ob2 = obf[q4 % 2]
                    cx.copy("pool", ob2.t[:], oacc.t[:, q4, :], [oacc], [ob2])
                    tp = nxt("t", TP)
                    for j in range(3):
                        cx.tr(tp.t[:, j, :], ob2.t[:, j * 128:(j + 1) * 128], ident.t[:], [ob2, ident], [tp])
                    cx.copy("act", ys.t[:, :, q4 * 128:(q4 + 1) * 128], tp.t[:, 0:3, :], [tp], [ys])
                for j in range(3):
                    cx.dma("sp", S["yt"].t[640 + j * 128:640 + (j + 1) * 128, cs], ys.t[:, j, :], [ys], [S["yt"]])
            return (lambda: None), PV_

        for c in range(8):
            for h in range(6):
                items.append(cmp_item(c, h))
            if dstop >= 5:
                for branch in ((1, 2) if dstop >= 6 else (1,)):
                    for h in range(6):
                        items.extend(att_items(c, branch, h))
            items.append(out_item(c))
        for i in range(len(items) + 1):
            if i < len(items):
                items[i][0]()
            if i >= 1:
                items[i - 1][1]()


def phase_E(cx, l, xd, x1d, W, G, S):
    with cx.phase() as es:
        wo = load_cast(cx, "wout", W["wout"], [128, 8, D])
        g_t = load_f32(cx, "ln1g", W["ln1_g"].t.broadcast_to([128, D]), W["ln1_g"], [128, D])
        b_t = load_f32(cx, "ln1b", W["ln1_b"].t.broadcast_to([128, D]), W["ln1_b"], [128, D])
        yts = [cx.sb(f"eyt{i}", [128, 8, 512], BF16) for i in range(2)]
        xrs = [cx.sb(f"exr{i}", [128, D], F32) for i in range(2)]
        pss = [cx.ps(f"eps{i}", [128, 512], F32) for i in range(4)]
        tmps = ln_tmps(cx, es)
        def load_y(tc):
            cx.dma("sp", yts[tc % 2].t[:], S["yt"].t[:, tc * 512:(tc + 1) * 512].rearrange("(k p) t -> p k t", p=128),
                   [S["yt"]], [yts[tc % 2]])
        load_y(0)
        for tc in range(8):
            y_ = yts[tc % 2]
            if tc + 1 < 8:
                load_y(tc + 1)
            for q in range(4):
                tt = tc * 4 + q
                xr = xrs[tt % 2]
                cx.dma("sp", xr.t[:], xd.t[tt * 128:(tt + 1) * 128, :], [xd], [xr])
                zp = pss[(tt % 2) * 2:(tt % 2) * 2 + 2]
                for hf in range(2):
                    for k in range(8):
                        cx.mm(zp[hf].t[:], y_.t[:, k, q * 128:(q + 1) * 128], wo.t[:, k, hf * 512:(hf + 1) * 512],
                              k == 0, k == 7, [y_, wo], [zp[hf]])
                layer_norm_store(cx, zp, xr, g_t, b_t, x1d, tt * 128, tmps[tt % 2])


def phase_F(cx, l, x1d, x2d, W, G, S):
    TC = 1024
    ident = G["ident"]
    with cx.phase() as es:
        wd = cx.sb("wd_b", [128, NF, D], BF16)
        for f in range(NF):
            cx.dma("pool", wd.t[:, f, :], W["wd"].t[:, f, :], [W["wd"]], [wd])
        cw = load_f32(cx, "cw", W["cw"].t, W["cw"], [128, NF, 3])
        cb = load_f32(cx, "cb", W["cb"].t, W["cb"], [128, NF])
        g_t = load_f32(cx, "ln2g", W["ln2_g"].t.broadcast_to([128, D]), W["ln2_g"], [128, D])
        b_t = load_f32(cx, "ln2b", W["ln2_b"].t.broadcast_to([128, D]), W["ln2_b"], [128, D])
        carry = cx.sb("carry", [128, NF, 2], F32)
        cx.op("pool", lambda e: e.memset(carry.t[:], 0.0), [], [carry])
        xT = cx.sb("x1T", [128, 8, TC], BF16)
        act = cx.sb("ffact", [128, NF, TC], BF16)
        wb = [cx.sb(f"fwb{i}", [128, 2, 8, 128], BF16) for i in range(3)]
        hts = [cx.sb(f"fhb{i}", [128, 2 + TC], F32) for i in range(2)]
        hbA = [Tile(t.t, Buf(f"fhbA{i}")) for i, t in enumerate(hts)]
        hbB = [Tile(t.t, Buf(f"fhbB{i}")) for i, t in enumerate(hts)]
        hc = [cx.sb(f"fhc{i}", [128, 512], F32) for i in range(2)]
        gl = [cx.sb(f"fgl{i}", [128, 512], F32) for i in range(2)]
        xrs = [cx.sb(f"fxr{i}", [128, D], F32) for i in range(2)]
        tmps = ln_tmps(cx, es)
        pss = [cx.ps(f"fps{i}", [128, 512], F32) for i in range(6)]
        gi = 0
        nsteps = (T // TC) * NF

        def load_w(step):
            f = step % NF
            b_ = wb[step % 3]
            cx.dma("pool", b_.t[:, 0], W["wg"].t[f], [W["wg"]], [b_])
            cx.dma("pool", b_.t[:, 1], W["wu"].t[f], [W["wu"]], [b_])
        load_w(0)
        load_w(1)
        step = 0
        for tc in range(T // TC):
            build_xT(cx, x1d, xT, ident, TC // 128, tok0=tc * TC)
            for f in range(NF):
                b_ = wb[step % 3]
                if step + 2 < nsteps:
                    load_w(step + 2)
                step += 1
                hA, hB = hbA[f % 2], hbB[f % 2]
                ht = hts[f % 2].t
                cx.copy("act", ht[:, 0:2], carry.t[:, f, :], [carry], [hA])
                for hf in range(TC // 512):
                    ts_ = slice(hf * 512, (hf + 1) * 512)
                    pg, pu = pss[(gi % 2) * 2], pss[(gi % 2) * 2 + 1]
                    c_, g_ = hc[gi % 2], gl[gi % 2]
                    gi += 1
                    for k in range(8):
                        cx.mm(pg.t[:], b_.t[:, 0, k, :], xT.t[:, k, ts_], k == 0, k == 7, [b_, xT], [pg])
                    for k in range(8):
                        cx.mm(pu.t[:], b_.t[:, 1, k, :], xT.t[:, k, ts_], k == 0, k == 7, [b_, xT], [pu])
                    hw_ = [hA] if hf == 0 else [hB]
                    hr_ = [hA] if hf == 0 else [hA, hB]
                    o = hf * 512
                    cx.copy("act", ht[:, 2 + o:514 + o], pg.t[:], [pg], hw_)
                    cx.ts("dve", c_.t[:], ht[:, 2 + o:514 + o], cw.t[:, f, 2:3], ALU.mult, hr_ + [cw, cb], [c_],
                          s2=cb.t[:, f:f + 1], op1=ALU.add)
                    cx.stt(c_.t[:], ht[:, 1 + o:513 + o], cw.t[:, f, 1:2], c_.t[:], ALU.mult, ALU.add, hr_ + [cw, c_], [c_])
                    cx.stt(c_.t[:], ht[:, o:512 + o], cw.t[:, f, 0:1], c_.t[:], ALU.mult, ALU.add, hr_ + [cw, c_], [c_])
                    cx.act(g_.t[:], c_.t[:], AF.Gelu_apprx_tanh, [c_], [g_])
                    cx.tt("dve", act.t[:, f, ts_], g_.t[:], pu.t[:], ALU.mult, [g_, pu], [act])
                cx.copy("act", carry.t[:, f, :], ht[:, TC:TC + 2], [hB], [carry])
            for q in range(TC // 128):
                tt = tc * (TC // 128) + q
                xr = xrs[tt % 2]
                cx.dma("sp", xr.t[:], x1d.t[tt * 128:(tt + 1) * 128, :], [x1d], [xr])
                zp = pss[4:6]
                for hf in range(2):
                    for f in range(NF):
                        cx.mm(zp[hf].t[:], act.t[:, f, q * 128:(q + 1) * 128], wd.t[:, f, hf * 512:(hf + 1) * 512],
                              f == 0, f == NF - 1, [act, wd], [zp[hf]])
                layer_norm_store(cx, zp, xr, g_t, b_t, x2d, tt * 128, tmps[tt % 2])

SCRATCH = {
    "rope": ([2, 128, T], F32), "vp": ([256, T], F32),
    "qr": ([384, T], BF16), "kr": ([384, T], BF16), "qn": ([384, T], BF16),
    "ks": ([128, T], BF16), "kw": ([128, T], BF16), "kc": ([128, T], BF16), "vc": ([128, T], BF16),
    "tmb": ([T, 640], BF16), "gr": ([T, 384], F32), "gt": ([T, 18], F32),
    "yt": ([1024, T], BF16), "x1": ([T, D], F32), "xmid": ([T, D], F32),
}


def build_program(layers=(0, 1), phases="ABCDEF", ext_in=(), ext_out=(), prologue=True):
    nc = bass.Bass("TRN2", target_bir_lowering=False)
    cx = Ctx(nc, ext_in, ext_out)
    xd = cx.dr("x", [T, D], F32, kind="ExternalInput")
    posd = cx.dr("pos", [1, T], I32, kind="ExternalInput")
    Cd = {k: cx.dr(k, list(v.shape), F32, kind="ExternalInput") for k, v in CONSTS.items()}
    Wd = {l: {k: cx.dr(f"{k}_{l}", shp, F32, kind="ExternalInput") for k, shp in LAYER_SHAPES.items()} for l in layers}
    S = {k: cx.dr(k, shp, dt) for k, (shp, dt) in SCRATCH.items()}
    outd = cx.dr("y", [T, D], F32, kind="ExternalOutput")
    with contextlib.ExitStack() as gs:
        cx.stack = gs
        G = {"rope": S["rope"]}
        G["ident"] = load_cast(cx, "ident", Cd["c_ident"], [128, 128])
        G["inv"] = load_f32(cx, "inv", Cd["c_inv"].t, Cd["c_inv"], [128, 1])
        G["sgn"] = load_f32(cx, "sgn", Cd["c_sgn"].t, Cd["c_sgn"], [128, 1])
        G["pw"] = load_f32(cx, "pw", Cd["c_pw"].t, Cd["c_pw"], [128, 2])
        G["prc"] = load_f32(cx, "prc", Cd["c_prc"].t, Cd["c_prc"], [128, 2, 16])
        G["C"] = Cd
        if prologue:
            prologue_rope(cx, posd, G)
        cur = xd
        for li, l in enumerate(layers):
            nxt = outd if li == len(layers) - 1 else S["xmid"]
            W = Wd[l]
            if "A" in phases:
                phase_A(cx, l, cur, W, G, S)
            if "B" in phases:
                phase_B(cx, l, W, G, S)
            if "C" in phases:
                phase_C(cx, l, W, G, S)
            if "D" in phases:
                phase_D(cx, l, W, G, S)
            if "E" in phases:
                phase_E(cx, l, cur, S["x1"], W, G, S)
            if "F" in phases:
                phase_F(cx, l, S["x1"], nxt, W, G, S)
            cur = nxt
        cx.P.barrier()
        finals = [outd.b] + [cx.dram[n].b for n in cx.ext_out]
        cx.P.emit_all(final_bufs=finals)
    return nc, cx


def make_in_maps(inputs, layers=(0, 1), cores=range(8)):
    shared = dict(CONSTS)
    for l in layers:
        for k, v in _layer_arrays(inputs, l).items():
            assert list(v.shape) == LAYER_SHAPES[k], (k, v.shape)
            shared[f"{k}_{l}"] = v.astype(np.float32, copy=False)
    maps = []
    for b in cores:
        m = dict(shared)
        m["x"] = np.ascontiguousarray(inputs["x"][b])
        m["pos"] = np.ascontiguousarray(inputs["positions"][b].reshape(1, T).astype(np.int32))
        maps.append(m)
    return maps


def kernel(**inputs):
    inputs = {k: np.asarray(v) for k, v in inputs.items()}
    nc, cx = build_program()
    maps = make_in_maps(inputs)
    res = run_bass_kernel_spmd(nc, maps, core_ids=list(range(8)))
    return np.stack([np.asarray(r["y"]) for r in res.results], 0).astype(np.float32)
```

```python
import contextlib
import math
import numpy as np
import concourse.bass as bass
import concourse.mybir as mybir
from concourse.bass_utils import run_bass_kernel_spmd

F32 = mybir.dt.float32
BF16 = mybir.dt.bfloat16
I32 = mybir.dt.int32
AF = mybir.ActivationFunctionType
ALU = mybir.AluOpType
AX = mybir.AxisListType

T = 4096
D = 1024
DEPTH = 2
NT = T // 128
DFF = 2816
NF = DFF // 128
ALPHA = (2 * DEPTH) ** 0.25
NEG = -10000.0
DBG = {}

ENGS = ("pe", "act", "dve", "pool", "sp")


class Buf:
    __slots__ = ("name", "writers", "readers", "dsem", "dcount", "is_dram", "vsem", "excl")

    def __init__(self, name, is_dram=False, excl=False):
        self.name = name
        self.is_dram = is_dram
        self.excl = excl
        self.writers = []
        self.readers = []
        self.dsem = None
        self.dcount = 0
        self.vsem = None


class Op:
    __slots__ = ("eng", "emit", "waits", "is_dma", "dbuf", "dval", "needs_inc", "val", "vsem")

    def __init__(self, eng, emit, is_dma=False):
        self.eng = eng
        self.emit = emit
        self.waits = []
        self.is_dma = is_dma
        self.dbuf = None
        self.dval = 0
        self.needs_inc = False
        self.val = 0
        self.vsem = None


class Prog:
    def __init__(self, nc):
        self.nc = nc
        self.ops = {e: [] for e in ENGS}
        self.last = {e: None for e in ENGS}
        self.dma_bufs = {}
        self.pending_bar = {e: [] for e in ENGS}
        self.seq = 0
        self.bar_seq = 0
        self.vfree = {True: [], False: []}
        self.vkind = []
        self.vcount = []
        self.rsem = []

    def _dep(self, op, prod, force=False):
        if prod is op:
            return
        if (not force) and prod.val < self.bar_seq:
            return
        if not prod.is_dma and prod.eng == op.eng:
            if op.eng in ("pe", "sp"):
                return
        op.waits.append(prod)
        if not prod.is_dma:
            prod.needs_inc = True

    @staticmethod
    def _prune(lst):
        last = {}
        for r in lst:
            last[(r.eng, r.is_dma, r.vsem)] = r
        return list(last.values())

    def op(self, eng, emit, reads=(), writes=(), dma=False):
        o = Op(eng, emit, is_dma=dma)
        self.seq += 1
        o.val = self.seq
        if self.pending_bar[eng]:
            for p in self.pending_bar[eng]:
                self._dep(o, p, force=True)
            self.pending_bar[eng] = []
        for b in reads:
            for w in b.writers:
                self._dep(o, w)
            if b.excl:
                for r in b.readers:
                    if r.eng != eng:
                        self._dep(o, r)
        for b in writes:
            for r in b.readers:
                if (not r.is_dma) and r.eng == eng and not dma:
                    continue
                self._dep(o, r)
            if not b.readers:
                for w in b.writers:
                    if w.is_dma and dma:
                        continue
                    if (not w.is_dma) and w.eng == eng and not dma:
                        continue
                    self._dep(o, w)
        if dma:
            assert len(writes) == 1
            b = writes[0]
            if b.is_dram:
                b = [r for r in reads if not r.is_dram][0]
            if b.vsem is None:
                sw = (eng == "pool")
                if self.vfree[sw]:
                    b.vsem = self.vfree[sw].pop()
                else:
                    b.vsem = len(self.vcount)
                    self.vcount.append(0)
                    self.vkind.append(sw)
                b.dcount = self.vcount[b.vsem]
            b.dcount += 16
            self.vcount[b.vsem] = b.dcount
            o.dbuf = b
            o.dval = b.dcount
            o.vsem = b.vsem
            self.dma_bufs[id(b)] = b
        for b in writes:
            if b.readers:
                b.writers = [o]
                b.readers = []
            else:
                b.writers.append(o)
                if len(b.writers) > 6:
                    b.writers = self._prune(b.writers)
        for b in reads:
            b.readers.append(o)
            if len(b.readers) > 6:
                b.readers = self._prune(b.readers)
        self.ops[eng].append(o)
        if not dma:
            self.last[eng] = o
        return o

    def dma(self, eng, out, in_, reads, writes):
        return self.op(eng, lambda e: e.dma_start(out=out, in_=in_), reads, writes, dma=True)

    def barrier(self):
        targets = []
        for e in ENGS:
            if self.last[e] is not None:
                targets.append(self.last[e])
        for b in self.dma_bufs.values():
            if b.vsem is not None:
                p = Op("sp", None, is_dma=True)
                p.dbuf = b
                p.dval = b.dcount
                p.vsem = b.vsem
                targets.append(p)
                self.vfree[self.vkind[b.vsem]].append(b.vsem)
                b.vsem = None
        self.dma_bufs = {}
        self.seq += 1
        self.bar_seq = self.seq
        for e in ENGS:
            self.pending_bar[e] = self._prune(self.pending_bar[e] + targets)

    def emit_all(self, final_bufs=()):
        nc = self.nc
        esem = {e: nc.alloc_semaphore(name=f"es_{e}") for e in ENGS}
        self.rsem = [nc.alloc_semaphore(name=f"ds_{i}") for i in range(len(self.vcount))]
        for e in ENGS:
            c = 0
            for o in self.ops[e]:
                if (not o.is_dma) and o.needs_inc:
                    c += 1
                    o.val = c
        engobj = {"pe": "tensor", "act": "scalar", "dve": "vector", "pool": "gpsimd", "sp": "sync"}
        with nc.Block() as block:
            for e in ENGS:
                def body(eng, ops=self.ops[e], e=e):
                    waited = {}
                    for o in ops:
                        need = {}
                        for p in o.waits:
                            if p.is_dma:
                                sem, val = self.rsem[p.vsem], p.dval
                            else:
                                sem, val = esem[p.eng], p.val
                            if sem is None:
                                continue
                            if need.get(sem.num, (None, 0))[1] < val:
                                need[sem.num] = (sem, val)
                        for k, (sem, val) in need.items():
                            if waited.get(k, 0) >= val:
                                continue
                            waited[k] = val
                            eng.wait_ge(sem, val)
                        ins = o.emit(eng)
                        if o.is_dma:
                            ins.then_inc(self.rsem[o.vsem], 16)
                        elif o.needs_inc:
                            ins.then_inc(esem[e], 1)
                    if e == "sp":
                        for i, sem in enumerate(self.rsem):
                            if waited.get(sem.num, 0) < self.vcount[i]:
                                eng.wait_ge(sem, self.vcount[i])

                getattr(block, engobj[e])(body)


OFF = {}
_o = 0
for _n, _w in (("v_pool", 256), ("q_ret", 384), ("k_ret", 384), ("v_ret", 384), ("g_ret", 384),
               ("q_nsa", 384), ("k_cmp", 128), ("v_cmp", 128), ("k_slc", 128), ("v_slc", 128),
               ("k_win", 128), ("v_win", 128), ("gate", 18)):
    OFF[_n] = _o
    _o += _w


def _swap_cols(cols):
    cols = np.asarray(cols).reshape(-1, 64)
    return np.concatenate([cols[:, 32:], cols[:, :32]], axis=1).reshape(-1)


def _fm_cols():
    ch = []
    for c in range(2):
        ch.append(np.arange(OFF["v_pool"] + 128 * c, OFF["v_pool"] + 128 * (c + 1)))
    for name in ("q_ret", "k_ret", "q_nsa"):
        for c in range(3):
            cols = np.arange(OFF[name] + 128 * c, OFF[name] + 128 * (c + 1))
            ch.append(cols)
            ch.append(_swap_cols(cols))
    for name in ("k_slc", "k_win"):
        cols = np.arange(OFF[name], OFF[name] + 128)
        ch.append(cols)
        ch.append(_swap_cols(cols))
    for name in ("k_cmp", "v_cmp"):
        ch.append(np.arange(OFF[name], OFF[name] + 128))
    return ch


FM_COLS = _fm_cols()
NFM = len(FM_COLS)
TM_COLS = np.concatenate([np.arange(OFF["v_ret"], OFF["v_ret"] + 384),
                          np.arange(OFF["v_slc"], OFF["v_slc"] + 128),
                          np.arange(OFF["v_win"], OFF["v_win"] + 128),
                          np.arange(OFF["g_ret"], OFF["g_ret"] + 384),
                          np.arange(OFF["gate"], OFF["gate"] + 18)])
NTM = len(TM_COLS)


def _const_tables():
    c = {}
    p = np.arange(128)
    inv = (10000.0 ** (-np.arange(0, 64, 2, dtype=np.float32) / 64)).astype(np.float32)
    c["c_inv"] = inv[p % 32].reshape(128, 1).astype(np.float32)
    c["c_sgn"] = np.where((p % 64) < 32, -1.0, 1.0).reshape(128, 1).astype(np.float32)
    h = np.arange(6, dtype=np.float64)
    lg = np.log1p(-np.power(2.0, -5.0 - h))
    i = np.arange(128, dtype=np.float64)
    dm = np.zeros((128, 6, 128), np.float32)
    for hh in range(6):
        diff = i[None, :] - i[:, None]
        dm[:, hh, :] = np.where(diff >= 0, 0.125 * np.exp(lg[hh] * np.maximum(diff, 0)), 0.0)
    c["c_dm"] = dm
    xi = np.exp(lg[:, None] * (i[None, :] + 1.0))
    zeta = 0.125 * np.exp(lg[:, None] * (127.0 - i[None, :]))
    gam = np.exp(lg * 128.0)
    xir = np.zeros((128, 3, 128), np.float32)
    zt = np.zeros((128, 3, 128), np.float32)
    gc = np.zeros((128, 3), np.float32)
    for ck in range(3):
        for hh in range(2):
            xir[hh * 64:(hh + 1) * 64, ck, :] = xi[2 * ck + hh][None, :]
            zt[:, ck, hh * 64:(hh + 1) * 64] = zeta[2 * ck + hh][:, None]
            gc[hh * 64:(hh + 1) * 64, ck] = gam[2 * ck + hh]
    c["c_xir"] = xir
    c["c_zt"] = zt
    c["c_gc"] = gc
    win = np.zeros((128, 2), np.float32)
    rc = np.zeros((128, 2, 16), np.float32)
    for ck in range(2):
        for hh in range(2):
            w = (2, 4, 8, 16)[2 * ck + hh]
            win[hh * 64:(hh + 1) * 64, ck] = 1.0 / w
            rc[hh * 64:(hh + 1) * 64, ck, :] = 1.0 / np.minimum(np.arange(16) + 1, w)
    c["c_pw"] = win
    c["c_prc"] = rc
    kk = np.arange(128)[:, None]
    qq = np.arange(128)[None, :]
    c["c_caus"] = np.where(kk > qq, NEG, 0.0).astype(np.float32)
    c["c_upper"] = np.where(kk <= qq, NEG, 0.0).astype(np.float32)
    c["c_ident"] = np.eye(128, dtype=np.float32)
    ex = np.zeros((64, T), np.float32)
    ex[np.arange(T) // 64, np.arange(T)] = 1.0
    c["c_expand"] = ex
    n = np.arange(256)
    ends = 16 * n + 31
    cm = np.where(ends[:, None] > np.arange(T)[None, :], NEG, 0.0).astype(np.float32)
    c["c_cmn"] = np.ascontiguousarray(cm.reshape(2, 128, T).transpose(1, 0, 2))
    ci = np.arange(256)[:, None]
    sj = np.arange(64)[None, :]
    ov = np.clip(np.minimum(ci * 16 + 32, (sj + 1) * 64) - np.maximum(ci * 16, sj * 64), 0, None) / 16.0
    ov[255, :] = 0.0
    c["c_ovl"] = np.ascontiguousarray(ov.astype(np.float32).reshape(2, 128, 64).transpose(1, 0, 2))
    tq = np.arange(T)
    cur = tq // 64
    blk = np.arange(64)[None, :]
    forced = (blk == 0) | (blk == cur[:, None]) | (blk == cur[:, None] - 1)
    valid = blk * 64 <= tq[:, None]
    bias = np.where(valid, np.where(forced, 1e6, 0.0), -100.0).astype(np.float32)
    c["c_sbias"] = np.ascontiguousarray(bias.reshape(32, 128, 64).transpose(1, 0, 2))
    return c


CONSTS = _const_tables()


def _layer_arrays(inp, l):
    a = {}
    w_in = inp["w_in"][l]
    wk = w_in.reshape(8, 128, -1)
    a["wfm"] = np.ascontiguousarray(
        np.stack([wk[:, :, cols].transpose(1, 0, 2) for cols in FM_COLS], 0))
    a["wtm"] = np.ascontiguousarray(wk[:, :, TM_COLS].transpose(1, 0, 2))
    a["wout"] = np.ascontiguousarray(inp["w_out"][l].reshape(8, 128, D).transpose(1, 0, 2))
    pw = inp["pool_w"][l]
    bd = np.zeros((2, 128, 128), np.float32)
    for ck in range(2):
        for hh in range(2):
            bd[ck, hh * 64:(hh + 1) * 64, hh * 64:(hh + 1) * 64] = pw[2 * ck + hh]
    a["bd"] = bd
    a["psc"] = np.ascontiguousarray(inp["pool_scale"][l].reshape(2, 128).T)
    a["gng"] = np.ascontiguousarray(inp["ret_gn_g"][l].reshape(1, 384))
    for kv in ("k", "v"):
        w1 = inp[f"cmp_w1_{kv}"][l].reshape(32, 64, 128)
        w1d = np.concatenate([w1, w1], axis=1).transpose(1, 0, 2)
        a[f"w1{kv}"] = np.ascontiguousarray(w1d)
        a[f"b1{kv}"] = np.ascontiguousarray(inp[f"cmp_b1_{kv}"][l].reshape(128, 1))
        a[f"pos{kv}"] = np.ascontiguousarray(inp[f"cmp_pos_{kv}"][l].T)
        a[f"w2{kv}"] = np.ascontiguousarray(inp[f"cmp_w2_{kv}"][l])
    a["w2ks"] = np.ascontiguousarray(inp["cmp_w2_k"][l][:, _swap_cols(np.arange(64))])
    a["wg"] = np.ascontiguousarray(inp["ffn_w_gate"][l].reshape(8, 128, NF, 128).transpose(2, 1, 0, 3))
    a["wu"] = np.ascontiguousarray(inp["ffn_w_up"][l].reshape(8, 128, NF, 128).transpose(2, 1, 0, 3))
    a["wd"] = np.ascontiguousarray(inp["ffn_w_down"][l].reshape(NF, 128, D).transpose(1, 0, 2))
    a["cw"] = np.ascontiguousarray(inp["ffn_conv_w"][l].reshape(3, NF, 128).transpose(2, 1, 0))
    a["cb"] = np.ascontiguousarray(inp["ffn_conv_b"][l].reshape(NF, 128).T)
    for nme in ("ln1_g", "ln1_b", "ln2_g", "ln2_b"):
        a[nme] = np.ascontiguousarray(inp[nme][l].reshape(1, D))
    return a


LAYER_SHAPES = {
    "wfm": [NFM, 128, 8, 128], "wtm": [128, 8, NTM], "wout": [128, 8, D], "bd": [2, 128, 128],
    "psc": [128, 2], "gng": [1, 384],
    "w1k": [128, 32, 128], "b1k": [128, 1], "posk": [64, 32], "w2k": [128, 64],
    "w1v": [128, 32, 128], "b1v": [128, 1], "posv": [64, 32], "w2v": [128, 64], "w2ks": [128, 64],
    "wg": [NF, 128, 8, 128], "wu": [NF, 128, 8, 128], "wd": [128, NF, D], "cw": [128, NF, 3], "cb": [128, NF],
    "ln1_g": [1, D], "ln1_b": [1, D], "ln2_g": [1, D], "ln2_b": [1, D],
}


class Tile:
    __slots__ = ("t", "b")

    def __init__(self, t, b):
        self.t = t
        self.b = b


class Ctx:
    def __init__(self, nc, ext_in=(), ext_out=()):
        self.nc = nc
        self.P = Prog(nc)
        self.ext_in = set(ext_in)
        self.ext_out = set(ext_out)
        self.dram = {}
        self.stack = None
        self.uid = 0

    def dr(self, name, shape, dt, kind=None):
        if kind is None:
            kind = "ExternalInput" if name in self.ext_in else ("ExternalOutput" if name in self.ext_out else "Internal")
        t = self.nc.dram_tensor(name, list(shape), dt, kind=kind).ap()
        tl = Tile(t, Buf(name, is_dram=True))
        self.dram[name] = tl
        return tl

    def sb(self, name, shape, dt, es=None):
        self.uid += 1
        t = (es or self.stack).enter_context(self.nc.sbuf_tensor(f"{name}_{self.uid}", list(shape), dt))
        return Tile(t, Buf(name))

    def ps(self, name, shape, dt, es=None):
        self.uid += 1
        t = (es or self.stack).enter_context(self.nc.psum_tensor(f"{name}_{self.uid}", list(shape), dt))
        return Tile(t, Buf(name, excl=True))

    @contextlib.contextmanager
    def phase(self):
        old = self.stack
        with contextlib.ExitStack() as es:
            self.stack = es
            yield es
            self.P.barrier()
        self.stack = old

    def dma(self, eng, out, in_, reads, writes):
        self.P.dma(eng, out, in_, [x.b for x in reads], [x.b for x in writes])

    def op(self, eng, fn, reads, writes):
        self.P.op(eng, fn, [x.b for x in reads], [x.b for x in writes])

    def mm(self, out, lhsT, rhs, start, stop, reads, writes, skip=False):
        kw = dict(start=start, stop=stop)
        if skip:
            kw["skip_group_check"] = True
        self.op("pe", lambda e: e.matmul(out, lhsT=lhsT, rhs=rhs, **kw), reads, writes)

    def tr(self, out, in_, ident, reads, writes):
        self.op("pe", lambda e: e.transpose(out, in_, ident), reads, writes)

    def copy(self, eng, out, in_, reads, writes):
        if eng == "act":
            self.op("act", lambda e: e.copy(out=out, in_=in_), reads, writes)
        else:
            self.op(eng, lambda e: e.tensor_copy(out=out, in_=in_), reads, writes)

    def act(self, out, in_, func, reads, writes, **kw):
        self.op("act", lambda e: e.activation(out=out, in_=in_, func=func, **kw), reads, writes)

    def tt(self, eng, out, in0, in1, op, reads, writes):
        self.op(eng, lambda e: e.tensor_tensor(out=out, in0=in0, in1=in1, op=op), reads, writes)

    def ts(self, eng, out, in0, s1, op0, reads, writes, s2=None, op1=None):
        if op1 is None:
            self.op(eng, lambda e: e.tensor_scalar(out=out, in0=in0, scalar1=s1, scalar2=None, op0=op0), reads, writes)
        else:
            self.op(eng, lambda e: e.tensor_scalar(out=out, in0=in0, scalar1=s1, scalar2=s2, op0=op0, op1=op1), reads, writes)

    def stt(self, out, in0, scalar, in1, op0, op1, reads, writes):
        self.op("dve", lambda e: e.scalar_tensor_tensor(out=out, in0=in0, scalar=scalar, in1=in1, op0=op0, op1=op1),
                reads, writes)


def load_cast(cx, name, src_tile, shape, es=None, eng=None, q=None):
    b = cx.sb(name + "_b", shape, BF16, es)
    cx.dma("pool", b.t[:], src_tile.t, [src_tile], [b])
    return b


def load_f32(cx, name, src_ap, src_tile, shape, es=None, q="sp"):
    f = cx.sb(name, shape, F32, es)
    cx.dma(q, f.t[:], src_ap, [src_tile], [f])
    return f


def build_xT(cx, xd, xT, ident, ntiles, tok0=0):
    with contextlib.ExitStack() as es:
        xb = [cx.sb(f"xb{i}", [128, D], BF16, es) for i in range(3)]
        pt = [cx.ps(f"xpt{i}", [128, 8, 128], BF16, es) for i in range(2)]
        for tt in range(ntiles):
            bb, p = xb[tt % 3], pt[tt % 2]
            r0 = tok0 + tt * 128
            cx.dma("pool", bb.t[:], xd.t[r0:r0 + 128, :], [xd], [bb])
            for k in range(8):
                cx.tr(p.t[:, k, :], bb.t[:, k * 128:(k + 1) * 128], ident.t[:], [bb, ident], [p])
            cx.copy("act" if tt % 2 == 0 else "dve", xT.t[:, :, tt * 128:(tt + 1) * 128], p.t[:], [p], [xT])
        cx.P.barrier()


def layer_norm_store(cx, zps, xres, g_t, b_t, outd, r0, tmp, eng_q="pool"):
    z, st, mv, rs, o = tmp
    for hf in range(2):
        cx.stt(z.t[:, hf * 512:(hf + 1) * 512], xres.t[:, hf * 512:(hf + 1) * 512], ALPHA, zps[hf].t[:],
               ALU.mult, ALU.add, [xres, zps[hf]], [z])
    for hf in range(2):
        cx.op("dve", lambda e, hf=hf: e.bn_stats(out=st.t[:, hf, :], in_=z.t[:, hf * 512:(hf + 1) * 512]), [z], [st])
    cx.op("dve", lambda e: e.bn_aggr(out=mv.t[:], in_=st.t[:]), [st], [mv])
    cx.ts("dve", rs.t[:], mv.t[:, 1:2], 1e-5, ALU.add, [mv], [rs])
    cx.act(rs.t[:], rs.t[:], AF.Sqrt, [rs], [rs])
    cx.op("dve", lambda e: e.reciprocal(out=rs.t[:], in_=rs.t[:]), [rs], [rs])
    cx.ts("dve", o.t[:], z.t[:], mv.t[:, 0:1], ALU.subtract, [z, mv, rs], [o], s2=rs.t[:, 0:1], op1=ALU.mult)
    cx.tt("pool", o.t[:], o.t[:], g_t.t[:], ALU.mult, [o, g_t], [o])
    cx.tt("pool", o.t[:], o.t[:], b_t.t[:], ALU.add, [o, b_t], [o])
    cx.dma(eng_q, outd.t[r0:r0 + 128, :], o.t[:], [o], [outd])


def ln_tmps(cx, es, n=2):
    res = []
    for i in range(n):
        res.append((cx.sb(f"lnz{i}", [128, D], F32, es), cx.sb(f"lnst{i}", [128, 2, 6], F32, es),
                    cx.sb(f"lnmv{i}", [128, 2], F32, es), cx.sb(f"lnrs{i}", [128, 1], F32, es),
                    cx.sb(f"lno{i}", [128, D], F32, es)))
    return res


def prologue_rope(cx, posd, G):
    rope = G["rope"]
    with cx.phase() as es:
        pi_ = cx.sb("posi", [128, T], I32)
        ang = cx.sb("ang", [128, T], F32)
        m = cx.sb("rm", [128, T], F32)
        o = cx.sb("ro", [128, T], F32)
        cx.dma("sp", pi_.t[:], posd.t.broadcast_to([128, T]), [posd], [pi_])
        cx.copy("dve", ang.t[:], pi_.t[:], [pi_], [ang])
        cx.ts("dve", ang.t[:], ang.t[:], G["inv"].t[:, 0:1], ALU.mult, [ang, G["inv"]], [ang])
        ki = cx.sb("rki", [128, T], I32)
        C1 = 6.28125
        C2 = 2.0 * math.pi - 6.28125
        for which, shift in ((0, 0.25), (1, 0.0)):
            cx.ts("dve", m.t[:], ang.t[:], 1.0 / (2.0 * math.pi), ALU.mult, [ang], [m], s2=shift, op1=ALU.add)
            cx.copy("dve", ki.t[:], m.t[:], [m], [ki])
            cx.copy("dve", m.t[:], ki.t[:], [ki], [m])
            cx.stt(o.t[:], m.t[:], -C1, ang.t[:], ALU.mult, ALU.add, [m, ang], [o])
            cx.stt(o.t[:], m.t[:], -C2, o.t[:], ALU.mult, ALU.add, [m, o], [o])
            if which == 0:
                cx.ts("dve", o.t[:], o.t[:], 0.5 * math.pi, ALU.add, [o], [o])
            cx.ts("dve", o.t[:], o.t[:], math.pi, ALU.min, [o], [o], s2=-math.pi, op1=ALU.max)
            cx.act(o.t[:], o.t[:], AF.Sin, [o], [o])
            if which == 1:
                cx.ts("dve", o.t[:], o.t[:], G["sgn"].t[:, 0:1], ALU.mult, [o, G["sgn"]], [o])
            cx.dma("sp", rope.t[which], o.t[:], [o], [rope])


def phase_A(cx, l, xd, W, G, S):
    ident = G["ident"]
    rope = G["rope"]
    with cx.phase():
        xT = cx.sb("xT", [128, 8, T], BF16)
        build_xT(cx, xd, xT, ident, NT)
        with cx.phase() as es:
          if DBG.get("tm", True):
              wtm = load_cast(cx, "wtm", W["wtm"], [128, 8, NTM])
              pss = [cx.ps(f"tmps{i}", [128, 512], F32) for i in range(6)]
              ob = [cx.sb(f"tmob{i}", [128, 640], BF16) for i in range(2)]
              og = [cx.sb(f"tmog{i}", [128, 384], F32) for i in range(2)]
              ogt = [cx.sb(f"tmogt{i}", [128, 18], F32) for i in range(2)]
              for tt in range(NT):
                  p0, p1, p2 = pss[(tt % 2) * 3:(tt % 2) * 3 + 3]
                  for (pp, c0, c1) in ((p0, 0, 512), (p1, 512, 1024), (p2, 1024, NTM)):
                      for k in range(8):
                          cx.mm(pp.t[:, 0:c1 - c0], xT.t[:, k, tt * 128:(tt + 1) * 128], wtm.t[:, k, c0:c1],
                                k == 0, k == 7, [xT, wtm], [pp])
                  b_, g_, t_ = ob[tt % 2], og[tt % 2], ogt[tt % 2]
                  cx.copy("dve", b_.t[:, 0:512], p0.t[:], [p0], [b_])
                  cx.copy("dve", b_.t[:, 512:640], p1.t[:, 0:128], [p1], [b_])
                  cx.act(g_.t[:], p1.t[:, 128:512], AF.Silu, [p1], [g_])
                  cx.act(t_.t[:], p2.t[:, 0:18], AF.Sigmoid, [p2], [t_])
                  r0 = tt * 128
                  cx.dma("sp", S["tmb"].t[r0:r0 + 128, :], b_.t[:], [b_], [S["tmb"]])
                  cx.dma("sp", S["gr"].t[r0:r0 + 128, :], g_.t[:], [g_], [S["gr"]])
                  cx.dma("sp", S["gt"].t[r0:r0 + 128, :], t_.t[:], [t_], [S["gt"]])
        with cx.phase() as es:
          if DBG.get("fm", True):
              C = cx.sb("ropeC", [128, T], F32)
              Sn = cx.sb("ropeS", [128, T], F32)
              cx.dma("sp", C.t[:], rope.t[0], [rope], [C])
              cx.dma("sp", Sn.t[:], rope.t[1], [rope], [Sn])
              wb = [cx.sb(f"wb{i}", [128, 2, 8, 128], BF16) for i in range(2)]
              pss = [cx.ps(f"fmps{i}", [128, 512], F32) for i in range(8)]
              ost = [cx.sb(f"fmo{i}", [128, T], BF16) for i in range(2)]
              vst = [cx.sb(f"fmv{i}", [128, 512], F32) for i in range(2)]
              t1s = [cx.sb(f"fmt1{i}", [128, 512], F32) for i in range(2)]
              t2s = [cx.sb(f"fmt2{i}", [128, 512], F32) for i in range(2)]
              units = [([0], S["vp"], 0, False), ([1], S["vp"], 128, False)]
              ci = 2
              for dest in ("qr", "kr", "qn"):
                  for c in range(3):
                      units.append(([ci, ci + 1], S[dest], 128 * c, True))
                      ci += 2
              for dest in ("ks", "kw"):
                  units.append(([ci, ci + 1], S[dest], 0, True))
                  ci += 2
              units.append(([ci], S["kc"], 0, False))
              units.append(([ci + 1], S["vc"], 0, False))
              gi = 0

              def load_w(ui):
                  for j, cid in enumerate(units[ui][0]):
                      cx.dma("pool", wb[ui % 2].t[:, j], W["wfm"].t[cid], [W["wfm"]], [wb[ui % 2]])
              load_w(0)
              for ui, (cids, dest, row0, is_rope) in enumerate(units):
                  b_ = wb[ui % 2]
                  n = len(cids)
                  if ui + 1 < len(units):
                      load_w(ui + 1)
                  o_ = ost[ui % 2]
                  for tc in range(8):
                      ts_ = slice(tc * 512, (tc + 1) * 512)
                      pp = [pss[(gi % 4) * 2 + j] for j in range(n)]
                      gi += 1
                      for j in range(n):
                          for k in range(8):
                              cx.mm(pp[j].t[:], b_.t[:, j, k, :], xT.t[:, k, ts_], k == 0, k == 7, [b_, xT], [pp[j]])
                      if is_rope:
                          t1, t2 = t1s[tc % 2], t2s[tc % 2]
                          cx.tt("dve", t1.t[:], pp[0].t[:], C.t[:, ts_], ALU.mult, [pp[0], C], [t1])
                          cx.tt("dve", t2.t[:], pp[1].t[:], Sn.t[:, ts_], ALU.mult, [pp[1], Sn], [t2])
                          cx.tt("pool", o_.t[:, ts_], t1.t[:], t2.t[:], ALU.add, [t1, t2], [o_])
                      elif dest is S["vp"]:
                          v_ = vst[tc % 2]
                          cx.copy("act", v_.t[:], pp[0].t[:], [pp[0]], [v_])
                          cx.dma("sp", dest.t[row0:row0 + 128, ts_], v_.t[:], [v_], [dest])
                      else:
                          cx.copy("act", o_.t[:, ts_], pp[0].t[:], [pp[0]], [o_])
                  if dest is not S["vp"]:
                      cx.dma("sp", dest.t[row0:row0 + 128, :], o_.t[:], [o_], [dest])


def phase_B(cx, l, W, G, S):
    with cx.phase():
        psc = load_f32(cx, "psc", W["psc"].t, W["psc"], [128, 2])
        pw = G["pw"]
        prc = G["prc"]
        pss = [cx.ps(f"bps{i}", [128, 512], F32) for i in range(4)]
        for ck in range(2):
            bd = load_cast(cx, f"bd{ck}", Tile(W["bd"].t[ck], W["bd"].b), [128, 128])
            v = cx.sb(f"pv{ck}", [128, 16 + T], F32)
            s2 = cx.sb(f"ps2{ck}", [128, 16 + T], F32)
            s4 = cx.sb(f"ps4{ck}", [128, 16 + T], F32)
            mx = cx.sb(f"pmx{ck}", [128, T], BF16)
            o = cx.sb(f"pbo{ck}", [128, T], BF16)
            for t_ in (v, s2, s4):
                cx.op("pool", lambda e, t_=t_: e.memset(t_.t[:, 0:16], 0.0), [], [t_])
            cx.dma("sp", v.t[:, 16:], S["vp"].t[ck * 128:(ck + 1) * 128, :], [S["vp"]], [v])
            if ck == 0:
                cx.tt("dve", s2.t[:, 16:], v.t[:, 16:], v.t[:, 15:15 + T], ALU.add, [v], [s2])
                cx.tt("dve", s4.t[64:128, 16:], s2.t[64:128, 16:], s2.t[64:128, 14:14 + T], ALU.add, [s2], [s4])
                lo, hi = s2, s4
            else:
                cx.tt("dve", s2.t[:, 16:], v.t[:, 16:], v.t[:, 15:15 + T], ALU.add, [v], [s2])
                cx.tt("dve", s4.t[:, 16:], s2.t[:, 16:], s2.t[:, 14:14 + T], ALU.add, [s2], [s4])
                cx.tt("dve", s2.t[:, 16:], s4.t[:, 16:], s4.t[:, 12:12 + T], ALU.add, [s4], [s2])
                cx.tt("dve", s4.t[64:128, 16:], s2.t[64:128, 16:], s2.t[64:128, 8:8 + T], ALU.add, [s2], [s4])
                lo, hi = s2, s4
            for (src, r) in ((lo, slice(0, 64)), (hi, slice(64, 128))):
                cx.stt(mx.t[r, :], src.t[r, 16:], pw.t[r, ck:ck + 1], v.t[r, 16:], ALU.mult, ALU.subtract,
                       [src, pw, v], [mx])
                cx.tt("dve", src.t[r, 0:16], src.t[r, 16:32], prc.t[r, ck, :], ALU.mult, [src, prc, mx], [src])
                cx.tt("dve", mx.t[r, 0:16], src.t[r, 0:16], v.t[r, 16:32], ALU.subtract, [src, v], [mx])
            for tc in range(8):
                ts_ = slice(tc * 512, (tc + 1) * 512)
                pp = pss[tc % 4]
                cx.mm(pp.t[:], bd.t[:], mx.t[:, ts_], True, True, [bd, mx], [pp])
                cx.act(o.t[:, ts_], pp.t[:], AF.Copy, [pp, psc], [o], scale=psc.t[:, ck:ck + 1])
            cx.dma("sp", S["yt"].t[ck * 128:(ck + 1) * 128, :], o.t[:], [o], [S["yt"]])


def phase_C(cx, l, W, G, S):
    Cd = G["C"]
    ident = G["ident"]
    with cx.phase() as es:
        dm = load_f32(cx, "dm", Cd["c_dm"].t, Cd["c_dm"], [128, 6, 128])
        xir = load_f32(cx, "xir", Cd["c_xir"].t, Cd["c_xir"], [128, 3, 128])
        zt = load_f32(cx, "zt", Cd["c_zt"].t, Cd["c_zt"], [128, 3, 128])
        gc = load_f32(cx, "gc", Cd["c_gc"].t, Cd["c_gc"], [128, 3])
        gng = load_f32(cx, "gng", W["gng"].t.broadcast_to([128, 384]), W["gng"], [128, 384])
        bankA = [cx.ps(f"cA{i}", [128, 512], F32) for i in range(2)]
        bankO = [cx.ps(f"cO{i}", [128, 512], F32) for i in range(2)]
        bankV = [cx.ps(f"cV{i}", [128, 512], F32) for i in range(2)]
        bankY = cx.ps("cY", [128, 8, 128], BF16)
        bankK = cx.ps("cK", [128, 8, 128], BF16)
        ktp = [bankK.t[:, i, :] for i in range(3)]
        opv2 = [[b.t[:, ck * 128:(ck + 1) * 128] for ck in range(3)] for b in bankO]
        kvp2 = [[b.t[:, ck * 128:(ck + 1) * 128] for ck in range(3)] for b in bankV]
        ytp = [bankY.t[:, i, :] for i in range(3)]
        qT, kT, qx, v, vz, R, Rb = [], [], [], [], [], [], []
        for ck in range(3):
            rows = slice(ck * 128, (ck + 1) * 128)
            qT.append(cx.sb(f"cqT{ck}", [128, T], BF16))
            kT.append(cx.sb(f"ckT{ck}", [128, T], BF16))
            qx.append(cx.sb(f"cqx{ck}", [128, T], BF16))
            v.append(cx.sb(f"cv{ck}", [128, NT, 128], BF16))
            vz.append(cx.sb(f"cvz{ck}", [128, NT, 128], BF16))
            R.append(cx.sb(f"cR{ck}", [128, 64], F32))
            Rb.append(cx.sb(f"cRb{ck}", [128, 64], BF16))
            cx.dma("sp", qT[ck].t[:], S["qr"].t[rows, :], [S["qr"]], [qT[ck]])
            cx.dma("sp", kT[ck].t[:], S["kr"].t[rows, :], [S["kr"]], [kT[ck]])
            cx.dma("pool", v[ck].t[:], S["tmb"].t[:, ck * 128:(ck + 1) * 128].rearrange("(n p) c -> p n c", p=128),
                   [S["tmb"]], [v[ck]])
            cx.tt("pool", qx[ck].t[:].rearrange("p (n i) -> p n i", i=128), qT[ck].t[:].rearrange("p (n i) -> p n i", i=128),
                  xir.t[:, ck:ck + 1, :].broadcast_to([128, NT, 128]), ALU.mult, [qT[ck], xir], [qx[ck]])
            cx.tt("pool", vz[ck].t[:], v[ck].t[:], zt.t[:, ck:ck + 1, :].broadcast_to([128, NT, 128]), ALU.mult,
                  [v[ck], zt], [vz[ck]])
            cx.op("pool", lambda e, ck=ck: e.memset(R[ck].t[:], 0.0), [], [R[ck]])
            cx.op("pool", lambda e, ck=ck: e.memset(Rb[ck].t[:], 0.0), [], [Rb[ck]])
        NB = 2
        sgs = [[cx.sb(f"csg{ck}_{i}", [128, 8, 128], F32) for i in range(2)] for ck in range(3)]
        kts = [[cx.sb(f"ckt{ck}_{i}", [128, 128], BF16) for i in range(NB)] for ck in range(3)]
        sms = [[cx.sb(f"csm{hh}_{i}", [128, 3, 128], BF16) for i in range(NB)] for hh in range(2)]
        sts = [[cx.sb(f"cst{ck}_{i}", [128, 2, 6], F32) for i in range(NB)] for ck in range(3)]
        mvs = [[cx.sb(f"cmv{ck}_{i}", [128, 2, 2], F32) for i in range(NB)] for ck in range(3)]
        rss = [[cx.sb(f"crs{ck}_{i}", [128, 2, 2], F32) for i in range(NB)] for ck in range(3)]
        ons = [[cx.sb(f"con{ck}_{i}", [128, 128], F32) for i in range(NB)] for ck in range(3)]
        onb = [[cx.sb(f"conb{ck}_{i}", [128, 128], BF16) for i in range(NB)] for ck in range(3)]
        yts = [[cx.sb(f"cyt{ck}_{i}", [128, 128], BF16) for i in range(NB)] for ck in range(3)]

        def load_sg(ck, blk):
            cx.dma("pool", sgs[ck][blk % 2].t[:],
                   S["gr"].t[blk * 1024:(blk + 1) * 1024, ck * 128:(ck + 1) * 128].rearrange("(n p) c -> p n c", p=128),
                   [S["gr"]], [sgs[ck][blk % 2]])
        for ck in range(3):
            load_sg(ck, 0)
        def ctx(n):
            return slice(n * 128, (n + 1) * 128), n % NB, opv2[n % 2], kvp2[n % 2], bankO[n % 2], bankV[n % 2]

        def st_A(n):
            ns, i, opv, kvp, BO, BV = ctx(n)
            for ck in range(3):
                cx.tr(ktp[ck], kT[ck].t[:, ns], ident.t[:], [kT[ck], ident], [bankK])
                for hh in range(2):
                    r = slice(hh * 64, (hh + 1) * 64)
                    cx.mm(bankA[hh].t[:, ck * 128:(ck + 1) * 128], kT[ck].t[r, ns], qT[ck].t[r, ns], True, True,
                          [kT[ck], qT[ck]], [bankA[hh]])

        def st_1(n):
            ns, i, opv, kvp, BO, BV = ctx(n)
            for ck in range(3):
                cx.copy("act", kts[ck][i].t[:], ktp[ck], [bankK], [kts[ck][i]])
            for hh in range(2):
                cx.tt("dve", sms[hh][i].t[:], bankA[hh].t[:, 0:384].rearrange("p (a b) -> p a b", b=128),
                      dm.t[:, hh::2, :], ALU.mult, [bankA[hh], dm], [sms[hh][i]])

        def st_B(n):
            ns, i, opv, kvp, BO, BV = ctx(n)
            for ck in range(3):
                for hh in range(2):
                    r = slice(hh * 64, (hh + 1) * 64)
                    cx.mm(opv[ck][:, r], sms[hh][i].t[:, ck, :], v[ck].t[:, n, r], True, False,
                          [sms[hh][i], v[ck]], [BO])
                    cx.mm(opv[ck][:, r], qx[ck].t[r, ns], Rb[ck].t[r, :], False, True, [qx[ck], Rb[ck]], [BO])
                cx.mm(kvp[ck], kts[ck][i].t[:], vz[ck].t[:, n, :], True, True, [kts[ck][i], vz[ck]], [BV])

        def st_3a(n):
            ns, i, opv, kvp, BO, BV = ctx(n)
            for ck in range(3):
                for hh in range(2):
                    r = slice(hh * 64, (hh + 1) * 64)
                    cx.stt(R[ck].t[r, :], R[ck].t[r, :], gc.t[r, ck:ck + 1], kvp[ck][r, r], ALU.mult, ALU.add,
                           [R[ck], gc, BV], [R[ck]])
                cx.copy("pool", Rb[ck].t[:], R[ck].t[:], [R[ck]], [Rb[ck]])
            for ck in range(3):
                st, mv, rs = sts[ck][i], mvs[ck][i], rss[ck][i]
                for hh in range(2):
                    r = slice(hh * 64, (hh + 1) * 64)
                    cx.op("dve", lambda e, hh=hh, r=r, st=st, ck=ck, opv_=opv: e.bn_stats(out=st.t[:, hh, :], in_=opv_[ck][:, r]), [BO], [st])
                    cx.op("dve", lambda e, hh=hh, st=st, mv=mv: e.bn_aggr(out=mv.t[:, hh, :], in_=st.t[:, hh, :]), [st], [mv])
                cx.ts("dve", rs.t[:, 0, :], mv.t[:, :, 1], 1e-5, ALU.add, [mv], [rs])

        def st_sq(n):
            i = n % NB
            for ck in range(3):
                rs = rss[ck][i]
                cx.act(rs.t[:, 0, :], rs.t[:, 0, :], AF.Sqrt, [rs], [rs])

        def st_3b(n):
            i = n % NB
            for ck in range(3):
                rs, mv = rss[ck][i], mvs[ck][i]
                cx.op("dve", lambda e, rs=rs: e.reciprocal(out=rs.t[:, 0, :], in_=rs.t[:, 0, :]), [rs], [rs])
                cx.stt(rs.t[:, 1, :], mv.t[:, :, 0], -1.0, rs.t[:, 0, :], ALU.mult, ALU.mult, [mv, rs], [rs])

        def st_4(n):
            ns, i, opv, kvp, BO, BV = ctx(n)
            for ck in range(3):
                rs, on, ob = rss[ck][i], ons[ck][i], onb[ck][i]
                for hh in range(2):
                    r = slice(hh * 64, (hh + 1) * 64)
                    cx.act(on.t[:, r], opv[ck][:, r], AF.Identity, [BO, rs], [on], scale=rs.t[:, 0, hh:hh + 1],
                           bias=rs.t[:, 1, hh:hh + 1])
                cx.tt("pool", on.t[:], on.t[:], gng.t[:, ck * 128:(ck + 1) * 128], ALU.mult, [on, gng], [on])
                cx.tt("pool", ob.t[:], on.t[:], sgs[ck][(n // 8) % 2].t[:, n % 8, :], ALU.mult, [on, sgs[ck][(n // 8) % 2]], [ob])

        def st_C(n):
            ns, i, opv, kvp, BO, BV = ctx(n)
            for ck in range(3):
                cx.tr(ytp[ck], onb[ck][i].t[:], ident.t[:], [onb[ck][i], ident], [bankY])
            for ck in range(3):
                cx.copy("act", yts[ck][i].t[:], ytp[ck], [bankY], [yts[ck][i]])
                cx.dma("sp", S["yt"].t[256 + ck * 128:256 + (ck + 1) * 128, ns], yts[ck][i].t[:], [yts[ck][i]], [S["yt"]])

        st_A(0)
        st_1(0)
        for n in range(NT):
            st_B(n)
            if n >= 1:
                st_C(n - 1)
            st_3a(n)
            st_sq(n)
            if n + 1 < NT:
                st_A(n + 1)
                st_1(n + 1)
            st_3b(n)
            st_4(n)
            if n % 8 == 0 and n // 8 + 1 < NT // 8:
                for ck in range(3):
                    load_sg(ck, n // 8 + 1)
        st_C(NT - 1)


def phase_D(cx, l, W, G, S):
    Cd = G["C"]
    ident = G["ident"]
    rope = G["rope"]
    with cx.phase() as es:
        KCT = cx.sb("KCT", [64, 2, 256], BF16)
        VCX = cx.sb("VCX", [128, 2, 2, 129], BF16)
        with cx.phase():
            Cc = cx.sb("dCc", [64, T], F32)
            Sc = cx.sb("dSc", [64, T], F32)
            cx.dma("sp", Cc.t[:], rope.t[0][0:64, :], [rope], [Cc])
            cx.dma("sp", Sc.t[:], rope.t[1][0:64, :], [rope], [Sc])
            ovl = load_f32(cx, "ovl", Cd["c_ovl"].t, Cd["c_ovl"], [128, 2, 64])
            hps = [cx.ps(f"dhp{i}", [128, 512], F32) for i in range(2)]
            cps = cx.ps("dcp", [128, 512], F32)
            kp = cx.ps("dkp", [128, 2, 256], F32)
            ksp = cx.ps("dksp", [128, 2, 256], F32)
            vps = [cx.ps(f"dvp{i}", [128, 512], F32) for i in range(2)]
            cx.op("pool", lambda e: e.memset(KCT.t[:], 0.0), [], [KCT])
            cx.op("pool", lambda e: e.memset(VCX.t[:, :, :, 64:65], 1.0), [], [VCX])
            for g in range(2):
                cx.copy("pool", VCX.t[:, g, :, 65:129], ovl.t[:], [ovl], [VCX])
            for kv in ("k", "v"):
                src = S["kc"] if kv == "k" else S["vc"]
                kvT = cx.sb(f"dkvT{kv}", [128, T], BF16)
                cx.dma("sp", kvT.t[:], src.t, [src], [kvT])
                w1 = load_cast(cx, f"w1{kv}", W[f"w1{kv}"], [128, 32, 128])
                pos = load_cast(cx, f"pos{kv}", W[f"pos{kv}"], [64, 32], eng="dve")
                b1 = load_f32(cx, f"b1{kv}", W[f"b1{kv}"].t, W[f"b1{kv}"], [128, 1])
                w2 = load_cast(cx, f"w2{kv}", W[f"w2{kv}"], [128, 64], eng="dve")
                cb = cx.sb(f"dcb{kv}", [128, 1], F32)
                h1 = cx.sb(f"dh1{kv}", [128, 2, 256], BF16)
                cx.op("pool", lambda e, h1=h1: e.memset(h1.t[:], 0.0), [], [h1])
                for i in range(32):
                    cx.mm(cps.t[:, 0:1], w1.t[0:64, i, :], pos.t[0:64, i:i + 1], i == 0, i == 31, [w1, pos], [cps])
                cx.tt("dve", cb.t[:], cps.t[:, 0:1], b1.t[:], ALU.add, [cps, b1], [cb])
                for g in range(2):
                    r = slice(g * 64, (g + 1) * 64)
                    for i in range(32):
                        cx.mm(hps[g].t[:, 0:255], w1.t[r, i, :], kvT.t[r, i:i + 16 * 254 + 1:16], i == 0, i == 31,
                              [w1, kvT], [hps[g]])
                    cx.act(h1.t[:, g, 0:255], hps[g].t[:, 0:255], AF.Gelu_apprx_tanh, [hps[g], cb], [h1],
                           bias=cb.t[:, 0:1])
                if kv == "k":
                    w2s = load_cast(cx, "w2ks", W["w2ks"], [128, 64], eng="dve")
                    cx.mm(kp.t[0:64], w2.t[:], h1.t[:], True, True, [w2, h1], [kp])
                    cx.mm(ksp.t[0:64], w2s.t[:], h1.t[:], True, True, [w2s, h1], [ksp])
                    t1 = cx.sb("dkt1", [64, 2, 255], F32)
                    t2 = cx.sb("dkt2", [64, 2, 255], F32)
                    cview = Cc.t[:, 31::16].unsqueeze(1).broadcast_to([64, 2, 255])
                    sview = Sc.t[:, 31::16].unsqueeze(1).broadcast_to([64, 2, 255])
                    cx.tt("dve", t1.t[:], kp.t[0:64, :, 0:255], cview, ALU.mult, [kp, Cc], [t1])
                    cx.tt("dve", t2.t[:], ksp.t[0:64, :, 0:255], sview, ALU.mult, [ksp, Sc], [t2])
                    cx.tt("dve", KCT.t[:, :, 0:255], t1.t[:], t2.t[:], ALU.add, [t1, t2], [KCT])
                else:
                    for g in range(2):
                        for nt in range(2):
                            vp_ = vps[(g * 2 + nt) % 2]
                            cx.mm(vp_.t[:, 0:64], h1.t[:, g, nt * 128:(nt + 1) * 128], w2.t[:], True, True, [h1, w2], [vp_])
                            cx.copy("act", VCX.t[:, g, nt, 0:64], vp_.t[:, 0:64], [vp_], [VCX])
        dstop = DBG.get("d_stop", 9)
        if dstop <= 1:
            return
        identb = ident
        QA = [cx.sb(f"QA{h}", [128, T], BF16) for h in range(6)]
        KSA = [cx.sb(f"KSA{g}", [128, T], BF16) for g in range(2)]
        KW = [cx.sb(f"KW{g}", [64, T], BF16) for g in range(2)]
        VSX = cx.sb("VSX", [128, NT, 2, 65], BF16)
        VWX = cx.sb("VWX", [128, NT, 2, 65], BF16)
        CMN = cx.sb("CMN", [128, 2, T], BF16)
        SB_ = load_f32(cx, "sbias", Cd["c_sbias"].t, Cd["c_sbias"], [128, NT, 64])
        GT = cx.sb("GTs", [128, NT, 18], F32)
        cx.dma("sp", GT.t[:], S["gt"].t.rearrange("(n p) c -> p n c", p=128), [S["gt"]], [GT])
        causb = cx.sb("causb", [128, 128], BF16)
        upperb = cx.sb("upperb", [128, 128], BF16)
        cx.dma("pool", causb.t[:], Cd["c_caus"].t, [Cd["c_caus"]], [causb])
        cx.dma("pool", upperb.t[:], Cd["c_upper"].t, [Cd["c_upper"]], [upperb])
        for nt in range(2):
            cx.dma("pool", CMN.t[:, nt, :], Cd["c_cmn"].t[:, nt, :], [Cd["c_cmn"]], [CMN])
        for g in range(2):
            cx.dma("pool", KSA[g].t[64:128, :], Cd["c_expand"].t, [Cd["c_expand"]], [KSA[g]])
        for h in range(6):
            cx.dma("sp", QA[h].t[0:64, :], S["qn"].t[h * 64:(h + 1) * 64, :], [S["qn"]], [QA[h]])
            cx.op("pool", lambda e, h=h: e.memset(QA[h].t[64:128, :], 0.0), [], [QA[h]])
        for g in range(2):
            cx.dma("pool", KSA[g].t[0:64, :], S["ks"].t[g * 64:(g + 1) * 64, :], [S["ks"]], [KSA[g]])
            cx.dma("sp", KW[g].t[:], S["kw"].t[g * 64:(g + 1) * 64, :], [S["kw"]], [KW[g]])
            cx.dma("pool", VSX.t[:, :, g, 0:64],
                   S["tmb"].t[:, 384 + g * 64:384 + (g + 1) * 64].rearrange("(n p) c -> p n c", p=128), [S["tmb"]], [VSX])
            cx.dma("pool", VWX.t[:, :, g, 0:64],
                   S["tmb"].t[:, 512 + g * 64:512 + (g + 1) * 64].rearrange("(n p) c -> p n c", p=128), [S["tmb"]], [VWX])
        cx.op("pool", lambda e: e.memset(VSX.t[:, :, :, 64:65], 1.0), [], [VSX])
        cx.op("pool", lambda e: e.memset(VWX.t[:, :, :, 64:65], 1.0), [], [VWX])
        if dstop <= 2:
            return
        SP = [cx.ps(f"dS{i}", [128, 512], F32) for i in range(4)]
        OP = [cx.ps(f"dO{i}", [128, 512], F32) for i in range(3)]
        tpt = cx.ps("dT", [128, 8, 128], BF16)
        _tb = Buf("dT", excl=True)
        TP = [Tile(tpt.t[:, 4 * i:4 * i + 4, :], _tb) for i in range(2)]
        pTs = [cx.sb(f"dpT{i}", [128, 512], BF16) for i in range(4)]
        OACC = [cx.sb(f"dOACC{i}", [128, 4, 384], F32) for i in range(2)]
        IMP = [cx.sb(f"dIMP{i}", [128, 4, 64], F32) for i in range(2)]
        recs = [cx.sb(f"drec{i}", [128, 2, 4], F32) for i in range(4)]
        sc1 = [cx.sb(f"dsc1{i}", [128, 64], F32) for i in range(2)]
        sc2 = [cx.sb(f"dsc2{i}", [128, 64], F32) for i in range(2)]
        m8 = [cx.sb(f"dm8{i}", [128, 16], F32) for i in range(2)]
        nm = [cx.sb(f"dnm{i}", [128, 128], BF16) for i in range(2)]
        for t_ in nm:
            cx.op("pool", lambda e, t_=t_: e.memset(t_.t[:], 0.0), [], [t_])
        obf = [cx.sb(f"dobf{i}", [128, 384], BF16) for i in range(2)]
        yst = [cx.sb(f"dyst{i}", [128, 3, 512], BF16) for i in range(2)]
        cnt = {"s": 0, "o": 0, "p": 0, "r": 0, "t": 0}

        def nxt(key, lst):
            x = lst[cnt[key] % len(lst)]
            cnt[key] += 1
            return x

        items = []

        def cmp_item(c, h):
            g, rr = divmod(h, 3)
            cs = slice(c * 512, (c + 1) * 512)
            oacc, imp = OACC[c % 2], IMP[c % 2]
            nts = [0] + ([1] if c >= 4 else [])
            st = {}

            def S_():
                st["pts"] = {}
                for nt in nts:
                    sp_ = nxt("s", SP)
                    need_mask = (c <= 4) if nt == 0 else True
                    cx.mm(sp_.t[:], KCT.t[0:64, g, nt * 128:(nt + 1) * 128], QA[h].t[0:64, cs], True, True,
                          [KCT, QA[h]], [sp_])
                    if need_mask:
                        cx.mm(sp_.t[:], identb.t[:], CMN.t[:, nt, cs], False, True, [identb, CMN], [sp_], skip=True)
                    pT = nxt("p", pTs)
                    cx.act(pT.t[:], sp_.t[:], AF.Exp, [sp_], [pT], scale=0.125)
                    st["pts"][nt] = pT

            def PV_():
                pts = st["pts"]
                for q4 in range(4):
                    qt = 4 * c + q4
                    ob_ = nxt("o", OP)
                    for j, nt in enumerate(nts):
                        cx.mm(ob_.t[:, 0:129], pts[nt].t[:, q4 * 128:(q4 + 1) * 128], VCX.t[:, g, nt, :],
                              j == 0, j == len(nts) - 1, [pts[nt], VCX], [ob_])
                    rc = nxt("r", recs)
                    cx.ts("dve", rc.t[:, 0, 0:1], ob_.t[:, 64:65], 1e-30, ALU.add, [ob_], [rc])
                    cx.op("dve", lambda e, rc=rc: e.reciprocal(out=rc.t[:, 0, 0:1], in_=rc.t[:, 0, 0:1]), [rc], [rc])
                    cx.tt("dve", rc.t[:, 1, 0:1], rc.t[:, 0, 0:1], GT.t[:, qt, 3 * h:3 * h + 1], ALU.mult, [rc, GT], [rc])
                    cx.ts("dve", oacc.t[:, q4, h * 64:(h + 1) * 64], ob_.t[:, 0:64], rc.t[:, 1, 0:1], ALU.mult,
                          [ob_, rc], [oacc])
                    if rr == 0:
                        cx.ts("dve", imp.t[:, q4, :], ob_.t[:, 65:129], rc.t[:, 0, 0:1], ALU.mult, [ob_, rc], [imp])
                    else:
                        cx.stt(imp.t[:, q4, :], ob_.t[:, 65:129], rc.t[:, 0, 0:1], imp.t[:, q4, :], ALU.mult, ALU.add,
                               [ob_, rc, imp], [imp])
                if rr == 2:
                    for q4 in range(4):
                        qt = 4 * c + q4
                        s1, s2, mm8, nm_ = sc1[q4 % 2], sc2[q4 % 2], m8[q4 % 2], nm[q4 % 2]
                        cx.tt("dve", s1.t[:], imp.t[:, q4, :], SB_.t[:, qt, :], ALU.add, [imp, SB_], [s1])
                        cx.op("dve", lambda e, mm8=mm8, s1=s1: e.max(out=mm8.t[:, 0:8], in_=s1.t[:]), [s1], [mm8])
                        cx.op("dve", lambda e, mm8=mm8, s1=s1, s2=s2: e.match_replace(
                            out=s2.t[:], in_to_replace=mm8.t[:, 0:8], in_values=s1.t[:], imm_value=-1e9), [s1, mm8], [s2])
                        cx.op("dve", lambda e, mm8=mm8, s2=s2: e.max(out=mm8.t[:, 8:16], in_=s2.t[:]), [s2], [mm8])
                        cx.ts("dve", mm8.t[:, 15:16], mm8.t[:, 15:16], 0.0, ALU.max, [mm8], [mm8])
                        cx.ts("dve", nm_.t[:, 64:128], s1.t[:], mm8.t[:, 15:16], ALU.is_lt, [s1, mm8], [nm_],
                              s2=NEG, op1=ALU.mult)
                        tp = nxt("t", TP)
                        cx.tr(tp.t[:, 0, :], nm_.t[:], ident.t[:], [nm_, ident], [tp])
                        for r3 in range(3):
                            hh = 3 * g + r3
                            cx.copy("dve",
                                    QA[hh].t[64:128, qt * 128:(qt + 1) * 128], tp.t[64:128, 0, :], [tp], [QA[hh]])
            return S_, PV_

        def att_items(c, branch, h):
            g = h // 3
            oacc = OACC[c % 2]
            kts = list(range(0, 4 * c + 4) if branch == 1 else range(max(4 * c - 4, 0), 4 * c + 4))
            shared = {"first": True}
            res = []
            for kt in kts:
                lo = max(kt - 4 * c, 0)
                hi = 3 if branch == 1 else min(kt + 4 - 4 * c, 3)
                n_ = (hi - lo + 1) * 128
                q0 = c * 512 + lo * 128
                ks_ = slice(kt * 128, (kt + 1) * 128)
                st = {}

                def S_(kt=kt, lo=lo, hi=hi, n_=n_, q0=q0, ks_=ks_, st=st):
                    sp_ = nxt("s", SP)
                    if branch == 1:
                        cx.mm(sp_.t[:, 0:n_], KSA[g].t[:, ks_], QA[h].t[:, q0:q0 + n_], True, True,
                              [KSA[g], QA[h]], [sp_])
                    else:
                        cx.mm(sp_.t[:, 0:n_], KW[g].t[0:64, ks_], QA[h].t[0:64, q0:q0 + n_], True, True,
                              [KW[g], QA[h]], [sp_])
                    if kt >= 4 * c:
                        cx.mm(sp_.t[:, 0:128], identb.t[:], causb.t[:], False, True, [identb, causb], [sp_], skip=True)
                    if branch == 2 and 4 * c <= kt + 4 <= 4 * c + 3:
                        cx.mm(sp_.t[:, n_ - 128:n_], identb.t[:], upperb.t[:], False, True, [identb, upperb], [sp_],
                              skip=True)
                    pT = nxt("p", pTs)
                    cx.act(pT.t[:, 0:n_], sp_.t[:, 0:n_], AF.Exp, [sp_], [pT], scale=0.125)
                    st["pT"] = pT

                def PV_(kt=kt, lo=lo, hi=hi, st=st, last=(kt == kts[-1])):
                    if shared["first"]:
                        shared["ob"] = nxt("o", OP)
                    ob_ = shared["ob"]
                    ov = ob_.t[:, 0:260].rearrange("p (a b) -> p a b", b=65)
                    pT = st["pT"]
                    vx = VSX if branch == 1 else VWX
                    for q4 in range(lo, hi + 1):
                        cx.mm(ov[:, q4, :], pT.t[:, (q4 - lo) * 128:(q4 - lo + 1) * 128], vx.t[:, kt, g, :],
                              shared["first"], True, [pT, vx], [ob_], skip=not shared["first"])
                        shared["first"] = False
                    if last:
                        rc = nxt("r", recs)
                        cx.ts("dve", rc.t[:, 0, :], ov[:, :, 64], 1e-30, ALU.add, [ob_], [rc])
                        cx.op("dve", lambda e, rc=rc: e.reciprocal(out=rc.t[:, 0, :], in_=rc.t[:, 0, :]), [rc], [rc])
                        col = 3 * h + branch
                        cx.tt("dve", rc.t[:, 1, :], rc.t[:, 0, :], GT.t[:, 4 * c:4 * c + 4, col], ALU.mult, [rc, GT], [rc])
                        for q4 in range(4):
                            av = oacc.t[:, q4, h * 64:(h + 1) * 64]
                            cx.stt(av, ov[:, q4, 0:64], rc.t[:, 1, q4:q4 + 1], av, ALU.mult, ALU.add,
                                   [ob_, rc, oacc], [oacc])
                res.append((S_, PV_))
            return res

        def out_item(c):
            cs = slice(c * 512, (c + 1) * 512)
            oacc = OACC[c % 2]

            def PV_():
                ys = yst[c % 2]
                for q4 in range(4):
                    ob2 = obf[q4 % 2]
                    cx.copy("pool", ob2.t[:], oacc.t[:, q4, :], [oacc], [ob2])
                    tp = nxt("t", TP)
                    for j in range(3):
                        cx.tr(tp.t[:, j, :], ob2.t[:, j * 128:(j + 1) * 128], ident.t[:], [ob2, ident], [tp])
                    cx.copy("act", ys.t[:, :, q4 * 128:(q4 + 1) * 128], tp.t[:, 0:3, :], [tp], [ys])
                for j in range(3):
                    cx.dma("sp", S["yt"].t[640 + j * 128:640 + (j + 1) * 128, cs], ys.t[:, j, :], [ys], [S["yt"]])
            return (lambda: None), PV_

        for c in range(8):
            for h in range(6):
                items.append(cmp_item(c, h))
            if dstop >= 5:
                for branch in ((1, 2) if dstop >= 6 else (1,)):
                    for h in range(6):
                        items.extend(att_items(c, branch, h))
            items.append(out_item(c))
        for i in range(len(items) + 1):
            if i < len(items):
                items[i][0]()
            if i >= 1:
                items[i - 1][1]()


def phase_E(cx, l, xd, x1d, W, G, S):
    with cx.phase() as es:
        wo = load_cast(cx, "wout", W["wout"], [128, 8, D])
        g_t = load_f32(cx, "ln1g", W["ln1_g"].t.broadcast_to([128, D]), W["ln1_g"], [128, D])
        b_t = load_f32(cx, "ln1b", W["ln1_b"].t.broadcast_to([128, D]), W["ln1_b"], [128, D])
        yts = [cx.sb(f"eyt{i}", [128, 8, 512], BF16) for i in range(2)]
        xrs = [cx.sb(f"exr{i}", [128, D], F32) for i in range(2)]
        pss = [cx.ps(f"eps{i}", [128, 512], F32) for i in range(4)]
        tmps = ln_tmps(cx, es)
        def load_y(tc):
            cx.dma("sp", yts[tc % 2].t[:], S["yt"].t[:, tc * 512:(tc + 1) * 512].rearrange("(k p) t -> p k t", p=128),
                   [S["yt"]], [yts[tc % 2]])
        load_y(0)
        for tc in range(8):
            y_ = yts[tc % 2]
            if tc + 1 < 8:
                load_y(tc + 1)
            for q in range(4):
                tt = tc * 4 + q
                xr = xrs[tt % 2]
                cx.dma("sp", xr.t[:], xd.t[tt * 128:(tt + 1) * 128, :], [xd], [xr])
                zp = pss[(tt % 2) * 2:(tt % 2) * 2 + 2]
                for hf in range(2):
                    for k in range(8):
                        cx.mm(zp[hf].t[:], y_.t[:, k, q * 128:(q + 1) * 128], wo.t[:, k, hf * 512:(hf + 1) * 512],
                              k == 0, k == 7, [y_, wo], [zp[hf]])
                layer_norm_store(cx, zp, xr, g_t, b_t, x1d, tt * 128, tmps[tt % 2])


def phase_F(cx, l, x1d, x2d, W, G, S):
    TC = 1024
    ident = G["ident"]
    with cx.phase() as es:
        wd = cx.sb("wd_b", [128, NF, D], BF16)
        for f in range(NF):
            cx.dma("pool", wd.t[:, f, :], W["wd"].t[:, f, :], [W["wd"]], [wd])
        cw = load_f32(cx, "cw", W["cw"].t, W["cw"], [128, NF, 3])
        cb = load_f32(cx, "cb", W["cb"].t, W["cb"], [128, NF])
        g_t = load_f32(cx, "ln2g", W["ln2_g"].t.broadcast_to([128, D]), W["ln2_g"], [128, D])
        b_t = load_f32(cx, "ln2b", W["ln2_b"].t.broadcast_to([128, D]), W["ln2_b"], [128, D])
        carry = cx.sb("carry", [128, NF, 2], F32)
        cx.op("pool", lambda e: e.memset(carry.t[:], 0.0), [], [carry])
        xT = cx.sb("x1T", [128, 8, TC], BF16)
        act = cx.sb("ffact", [128, NF, TC], BF16)
        wb = [cx.sb(f"fwb{i}", [128, 2, 8, 128], BF16) for i in range(3)]
        hts = [cx.sb(f"fhb{i}", [128, 2 + TC], F32) for i in range(2)]
        hbA = [Tile(t.t, Buf(f"fhbA{i}")) for i, t in enumerate(hts)]
        hbB = [Tile(t.t, Buf(f"fhbB{i}")) for i, t in enumerate(hts)]
        hc = [cx.sb(f"fhc{i}", [128, 512], F32) for i in range(2)]
        gl = [cx.sb(f"fgl{i}", [128, 512], F32) for i in range(2)]
        xrs = [cx.sb(f"fxr{i}", [128, D], F32) for i in range(2)]
        tmps = ln_tmps(cx, es)
        pss = [cx.ps(f"fps{i}", [128, 512], F32) for i in range(6)]
        gi = 0
        nsteps = (T // TC) * NF

        def load_w(step):
            f = step % NF
            b_ = wb[step % 3]
            cx.dma("pool", b_.t[:, 0], W["wg"].t[f], [W["wg"]], [b_])
            cx.dma("pool", b_.t[:, 1], W["wu"].t[f], [W["wu"]], [b_])
        load_w(0)
        load_w(1)
        step = 0
        for tc in range(T // TC):
            build_xT(cx, x1d, xT, ident, TC // 128, tok0=tc * TC)
            for f in range(NF):
                b_ = wb[step % 3]
                if step + 2 < nsteps:
                    load_w(step + 2)
                step += 1
                hA, hB = hbA[f % 2], hbB[f % 2]
                ht = hts[f % 2].t
                cx.copy("act", ht[:, 0:2], carry.t[:, f, :], [carry], [hA])
                for hf in range(TC // 512):
                    ts_ = slice(hf * 512, (hf + 1) * 512)
                    pg, pu = pss[(gi % 2) * 2], pss[(gi % 2) * 2 + 1]
                    c_, g_ = hc[gi % 2], gl[gi % 2]
                    gi += 1
                    for k in range(8):
                        cx.mm(pg.t[:], b_.t[:, 0, k, :], xT.t[:, k, ts_], k == 0, k == 7, [b_, xT], [pg])
                    for k in range(8):
                        cx.mm(pu.t[:], b_.t[:, 1, k, :], xT.t[:, k, ts_], k == 0, k == 7, [b_, xT], [pu])
                    hw_ = [hA] if hf == 0 else [hB]
                    hr_ = [hA] if hf == 0 else [hA, hB]
                    o = hf * 512
                    cx.copy("act", ht[:, 2 + o:514 + o], pg.t[:], [pg], hw_)
                    cx.ts("dve", c_.t[:], ht[:, 2 + o:514 + o], cw.t[:, f, 2:3], ALU.mult, hr_ + [cw, cb], [c_],
                          s2=cb.t[:, f:f + 1], op1=ALU.add)
                    cx.stt(c_.t[:], ht[:, 1 + o:513 + o], cw.t[:, f, 1:2], c_.t[:], ALU.mult, ALU.add, hr_ + [cw, c_], [c_])
                    cx.stt(c_.t[:], ht[:, o:512 + o], cw.t[:, f, 0:1], c_.t[:], ALU.mult, ALU.add, hr_ + [cw, c_], [c_])
                    cx.act(g_.t[:], c_.t[:], AF.Gelu_apprx_tanh, [c_], [g_])
                    cx.tt("dve", act.t[:, f, ts_], g_.t[:], pu.t[:], ALU.mult, [g_, pu], [act])
                cx.copy("act", carry.t[:, f, :], ht[:, TC:TC + 2], [hB], [carry])
            for q in range(TC // 128):
                tt = tc * (TC // 128) + q
                xr = xrs[tt % 2]
                cx.dma("sp", xr.t[:], x1d.t[tt * 128:(tt + 1) * 128, :], [x1d], [xr])
                zp = pss[4:6]
                for hf in range(2):
                    for f in range(NF):
                        cx.mm(zp[hf].t[:], act.t[:, f, q * 128:(q + 1) * 128], wd.t[:, f, hf * 512:(hf + 1) * 512],
                              f == 0, f == NF - 1, [act, wd], [zp[hf]])
                layer_norm_store(cx, zp, xr, g_t, b_t, x2d, tt * 128, tmps[tt % 2])

SCRATCH = {
    "rope": ([2, 128, T], F32), "vp": ([256, T], F32),
    "qr": ([384, T], BF16), "kr": ([384, T], BF16), "qn": ([384, T], BF16),
    "ks": ([128, T], BF16), "kw": ([128, T], BF16), "kc": ([128, T], BF16), "vc": ([128, T], BF16),
    "tmb": ([T, 640], BF16), "gr": ([T, 384], F32), "gt": ([T, 18], F32),
    "yt": ([1024, T], BF16), "x1": ([T, D], F32), "xmid": ([T, D], F32),
}


def build_program(layers=(0, 1), phases="ABCDEF", ext_in=(), ext_out=(), prologue=True):
    nc = bass.Bass("TRN2", target_bir_lowering=False)
    cx = Ctx(nc, ext_in, ext_out)
    xd = cx.dr("x", [T, D], F32, kind="ExternalInput")
    posd = cx.dr("pos", [1, T], I32, kind="ExternalInput")
    Cd = {k: cx.dr(k, list(v.shape), F32, kind="ExternalInput") for k, v in CONSTS.items()}
    Wd = {l: {k: cx.dr(f"{k}_{l}", shp, F32, kind="ExternalInput") for k, shp in LAYER_SHAPES.items()} for l in layers}
    S = {k: cx.dr(k, shp, dt) for k, (shp, dt) in SCRATCH.items()}
    outd = cx.dr("y", [T, D], F32, kind="ExternalOutput")
    with contextlib.ExitStack() as gs:
        cx.stack = gs
        G = {"rope": S["rope"]}
        G["ident"] = load_cast(cx, "ident", Cd["c_ident"], [128, 128])
        G["inv"] = load_f32(cx, "inv", Cd["c_inv"].t, Cd["c_inv"], [128, 1])
        G["sgn"] = load_f32(cx, "sgn", Cd["c_sgn"].t, Cd["c_sgn"], [128, 1])
        G["pw"] = load_f32(cx, "pw", Cd["c_pw"].t, Cd["c_pw"], [128, 2])
        G["prc"] = load_f32(cx, "prc", Cd["c_prc"].t, Cd["c_prc"], [128, 2, 16])
        G["C"] = Cd
        if prologue:
            prologue_rope(cx, posd, G)
        cur = xd
        for li, l in enumerate(layers):
            nxt = outd if li == len(layers) - 1 else S["xmid"]
            W = Wd[l]
            if "A" in phases:
                phase_A(cx, l, cur, W, G, S)
            if "B" in phases:
                phase_B(cx, l, W, G, S)
            if "C" in phases:
                phase_C(cx, l, W, G, S)
            if "D" in phases:
                phase_D(cx, l, W, G, S)
            if "E" in phases:
                phase_E(cx, l, cur, S["x1"], W, G, S)
            if "F" in phases:
                phase_F(cx, l, S["x1"], nxt, W, G, S)
            cur = nxt
        cx.P.barrier()
        finals = [outd.b] + [cx.dram[n].b for n in cx.ext_out]
        cx.P.emit_all(final_bufs=finals)
    return nc, cx


def make_in_maps(inputs, layers=(0, 1), cores=range(8)):
    shared = dict(CONSTS)
    for l in layers:
        for k, v in _layer_arrays(inputs, l).items():
            assert list(v.shape) == LAYER_SHAPES[k], (k, v.shape)
            shared[f"{k}_{l}"] = v.astype(np.float32, copy=False)
    maps = []
    for b in cores:
        m = dict(shared)
        m["x"] = np.ascontiguousarray(inputs["x"][b])
        m["pos"] = np.ascontiguousarray(inputs["positions"][b].reshape(1, T).astype(np.int32))
        maps.append(m)
    return maps


def kernel(**inputs):
    inputs = {k: np.asarray(v) for k, v in inputs.items()}
    nc, cx = build_program()
    maps = make_in_maps(inputs)
    res = run_bass_kernel_spmd(nc, maps, core_ids=list(range(8)))
    return np.stack([np.asarray(r["y"]) for r in res.results], 0).astype(np.float32)
```

```python
import contextlib
import math
import numpy as np
import concourse.bass as bass
import concourse.mybir as mybir
from concourse.bass_utils import run_bass_kernel_spmd

F32 = mybir.dt.float32
BF16 = mybir.dt.bfloat16
I32 = mybir.dt.int32
AF = mybir.ActivationFunctionType
ALU = mybir.AluOpType
AX = mybir.AxisListType

T = 4096
D = 1024
DEPTH = 2
NT = T // 128
DFF = 2816
NF = DFF // 128
ALPHA = (2 * DEPTH) ** 0.25
NEG = -10000.0
DBG = {}

ENGS = ("pe", "act", "dve", "pool", "sp")


class Buf:
    __slots__ = ("name", "writers", "readers", "dsem", "dcount", "is_dram", "vsem", "excl")

    def __init__(self, name, is_dram=False, excl=False):
        self.name = name
        self.is_dram = is_dram
        self.excl = excl
        self.writers = []
        self.readers = []
        self.dsem = None
        self.dcount = 0
        self.vsem = None


class Op:
    __slots__ = ("eng", "emit", "waits", "is_dma", "dbuf", "dval", "needs_inc", "val", "vsem")

    def __init__(self, eng, emit, is_dma=False):
        self.eng = eng
        self.emit = emit
        self.waits = []
        self.is_dma = is_dma
        self.dbuf = None
        self.dval = 0
        self.needs_inc = False
        self.val = 0
        self.vsem = None


class Prog:
    def __init__(self, nc):
        self.nc = nc
        self.ops = {e: [] for e in ENGS}
        self.last = {e: None for e in ENGS}
        self.dma_bufs = {}
        self.pending_bar = {e: [] for e in ENGS}
        self.seq = 0
        self.bar_seq = 0
        self.vfree = {True: [], False: []}
        self.vkind = []
        self.vcount = []
        self.rsem = []

    def _dep(self, op, prod, force=False):
        if prod is op:
            return
        if (not force) and prod.val < self.bar_seq:
            return
        if not prod.is_dma and prod.eng == op.eng:
            if op.eng in ("pe", "sp"):
                return
        op.waits.append(prod)
        if not prod.is_dma:
            prod.needs_inc = True

    @staticmethod
    def _prune(lst):
        last = {}
        for r in lst:
            last[(r.eng, r.is_dma, r.vsem)] = r
        return list(last.values())

    def op(self, eng, emit, reads=(), writes=(), dma=False):
        o = Op(eng, emit, is_dma=dma)
        self.seq += 1
        o.val = self.seq
        if self.pending_bar[eng]:
            for p in self.pending_bar[eng]:
                self._dep(o, p, force=True)
            self.pending_bar[eng] = []
        for b in reads:
            for w in b.writers:
                self._dep(o, w)
            if b.excl:
                for r in b.readers:
                    if r.eng != eng:
                        self._dep(o, r)
        for b in writes:
            for r in b.readers:
                if (not r.is_dma) and r.eng == eng and not dma:
                    continue
                self._dep(o, r)
            if not b.readers:
                for w in b.writers:
                    if w.is_dma and dma:
                        continue
                    if (not w.is_dma) and w.eng == eng and not dma:
                        continue
                    self._dep(o, w)
        if dma:
            assert len(writes) == 1
            b = writes[0]
            if b.is_dram:
                b = [r for r in reads if not r.is_dram][0]
            if b.vsem is None:
                sw = (eng == "pool")
                if self.vfree[sw]:
                    b.vsem = self.vfree[sw].pop()
                else:
                    b.vsem = len(self.vcount)
                    self.vcount.append(0)
                    self.vkind.append(sw)
                b.dcount = self.vcount[b.vsem]
            b.dcount += 16
            self.vcount[b.vsem] = b.dcount
            o.dbuf = b
            o.dval = b.dcount
            o.vsem = b.vsem
            self.dma_bufs[id(b)] = b
        for b in writes:
            if b.readers:
                b.writers = [o]
                b.readers = []
            else:
                b.writers.append(o)
                if len(b.writers) > 6:
                    b.writers = self._prune(b.writers)
        for b in reads:
            b.readers.append(o)
            if len(b.readers) > 6:
                b.readers = self._prune(b.readers)
        self.ops[eng].append(o)
        if not dma:
            self.last[eng] = o
        return o

    def dma(self, eng, out, in_, reads, writes):
        return self.op(eng, lambda e: e.dma_start(out=out, in_=in_), reads, writes, dma=True)

    def barrier(self):
        targets = []
        for e in ENGS:
            if self.last[e] is not None:
                targets.append(self.last[e])
        for b in self.dma_bufs.values():
            if b.vsem is not None:
                p = Op("sp", None, is_dma=True)
                p.dbuf = b
                p.dval = b.dcount
                p.vsem = b.vsem
                targets.append(p)
                self.vfree[self.vkind[b.vsem]].append(b.vsem)
                b.vsem = None
        self.dma_bufs = {}
        self.seq += 1
        self.bar_seq = self.seq
        for e in ENGS:
            self.pending_bar[e] = self._prune(self.pending_bar[e] + targets)

    def emit_all(self, final_bufs=()):
        nc = self.nc
        esem = {e: nc.alloc_semaphore(name=f"es_{e}") for e in ENGS}
        self.rsem = [nc.alloc_semaphore(name=f"ds_{i}") for i in range(len(self.vcount))]
        for e in ENGS:
            c = 0
            for o in self.ops[e]:
                if (not o.is_dma) and o.needs_inc:
                    c += 1
                    o.val = c
        engobj = {"pe": "tensor", "act": "scalar", "dve": "vector", "pool": "gpsimd", "sp": "sync"}
        with nc.Block() as block:
            for e in ENGS:
                def body(eng, ops=self.ops[e], e=e):
                    waited = {}
                    for o in ops:
                        need = {}
                        for p in o.waits:
                            if p.is_dma:
                                sem, val = self.rsem[p.vsem], p.dval
                            else:
                                sem, val = esem[p.eng], p.val
                            if sem is None:
                                continue
                            if need.get(sem.num, (None, 0))[1] < val:
                                need[sem.num] = (sem, val)
                        for k, (sem, val) in need.items():
                            if waited.get(k, 0) >= val:
                                continue
                            waited[k] = val
                            eng.wait_ge(sem, val)
                        ins = o.emit(eng)
                        if o.is_dma:
                            ins.then_inc(self.rsem[o.vsem], 16)
                        elif o.needs_inc:
                            ins.then_inc(esem[e], 1)
                    if e == "sp":
                        for i, sem in enumerate(self.rsem):
                            if waited.get(sem.num, 0) < self.vcount[i]:
                                eng.wait_ge(sem, self.vcount[i])

                getattr(block, engobj[e])(body)


OFF = {}
_o = 0
for _n, _w in (("v_pool", 256), ("q_ret", 384), ("k_ret", 384), ("v_ret", 384), ("g_ret", 384),
               ("q_nsa", 384), ("k_cmp", 128), ("v_cmp", 128), ("k_slc", 128), ("v_slc", 128),
               ("k_win", 128), ("v_win", 128), ("gate", 18)):
    OFF[_n] = _o
    _o += _w


def _swap_cols(cols):
    cols = np.asarray(cols).reshape(-1, 64)
    return np.concatenate([cols[:, 32:], cols[:, :32]], axis=1).reshape(-1)


def _fm_cols():
    ch = []
    for c in range(2):
        ch.append(np.arange(OFF["v_pool"] + 128 * c, OFF["v_pool"] + 128 * (c + 1)))
    for name in ("q_ret", "k_ret", "q_nsa"):
        for c in range(3):
            cols = np.arange(OFF[name] + 128 * c, OFF[name] + 128 * (c + 1))
            ch.append(cols)
            ch.append(_swap_cols(cols))
    for name in ("k_slc", "k_win"):
        cols = np.arange(OFF[name], OFF[name] + 128)
        ch.append(cols)
        ch.append(_swap_cols(cols))
    for name in ("k_cmp", "v_cmp"):
        ch.append(np.arange(OFF[name], OFF[name] + 128))
    return ch


FM_COLS = _fm_cols()
NFM = len(FM_COLS)
TM_COLS = np.concatenate([np.arange(OFF["v_ret"], OFF["v_ret"] + 384),
                          np.arange(OFF["v_slc"], OFF["v_slc"] + 128),
                          np.arange(OFF["v_win"], OFF["v_win"] + 128),
                          np.arange(OFF["g_ret"], OFF["g_ret"] + 384),
                          np.arange(OFF["gate"], OFF["gate"] + 18)])
NTM = len(TM_COLS)


def _const_tables():
    c = {}
    p = np.arange(128)
    inv = (10000.0 ** (-np.arange(0, 64, 2, dtype=np.float32) / 64)).astype(np.float32)
    c["c_inv"] = inv[p % 32].reshape(128, 1).astype(np.float32)
    c["c_sgn"] = np.where((p % 64) < 32, -1.0, 1.0).reshape(128, 1).astype(np.float32)
    h = np.arange(6, dtype=np.float64)
    lg = np.log1p(-np.power(2.0, -5.0 - h))
    i = np.arange(128, dtype=np.float64)
    dm = np.zeros((128, 6, 128), np.float32)
    for hh in range(6):
        diff = i[None, :] - i[:, None]
        dm[:, hh, :] = np.where(diff >= 0, 0.125 * np.exp(lg[hh] * np.maximum(diff, 0)), 0.0)
    c["c_dm"] = dm
    xi = np.exp(lg[:, None] * (i[None, :] + 1.0))
    zeta = 0.125 * np.exp(lg[:, None] * (127.0 - i[None, :]))
    gam = np.exp(lg * 128.0)
    xir = np.zeros((128, 3, 128), np.float32)
    zt = np.zeros((128, 3, 128), np.float32)
    gc = np.zeros((128, 3), np.float32)
    for ck in range(3):
        for hh in range(2):
            xir[hh * 64:(hh + 1) * 64, ck, :] = xi[2 * ck + hh][None, :]
            zt[:, ck, hh * 64:(hh + 1) * 64] = zeta[2 * ck + hh][:, None]
            gc[hh * 64:(hh + 1) * 64, ck] = gam[2 * ck + hh]
    c["c_xir"] = xir
    c["c_zt"] = zt
    c["c_gc"] = gc
    win = np.zeros((128, 2), np.float32)
    rc = np.zeros((128, 2, 16), np.float32)
    for ck in range(2):
        for hh in range(2):
            w = (2, 4, 8, 16)[2 * ck + hh]
            win[hh * 64:(hh + 1) * 64, ck] = 1.0 / w
            rc[hh * 64:(hh + 1) * 64, ck, :] = 1.0 / np.minimum(np.arange(16) + 1, w)
    c["c_pw"] = win
    c["c_prc"] = rc
    kk = np.arange(128)[:, None]
    qq = np.arange(128)[None, :]
    c["c_caus"] = np.where(kk > qq, NEG, 0.0).astype(np.float32)
    c["c_upper"] = np.where(kk <= qq, NEG, 0.0).astype(np.float32)
    c["c_ident"] = np.eye(128, dtype=np.float32)
    ex = np.zeros((64, T), np.float32)
    ex[np.arange(T) // 64, np.arange(T)] = 1.0
    c["c_expand"] = ex
    n = np.arange(256)
    ends = 16 * n + 31
    cm = np.where(ends[:, None] > np.arange(T)[None, :], NEG, 0.0).astype(np.float32)
    c["c_cmn"] = np.ascontiguousarray(cm.reshape(2, 128, T).transpose(1, 0, 2))
    ci = np.arange(256)[:, None]
    sj = np.arange(64)[None, :]
    ov = np.clip(np.minimum(ci * 16 + 32, (sj + 1) * 64) - np.maximum(ci * 16, sj * 64), 0, None) / 16.0
    ov[255, :] = 0.0
    c["c_ovl"] = np.ascontiguousarray(ov.astype(np.float32).reshape(2, 128, 64).transpose(1, 0, 2))
    tq = np.arange(T)
    cur = tq // 64
    blk = np.arange(64)[None, :]
    forced = (blk == 0) | (blk == cur[:, None]) | (blk == cur[:, None] - 1)
    valid = blk * 64 <= tq[:, None]
    bias = np.where(valid, np.where(forced, 1e6, 0.0), -100.0).astype(np.float32)
    c["c_sbias"] = np.ascontiguousarray(bias.reshape(32, 128, 64).transpose(1, 0, 2))
    return c


CONSTS = _const_tables()


def _layer_arrays(inp, l):
    a = {}
    w_in = inp["w_in"][l]
    wk = w_in.reshape(8, 128, -1)
    a["wfm"] = np.ascontiguousarray(
        np.stack([wk[:, :, cols].transpose(1, 0, 2) for cols in FM_COLS], 0))
    a["wtm"] = np.ascontiguousarray(wk[:, :, TM_COLS].transpose(1, 0, 2))
    a["wout"] = np.ascontiguousarray(inp["w_out"][l].reshape(8, 128, D).transpose(1, 0, 2))
    pw = inp["pool_w"][l]
    bd = np.zeros((2, 128, 128), np.float32)
    for ck in range(2):
        for hh in range(2):
            bd[ck, hh * 64:(hh + 1) * 64, hh * 64:(hh + 1) * 64] = pw[2 * ck + hh]
    a["bd"] = bd
    a["psc"] = np.ascontiguousarray(inp["pool_scale"][l].reshape(2, 128).T)
    a["gng"] = np.ascontiguousarray(inp["ret_gn_g"][l].reshape(1, 384))
    for kv in ("k", "v"):
        w1 = inp[f"cmp_w1_{kv}"][l].reshape(32, 64, 128)
        w1d = np.concatenate([w1, w1], axis=1).transpose(1, 0, 2)
        a[f"w1{kv}"] = np.ascontiguousarray(w1d)
        a[f"b1{kv}"] = np.ascontiguousarray(inp[f"cmp_b1_{kv}"][l].reshape(128, 1))
        a[f"pos{kv}"] = np.ascontiguousarray(inp[f"cmp_pos_{kv}"][l].T)
        a[f"w2{kv}"] = np.ascontiguousarray(inp[f"cmp_w2_{kv}"][l])
    a["w2ks"] = np.ascontiguousarray(inp["cmp_w2_k"][l][:, _swap_cols(np.arange(64))])
    a["wg"] = np.ascontiguousarray(inp["ffn_w_gate"][l].reshape(8, 128, NF, 128).transpose(2, 1, 0, 3))
    a["wu"] = np.ascontiguousarray(inp["ffn_w_up"][l].reshape(8, 128, NF, 128).transpose(2, 1, 0, 3))
    a["wd"] = np.ascontiguousarray(inp["ffn_w_down"][l].reshape(NF, 128, D).transpose(1, 0, 2))
    a["cw"] = np.ascontiguousarray(inp["ffn_conv_w"][l].reshape(3, NF, 128).transpose(2, 1, 0))
    a["cb"] = np.ascontiguousarray(inp["ffn_conv_b"][l].reshape(NF, 128).T)
    for nme in ("ln1_g", "ln1_b", "ln2_g", "ln2_b"):
        a[nme] = np.ascontiguousarray(inp[nme][l].reshape(1, D))
    return a


LAYER_SHAPES = {
    "wfm": [NFM, 128, 8, 128], "wtm": [128, 8, NTM], "wout": [128, 8, D], "bd": [2, 128, 128],
    "psc": [128, 2], "gng": [1, 384],
    "w1k": [128, 32, 128], "b1k": [128, 1], "posk": [64, 32], "w2k": [128, 64],
    "w1v": [128, 32, 128], "b1v": [128, 1], "posv": [64, 32], "w2v": [128, 64], "w2ks": [128, 64],
    "wg": [NF, 128, 8, 128], "wu": [NF, 128, 8, 128], "wd": [128, NF, D], "cw": [128, NF, 3], "cb": [128, NF],
    "ln1_g": [1, D], "ln1_b": [1, D], "ln2_g": [1, D], "ln2_b": [1, D],
}


class Tile:
    __slots__ = ("t", "b")

    def __init__(self, t, b):
        self.t = t
        self.b = b


class Ctx:
    def __init__(self, nc, ext_in=(), ext_out=()):
        self.nc = nc
        self.P = Prog(nc)
        self.ext_in = set(ext_in)
        self.ext_out = set(ext_out)
        self.dram = {}
        self.stack = None
        self.uid = 0

    def dr(self, name, shape, dt, kind=None):
        if kind is None:
            kind = "ExternalInput" if name in self.ext_in else ("ExternalOutput" if name in self.ext_out else "Internal")
        t = self.nc.dram_tensor(name, list(shape), dt, kind=kind).ap()
        tl = Tile(t, Buf(name, is_dram=True))
        self.dram[name] = tl
        return tl

    def sb(self, name, shape, dt, es=None):
        self.uid += 1
        t = (es or self.stack).enter_context(self.nc.sbuf_tensor(f"{name}_{self.uid}", list(shape), dt))
        return Tile(t, Buf(name))

    def ps(self, name, shape, dt, es=None):
        self.uid += 1
        t = (es or self.stack).enter_context(self.nc.psum_tensor(f"{name}_{self.uid}", list(shape), dt))
        return Tile(t, Buf(name, excl=True))

    @contextlib.contextmanager
    def phase(self):
        old = self.stack
        with contextlib.ExitStack() as es:
            self.stack = es
            yield es
            self.P.barrier()
        self.stack = old

    def dma(self, eng, out, in_, reads, writes):
        self.P.dma(eng, out, in_, [x.b for x in reads], [x.b for x in writes])

    def op(self, eng, fn, reads, writes):
        self.P.op(eng, fn, [x.b for x in reads], [x.b for x in writes])

    def mm(self, out, lhsT, rhs, start, stop, reads, writes, skip=False):
        kw = dict(start=start, stop=stop)
        if skip:
            kw["skip_group_check"] = True
        self.op("pe", lambda e: e.matmul(out, lhsT=lhsT, rhs=rhs, **kw), reads, writes)

    def tr(self, out, in_, ident, reads, writes):
        self.op("pe", lambda e: e.transpose(out, in_, ident), reads, writes)

    def copy(self, eng, out, in_, reads, writes):
        if eng == "act":
            self.op("act", lambda e: e.copy(out=out, in_=in_), reads, writes)
        else:
            self.op(eng, lambda e: e.tensor_copy(out=out, in_=in_), reads, writes)

    def act(self, out, in_, func, reads, writes, **kw):
        self.op("act", lambda e: e.activation(out=out, in_=in_, func=func, **kw), reads, writes)

    def tt(self, eng, out, in0, in1, op, reads, writes):
        self.op(eng, lambda e: e.tensor_tensor(out=out, in0=in0, in1=in1, op=op), reads, writes)

    def ts(self, eng, out, in0, s1, op0, reads, writes, s2=None, op1=None):
        if op1 is None:
            self.op(eng, lambda e: e.tensor_scalar(out=out, in0=in0, scalar1=s1, scalar2=None, op0=op0), reads, writes)
        else:
            self.op(eng, lambda e: e.tensor_scalar(out=out, in0=in0, scalar1=s1, scalar2=s2, op0=op0, op1=op1), reads, writes)

    def stt(self, out, in0, scalar, in1, op0, op1, reads, writes):
        self.op("dve", lambda e: e.scalar_tensor_tensor(out=out, in0=in0, scalar=scalar, in1=in1, op0=op0, op1=op1),
                reads, writes)


def load_cast(cx, name, src_tile, shape, es=None, eng=None, q=None):
    b = cx.sb(name + "_b", shape, BF16, es)
    cx.dma("pool", b.t[:], src_tile.t, [src_tile], [b])
    return b


def load_f32(cx, name, src_ap, src_tile, shape, es=None, q="sp"):
    f = cx.sb(name, shape, F32, es)
    cx.dma(q, f.t[:], src_ap, [src_tile], [f])
    return f


def build_xT(cx, xd, xT, ident, ntiles, tok0=0):
    with contextlib.ExitStack() as es:
        xb = [cx.sb(f"xb{i}", [128, D], BF16, es) for i in range(3)]
        pt = [cx.ps(f"xpt{i}", [128, 8, 128], BF16, es) for i in range(2)]
        for tt in range(ntiles):
            bb, p = xb[tt % 3], pt[tt % 2]
            r0 = tok0 + tt * 128
            cx.dma("pool", bb.t[:], xd.t[r0:r0 + 128, :], [xd], [bb])
            for k in range(8):
                cx.tr(p.t[:, k, :], bb.t[:, k * 128:(k + 1) * 128], ident.t[:], [bb, ident], [p])
            cx.copy("act" if tt % 2 == 0 else "dve", xT.t[:, :, tt * 128:(tt + 1) * 128], p.t[:], [p], [xT])
        cx.P.barrier()


def layer_norm_store(cx, zps, xres, g_t, b_t, outd, r0, tmp, eng_q="pool"):
    z, st, mv, rs, o = tmp
    for hf in range(2):
        cx.stt(z.t[:, hf * 512:(hf + 1) * 512], xres.t[:, hf * 512:(hf + 1) * 512], ALPHA, zps[hf].t[:],
               ALU.mult, ALU.add, [xres, zps[hf]], [z])
    for hf in range(2):
        cx.op("dve", lambda e, hf=hf: e.bn_stats(out=st.t[:, hf, :], in_=z.t[:, hf * 512:(hf + 1) * 512]), [z], [st])
    cx.op("dve", lambda e: e.bn_aggr(out=mv.t[:], in_=st.t[:]), [st], [mv])
    cx.ts("dve", rs.t[:, 0:1], mv.t[:, 1:2], 1e-5, ALU.add, [mv], [rs])
    cx.act(rs.t[:, 0:1], rs.t[:, 0:1], AF.Sqrt, [rs], [rs])
    cx.op("dve", lambda e: e.reciprocal(out=rs.t[:, 0:1], in_=rs.t[:, 0:1]), [rs], [rs])
    cx.stt(rs.t[:, 1:2], mv.t[:, 0:1], -1.0, rs.t[:, 0:1], ALU.mult, ALU.mult, [mv, rs], [rs])
    cx.act(o.t[:], z.t[:], AF.Identity, [z, rs], [o], scale=rs.t[:, 0:1], bias=rs.t[:, 1:2])
    cx.tt("dve", o.t[:], o.t[:], g_t.t[:], ALU.mult, [o, g_t], [o])
    cx.tt("pool", o.t[:], o.t[:], b_t.t[:], ALU.add, [o, b_t], [o])
    cx.dma(eng_q, outd.t[r0:r0 + 128, :], o.t[:], [o], [outd])


def ln_stages(cx, zps, xres, g_t, b_t, outd, r0, tmp, eng_q="pool"):
    z, st, mv, rs, o = tmp

    def s1():
        for hf in range(2):
            cx.stt(z.t[:, hf * 512:(hf + 1) * 512], xres.t[:, hf * 512:(hf + 1) * 512], ALPHA, zps[hf].t[:],
                   ALU.mult, ALU.add, [xres, zps[hf]], [z])
        for hf in range(2):
            cx.op("dve", lambda e, hf=hf: e.bn_stats(out=st.t[:, hf, :], in_=z.t[:, hf * 512:(hf + 1) * 512]), [z], [st])
        cx.op("dve", lambda e: e.bn_aggr(out=mv.t[:], in_=st.t[:]), [st], [mv])
        cx.ts("dve", rs.t[:, 0:1], mv.t[:, 1:2], 1e-5, ALU.add, [mv], [rs])

    def s2():
        cx.act(rs.t[:, 0:1], rs.t[:, 0:1], AF.Sqrt, [rs], [rs])

    def s3():
        cx.op("dve", lambda e: e.reciprocal(out=rs.t[:, 0:1], in_=rs.t[:, 0:1]), [rs], [rs])
        cx.stt(rs.t[:, 1:2], mv.t[:, 0:1], -1.0, rs.t[:, 0:1], ALU.mult, ALU.mult, [mv, rs], [rs])

    def s4():
        cx.act(o.t[:], z.t[:], AF.Identity, [z, rs], [o], scale=rs.t[:, 0:1], bias=rs.t[:, 1:2])

    def s5():
        cx.tt("dve", o.t[:], o.t[:], g_t.t[:], ALU.mult, [o, g_t], [o])
        cx.tt("pool", o.t[:], o.t[:], b_t.t[:], ALU.add, [o, b_t], [o])
        cx.dma(eng_q, outd.t[r0:r0 + 128, :], o.t[:], [o], [outd])
    return s1, s2, s3, s4, s5


def ln_tmps(cx, es, n=2):
    res = []
    for i in range(n):
        res.append((cx.sb(f"lnz{i}", [128, D], F32, es), cx.sb(f"lnst{i}", [128, 2, 6], F32, es),
                    cx.sb(f"lnmv{i}", [128, 2], F32, es), cx.sb(f"lnrs{i}", [128, 2], F32, es),
                    cx.sb(f"lno{i}", [128, D], F32, es)))
    return res


def prologue_rope(cx, posd, G):
    rope = G["rope"]
    with cx.phase() as es:
        pi_ = cx.sb("posi", [128, T], I32)
        ang = cx.sb("ang", [128, T], F32)
        m = cx.sb("rm", [128, T], F32)
        o = cx.sb("ro", [128, T], F32)
        cx.dma("sp", pi_.t[:], posd.t.broadcast_to([128, T]), [posd], [pi_])
        cx.copy("dve", ang.t[:], pi_.t[:], [pi_], [ang])
        cx.ts("dve", ang.t[:], ang.t[:], G["inv"].t[:, 0:1], ALU.mult, [ang, G["inv"]], [ang])
        ki = cx.sb("rki", [128, T], I32)
        C1 = 6.28125
        C2 = 2.0 * math.pi - 6.28125
        for which, shift in ((0, 0.25), (1, 0.0)):
            cx.ts("dve", m.t[:], ang.t[:], 1.0 / (2.0 * math.pi), ALU.mult, [ang], [m], s2=shift, op1=ALU.add)
            cx.copy("dve", ki.t[:], m.t[:], [m], [ki])
            cx.copy("dve", m.t[:], ki.t[:], [ki], [m])
            cx.stt(o.t[:], m.t[:], -C1, ang.t[:], ALU.mult, ALU.add, [m, ang], [o])
            cx.stt(o.t[:], m.t[:], -C2, o.t[:], ALU.mult, ALU.add, [m, o], [o])
            if which == 0:
                cx.ts("dve", o.t[:], o.t[:], 0.5 * math.pi, ALU.add, [o], [o])
            cx.ts("dve", o.t[:], o.t[:], math.pi, ALU.min, [o], [o], s2=-math.pi, op1=ALU.max)
            cx.act(o.t[:], o.t[:], AF.Sin, [o], [o])
            if which == 1:
                cx.ts("dve", o.t[:], o.t[:], G["sgn"].t[:, 0:1], ALU.mult, [o, G["sgn"]], [o])
            cx.dma("sp", rope.t[which], o.t[:], [o], [rope])


def phase_A(cx, l, xd, W, G, S):
    ident = G["ident"]
    rope = G["rope"]
    with cx.phase():
        xT = cx.sb("xT", [128, 8, T], BF16)
        build_xT(cx, xd, xT, ident, NT)
        with cx.phase() as es:
          if DBG.get("tm", True):
              wtm = load_cast(cx, "wtm", W["wtm"], [128, 8, NTM])
              pss = [cx.ps(f"tmps{i}", [128, 512], F32) for i in range(6)]
              ob = [cx.sb(f"tmob{i}", [128, 640], BF16) for i in range(2)]
              og = [cx.sb(f"tmog{i}", [128, 384], F32) for i in range(2)]
              ogt = [cx.sb(f"tmogt{i}", [128, 18], F32) for i in range(2)]
              for tt in range(NT):
                  p0, p1, p2 = pss[(tt % 2) * 3:(tt % 2) * 3 + 3]
                  for (pp, c0, c1) in ((p0, 0, 512), (p1, 512, 1024), (p2, 1024, NTM)):
                      for k in range(8):
                          cx.mm(pp.t[:, 0:c1 - c0], xT.t[:, k, tt * 128:(tt + 1) * 128], wtm.t[:, k, c0:c1],
                                k == 0, k == 7, [xT, wtm], [pp])
                  b_, g_, t_ = ob[tt % 2], og[tt % 2], ogt[tt % 2]
                  cx.copy("dve", b_.t[:, 0:512], p0.t[:], [p0], [b_])
                  cx.copy("dve", b_.t[:, 512:640], p1.t[:, 0:128], [p1], [b_])
                  cx.act(g_.t[:], p1.t[:, 128:512], AF.Silu, [p1], [g_])
                  cx.act(t_.t[:], p2.t[:, 0:18], AF.Sigmoid, [p2], [t_])
                  r0 = tt * 128
                  cx.dma("sp", S["tmb"].t[r0:r0 + 128, :], b_.t[:], [b_], [S["tmb"]])
                  cx.dma("sp", S["gr"].t[r0:r0 + 128, :], g_.t[:], [g_], [S["gr"]])
                  cx.dma("sp", S["gt"].t[r0:r0 + 128, :], t_.t[:], [t_], [S["gt"]])
        with cx.phase() as es:
          if DBG.get("fm", True):
              C = cx.sb("ropeC", [128, T], F32)
              Sn = cx.sb("ropeS", [128, T], F32)
              cx.dma("sp", C.t[:], rope.t[0], [rope], [C])
              cx.dma("sp", Sn.t[:], rope.t[1], [rope], [Sn])
              wb = [cx.sb(f"wb{i}", [128, 2, 8, 128], BF16) for i in range(2)]
              pss = [cx.ps(f"fmps{i}", [128, 512], F32) for i in range(8)]
              ost = [cx.sb(f"fmo{i}", [128, T], BF16) for i in range(2)]
              vst = [cx.sb(f"fmv{i}", [128, 512], F32) for i in range(2)]
              t1s = [cx.sb(f"fmt1{i}", [128, 512], F32) for i in range(2)]
              t2s = [cx.sb(f"fmt2{i}", [128, 512], F32) for i in range(2)]
              units = [([0], S["vp"], 0, False), ([1], S["vp"], 128, False)]
              ci = 2
              for dest in ("qr", "kr", "qn"):
                  for c in range(3):
                      units.append(([ci, ci + 1], S[dest], 128 * c, True))
                      ci += 2
              for dest in ("ks", "kw"):
                  units.append(([ci, ci + 1], S[dest], 0, True))
                  ci += 2
              units.append(([ci], S["kc"], 0, False))
              units.append(([ci + 1], S["vc"], 0, False))
              gi = 0

              def load_w(ui):
                  for j, cid in enumerate(units[ui][0]):
                      cx.dma("pool", wb[ui % 2].t[:, j], W["wfm"].t[cid], [W["wfm"]], [wb[ui % 2]])
              load_w(0)
              for ui, (cids, dest, row0, is_rope) in enumerate(units):
                  b_ = wb[ui % 2]
                  n = len(cids)
                  if ui + 1 < len(units):
                      load_w(ui + 1)
                  o_ = ost[ui % 2]
                  for tc in range(8):
                      ts_ = slice(tc * 512, (tc + 1) * 512)
                      pp = [pss[(gi % 4) * 2 + j] for j in range(n)]
                      gi += 1
                      for j in range(n):
                          for k in range(8):
                              cx.mm(pp[j].t[:], b_.t[:, j, k, :], xT.t[:, k, ts_], k == 0, k == 7, [b_, xT], [pp[j]])
                      if is_rope:
                          t1, t2 = t1s[tc % 2], t2s[tc % 2]
                          cx.tt("dve", t1.t[:], pp[0].t[:], C.t[:, ts_], ALU.mult, [pp[0], C], [t1])
                          cx.tt("dve", t2.t[:], pp[1].t[:], Sn.t[:, ts_], ALU.mult, [pp[1], Sn], [t2])
                          cx.tt("pool", o_.t[:, ts_], t1.t[:], t2.t[:], ALU.add, [t1, t2], [o_])
                      elif dest is S["vp"]:
                          v_ = vst[tc % 2]
                          cx.copy("act", v_.t[:], pp[0].t[:], [pp[0]], [v_])
                          cx.dma("sp", dest.t[row0:row0 + 128, ts_], v_.t[:], [v_], [dest])
                      else:
                          cx.copy("act", o_.t[:, ts_], pp[0].t[:], [pp[0]], [o_])
                  if dest is not S["vp"]:
                      cx.dma("sp", dest.t[row0:row0 + 128, :], o_.t[:], [o_], [dest])


def phase_B(cx, l, W, G, S):
    with cx.phase():
        psc = load_f32(cx, "psc", W["psc"].t, W["psc"], [128, 2])
        pw = G["pw"]
        prc = G["prc"]
        pss = [cx.ps(f"bps{i}", [128, 512], F32) for i in range(4)]
        for ck in range(2):
            bd = load_cast(cx, f"bd{ck}", Tile(W["bd"].t[ck], W["bd"].b), [128, 128])
            v = cx.sb(f"pv{ck}", [128, 16 + T], F32)
            s2 = cx.sb(f"ps2{ck}", [128, 16 + T], F32)
            s4 = cx.sb(f"ps4{ck}", [128, 16 + T], F32)
            mx = cx.sb(f"pmx{ck}", [128, T], BF16)
            o = cx.sb(f"pbo{ck}", [128, T], BF16)
            for t_ in (v, s2, s4):
                cx.op("pool", lambda e, t_=t_: e.memset(t_.t[:, 0:16], 0.0), [], [t_])
            cx.dma("sp", v.t[:, 16:], S["vp"].t[ck * 128:(ck + 1) * 128, :], [S["vp"]], [v])
            if ck == 0:
                cx.tt("dve", s2.t[:, 16:], v.t[:, 16:], v.t[:, 15:15 + T], ALU.add, [v], [s2])
                cx.tt("dve", s4.t[64:128, 16:], s2.t[64:128, 16:], s2.t[64:128, 14:14 + T], ALU.add, [s2], [s4])
                lo, hi = s2, s4
            else:
                cx.tt("dve", s2.t[:, 16:], v.t[:, 16:], v.t[:, 15:15 + T], ALU.add, [v], [s2])
                cx.tt("dve", s4.t[:, 16:], s2.t[:, 16:], s2.t[:, 14:14 + T], ALU.add, [s2], [s4])
                cx.tt("dve", s2.t[:, 16:], s4.t[:, 16:], s4.t[:, 12:12 + T], ALU.add, [s4], [s2])
                cx.tt("dve", s4.t[64:128, 16:], s2.t[64:128, 16:], s2.t[64:128, 8:8 + T], ALU.add, [s2], [s4])
                lo, hi = s2, s4
            for (src, r) in ((lo, slice(0, 64)), (hi, slice(64, 128))):
                cx.stt(mx.t[r, :], src.t[r, 16:], pw.t[r, ck:ck + 1], v.t[r, 16:], ALU.mult, ALU.subtract,
                       [src, pw, v], [mx])
                cx.tt("dve", src.t[r, 0:16], src.t[r, 16:32], prc.t[r, ck, :], ALU.mult, [src, prc, mx], [src])
                cx.tt("dve", mx.t[r, 0:16], src.t[r, 0:16], v.t[r, 16:32], ALU.subtract, [src, v], [mx])
            for tc in range(8):
                ts_ = slice(tc * 512, (tc + 1) * 512)
                pp = pss[tc % 4]
                cx.mm(pp.t[:], bd.t[:], mx.t[:, ts_], True, True, [bd, mx], [pp])
                cx.act(o.t[:, ts_], pp.t[:], AF.Copy, [pp, psc], [o], scale=psc.t[:, ck:ck + 1])
            cx.dma("sp", S["yt"].t[ck * 128:(ck + 1) * 128, :], o.t[:], [o], [S["yt"]])


def phase_C(cx, l, W, G, S):
    Cd = G["C"]
    ident = G["ident"]
    with cx.phase() as es:
        dm = load_f32(cx, "dm", Cd["c_dm"].t, Cd["c_dm"], [128, 6, 128])
        xir = load_f32(cx, "xir", Cd["c_xir"].t, Cd["c_xir"], [128, 3, 128])
        zt = load_f32(cx, "zt", Cd["c_zt"].t, Cd["c_zt"], [128, 3, 128])
        gc = load_f32(cx, "gc", Cd["c_gc"].t, Cd["c_gc"], [128, 3])
        gng = load_f32(cx, "gng", W["gng"].t.broadcast_to([128, 384]), W["gng"], [128, 384])
        bankA = [cx.ps(f"cA{i}", [128, 512], F32) for i in range(2)]
        bankO = [cx.ps(f"cO{i}", [128, 512], F32) for i in range(2)]
        bankV = [cx.ps(f"cV{i}", [128, 512], F32) for i in range(2)]
        bankY = cx.ps("cY", [128, 8, 128], BF16)
        bankK = cx.ps("cK", [128, 8, 128], BF16)
        ktp = [bankK.t[:, i, :] for i in range(3)]
        opv2 = [[b.t[:, ck * 128:(ck + 1) * 128] for ck in range(3)] for b in bankO]
        kvp2 = [[b.t[:, ck * 128:(ck + 1) * 128] for ck in range(3)] for b in bankV]
        ytp = [bankY.t[:, i, :] for i in range(3)]
        qT, kT, qx, v, vz, R, Rb = [], [], [], [], [], [], []
        for ck in range(3):
            rows = slice(ck * 128, (ck + 1) * 128)
            qT.append(cx.sb(f"cqT{ck}", [128, T], BF16))
            kT.append(cx.sb(f"ckT{ck}", [128, T], BF16))
            qx.append(cx.sb(f"cqx{ck}", [128, T], BF16))
            v.append(cx.sb(f"cv{ck}", [128, NT, 128], BF16))
            vz.append(cx.sb(f"cvz{ck}", [128, NT, 128], BF16))
            R.append(cx.sb(f"cR{ck}", [128, 64], F32))
            Rb.append(cx.sb(f"cRb{ck}", [128, 64], BF16))
            cx.dma("sp", qT[ck].t[:], S["qr"].t[rows, :], [S["qr"]], [qT[ck]])
            cx.dma("sp", kT[ck].t[:], S["kr"].t[rows, :], [S["kr"]], [kT[ck]])
            cx.dma("pool", v[ck].t[:], S["tmb"].t[:, ck * 128:(ck + 1) * 128].rearrange("(n p) c -> p n c", p=128),
                   [S["tmb"]], [v[ck]])
            cx.tt("pool", qx[ck].t[:].rearrange("p (n i) -> p n i", i=128), qT[ck].t[:].rearrange("p (n i) -> p n i", i=128),
                  xir.t[:, ck:ck + 1, :].broadcast_to([128, NT, 128]), ALU.mult, [qT[ck], xir], [qx[ck]])
            cx.tt("pool", vz[ck].t[:], v[ck].t[:], zt.t[:, ck:ck + 1, :].broadcast_to([128, NT, 128]), ALU.mult,
                  [v[ck], zt], [vz[ck]])
            cx.op("pool", lambda e, ck=ck: e.memset(R[ck].t[:], 0.0), [], [R[ck]])
            cx.op("pool", lambda e, ck=ck: e.memset(Rb[ck].t[:], 0.0), [], [Rb[ck]])
        NB = 2
        sgs = [[cx.sb(f"csg{ck}_{i}", [128, 8, 128], F32) for i in range(2)] for ck in range(3)]
        kts = [[cx.sb(f"ckt{ck}_{i}", [128, 128], BF16) for i in range(NB)] for ck in range(3)]
        sms = [[cx.sb(f"csm{hh}_{i}", [128, 3, 128], BF16) for i in range(NB)] for hh in range(2)]
        sts = [[cx.sb(f"cst{ck}_{i}", [128, 2, 6], F32) for i in range(NB)] for ck in range(3)]
        mvs = [[cx.sb(f"cmv{ck}_{i}", [128, 2, 2], F32) for i in range(NB)] for ck in range(3)]
        rss = [[cx.sb(f"crs{ck}_{i}", [128, 2, 2], F32) for i in range(NB)] for ck in range(3)]
        ons = [[cx.sb(f"con{ck}_{i}", [128, 128], F32) for i in range(NB)] for ck in range(3)]
        onb = [[cx.sb(f"conb{ck}_{i}", [128, 128], BF16) for i in range(NB)] for ck in range(3)]
        yts = [[cx.sb(f"cyt{ck}_{i}", [128, 128], BF16) for i in range(NB)] for ck in range(3)]

        def load_sg(ck, blk):
            cx.dma("pool", sgs[ck][blk % 2].t[:],
                   S["gr"].t[blk * 1024:(blk + 1) * 1024, ck * 128:(ck + 1) * 128].rearrange("(n p) c -> p n c", p=128),
                   [S["gr"]], [sgs[ck][blk % 2]])
        for ck in range(3):
            load_sg(ck, 0)
        def ctx(n):
            return slice(n * 128, (n + 1) * 128), n % NB, opv2[n % 2], kvp2[n % 2], bankO[n % 2], bankV[n % 2]

        def st_A(n):
            ns, i, opv, kvp, BO, BV = ctx(n)
            for ck in range(3):
                cx.tr(ktp[ck], kT[ck].t[:, ns], ident.t[:], [kT[ck], ident], [bankK])
                for hh in range(2):
                    r = slice(hh * 64, (hh + 1) * 64)
                    cx.mm(bankA[hh].t[:, ck * 128:(ck + 1) * 128], kT[ck].t[r, ns], qT[ck].t[r, ns], True, True,
                          [kT[ck], qT[ck]], [bankA[hh]])

        def st_1(n):
            ns, i, opv, kvp, BO, BV = ctx(n)
            for ck in range(3):
                cx.copy("act", kts[ck][i].t[:], ktp[ck], [bankK], [kts[ck][i]])
            for hh in range(2):
                cx.tt("dve", sms[hh][i].t[:], bankA[hh].t[:, 0:384].rearrange("p (a b) -> p a b", b=128),
                      dm.t[:, hh::2, :], ALU.mult, [bankA[hh], dm], [sms[hh][i]])

        def st_B(n):
            ns, i, opv, kvp, BO, BV = ctx(n)
            for ck in range(3):
                for hh in range(2):
                    r = slice(hh * 64, (hh + 1) * 64)
                    cx.mm(opv[ck][:, r], sms[hh][i].t[:, ck, :], v[ck].t[:, n, r], True, False,
                          [sms[hh][i], v[ck]], [BO])
                    cx.mm(opv[ck][:, r], qx[ck].t[r, ns], Rb[ck].t[r, :], False, True, [qx[ck], Rb[ck]], [BO])
                cx.mm(kvp[ck], kts[ck][i].t[:], vz[ck].t[:, n, :], True, True, [kts[ck][i], vz[ck]], [BV])

        def st_3a(n):
            ns, i, opv, kvp, BO, BV = ctx(n)
            for ck in range(3):
                for hh in range(2):
                    r = slice(hh * 64, (hh + 1) * 64)
                    cx.stt(R[ck].t[r, :], R[ck].t[r, :], gc.t[r, ck:ck + 1], kvp[ck][r, r], ALU.mult, ALU.add,
                           [R[ck], gc, BV], [R[ck]])
                cx.copy("pool", Rb[ck].t[:], R[ck].t[:], [R[ck]], [Rb[ck]])
            for ck in range(3):
                st, mv, rs = sts[ck][i], mvs[ck][i], rss[ck][i]
                for hh in range(2):
                    r = slice(hh * 64, (hh + 1) * 64)
                    cx.op("dve", lambda e, hh=hh, r=r, st=st, ck=ck, opv_=opv: e.bn_stats(out=st.t[:, hh, :], in_=opv_[ck][:, r]), [BO], [st])
                    cx.op("dve", lambda e, hh=hh, st=st, mv=mv: e.bn_aggr(out=mv.t[:, hh, :], in_=st.t[:, hh, :]), [st], [mv])
                cx.ts("dve", rs.t[:, 0, :], mv.t[:, :, 1], 1e-5, ALU.add, [mv], [rs])

        def st_sq(n):
            i = n % NB
            for ck in range(3):
                rs = rss[ck][i]
                cx.act(rs.t[:, 0, :], rs.t[:, 0, :], AF.Sqrt, [rs], [rs])

        def st_3b(n):
            i = n % NB
            for ck in range(3):
                rs, mv = rss[ck][i], mvs[ck][i]
                cx.op("dve", lambda e, rs=rs: e.reciprocal(out=rs.t[:, 0, :], in_=rs.t[:, 0, :]), [rs], [rs])
                cx.stt(rs.t[:, 1, :], mv.t[:, :, 0], -1.0, rs.t[:, 0, :], ALU.mult, ALU.mult, [mv, rs], [rs])

        def st_4(n):
            ns, i, opv, kvp, BO, BV = ctx(n)
            for ck in range(3):
                rs, on, ob = rss[ck][i], ons[ck][i], onb[ck][i]
                for hh in range(2):
                    r = slice(hh * 64, (hh + 1) * 64)
                    cx.act(on.t[:, r], opv[ck][:, r], AF.Identity, [BO, rs], [on], scale=rs.t[:, 0, hh:hh + 1],
                           bias=rs.t[:, 1, hh:hh + 1])
                cx.tt("pool", on.t[:], on.t[:], gng.t[:, ck * 128:(ck + 1) * 128], ALU.mult, [on, gng], [on])
                cx.tt("pool", ob.t[:], on.t[:], sgs[ck][(n // 8) % 2].t[:, n % 8, :], ALU.mult, [on, sgs[ck][(n // 8) % 2]], [ob])

        def st_C(n):
            ns, i, opv, kvp, BO, BV = ctx(n)
            for ck in range(3):
                cx.tr(ytp[ck], onb[ck][i].t[:], ident.t[:], [onb[ck][i], ident], [bankY])
            for ck in range(3):
                cx.copy("act", yts[ck][i].t[:], ytp[ck], [bankY], [yts[ck][i]])
                cx.dma("sp", S["yt"].t[256 + ck * 128:256 + (ck + 1) * 128, ns], yts[ck][i].t[:], [yts[ck][i]], [S["yt"]])

        st_A(0)
        st_1(0)
        for n in range(NT):
            st_B(n)
            if n >= 1:
                st_C(n - 1)
            st_3a(n)
            st_sq(n)
            if n + 1 < NT:
                st_A(n + 1)
                st_1(n + 1)
            st_3b(n)
            st_4(n)
            if n % 8 == 0 and n // 8 + 1 < NT // 8:
                for ck in range(3):
                    load_sg(ck, n // 8 + 1)
        st_C(NT - 1)


def phase_D(cx, l, W, G, S):
    Cd = G["C"]
    ident = G["ident"]
    rope = G["rope"]
    with cx.phase() as es:
        KCT = cx.sb("KCT", [64, 2, 256], BF16)
        VCX = cx.sb("VCX", [128, 2, 2, 129], BF16)
        with cx.phase():
            Cc = cx.sb("dCc", [64, T], F32)
            Sc = cx.sb("dSc", [64, T], F32)
            cx.dma("sp", Cc.t[:], rope.t[0][0:64, :], [rope], [Cc])
            cx.dma("sp", Sc.t[:], rope.t[1][0:64, :], [rope], [Sc])
            ovl = load_f32(cx, "ovl", Cd["c_ovl"].t, Cd["c_ovl"], [128, 2, 64])
            hps = [cx.ps(f"dhp{i}", [128, 512], F32) for i in range(2)]
            cps = cx.ps("dcp", [128, 512], F32)
            kp = cx.ps("dkp", [128, 2, 256], F32)
            ksp = cx.ps("dksp", [128, 2, 256], F32)
            vps = [cx.ps(f"dvp{i}", [128, 512], F32) for i in range(2)]
            cx.op("pool", lambda e: e.memset(KCT.t[:], 0.0), [], [KCT])
            cx.op("pool", lambda e: e.memset(VCX.t[:, :, :, 64:65], 1.0), [], [VCX])
            for g in range(2):
                cx.copy("pool", VCX.t[:, g, :, 65:129], ovl.t[:], [ovl], [VCX])
            for kv in ("k", "v"):
                src = S["kc"] if kv == "k" else S["vc"]
                kvT = cx.sb(f"dkvT{kv}", [128, T], BF16)
                cx.dma("sp", kvT.t[:], src.t, [src], [kvT])
                w1 = load_cast(cx, f"w1{kv}", W[f"w1{kv}"], [128, 32, 128])
                pos = load_cast(cx, f"pos{kv}", W[f"pos{kv}"], [64, 32], eng="dve")
                b1 = load_f32(cx, f"b1{kv}", W[f"b1{kv}"].t, W[f"b1{kv}"], [128, 1])
                w2 = load_cast(cx, f"w2{kv}", W[f"w2{kv}"], [128, 64], eng="dve")
                cb = cx.sb(f"dcb{kv}", [128, 1], F32)
                h1 = cx.sb(f"dh1{kv}", [128, 2, 256], BF16)
                cx.op("pool", lambda e, h1=h1: e.memset(h1.t[:], 0.0), [], [h1])
                for i in range(32):
                    cx.mm(cps.t[:, 0:1], w1.t[0:64, i, :], pos.t[0:64, i:i + 1], i == 0, i == 31, [w1, pos], [cps])
                cx.tt("dve", cb.t[:], cps.t[:, 0:1], b1.t[:], ALU.add, [cps, b1], [cb])
                for g in range(2):
                    r = slice(g * 64, (g + 1) * 64)
                    for i in range(32):
                        cx.mm(hps[g].t[:, 0:255], w1.t[r, i, :], kvT.t[r, i:i + 16 * 254 + 1:16], i == 0, i == 31,
                              [w1, kvT], [hps[g]])
                    cx.act(h1.t[:, g, 0:255], hps[g].t[:, 0:255], AF.Gelu_apprx_tanh, [hps[g], cb], [h1],
                           bias=cb.t[:, 0:1])
                if kv == "k":
                    w2s = load_cast(cx, "w2ks", W["w2ks"], [128, 64], eng="dve")
                    cx.mm(kp.t[0:64], w2.t[:], h1.t[:], True, True, [w2, h1], [kp])
                    cx.mm(ksp.t[0:64], w2s.t[:], h1.t[:], True, True, [w2s, h1], [ksp])
                    t1 = cx.sb("dkt1", [64, 2, 255], F32)
                    t2 = cx.sb("dkt2", [64, 2, 255], F32)
                    cview = Cc.t[:, 31::16].unsqueeze(1).broadcast_to([64, 2, 255])
                    sview = Sc.t[:, 31::16].unsqueeze(1).broadcast_to([64, 2, 255])
                    cx.tt("dve", t1.t[:], kp.t[0:64, :, 0:255], cview, ALU.mult, [kp, Cc], [t1])
                    cx.tt("dve", t2.t[:], ksp.t[0:64, :, 0:255], sview, ALU.mult, [ksp, Sc], [t2])
                    cx.tt("dve", KCT.t[:, :, 0:255], t1.t[:], t2.t[:], ALU.add, [t1, t2], [KCT])
                else:
                    for g in range(2):
                        for nt in range(2):
                            vp_ = vps[(g * 2 + nt) % 2]
                            cx.mm(vp_.t[:, 0:64], h1.t[:, g, nt * 128:(nt + 1) * 128], w2.t[:], True, True, [h1, w2], [vp_])
                            cx.copy("act", VCX.t[:, g, nt, 0:64], vp_.t[:, 0:64], [vp_], [VCX])
        dstop = DBG.get("d_stop", 9)
        if dstop <= 1:
            return
        identb = ident
        QA = [cx.sb(f"QA{h}", [128, T], BF16) for h in range(6)]
        KSA = [cx.sb(f"KSA{g}", [128, T], BF16) for g in range(2)]
        KW = [cx.sb(f"KW{g}", [64, T], BF16) for g in range(2)]
        VSX = cx.sb("VSX", [128, NT, 2, 65], BF16)
        VWX = cx.sb("VWX", [128, NT, 2, 65], BF16)
        CMN = cx.sb("CMN", [128, 2, T], BF16)
        SB_ = load_f32(cx, "sbias", Cd["c_sbias"].t, Cd["c_sbias"], [128, NT, 64])
        GT = cx.sb("GTs", [128, NT, 18], F32)
        cx.dma("sp", GT.t[:], S["gt"].t.rearrange("(n p) c -> p n c", p=128), [S["gt"]], [GT])
        causb = cx.sb("causb", [128, 128], BF16)
        upperb = cx.sb("upperb", [128, 128], BF16)
        cx.dma("pool", causb.t[:], Cd["c_caus"].t, [Cd["c_caus"]], [causb])
        cx.dma("pool", upperb.t[:], Cd["c_upper"].t, [Cd["c_upper"]], [upperb])
        for nt in range(2):
            cx.dma("pool", CMN.t[:, nt, :], Cd["c_cmn"].t[:, nt, :], [Cd["c_cmn"]], [CMN])
        for g in range(2):
            cx.dma("pool", KSA[g].t[64:128, :], Cd["c_expand"].t, [Cd["c_expand"]], [KSA[g]])
        for h in range(6):
            cx.dma("sp", QA[h].t[0:64, :], S["qn"].t[h * 64:(h + 1) * 64, :], [S["qn"]], [QA[h]])
            cx.op("pool", lambda e, h=h: e.memset(QA[h].t[64:128, :], 0.0), [], [QA[h]])
        for g in range(2):
            cx.dma("pool", KSA[g].t[0:64, :], S["ks"].t[g * 64:(g + 1) * 64, :], [S["ks"]], [KSA[g]])
            cx.dma("sp", KW[g].t[:], S["kw"].t[g * 64:(g + 1) * 64, :], [S["kw"]], [KW[g]])
            cx.dma("pool", VSX.t[:, :, g, 0:64],
                   S["tmb"].t[:, 384 + g * 64:384 + (g + 1) * 64].rearrange("(n p) c -> p n c", p=128), [S["tmb"]], [VSX])
            cx.dma("pool", VWX.t[:, :, g, 0:64],
                   S["tmb"].t[:, 512 + g * 64:512 + (g + 1) * 64].rearrange("(n p) c -> p n c", p=128), [S["tmb"]], [VWX])
        cx.op("pool", lambda e: e.memset(VSX.t[:, :, :, 64:65], 1.0), [], [VSX])
        cx.op("pool", lambda e: e.memset(VWX.t[:, :, :, 64:65], 1.0), [], [VWX])
        if dstop <= 2:
            return
        SP = [cx.ps(f"dS{i}", [128, 512], F32) for i in range(4)]
        OP = [cx.ps(f"dO{i}", [128, 512], F32) for i in range(3)]
        tpt = cx.ps("dT", [128, 8, 128], BF16)
        _tb = Buf("dT", excl=True)
        TP = [Tile(tpt.t[:, 4 * i:4 * i + 4, :], _tb) for i in range(2)]
        pTs = [cx.sb(f"dpT{i}", [128, 512], BF16) for i in range(4)]
        OACC = [cx.sb(f"dOACC{i}", [128, 4, 384], F32) for i in range(2)]
        IMP = [cx.sb(f"dIMP{i}", [128, 4, 64], F32) for i in range(2)]
        recs = [cx.sb(f"drec{i}", [128, 2, 4], F32) for i in range(4)]
        sc1 = [cx.sb(f"dsc1{i}", [128, 64], F32) for i in range(2)]
        sc2 = [cx.sb(f"dsc2{i}", [128, 64], F32) for i in range(2)]
        m8 = [cx.sb(f"dm8{i}", [128, 16], F32) for i in range(2)]
        nm = [cx.sb(f"dnm{i}", [128, 128], BF16) for i in range(2)]
        for t_ in nm:
            cx.op("pool", lambda e, t_=t_: e.memset(t_.t[:], 0.0), [], [t_])
        obf = [cx.sb(f"dobf{i}", [128, 384], BF16) for i in range(2)]
        yst = [cx.sb(f"dyst{i}", [128, 3, 512], BF16) for i in range(2)]
        cnt = {"s": 0, "o": 0, "p": 0, "r": 0, "t": 0}

        def nxt(key, lst):
            x = lst[cnt[key] % len(lst)]
            cnt[key] += 1
            return x

        items = []

        def cmp_item(c, h):
            g, rr = divmod(h, 3)
            cs = slice(c * 512, (c + 1) * 512)
            oacc, imp = OACC[c % 2], IMP[c % 2]
            nts = [0] + ([1] if c >= 4 else [])
            st = {}

            def S_():
                st["pts"] = {}
                for nt in nts:
                    sp_ = nxt("s", SP)
                    need_mask = (c <= 4) if nt == 0 else True
                    cx.mm(sp_.t[:], KCT.t[0:64, g, nt * 128:(nt + 1) * 128], QA[h].t[0:64, cs], True, True,
                          [KCT, QA[h]], [sp_])
                    if need_mask:
                        cx.mm(sp_.t[:], identb.t[:], CMN.t[:, nt, cs], False, True, [identb, CMN], [sp_], skip=True)
                    pT = nxt("p", pTs)
                    cx.act(pT.t[:], sp_.t[:], AF.Exp, [sp_], [pT], scale=0.125)
                    st["pts"][nt] = pT

            def PV_():
                pts = st["pts"]
                for q4 in range(4):
                    qt = 4 * c + q4
                    ob_ = nxt("o", OP)
                    for j, nt in enumerate(nts):
                        cx.mm(ob_.t[:, 0:129], pts[nt].t[:, q4 * 128:(q4 + 1) * 128], VCX.t[:, g, nt, :],
                              j == 0, j == len(nts) - 1, [pts[nt], VCX], [ob_])
                    rc = nxt("r", recs)
                    cx.ts("dve", rc.t[:, 0, 0:1], ob_.t[:, 64:65], 1e-30, ALU.add, [ob_], [rc])
                    cx.op("dve", lambda e, rc=rc: e.reciprocal(out=rc.t[:, 0, 0:1], in_=rc.t[:, 0, 0:1]), [rc], [rc])
                    cx.tt("dve", rc.t[:, 1, 0:1], rc.t[:, 0, 0:1], GT.t[:, qt, 3 * h:3 * h + 1], ALU.mult, [rc, GT], [rc])
                    cx.ts("dve", oacc.t[:, q4, h * 64:(h + 1) * 64], ob_.t[:, 0:64], rc.t[:, 1, 0:1], ALU.mult,
                          [ob_, rc], [oacc])
                    if rr == 0:
                        cx.ts("dve", imp.t[:, q4, :], ob_.t[:, 65:129], rc.t[:, 0, 0:1], ALU.mult, [ob_, rc], [imp])
                    else:
                        cx.stt(imp.t[:, q4, :], ob_.t[:, 65:129], rc.t[:, 0, 0:1], imp.t[:, q4, :], ALU.mult, ALU.add,
                               [ob_, rc, imp], [imp])
                if rr == 2:
                    for q4 in range(4):
                        qt = 4 * c + q4
                        s1, s2, mm8, nm_ = sc1[q4 % 2], sc2[q4 % 2], m8[q4 % 2], nm[q4 % 2]
                        cx.tt("dve", s1.t[:], imp.t[:, q4, :], SB_.t[:, qt, :], ALU.add, [imp, SB_], [s1])
                        cx.op("dve", lambda e, mm8=mm8, s1=s1: e.max(out=mm8.t[:, 0:8], in_=s1.t[:]), [s1], [mm8])
                        cx.op("dve", lambda e, mm8=mm8, s1=s1, s2=s2: e.match_replace(
                            out=s2.t[:], in_to_replace=mm8.t[:, 0:8], in_values=s1.t[:], imm_value=-1e9), [s1, mm8], [s2])
                        cx.op("dve", lambda e, mm8=mm8, s2=s2: e.max(out=mm8.t[:, 8:16], in_=s2.t[:]), [s2], [mm8])
                        cx.ts("dve", mm8.t[:, 15:16], mm8.t[:, 15:16], 0.0, ALU.max, [mm8], [mm8])
                        cx.ts("dve", nm_.t[:, 64:128], s1.t[:], mm8.t[:, 15:16], ALU.is_lt, [s1, mm8], [nm_],
                              s2=NEG, op1=ALU.mult)
                        tp = nxt("t", TP)
                        cx.tr(tp.t[:, 0, :], nm_.t[:], ident.t[:], [nm_, ident], [tp])
                        for r3 in range(3):
                            hh = 3 * g + r3
                            cx.copy("dve",
                                    QA[hh].t[64:128, qt * 128:(qt + 1) * 128], tp.t[64:128, 0, :], [tp], [QA[hh]])
            return S_, PV_

        def att_items(c, branch, h):
            g = h // 3
            oacc = OACC[c % 2]
            kts = list(range(0, 4 * c + 4) if branch == 1 else range(max(4 * c - 4, 0), 4 * c + 4))
            shared = {"first": True}
            res = []
            for kt in kts:
                lo = max(kt - 4 * c, 0)
                hi = 3 if branch == 1 else min(kt + 4 - 4 * c, 3)
                n_ = (hi - lo + 1) * 128
                q0 = c * 512 + lo * 128
                ks_ = slice(kt * 128, (kt + 1) * 128)
                st = {}

                def S_(kt=kt, lo=lo, hi=hi, n_=n_, q0=q0, ks_=ks_, st=st):
                    sp_ = nxt("s", SP)
                    if branch == 1:
                        cx.mm(sp_.t[:, 0:n_], KSA[g].t[:, ks_], QA[h].t[:, q0:q0 + n_], True, True,
                              [KSA[g], QA[h]], [sp_])
                    else:
                        cx.mm(sp_.t[:, 0:n_], KW[g].t[0:64, ks_], QA[h].t[0:64, q0:q0 + n_], True, True,
                              [KW[g], QA[h]], [sp_])
                    if kt >= 4 * c:
                        cx.mm(sp_.t[:, 0:128], identb.t[:], causb.t[:], False, True, [identb, causb], [sp_], skip=True)
                    if branch == 2 and 4 * c <= kt + 4 <= 4 * c + 3:
                        cx.mm(sp_.t[:, n_ - 128:n_], identb.t[:], upperb.t[:], False, True, [identb, upperb], [sp_],
                              skip=True)
                    pT = nxt("p", pTs)
                    cx.act(pT.t[:, 0:n_], sp_.t[:, 0:n_], AF.Exp, [sp_], [pT], scale=0.125)
                    st["pT"] = pT

                def PV_(kt=kt, lo=lo, hi=hi, st=st, last=(kt == kts[-1])):
                    if shared["first"]:
                        shared["ob"] = nxt("o", OP)
                    ob_ = shared["ob"]
                    ov = ob_.t[:, 0:260].rearrange("p (a b) -> p a b", b=65)
                    pT = st["pT"]
                    vx = VSX if branch == 1 else VWX
                    for q4 in range(lo, hi + 1):
                        cx.mm(ov[:, q4, :], pT.t[:, (q4 - lo) * 128:(q4 - lo + 1) * 128], vx.t[:, kt, g, :],
                              shared["first"], True, [pT, vx], [ob_], skip=not shared["first"])
                        shared["first"] = False
                    if last:
                        rc = nxt("r", recs)
                        cx.ts("dve", rc.t[:, 0, :], ov[:, :, 64], 1e-30, ALU.add, [ob_], [rc])
                        cx.op("dve", lambda e, rc=rc: e.reciprocal(out=rc.t[:, 0, :], in_=rc.t[:, 0, :]), [rc], [rc])
                        col = 3 * h + branch
                        cx.tt("dve", rc.t[:, 1, :], rc.t[:, 0, :], GT.t[:, 4 * c:4 * c + 4, col], ALU.mult, [rc, GT], [rc])
                        for q4 in range(4):
                            av = oacc.t[:, q4, h * 64:(h + 1) * 64]
                            cx.stt(av, ov[:, q4, 0:64], rc.t[:, 1, q4:q4 + 1], av, ALU.mult, ALU.add,
                                   [ob_, rc, oacc], [oacc])
                res.append((S_, PV_))
            return res

        def out_item(c):
            cs = slice(c * 512, (c + 1) * 512)
            oacc = OACC[c % 2]

            def PV_():
                ys = yst[c % 2]
                for q4 in range(4):
                    ob2 = obf[q4 % 2]
                    cx.copy("pool", ob2.t[:], oacc.t[:, q4, :], [oacc], [ob2])
                    tp = nxt("t", TP)
                    for j in range(3):
                        cx.tr(tp.t[:, j, :], ob2.t[:, j * 128:(j + 1) * 128], ident.t[:], [ob2, ident], [tp])
                    cx.copy("act", ys.t[:, :, q4 * 128:(q4 + 1) * 128], tp.t[:, 0:3, :], [tp], [ys])
                for j in range(3):
                    cx.dma("sp", S["yt"].t[640 + j * 128:640 + (j + 1) * 128, cs], ys.t[:, j, :], [ys], [S["yt"]])
            return (lambda: None), PV_

        for c in range(8):
            for h in range(6):
                items.append(cmp_item(c, h))
            if dstop >= 5:
                for branch in ((1, 2) if dstop >= 6 else (1,)):
                    for h in range(6):
                        items.extend(att_items(c, branch, h))
            items.append(out_item(c))
        for i in range(len(items) + 1):
            if i < len(items):
                items[i][0]()
            if i >= 1:
                items[i - 1][1]()


def phase_E(cx, l, xd, x1d, W, G, S):
    with cx.phase() as es:
        wo = load_cast(cx, "wout", W["wout"], [128, 8, D])
        g_t = load_f32(cx, "ln1g", W["ln1_g"].t.broadcast_to([128, D]), W["ln1_g"], [128, D])
        b_t = load_f32(cx, "ln1b", W["ln1_b"].t.broadcast_to([128, D]), W["ln1_b"], [128, D])
        yts = [cx.sb(f"eyt{i}", [128, 8, 512], BF16) for i in range(2)]
        xrs = [cx.sb(f"exr{i}", [128, D], F32) for i in range(3)]
        pss = [cx.ps(f"eps{i}", [128, 512], F32) for i in range(6)]
        tmps = ln_tmps(cx, es, n=3)

        def load_y(tc):
            cx.dma("sp", yts[tc % 2].t[:], S["yt"].t[:, tc * 512:(tc + 1) * 512].rearrange("(k p) t -> p k t", p=128),
                   [S["yt"]], [yts[tc % 2]])
        stages = {}

        def mm_tile(tt):
            tc, q = divmod(tt, 4)
            if q == 0 and tc + 1 < 8:
                load_y(tc + 1)
            y_ = yts[tc % 2]
            xr = xrs[tt % 3]
            cx.dma("sp", xr.t[:], xd.t[tt * 128:(tt + 1) * 128, :], [xd], [xr])
            zp = pss[(tt % 3) * 2:(tt % 3) * 2 + 2]
            for hf in range(2):
                for k in range(8):
                    cx.mm(zp[hf].t[:], y_.t[:, k, q * 128:(q + 1) * 128], wo.t[:, k, hf * 512:(hf + 1) * 512],
                          k == 0, k == 7, [y_, wo], [zp[hf]])
            stages[tt] = ln_stages(cx, zp, xr, g_t, b_t, x1d, tt * 128, tmps[tt % 3])
        load_y(0)
        mm_tile(0)
        stages[0][0]()
        stages[0][1]()
        for tt in range(NT):
            if tt + 1 < NT:
                mm_tile(tt + 1)
                stages[tt + 1][0]()
            stages[tt][2]()
            stages[tt][3]()
            if tt + 1 < NT:
                stages[tt + 1][1]()
            if tt >= 1:
                stages[tt - 1][4]()
        stages[NT - 1][4]()


def phase_F(cx, l, x1d, x2d, W, G, S):
    TC = 1024
    ident = G["ident"]
    with cx.phase() as es:
        wd = cx.sb("wd_b", [128, NF, D], BF16)
        for f in range(NF):
            cx.dma("pool", wd.t[:, f, :], W["wd"].t[:, f, :], [W["wd"]], [wd])
        cw = load_f32(cx, "cw", W["cw"].t, W["cw"], [128, NF, 3])
        cb = load_f32(cx, "cb", W["cb"].t, W["cb"], [128, NF])
        g_t = load_f32(cx, "ln2g", W["ln2_g"].t.broadcast_to([128, D]), W["ln2_g"], [128, D])
        b_t = load_f32(cx, "ln2b", W["ln2_b"].t.broadcast_to([128, D]), W["ln2_b"], [128, D])
        carry = cx.sb("carry", [128, NF, 2], F32)
        cx.op("pool", lambda e: e.memset(carry.t[:], 0.0), [], [carry])
        xT = cx.sb("x1T", [128, 8, TC], BF16)
        act = cx.sb("ffact", [128, NF, TC], BF16)
        wb = [cx.sb(f"fwb{i}", [128, 2, 8, 128], BF16) for i in range(3)]
        hts = [cx.sb(f"fhb{i}", [128, 2 + TC], F32) for i in range(2)]
        hbA = [Tile(t.t, Buf(f"fhbA{i}")) for i, t in enumerate(hts)]
        hbB = [Tile(t.t, Buf(f"fhbB{i}")) for i, t in enumerate(hts)]
        hc = [cx.sb(f"fhc{i}", [128, 512], F32) for i in range(2)]
        gl = [cx.sb(f"fgl{i}", [128, 512], F32) for i in range(2)]
        xrs = [cx.sb(f"fxr{i}", [128, D], F32) for i in range(2)]
        tmps = ln_tmps(cx, es)
        pss = [cx.ps(f"fps{i}", [128, 512], F32) for i in range(6)]
        gi = 0
        nsteps = (T // TC) * NF

        def load_w(step):
            f = step % NF
            b_ = wb[step % 3]
            cx.dma("pool", b_.t[:, 0], W["wg"].t[f], [W["wg"]], [b_])
            cx.dma("pool", b_.t[:, 1], W["wu"].t[f], [W["wu"]], [b_])
        load_w(0)
        load_w(1)
        step = 0
        for tc in range(T // TC):
            build_xT(cx, x1d, xT, ident, TC // 128, tok0=tc * TC)
            for f in range(NF):
                b_ = wb[step % 3]
                if step + 2 < nsteps:
                    load_w(step + 2)
                step += 1
                hA, hB = hbA[f % 2], hbB[f % 2]
                ht = hts[f % 2].t
                cx.copy("act", ht[:, 0:2], carry.t[:, f, :], [carry], [hA])
                for hf in range(TC // 512):
                    ts_ = slice(hf * 512, (hf + 1) * 512)
                    pg, pu = pss[(gi % 2) * 2], pss[(gi % 2) * 2 + 1]
                    c_, g_ = hc[gi % 2], gl[gi % 2]
                    gi += 1
                    for k in range(8):
                        cx.mm(pg.t[:], b_.t[:, 0, k, :], xT.t[:, k, ts_], k == 0, k == 7, [b_, xT], [pg])
                    for k in range(8):
                        cx.mm(pu.t[:], b_.t[:, 1, k, :], xT.t[:, k, ts_], k == 0, k == 7, [b_, xT], [pu])
                    hw_ = [hA] if hf == 0 else [hB]
                    hr_ = [hA] if hf == 0 else [hA, hB]
                    o = hf * 512
                    cx.copy("act", ht[:, 2 + o:514 + o], pg.t[:], [pg], hw_)
                    cx.ts("dve", c_.t[:], ht[:, 2 + o:514 + o], cw.t[:, f, 2:3], ALU.mult, hr_ + [cw, cb], [c_],
                          s2=cb.t[:, f:f + 1], op1=ALU.add)
                    cx.stt(c_.t[:], ht[:, 1 + o:513 + o], cw.t[:, f, 1:2], c_.t[:], ALU.mult, ALU.add, hr_ + [cw, c_], [c_])
                    cx.stt(c_.t[:], ht[:, o:512 + o], cw.t[:, f, 0:1], c_.t[:], ALU.mult, ALU.add, hr_ + [cw, c_], [c_])
                    cx.act(g_.t[:], c_.t[:], AF.Gelu_apprx_tanh, [c_], [g_])
                    cx.tt("dve", act.t[:, f, ts_], g_.t[:], pu.t[:], ALU.mult, [g_, pu], [act])
                cx.copy("act", carry.t[:, f, :], ht[:, TC:TC + 2], [hB], [carry])
            for q in range(TC // 128):
                tt = tc * (TC // 128) + q
                xr = xrs[tt % 2]
                cx.dma("sp", xr.t[:], x1d.t[tt * 128:(tt + 1) * 128, :], [x1d], [xr])
                zp = pss[4:6]
                for hf in range(2):
                    for f in range(NF):
                        cx.mm(zp[hf].t[:], act.t[:, f, q * 128:(q + 1) * 128], wd.t[:, f, hf * 512:(hf + 1) * 512],
                              f == 0, f == NF - 1, [act, wd], [zp[hf]])
                layer_norm_store(cx, zp, xr, g_t, b_t, x2d, tt * 128, tmps[tt % 2])

SCRATCH = {
    "rope": ([2, 128, T], F32), "vp": ([256, T], F32),
    "qr": ([384, T], BF16), "kr": ([384, T], BF16), "qn": ([384, T], BF16),
    "ks": ([128, T], BF16), "kw": ([128, T], BF16), "kc": ([128, T], BF16), "vc": ([128, T], BF16),
    "tmb": ([T, 640], BF16), "gr": ([T, 384], F32), "gt": ([T, 18], F32),
    "yt": ([1024, T], BF16), "x1": ([T, D], F32), "xmid": ([T, D], F32),
}


def build_program(layers=(0, 1), phases="ABCDEF", ext_in=(), ext_out=(), prologue=True):
    nc = bass.Bass("TRN2", target_bir_lowering=False)
    cx = Ctx(nc, ext_in, ext_out)
    xd = cx.dr("x", [T, D], F32, kind="ExternalInput")
    posd = cx.dr("pos", [1, T], I32, kind="ExternalInput")
    Cd = {k: cx.dr(k, list(v.shape), F32, kind="ExternalInput") for k, v in CONSTS.items()}
    Wd = {l: {k: cx.dr(f"{k}_{l}", shp, F32, kind="ExternalInput") for k, shp in LAYER_SHAPES.items()} for l in layers}
    S = {k: cx.dr(k, shp, dt) for k, (shp, dt) in SCRATCH.items()}
    outd = cx.dr("y", [T, D], F32, kind="ExternalOutput")
    with contextlib.ExitStack() as gs:
        cx.stack = gs
        G = {"rope": S["rope"]}
        G["ident"] = load_cast(cx, "ident", Cd["c_ident"], [128, 128])
        G["inv"] = load_f32(cx, "inv", Cd["c_inv"].t, Cd["c_inv"], [128, 1])
        G["sgn"] = load_f32(cx, "sgn", Cd["c_sgn"].t, Cd["c_sgn"], [128, 1])
        G["pw"] = load_f32(cx, "pw", Cd["c_pw"].t, Cd["c_pw"], [128, 2])
        G["prc"] = load_f32(cx, "prc", Cd["c_prc"].t, Cd["c_prc"], [128, 2, 16])
        G["C"] = Cd
        if prologue:
            prologue_rope(cx, posd, G)
        cur = xd
        for li, l in enumerate(layers):
            nxt = outd if li == len(layers) - 1 else S["xmid"]
            W = Wd[l]
            if "A" in phases:
                phase_A(cx, l, cur, W, G, S)
            if "B" in phases:
                phase_B(cx, l, W, G, S)
            if "C" in phases:
                phase_C(cx, l, W, G, S)
            if "D" in phases:
                phase_D(cx, l, W, G, S)
            if "E" in phases:
                phase_E(cx, l, cur, S["x1"], W, G, S)
            if "F" in phases:
                phase_F(cx, l, S["x1"], nxt, W, G, S)
            cur = nxt
        cx.P.barrier()
        finals = [outd.b] + [cx.dram[n].b for n in cx.ext_out]
        cx.P.emit_all(final_bufs=finals)
    return nc, cx


def make_in_maps(inputs, layers=(0, 1), cores=range(8)):
    shared = dict(CONSTS)
    for l in layers:
        for k, v in _layer_arrays(inputs, l).items():
            assert list(v.shape) == LAYER_SHAPES[k], (k, v.shape)
            shared[f"{k}_{l}"] = v.astype(np.float32, copy=False)
    maps = []
    for b in cores:
        m = dict(shared)
        m["x"] = np.ascontiguousarray(inputs["x"][b])
        m["pos"] = np.ascontiguousarray(inputs["positions"][b].reshape(1, T).astype(np.int32))
        maps.append(m)
    return maps


def kernel(**inputs):
    inputs = {k: np.asarray(v) for k, v in inputs.items()}
    nc, cx = build_program()
    maps = make_in_maps(inputs)
    res = run_bass_kernel_spmd(nc, maps, core_ids=list(range(8)))
    return np.stack([np.asarray(r["y"]) for r in res.results], 0).astype(np.float32)
```

```python
import contextlib
import math
import numpy as np
import concourse.bass as bass
import concourse.mybir as mybir
from concourse.bass_utils import run_bass_kernel_spmd

F32 = mybir.dt.float32
BF16 = mybir.dt.bfloat16
I32 = mybir.dt.int32
AF = mybir.ActivationFunctionType
ALU = mybir.AluOpType
AX = mybir.AxisListType

T = 4096
D = 1024
DEPTH = 2
NT = T // 128
DFF = 2816
NF = DFF // 128
ALPHA = (2 * DEPTH) ** 0.25
NEG = -10000.0
DBG = {}

ENGS = ("pe", "act", "dve", "pool", "sp")


class Buf:
    __slots__ = ("name", "writers", "readers", "dsem", "dcount", "is_dram", "vsem", "excl")

    def __init__(self, name, is_dram=False, excl=False):
        self.name = name
        self.is_dram = is_dram
        self.excl = excl
        self.writers = []
        self.readers = []
        self.dsem = None
        self.dcount = 0
        self.vsem = None


class Op:
    __slots__ = ("eng", "emit", "waits", "is_dma", "dbuf", "dval", "needs_inc", "val", "vsem")

    def __init__(self, eng, emit, is_dma=False):
        self.eng = eng
        self.emit = emit
        self.waits = []
        self.is_dma = is_dma
        self.dbuf = None
        self.dval = 0
        self.needs_inc = False
        self.val = 0
        self.vsem = None


class Prog:
    def __init__(self, nc):
        self.nc = nc
        self.ops = {e: [] for e in ENGS}
        self.last = {e: None for e in ENGS}
        self.dma_bufs = {}
        self.pending_bar = {e: [] for e in ENGS}
        self.seq = 0
        self.bar_seq = 0
        self.vfree = {True: [], False: []}
        self.vkind = []
        self.vcount = []
        self.rsem = []

    def _dep(self, op, prod, force=False):
        if prod is op:
            return
        if (not force) and prod.val < self.bar_seq:
            return
        if not prod.is_dma and prod.eng == op.eng:
            if op.eng in ("pe", "sp"):
                return
        op.waits.append(prod)
        if not prod.is_dma:
            prod.needs_inc = True

    @staticmethod
    def _prune(lst):
        last = {}
        for r in lst:
            last[(r.eng, r.is_dma, r.vsem)] = r
        return list(last.values())

    def op(self, eng, emit, reads=(), writes=(), dma=False):
        o = Op(eng, emit, is_dma=dma)
        self.seq += 1
        o.val = self.seq
        if self.pending_bar[eng]:
            for p in self.pending_bar[eng]:
                self._dep(o, p, force=True)
            self.pending_bar[eng] = []
        for b in reads:
            for w in b.writers:
                self._dep(o, w)
            if b.excl:
                for r in b.readers:
                    if r.eng != eng:
                        self._dep(o, r)
        for b in writes:
            for r in b.readers:
                if (not r.is_dma) and r.eng == eng and not dma:
                    continue
                self._dep(o, r)
            if not b.readers:
                for w in b.writers:
                    if w.is_dma and dma:
                        continue
                    if (not w.is_dma) and w.eng == eng and not dma:
                        continue
                    self._dep(o, w)
        if dma:
            assert len(writes) == 1
            b = writes[0]
            if b.is_dram:
                b = [r for r in reads if not r.is_dram][0]
            if b.vsem is None:
                sw = (eng == "pool")
                if self.vfree[sw]:
                    b.vsem = self.vfree[sw].pop()
                else:
                    b.vsem = len(self.vcount)
                    self.vcount.append(0)
                    self.vkind.append(sw)
                b.dcount = self.vcount[b.vsem]
            b.dcount += 16
            self.vcount[b.vsem] = b.dcount
            o.dbuf = b
            o.dval = b.dcount
            o.vsem = b.vsem
            self.dma_bufs[id(b)] = b
        for b in writes:
            if b.readers:
                b.writers = [o]
                b.readers = []
            else:
                b.writers.append(o)
                if len(b.writers) > 6:
                    b.writers = self._prune(b.writers)
        for b in reads:
            b.readers.append(o)
            if len(b.readers) > 6:
                b.readers = self._prune(b.readers)
        self.ops[eng].append(o)
        if not dma:
            self.last[eng] = o
        return o

    def dma(self, eng, out, in_, reads, writes):
        return self.op(eng, lambda e: e.dma_start(out=out, in_=in_), reads, writes, dma=True)

    def barrier(self):
        targets = []
        for e in ENGS:
            if self.last[e] is not None:
                targets.append(self.last[e])
        for b in self.dma_bufs.values():
            if b.vsem is not None:
                p = Op("sp", None, is_dma=True)
                p.dbuf = b
                p.dval = b.dcount
                p.vsem = b.vsem
                targets.append(p)
                self.vfree[self.vkind[b.vsem]].append(b.vsem)
                b.vsem = None
        self.dma_bufs = {}
        self.seq += 1
        self.bar_seq = self.seq
        for e in ENGS:
            self.pending_bar[e] = self._prune(self.pending_bar[e] + targets)

    def emit_all(self, final_bufs=()):
        nc = self.nc
        esem = {e: nc.alloc_semaphore(name=f"es_{e}") for e in ENGS}
        self.rsem = [nc.alloc_semaphore(name=f"ds_{i}") for i in range(len(self.vcount))]
        for e in ENGS:
            c = 0
            for o in self.ops[e]:
                if (not o.is_dma) and o.needs_inc:
                    c += 1
                    o.val = c
        engobj = {"pe": "tensor", "act": "scalar", "dve": "vector", "pool": "gpsimd", "sp": "sync"}
        with nc.Block() as block:
            for e in ENGS:
                def body(eng, ops=self.ops[e], e=e):
                    waited = {}
                    for o in ops:
                        need = {}
                        for p in o.waits:
                            if p.is_dma:
                                sem, val = self.rsem[p.vsem], p.dval
                            else:
                                sem, val = esem[p.eng], p.val
                            if sem is None:
                                continue
                            if need.get(sem.num, (None, 0))[1] < val:
                                need[sem.num] = (sem, val)
                        for k, (sem, val) in need.items():
                            if waited.get(k, 0) >= val:
                                continue
                            waited[k] = val
                            eng.wait_ge(sem, val)
                        ins = o.emit(eng)
                        if o.is_dma:
                            ins.then_inc(self.rsem[o.vsem], 16)
                        elif o.needs_inc:
                            ins.then_inc(esem[e], 1)
                    if e == "sp":
                        for i, sem in enumerate(self.rsem):
                            if waited.get(sem.num, 0) < self.vcount[i]:
                                eng.wait_ge(sem, self.vcount[i])

                getattr(block, engobj[e])(body)


OFF = {}
_o = 0
for _n, _w in (("v_pool", 256), ("q_ret", 384), ("k_ret", 384), ("v_ret", 384), ("g_ret", 384),
               ("q_nsa", 384), ("k_cmp", 128), ("v_cmp", 128), ("k_slc", 128), ("v_slc", 128),
               ("k_win", 128), ("v_win", 128), ("gate", 18)):
    OFF[_n] = _o
    _o += _w


def _swap_cols(cols):
    cols = np.asarray(cols).reshape(-1, 64)
    return np.concatenate([cols[:, 32:], cols[:, :32]], axis=1).reshape(-1)


def _fm_cols():
    ch = []
    for c in range(2):
        ch.append(np.arange(OFF["v_pool"] + 128 * c, OFF["v_pool"] + 128 * (c + 1)))
    for name in ("q_ret", "k_ret", "q_nsa"):
        for c in range(3):
            ch.append(np.arange(OFF[name] + 128 * c, OFF[name] + 128 * (c + 1)))
    for name in ("k_slc", "k_win", "k_cmp", "v_cmp"):
        ch.append(np.arange(OFF[name], OFF[name] + 128))
    return ch


FM_COLS = _fm_cols()
NFM = len(FM_COLS)
TM_COLS = np.concatenate([np.arange(OFF["v_ret"], OFF["v_ret"] + 384),
                          np.arange(OFF["v_slc"], OFF["v_slc"] + 128),
                          np.arange(OFF["v_win"], OFF["v_win"] + 128),
                          np.arange(OFF["g_ret"], OFF["g_ret"] + 384),
                          np.arange(OFF["gate"], OFF["gate"] + 18)])
NTM = len(TM_COLS)


def _const_tables():
    c = {}
    p = np.arange(128)
    inv = (10000.0 ** (-np.arange(0, 64, 2, dtype=np.float32) / 64)).astype(np.float32)
    c["c_inv"] = inv[p % 32].reshape(128, 1).astype(np.float32)
    c["c_sgn"] = np.where((p % 64) < 32, -1.0, 1.0).reshape(128, 1).astype(np.float32)
    h = np.arange(6, dtype=np.float64)
    lg = np.log1p(-np.power(2.0, -5.0 - h))
    i = np.arange(128, dtype=np.float64)
    dm = np.zeros((128, 6, 128), np.float32)
    for hh in range(6):
        diff = i[None, :] - i[:, None]
        dm[:, hh, :] = np.where(diff >= 0, 0.125 * np.exp(lg[hh] * np.maximum(diff, 0)), 0.0)
    c["c_dm"] = dm
    xi = np.exp(lg[:, None] * (i[None, :] + 1.0))
    zeta = 0.125 * np.exp(lg[:, None] * (127.0 - i[None, :]))
    gam = np.exp(lg * 128.0)
    xir = np.zeros((128, 3, 128), np.float32)
    zt = np.zeros((128, 3, 128), np.float32)
    gc = np.zeros((128, 3), np.float32)
    for ck in range(3):
        for hh in range(2):
            xir[hh * 64:(hh + 1) * 64, ck, :] = xi[2 * ck + hh][None, :]
            zt[:, ck, hh * 64:(hh + 1) * 64] = zeta[2 * ck + hh][:, None]
            gc[hh * 64:(hh + 1) * 64, ck] = gam[2 * ck + hh]
    c["c_xir"] = xir
    c["c_zt"] = zt
    c["c_gc"] = gc
    win = np.zeros((128, 2), np.float32)
    rc = np.zeros((128, 2, 16), np.float32)
    for ck in range(2):
        for hh in range(2):
            w = (2, 4, 8, 16)[2 * ck + hh]
            win[hh * 64:(hh + 1) * 64, ck] = 1.0 / w
            rc[hh * 64:(hh + 1) * 64, ck, :] = 1.0 / np.minimum(np.arange(16) + 1, w)
    c["c_pw"] = win
    c["c_prc"] = rc
    kk = np.arange(128)[:, None]
    qq = np.arange(128)[None, :]
    c["c_caus"] = np.where(kk > qq, NEG, 0.0).astype(np.float32)
    c["c_upper"] = np.where(kk <= qq, NEG, 0.0).astype(np.float32)
    c["c_ident"] = np.eye(128, dtype=np.float32)
    pm = np.zeros((128, 128), np.float32)
    mm_ = np.arange(128)
    pm[(mm_ % 64 + 32) % 64 + 64 * (mm_ // 64), mm_] = 1.0
    c["c_perm"] = pm
    ex = np.zeros((64, T), np.float32)
    ex[np.arange(T) // 64, np.arange(T)] = 1.0
    c["c_expand"] = ex
    n = np.arange(256)
    ends = 16 * n + 31
    cm = np.where(ends[:, None] > np.arange(T)[None, :], NEG, 0.0).astype(np.float32)
    c["c_cmn"] = np.ascontiguousarray(cm.reshape(2, 128, T).transpose(1, 0, 2))
    ci = np.arange(256)[:, None]
    sj = np.arange(64)[None, :]
    ov = np.clip(np.minimum(ci * 16 + 32, (sj + 1) * 64) - np.maximum(ci * 16, sj * 64), 0, None) / 16.0
    ov[255, :] = 0.0
    c["c_ovl"] = np.ascontiguousarray(ov.astype(np.float32).reshape(2, 128, 64).transpose(1, 0, 2))
    tq = np.arange(T)
    cur = tq // 64
    blk = np.arange(64)[None, :]
    forced = (blk == 0) | (blk == cur[:, None]) | (blk == cur[:, None] - 1)
    valid = blk * 64 <= tq[:, None]
    bias = np.where(valid, np.where(forced, 1e6, 0.0), -100.0).astype(np.float32)
    c["c_sbias"] = np.ascontiguousarray(bias.reshape(32, 128, 64).transpose(1, 0, 2))
    return c


CONSTS = _const_tables()


def _layer_arrays(inp, l):
    a = {}
    w_in = inp["w_in"][l]
    wk = w_in.reshape(8, 128, -1)
    a["wfm"] = np.ascontiguousarray(
        np.stack([wk[:, :, cols].transpose(1, 0, 2) for cols in FM_COLS], 0))
    a["wtm"] = np.ascontiguousarray(wk[:, :, TM_COLS].transpose(1, 0, 2))
    a["wout"] = np.ascontiguousarray(inp["w_out"][l].reshape(8, 128, D).transpose(1, 0, 2))
    pw = inp["pool_w"][l]
    bd = np.zeros((2, 128, 128), np.float32)
    for ck in range(2):
        for hh in range(2):
            bd[ck, hh * 64:(hh + 1) * 64, hh * 64:(hh + 1) * 64] = pw[2 * ck + hh]
    a["bd"] = bd
    a["psc"] = np.ascontiguousarray(inp["pool_scale"][l].reshape(2, 128).T)
    a["gng"] = np.ascontiguousarray(inp["ret_gn_g"][l].reshape(1, 384))
    for kv in ("k", "v"):
        w1 = inp[f"cmp_w1_{kv}"][l].reshape(32, 64, 128)
        w1d = np.concatenate([w1, w1], axis=1).transpose(1, 0, 2)
        a[f"w1{kv}"] = np.ascontiguousarray(w1d)
        a[f"b1{kv}"] = np.ascontiguousarray(inp[f"cmp_b1_{kv}"][l].reshape(128, 1))
        a[f"pos{kv}"] = np.ascontiguousarray(inp[f"cmp_pos_{kv}"][l].T)
        a[f"w2{kv}"] = np.ascontiguousarray(inp[f"cmp_w2_{kv}"][l])
    a["w2ks"] = np.ascontiguousarray(inp["cmp_w2_k"][l][:, _swap_cols(np.arange(64))])
    a["wg"] = np.ascontiguousarray(inp["ffn_w_gate"][l].reshape(8, 128, NF, 128).transpose(2, 1, 0, 3))
    a["wu"] = np.ascontiguousarray(inp["ffn_w_up"][l].reshape(8, 128, NF, 128).transpose(2, 1, 0, 3))
    a["wd"] = np.ascontiguousarray(inp["ffn_w_down"][l].reshape(NF, 128, D).transpose(1, 0, 2))
    a["cw"] = np.ascontiguousarray(inp["ffn_conv_w"][l].reshape(3, NF, 128).transpose(2, 1, 0))
    a["cb"] = np.ascontiguousarray(inp["ffn_conv_b"][l].reshape(NF, 128).T)
    for nme in ("ln1_g", "ln1_b", "ln2_g", "ln2_b"):
        a[nme] = np.ascontiguousarray(inp[nme][l].reshape(1, D))
    return a


LAYER_SHAPES = {
    "wfm": [NFM, 128, 8, 128], "wtm": [128, 8, NTM], "wout": [128, 8, D], "bd": [2, 128, 128],
    "psc": [128, 2], "gng": [1, 384],
    "w1k": [128, 32, 128], "b1k": [128, 1], "posk": [64, 32], "w2k": [128, 64],
    "w1v": [128, 32, 128], "b1v": [128, 1], "posv": [64, 32], "w2v": [128, 64], "w2ks": [128, 64],
    "wg": [NF, 128, 8, 128], "wu": [NF, 128, 8, 128], "wd": [128, NF, D], "cw": [128, NF, 3], "cb": [128, NF],
    "ln1_g": [1, D], "ln1_b": [1, D], "ln2_g": [1, D], "ln2_b": [1, D],
}


class Tile:
    __slots__ = ("t", "b")

    def __init__(self, t, b):
        self.t = t
        self.b = b


class Ctx:
    def __init__(self, nc, ext_in=(), ext_out=()):
        self.nc = nc
        self.P = Prog(nc)
        self.ext_in = set(ext_in)
        self.ext_out = set(ext_out)
        self.dram = {}
        self.stack = None
        self.uid = 0

    def dr(self, name, shape, dt, kind=None):
        if kind is None:
            kind = "ExternalInput" if name in self.ext_in else ("ExternalOutput" if name in self.ext_out else "Internal")
        t = self.nc.dram_tensor(name, list(shape), dt, kind=kind).ap()
        tl = Tile(t, Buf(name, is_dram=True))
        self.dram[name] = tl
        return tl

    def sb(self, name, shape, dt, es=None):
        self.uid += 1
        t = (es or self.stack).enter_context(self.nc.sbuf_tensor(f"{name}_{self.uid}", list(shape), dt))
        return Tile(t, Buf(name))

    def ps(self, name, shape, dt, es=None):
        self.uid += 1
        t = (es or self.stack).enter_context(self.nc.psum_tensor(f"{name}_{self.uid}", list(shape), dt))
        return Tile(t, Buf(name, excl=True))

    @contextlib.contextmanager
    def phase(self):
        old = self.stack
        with contextlib.ExitStack() as es:
            self.stack = es
            yield es
            self.P.barrier()
        self.stack = old

    def dma(self, eng, out, in_, reads, writes):
        self.P.dma(eng, out, in_, [x.b for x in reads], [x.b for x in writes])

    def op(self, eng, fn, reads, writes):
        self.P.op(eng, fn, [x.b for x in reads], [x.b for x in writes])

    def mm(self, out, lhsT, rhs, start, stop, reads, writes, skip=False):
        kw = dict(start=start, stop=stop)
        if skip:
            kw["skip_group_check"] = True
        self.op("pe", lambda e: e.matmul(out, lhsT=lhsT, rhs=rhs, **kw), reads, writes)

    def tr(self, out, in_, ident, reads, writes):
        self.op("pe", lambda e: e.transpose(out, in_, ident), reads, writes)

    def copy(self, eng, out, in_, reads, writes):
        if eng == "act":
            self.op("act", lambda e: e.copy(out=out, in_=in_), reads, writes)
        else:
            self.op(eng, lambda e: e.tensor_copy(out=out, in_=in_), reads, writes)

    def act(self, out, in_, func, reads, writes, **kw):
        self.op("act", lambda e: e.activation(out=out, in_=in_, func=func, **kw), reads, writes)

    def tt(self, eng, out, in0, in1, op, reads, writes):
        self.op(eng, lambda e: e.tensor_tensor(out=out, in0=in0, in1=in1, op=op), reads, writes)

    def ts(self, eng, out, in0, s1, op0, reads, writes, s2=None, op1=None):
        if op1 is None:
            self.op(eng, lambda e: e.tensor_scalar(out=out, in0=in0, scalar1=s1, scalar2=None, op0=op0), reads, writes)
        else:
            self.op(eng, lambda e: e.tensor_scalar(out=out, in0=in0, scalar1=s1, scalar2=s2, op0=op0, op1=op1), reads, writes)

    def stt(self, out, in0, scalar, in1, op0, op1, reads, writes):
        self.op("dve", lambda e: e.scalar_tensor_tensor(out=out, in0=in0, scalar=scalar, in1=in1, op0=op0, op1=op1),
                reads, writes)


def load_cast(cx, name, src_tile, shape, es=None, eng=None, q=None):
    b = cx.sb(name + "_b", shape, BF16, es)
    cx.dma("pool", b.t[:], src_tile.t, [src_tile], [b])
    return b


def load_f32(cx, name, src_ap, src_tile, shape, es=None, q="sp"):
    f = cx.sb(name, shape, F32, es)
    cx.dma(q, f.t[:], src_ap, [src_tile], [f])
    return f


def build_xT(cx, xd, xT, ident, ntiles, tok0=0):
    with contextlib.ExitStack() as es:
        xb = [cx.sb(f"xb{i}", [128, D], BF16, es) for i in range(3)]
        pt = [cx.ps(f"xpt{i}", [128, 8, 128], BF16, es) for i in range(2)]
        for tt in range(ntiles):
            bb, p = xb[tt % 3], pt[tt % 2]
            r0 = tok0 + tt * 128
            cx.dma("pool", bb.t[:], xd.t[r0:r0 + 128, :], [xd], [bb])
            for k in range(8):
                cx.tr(p.t[:, k, :], bb.t[:, k * 128:(k + 1) * 128], ident.t[:], [bb, ident], [p])
            cx.copy("act" if tt % 2 == 0 else "dve", xT.t[:, :, tt * 128:(tt + 1) * 128], p.t[:], [p], [xT])
        cx.P.barrier()


def layer_norm_store(cx, zps, xres, g_t, b_t, outd, r0, tmp, eng_q="pool"):
    z, st, mv, rs, o = tmp
    for hf in range(2):
        cx.stt(z.t[:, hf * 512:(hf + 1) * 512], xres.t[:, hf * 512:(hf + 1) * 512], ALPHA, zps[hf].t[:],
               ALU.mult, ALU.add, [xres, zps[hf]], [z])
    for hf in range(2):
        cx.op("dve", lambda e, hf=hf: e.bn_stats(out=st.t[:, hf, :], in_=z.t[:, hf * 512:(hf + 1) * 512]), [z], [st])
    cx.op("dve", lambda e: e.bn_aggr(out=mv.t[:], in_=st.t[:]), [st], [mv])
    cx.ts("dve", rs.t[:, 0:1], mv.t[:, 1:2], 1e-5, ALU.add, [mv], [rs])
    cx.act(rs.t[:, 0:1], rs.t[:, 0:1], AF.Sqrt, [rs], [rs])
    cx.op("dve", lambda e: e.reciprocal(out=rs.t[:, 0:1], in_=rs.t[:, 0:1]), [rs], [rs])
    cx.stt(rs.t[:, 1:2], mv.t[:, 0:1], -1.0, rs.t[:, 0:1], ALU.mult, ALU.mult, [mv, rs], [rs])
    cx.act(o.t[:], z.t[:], AF.Identity, [z, rs], [o], scale=rs.t[:, 0:1], bias=rs.t[:, 1:2])
    cx.tt("dve", o.t[:], o.t[:], g_t.t[:], ALU.mult, [o, g_t], [o])
    cx.tt("pool", o.t[:], o.t[:], b_t.t[:], ALU.add, [o, b_t], [o])
    cx.dma(eng_q, outd.t[r0:r0 + 128, :], o.t[:], [o], [outd])


def ln_stages(cx, zps, xres, g_t, b_t, outd, r0, tmp, eng_q="pool"):
    z, st, mv, rs, o = tmp

    def s1():
        for hf in range(2):
            cx.stt(z.t[:, hf * 512:(hf + 1) * 512], xres.t[:, hf * 512:(hf + 1) * 512], ALPHA, zps[hf].t[:],
                   ALU.mult, ALU.add, [xres, zps[hf]], [z])
        for hf in range(2):
            cx.op("dve", lambda e, hf=hf: e.bn_stats(out=st.t[:, hf, :], in_=z.t[:, hf * 512:(hf + 1) * 512]), [z], [st])
        cx.op("dve", lambda e: e.bn_aggr(out=mv.t[:], in_=st.t[:]), [st], [mv])
        cx.ts("dve", rs.t[:, 0:1], mv.t[:, 1:2], 1e-5, ALU.add, [mv], [rs])

    def s2():
        cx.act(rs.t[:, 0:1], rs.t[:, 0:1], AF.Sqrt, [rs], [rs])

    def s3():
        cx.op("dve", lambda e: e.reciprocal(out=rs.t[:, 0:1], in_=rs.t[:, 0:1]), [rs], [rs])
        cx.stt(rs.t[:, 1:2], mv.t[:, 0:1], -1.0, rs.t[:, 0:1], ALU.mult, ALU.mult, [mv, rs], [rs])

    def s4():
        cx.act(o.t[:], z.t[:], AF.Identity, [z, rs], [o], scale=rs.t[:, 0:1], bias=rs.t[:, 1:2])

    def s5():
        cx.tt("dve", o.t[:], o.t[:], g_t.t[:], ALU.mult, [o, g_t], [o])
        cx.tt("pool", o.t[:], o.t[:], b_t.t[:], ALU.add, [o, b_t], [o])
        cx.dma(eng_q, outd.t[r0:r0 + 128, :], o.t[:], [o], [outd])
    return s1, s2, s3, s4, s5


def ln_tmps(cx, es, n=2):
    res = []
    for i in range(n):
        res.append((cx.sb(f"lnz{i}", [128, D], F32, es), cx.sb(f"lnst{i}", [128, 2, 6], F32, es),
                    cx.sb(f"lnmv{i}", [128, 2], F32, es), cx.sb(f"lnrs{i}", [128, 2], F32, es),
                    cx.sb(f"lno{i}", [128, D], F32, es)))
    return res


def prologue_rope(cx, posd, G):
    rope = G["rope"]
    with cx.phase() as es:
        pi_ = cx.sb("posi", [128, T], I32)
        ang = cx.sb("ang", [128, T], F32)
        m = cx.sb("rm", [128, T], F32)
        o = cx.sb("ro", [128, T], F32)
        cx.dma("sp", pi_.t[:], posd.t.broadcast_to([128, T]), [posd], [pi_])
        cx.copy("dve", ang.t[:], pi_.t[:], [pi_], [ang])
        cx.ts("dve", ang.t[:], ang.t[:], G["inv"].t[:, 0:1], ALU.mult, [ang, G["inv"]], [ang])
        ki = cx.sb("rki", [128, T], I32)
        C1 = 6.28125
        C2 = 2.0 * math.pi - 6.28125
        for which, shift in ((0, 0.25), (1, 0.0)):
            cx.ts("dve", m.t[:], ang.t[:], 1.0 / (2.0 * math.pi), ALU.mult, [ang], [m], s2=shift, op1=ALU.add)
            cx.copy("dve", ki.t[:], m.t[:], [m], [ki])
            cx.copy("dve", m.t[:], ki.t[:], [ki], [m])
            cx.stt(o.t[:], m.t[:], -C1, ang.t[:], ALU.mult, ALU.add, [m, ang], [o])
            cx.stt(o.t[:], m.t[:], -C2, o.t[:], ALU.mult, ALU.add, [m, o], [o])
            if which == 0:
                cx.ts("dve", o.t[:], o.t[:], 0.5 * math.pi, ALU.add, [o], [o])
            cx.ts("dve", o.t[:], o.t[:], math.pi, ALU.min, [o], [o], s2=-math.pi, op1=ALU.max)
            cx.act(o.t[:], o.t[:], AF.Sin, [o], [o])
            if which == 1:
                cx.ts("dve", o.t[:], o.t[:], G["sgn"].t[:, 0:1], ALU.mult, [o, G["sgn"]], [o])
            cx.dma("sp", rope.t[which], o.t[:], [o], [rope])


def phase_A(cx, l, xd, W, G, S):
    ident = G["ident"]
    rope = G["rope"]
    with cx.phase():
        xT = cx.sb("xT", [128, 8, T], BF16)
        build_xT(cx, xd, xT, ident, NT)
        with cx.phase() as es:
          if DBG.get("tm", True):
              wtm = load_cast(cx, "wtm", W["wtm"], [128, 8, NTM])
              pss = [cx.ps(f"tmps{i}", [128, 512], F32) for i in range(6)]
              ob = [cx.sb(f"tmob{i}", [128, 640], BF16) for i in range(2)]
              og = [cx.sb(f"tmog{i}", [128, 384], F32) for i in range(2)]
              ogt = [cx.sb(f"tmogt{i}", [128, 18], F32) for i in range(2)]
              for tt in range(NT):
                  p0, p1, p2 = pss[(tt % 2) * 3:(tt % 2) * 3 + 3]
                  for (pp, c0, c1) in ((p0, 0, 512), (p1, 512, 1024), (p2, 1024, NTM)):
                      for k in range(8):
                          cx.mm(pp.t[:, 0:c1 - c0], xT.t[:, k, tt * 128:(tt + 1) * 128], wtm.t[:, k, c0:c1],
                                k == 0, k == 7, [xT, wtm], [pp])
                  b_, g_, t_ = ob[tt % 2], og[tt % 2], ogt[tt % 2]
                  cx.copy("dve", b_.t[:, 0:512], p0.t[:], [p0], [b_])
                  cx.copy("dve", b_.t[:, 512:640], p1.t[:, 0:128], [p1], [b_])
                  cx.act(g_.t[:], p1.t[:, 128:512], AF.Silu, [p1], [g_])
                  cx.act(t_.t[:], p2.t[:, 0:18], AF.Sigmoid, [p2], [t_])
                  r0 = tt * 128
                  cx.dma("sp", S["tmb"].t[r0:r0 + 128, :], b_.t[:], [b_], [S["tmb"]])
                  cx.dma("sp", S["gr"].t[r0:r0 + 128, :], g_.t[:], [g_], [S["gr"]])
                  cx.dma("sp", S["gt"].t[r0:r0 + 128, :], t_.t[:], [t_], [S["gt"]])
        with cx.phase() as es:
            C = cx.sb("ropeC", [128, T], F32)
            Sn = cx.sb("ropeS", [128, T], F32)
            cx.dma("sp", C.t[:], rope.t[0], [rope], [C])
            cx.dma("sp", Sn.t[:], rope.t[1], [rope], [Sn])
            perm = load_cast(cx, "perm", G["C"]["c_perm"], [128, 128])
            wb = [cx.sb(f"wb{i}", [128, 8, 128], BF16) for i in range(3)]
            pss = [cx.ps(f"fmps{i}", [128, 512], F32) for i in range(5)]
            ps2 = [cx.ps(f"fmps2{i}", [128, 512], F32) for i in range(3)]
            ost = [cx.sb(f"fmo{i}", [128, T], BF16) for i in range(2)]
            vst = [cx.sb(f"fmv{i}", [128, 512], F32) for i in range(2)]
            qbs = [cx.sb(f"fmqb{i}", [128, 512], BF16) for i in range(3)]
            t1s = [cx.sb(f"fmt1{i}", [128, 512], F32) for i in range(2)]
            t2s = [cx.sb(f"fmt2{i}", [128, 512], F32) for i in range(2)]
            units = [(0, S["vp"], 0, False), (1, S["vp"], 128, False)]
            ci = 2
            for dest in ("qr", "kr", "qn"):
                for c in range(3):
                    units.append((ci, S[dest], 128 * c, True))
                    ci += 1
            units.append((ci, S["ks"], 0, True))
            units.append((ci + 1, S["kw"], 0, True))
            units.append((ci + 2, S["kc"], 0, False))
            units.append((ci + 3, S["vc"], 0, False))

            def load_w(ui):
                cx.dma("pool", wb[ui % 3].t[:], W["wfm"].t[units[ui][0]], [W["wfm"]], [wb[ui % 3]])
            load_w(0)
            load_w(1)
            seq = [(ui, tc) for ui in range(len(units)) for tc in range(8)]
            state = {}

            def stage1(idx):
                ui, tc = seq[idx]
                cid, dest, row0, is_rope = units[ui]
                if tc == 0 and ui + 2 < len(units):
                    load_w(ui + 2)
                ts_ = slice(tc * 512, (tc + 1) * 512)
                pp = pss[idx % 5]
                for k in range(8):
                    cx.mm(pp.t[:], wb[ui % 3].t[:, k, :], xT.t[:, k, ts_], k == 0, k == 7, [wb[ui % 3], xT], [pp])
                state[idx] = pp
                if is_rope:
                    qb = qbs[idx % 3]
                    cx.copy("act", qb.t[:], pp.t[:], [pp], [qb])

            def stage2(idx):
                ui, tc = seq[idx]
                cid, dest, row0, is_rope = units[ui]
                ts_ = slice(tc * 512, (tc + 1) * 512)
                pp = state.pop(idx)
                o_ = ost[ui % 2]
                if is_rope:
                    qb, p2 = qbs[idx % 3], ps2[idx % 3]
                    cx.mm(p2.t[:], perm.t[:], qb.t[:], True, True, [perm, qb], [p2])
                    t1, t2 = t1s[idx % 2], t2s[idx % 2]
                    cx.tt("dve", t1.t[:], pp.t[:], C.t[:, ts_], ALU.mult, [pp, C], [t1])
                    cx.tt("dve", t2.t[:], p2.t[:], Sn.t[:, ts_], ALU.mult, [p2, Sn], [t2])
                    cx.tt("pool", o_.t[:, ts_], t1.t[:], t2.t[:], ALU.add, [t1, t2], [o_])
                elif dest is S["vp"]:
                    v_ = vst[tc % 2]
                    cx.copy("act", v_.t[:], pp.t[:], [pp], [v_])
                    cx.dma("sp", dest.t[row0:row0 + 128, ts_], v_.t[:], [v_], [dest])
                else:
                    cx.copy("act", o_.t[:, ts_], pp.t[:], [pp], [o_])
                if tc == 7 and dest is not S["vp"]:
                    cx.dma("sp", dest.t[row0:row0 + 128, :], o_.t[:], [o_], [dest])
            for idx in range(len(seq) + 1):
                if idx < len(seq):
                    stage1(idx)
                if idx >= 1:
                    stage2(idx - 1)


def phase_B(cx, l, W, G, S):
    with cx.phase():
        psc = load_f32(cx, "psc", W["psc"].t, W["psc"], [128, 2])
        pw = G["pw"]
        prc = G["prc"]
        pss = [cx.ps(f"bps{i}", [128, 512], F32) for i in range(4)]
        for ck in range(2):
            bd = load_cast(cx, f"bd{ck}", Tile(W["bd"].t[ck], W["bd"].b), [128, 128])
            v = cx.sb(f"pv{ck}", [128, 16 + T], F32)
            s2 = cx.sb(f"ps2{ck}", [128, 16 + T], F32)
            s4 = cx.sb(f"ps4{ck}", [128, 16 + T], F32)
            mx = cx.sb(f"pmx{ck}", [128, T], BF16)
            o = cx.sb(f"pbo{ck}", [128, T], BF16)
            for t_ in (v, s2, s4):
                cx.op("pool", lambda e, t_=t_: e.memset(t_.t[:, 0:16], 0.0), [], [t_])
            cx.dma("sp", v.t[:, 16:], S["vp"].t[ck * 128:(ck + 1) * 128, :], [S["vp"]], [v])
            if ck == 0:
                cx.tt("dve", s2.t[:, 16:], v.t[:, 16:], v.t[:, 15:15 + T], ALU.add, [v], [s2])
                cx.tt("dve", s4.t[64:128, 16:], s2.t[64:128, 16:], s2.t[64:128, 14:14 + T], ALU.add, [s2], [s4])
                lo, hi = s2, s4
            else:
                cx.tt("dve", s2.t[:, 16:], v.t[:, 16:], v.t[:, 15:15 + T], ALU.add, [v], [s2])
                cx.tt("dve", s4.t[:, 16:], s2.t[:, 16:], s2.t[:, 14:14 + T], ALU.add, [s2], [s4])
                cx.tt("dve", s2.t[:, 16:], s4.t[:, 16:], s4.t[:, 12:12 + T], ALU.add, [s4], [s2])
                cx.tt("dve", s4.t[64:128, 16:], s2.t[64:128, 16:], s2.t[64:128, 8:8 + T], ALU.add, [s2], [s4])
                lo, hi = s2, s4
            for (src, r) in ((lo, slice(0, 64)), (hi, slice(64, 128))):
                cx.stt(mx.t[r, :], src.t[r, 16:], pw.t[r, ck:ck + 1], v.t[r, 16:], ALU.mult, ALU.subtract,
                       [src, pw, v], [mx])
                cx.tt("dve", src.t[r, 0:16], src.t[r, 16:32], prc.t[r, ck, :], ALU.mult, [src, prc, mx], [src])
                cx.tt("dve", mx.t[r, 0:16], src.t[r, 0:16], v.t[r, 16:32], ALU.subtract, [src, v], [mx])
            for tc in range(8):
                ts_ = slice(tc * 512, (tc + 1) * 512)
                pp = pss[tc % 4]
                cx.mm(pp.t[:], bd.t[:], mx.t[:, ts_], True, True, [bd, mx], [pp])
                cx.act(o.t[:, ts_], pp.t[:], AF.Copy, [pp, psc], [o], scale=psc.t[:, ck:ck + 1])
            cx.dma("sp", S["yt"].t[ck * 128:(ck + 1) * 128, :], o.t[:], [o], [S["yt"]])


def phase_C(cx, l, W, G, S):
    Cd = G["C"]
    ident = G["ident"]
    with cx.phase() as es:
        dm = load_f32(cx, "dm", Cd["c_dm"].t, Cd["c_dm"], [128, 6, 128])
        xir = load_f32(cx, "xir", Cd["c_xir"].t, Cd["c_xir"], [128, 3, 128])
        zt = load_f32(cx, "zt", Cd["c_zt"].t, Cd["c_zt"], [128, 3, 128])
        gc = load_f32(cx, "gc", Cd["c_gc"].t, Cd["c_gc"], [128, 3])
        gng = load_f32(cx, "gng", W["gng"].t.broadcast_to([128, 384]), W["gng"], [128, 384])
        bankA = [cx.ps(f"cA{i}", [128, 512], F32) for i in range(2)]
        bankO = [cx.ps(f"cO{i}", [128, 512], F32) for i in range(2)]
        bankV = [cx.ps(f"cV{i}", [128, 512], F32) for i in range(2)]
        bankY = cx.ps("cY", [128, 8, 128], BF16)
        bankK = cx.ps("cK", [128, 8, 128], BF16)
        ktp = [bankK.t[:, i, :] for i in range(3)]
        opv2 = [[b.t[:, ck * 128:(ck + 1) * 128] for ck in range(3)] for b in bankO]
        kvp2 = [[b.t[:, ck * 128:(ck + 1) * 128] for ck in range(3)] for b in bankV]
        ytp = [bankY.t[:, i, :] for i in range(3)]
        qT, kT, qx, v, vz, R, Rb = [], [], [], [], [], [], []
        for ck in range(3):
            rows = slice(ck * 128, (ck + 1) * 128)
            qT.append(cx.sb(f"cqT{ck}", [128, T], BF16))
            kT.append(cx.sb(f"ckT{ck}", [128, T], BF16))
            qx.append(cx.sb(f"cqx{ck}", [128, T], BF16))
            v.append(cx.sb(f"cv{ck}", [128, NT, 128], BF16))
            vz.append(cx.sb(f"cvz{ck}", [128, NT, 128], BF16))
            R.append(cx.sb(f"cR{ck}", [128, 64], F32))
            Rb.append(cx.sb(f"cRb{ck}", [128, 64], BF16))
            cx.dma("sp", qT[ck].t[:], S["qr"].t[rows, :], [S["qr"]], [qT[ck]])
            cx.dma("sp", kT[ck].t[:], S["kr"].t[rows, :], [S["kr"]], [kT[ck]])
            cx.dma("pool", v[ck].t[:], S["tmb"].t[:, ck * 128:(ck + 1) * 128].rearrange("(n p) c -> p n c", p=128),
                   [S["tmb"]], [v[ck]])
            cx.tt("pool", qx[ck].t[:].rearrange("p (n i) -> p n i", i=128), qT[ck].t[:].rearrange("p (n i) -> p n i", i=128),
                  xir.t[:, ck:ck + 1, :].broadcast_to([128, NT, 128]), ALU.mult, [qT[ck], xir], [qx[ck]])
            cx.tt("pool", vz[ck].t[:], v[ck].t[:], zt.t[:, ck:ck + 1, :].broadcast_to([128, NT, 128]), ALU.mult,
                  [v[ck], zt], [vz[ck]])
            cx.op("pool", lambda e, ck=ck: e.memset(R[ck].t[:], 0.0), [], [R[ck]])
            cx.op("pool", lambda e, ck=ck: e.memset(Rb[ck].t[:], 0.0), [], [Rb[ck]])
        NB = 2
        sgs = [[cx.sb(f"csg{ck}_{i}", [128, 8, 128], F32) for i in range(2)] for ck in range(3)]
        kts = [[cx.sb(f"ckt{ck}_{i}", [128, 128], BF16) for i in range(NB)] for ck in range(3)]
        sms = [[cx.sb(f"csm{hh}_{i}", [128, 3, 128], BF16) for i in range(NB)] for hh in range(2)]
        sts = [[cx.sb(f"cst{ck}_{i}", [128, 2, 6], F32) for i in range(NB)] for ck in range(3)]
        mvs = [[cx.sb(f"cmv{ck}_{i}", [128, 2, 2], F32) for i in range(NB)] for ck in range(3)]
        rss = [[cx.sb(f"crs{ck}_{i}", [128, 2, 2], F32) for i in range(NB)] for ck in range(3)]
        ons = [[cx.sb(f"con{ck}_{i}", [128, 128], F32) for i in range(NB)] for ck in range(3)]
        onb = [[cx.sb(f"conb{ck}_{i}", [128, 128], BF16) for i in range(NB)] for ck in range(3)]
        yts = [[cx.sb(f"cyt{ck}_{i}", [128, 128], BF16) for i in range(NB)] for ck in range(3)]

        def load_sg(ck, blk):
            cx.dma("pool", sgs[ck][blk % 2].t[:],
                   S["gr"].t[blk * 1024:(blk + 1) * 1024, ck * 128:(ck + 1) * 128].rearrange("(n p) c -> p n c", p=128),
                   [S["gr"]], [sgs[ck][blk % 2]])
        for ck in range(3):
            load_sg(ck, 0)
        def ctx(n):
            return slice(n * 128, (n + 1) * 128), n % NB, opv2[n % 2], kvp2[n % 2], bankO[n % 2], bankV[n % 2]

        def st_A(n):
            ns, i, opv, kvp, BO, BV = ctx(n)
            for ck in range(3):
                cx.tr(ktp[ck], kT[ck].t[:, ns], ident.t[:], [kT[ck], ident], [bankK])
                for hh in range(2):
                    r = slice(hh * 64, (hh + 1) * 64)
                    cx.mm(bankA[hh].t[:, ck * 128:(ck + 1) * 128], kT[ck].t[r, ns], qT[ck].t[r, ns], True, True,
                          [kT[ck], qT[ck]], [bankA[hh]])

        def st_1(n):
            ns, i, opv, kvp, BO, BV = ctx(n)
            for ck in range(3):
                cx.copy("act", kts[ck][i].t[:], ktp[ck], [bankK], [kts[ck][i]])
            for hh in range(2):
                cx.tt("dve", sms[hh][i].t[:], bankA[hh].t[:, 0:384].rearrange("p (a b) -> p a b", b=128),
                      dm.t[:, hh::2, :], ALU.mult, [bankA[hh], dm], [sms[hh][i]])

        def st_B(n):
            ns, i, opv, kvp, BO, BV = ctx(n)
            for ck in range(3):
                for hh in range(2):
                    r = slice(hh * 64, (hh + 1) * 64)
                    cx.mm(opv[ck][:, r], sms[hh][i].t[:, ck, :], v[ck].t[:, n, r], True, False,
                          [sms[hh][i], v[ck]], [BO])
                    cx.mm(opv[ck][:, r], qx[ck].t[r, ns], Rb[ck].t[r, :], False, True, [qx[ck], Rb[ck]], [BO])
                cx.mm(kvp[ck], kts[ck][i].t[:], vz[ck].t[:, n, :], True, True, [kts[ck][i], vz[ck]], [BV])

        def st_3a(n):
            ns, i, opv, kvp, BO, BV = ctx(n)
            for ck in range(3):
                for hh in range(2):
                    r = slice(hh * 64, (hh + 1) * 64)
                    cx.stt(R[ck].t[r, :], R[ck].t[r, :], gc.t[r, ck:ck + 1], kvp[ck][r, r], ALU.mult, ALU.add,
                           [R[ck], gc, BV], [R[ck]])
                cx.copy("pool", Rb[ck].t[:], R[ck].t[:], [R[ck]], [Rb[ck]])
            for ck in range(3):
                st, mv, rs = sts[ck][i], mvs[ck][i], rss[ck][i]
                for hh in range(2):
                    r = slice(hh * 64, (hh + 1) * 64)
                    cx.op("dve", lambda e, hh=hh, r=r, st=st, ck=ck, opv_=opv: e.bn_stats(out=st.t[:, hh, :], in_=opv_[ck][:, r]), [BO], [st])
                    cx.op("dve", lambda e, hh=hh, st=st, mv=mv: e.bn_aggr(out=mv.t[:, hh, :], in_=st.t[:, hh, :]), [st], [mv])
                cx.ts("dve", rs.t[:, 0, :], mv.t[:, :, 1], 1e-5, ALU.add, [mv], [rs])

        def st_sq(n):
            i = n % NB
            for ck in range(3):
                rs = rss[ck][i]
                cx.act(rs.t[:, 0, :], rs.t[:, 0, :], AF.Sqrt, [rs], [rs])

        def st_3b(n):
            i = n % NB
            for ck in range(3):
                rs, mv = rss[ck][i], mvs[ck][i]
                cx.op("dve", lambda e, rs=rs: e.reciprocal(out=rs.t[:, 0, :], in_=rs.t[:, 0, :]), [rs], [rs])
                cx.stt(rs.t[:, 1, :], mv.t[:, :, 0], -1.0, rs.t[:, 0, :], ALU.mult, ALU.mult, [mv, rs], [rs])

        def st_4(n):
            ns, i, opv, kvp, BO, BV = ctx(n)
            for ck in range(3):
                rs, on, ob = rss[ck][i], ons[ck][i], onb[ck][i]
                for hh in range(2):
                    r = slice(hh * 64, (hh + 1) * 64)
                    cx.act(on.t[:, r], opv[ck][:, r], AF.Identity, [BO, rs], [on], scale=rs.t[:, 0, hh:hh + 1],
                           bias=rs.t[:, 1, hh:hh + 1])
                cx.tt("pool", on.t[:], on.t[:], gng.t[:, ck * 128:(ck + 1) * 128], ALU.mult, [on, gng], [on])
                cx.tt("pool", ob.t[:], on.t[:], sgs[ck][(n // 8) % 2].t[:, n % 8, :], ALU.mult, [on, sgs[ck][(n // 8) % 2]], [ob])

        def st_C(n):
            ns, i, opv, kvp, BO, BV = ctx(n)
            for ck in range(3):
                cx.tr(ytp[ck], onb[ck][i].t[:], ident.t[:], [onb[ck][i], ident], [bankY])
            for ck in range(3):
                cx.copy("act", yts[ck][i].t[:], ytp[ck], [bankY], [yts[ck][i]])
                cx.dma("sp", S["yt"].t[256 + ck * 128:256 + (ck + 1) * 128, ns], yts[ck][i].t[:], [yts[ck][i]], [S["yt"]])

        st_A(0)
        st_1(0)
        for n in range(NT):
            st_B(n)
            if n >= 1:
                st_C(n - 1)
            st_3a(n)
            st_sq(n)
            if n + 1 < NT:
                st_A(n + 1)
                st_1(n + 1)
            st_3b(n)
            st_4(n)
            if n % 8 == 0 and n // 8 + 1 < NT // 8:
                for ck in range(3):
                    load_sg(ck, n // 8 + 1)
        st_C(NT - 1)


def phase_D(cx, l, W, G, S):
    Cd = G["C"]
    ident = G["ident"]
    rope = G["rope"]
    with cx.phase() as es:
        KCT = cx.sb("KCT", [64, 2, 256], BF16)
        VCX = cx.sb("VCX", [128, 2, 2, 129], BF16)
        with cx.phase():
            Cc = cx.sb("dCc", [64, T], F32)
            Sc = cx.sb("dSc", [64, T], F32)
            cx.dma("sp", Cc.t[:], rope.t[0][0:64, :], [rope], [Cc])
            cx.dma("sp", Sc.t[:], rope.t[1][0:64, :], [rope], [Sc])
            ovl = load_f32(cx, "ovl", Cd["c_ovl"].t, Cd["c_ovl"], [128, 2, 64])
            hps = [cx.ps(f"dhp{i}", [128, 512], F32) for i in range(2)]
            cps = cx.ps("dcp", [128, 512], F32)
            kp = cx.ps("dkp", [128, 2, 256], F32)
            ksp = cx.ps("dksp", [128, 2, 256], F32)
            vps = [cx.ps(f"dvp{i}", [128, 512], F32) for i in range(2)]
            cx.op("pool", lambda e: e.memset(KCT.t[:], 0.0), [], [KCT])
            cx.op("pool", lambda e: e.memset(VCX.t[:, :, :, 64:65], 1.0), [], [VCX])
            for g in range(2):
                cx.copy("pool", VCX.t[:, g, :, 65:129], ovl.t[:], [ovl], [VCX])
            for kv in ("k", "v"):
                src = S["kc"] if kv == "k" else S["vc"]
                kvT = cx.sb(f"dkvT{kv}", [128, T], BF16)
                cx.dma("sp", kvT.t[:], src.t, [src], [kvT])
                w1 = load_cast(cx, f"w1{kv}", W[f"w1{kv}"], [128, 32, 128])
                pos = load_cast(cx, f"pos{kv}", W[f"pos{kv}"], [64, 32], eng="dve")
                b1 = load_f32(cx, f"b1{kv}", W[f"b1{kv}"].t, W[f"b1{kv}"], [128, 1])
                w2 = load_cast(cx, f"w2{kv}", W[f"w2{kv}"], [128, 64], eng="dve")
                cb = cx.sb(f"dcb{kv}", [128, 1], F32)
                h1 = cx.sb(f"dh1{kv}", [128, 2, 256], BF16)
                cx.op("pool", lambda e, h1=h1: e.memset(h1.t[:], 0.0), [], [h1])
                for i in range(32):
                    cx.mm(cps.t[:, 0:1], w1.t[0:64, i, :], pos.t[0:64, i:i + 1], i == 0, i == 31, [w1, pos], [cps])
                cx.tt("dve", cb.t[:], cps.t[:, 0:1], b1.t[:], ALU.add, [cps, b1], [cb])
                for g in range(2):
                    r = slice(g * 64, (g + 1) * 64)
                    for i in range(32):
                        cx.mm(hps[g].t[:, 0:255], w1.t[r, i, :], kvT.t[r, i:i + 16 * 254 + 1:16], i == 0, i == 31,
                              [w1, kvT], [hps[g]])
                    cx.act(h1.t[:, g, 0:255], hps[g].t[:, 0:255], AF.Gelu_apprx_tanh, [hps[g], cb], [h1],
                           bias=cb.t[:, 0:1])
                if kv == "k":
                    w2s = load_cast(cx, "w2ks", W["w2ks"], [128, 64], eng="dve")
                    cx.mm(kp.t[0:64], w2.t[:], h1.t[:], True, True, [w2, h1], [kp])
                    cx.mm(ksp.t[0:64], w2s.t[:], h1.t[:], True, True, [w2s, h1], [ksp])
                    t1 = cx.sb("dkt1", [64, 2, 255], F32)
                    t2 = cx.sb("dkt2", [64, 2, 255], F32)
                    cview = Cc.t[:, 31::16].unsqueeze(1).broadcast_to([64, 2, 255])
                    sview = Sc.t[:, 31::16].unsqueeze(1).broadcast_to([64, 2, 255])
                    cx.tt("dve", t1.t[:], kp.t[0:64, :, 0:255], cview, ALU.mult, [kp, Cc], [t1])
                    cx.tt("dve", t2.t[:], ksp.t[0:64, :, 0:255], sview, ALU.mult, [ksp, Sc], [t2])
                    cx.tt("dve", KCT.t[:, :, 0:255], t1.t[:], t2.t[:], ALU.add, [t1, t2], [KCT])
                else:
                    for g in range(2):
                        for nt in range(2):
                            vp_ = vps[(g * 2 + nt) % 2]
                            cx.mm(vp_.t[:, 0:64], h1.t[:, g, nt * 128:(nt + 1) * 128], w2.t[:], True, True, [h1, w2], [vp_])
                            cx.copy("act", VCX.t[:, g, nt, 0:64], vp_.t[:, 0:64], [vp_], [VCX])
        dstop = DBG.get("d_stop", 9)
        if dstop <= 1:
            return
        identb = ident
        QA = [cx.sb(f"QA{h}", [128, T], BF16) for h in range(6)]
        KSA = [cx.sb(f"KSA{g}", [128, T], BF16) for g in range(2)]
        KW = [cx.sb(f"KW{g}", [64, T], BF16) for g in range(2)]
        VSX = cx.sb("VSX", [128, NT, 2, 65], BF16)
        VWX = cx.sb("VWX", [128, NT, 2, 65], BF16)
        CMN = cx.sb("CMN", [128, 2, T], BF16)
        SB_ = load_f32(cx, "sbias", Cd["c_sbias"].t, Cd["c_sbias"], [128, NT, 64])
        GT = cx.sb("GTs", [128, NT, 18], F32)
        cx.dma("sp", GT.t[:], S["gt"].t.rearrange("(n p) c -> p n c", p=128), [S["gt"]], [GT])
        causb = cx.sb("causb", [128, 128], BF16)
        upperb = cx.sb("upperb", [128, 128], BF16)
        cx.dma("pool", causb.t[:], Cd["c_caus"].t, [Cd["c_caus"]], [causb])
        cx.dma("pool", upperb.t[:], Cd["c_upper"].t, [Cd["c_upper"]], [upperb])
        for nt in range(2):
            cx.dma("pool", CMN.t[:, nt, :], Cd["c_cmn"].t[:, nt, :], [Cd["c_cmn"]], [CMN])
        for g in range(2):
            cx.dma("pool", KSA[g].t[64:128, :], Cd["c_expand"].t, [Cd["c_expand"]], [KSA[g]])
        for h in range(6):
            cx.dma("sp", QA[h].t[0:64, :], S["qn"].t[h * 64:(h + 1) * 64, :], [S["qn"]], [QA[h]])
            cx.op("pool", lambda e, h=h: e.memset(QA[h].t[64:128, :], 0.0), [], [QA[h]])
        for g in range(2):
            cx.dma("pool", KSA[g].t[0:64, :], S["ks"].t[g * 64:(g + 1) * 64, :], [S["ks"]], [KSA[g]])
            cx.dma("sp", KW[g].t[:], S["kw"].t[g * 64:(g + 1) * 64, :], [S["kw"]], [KW[g]])
            cx.dma("pool", VSX.t[:, :, g, 0:64],
                   S["tmb"].t[:, 384 + g * 64:384 + (g + 1) * 64].rearrange("(n p) c -> p n c", p=128), [S["tmb"]], [VSX])
            cx.dma("pool", VWX.t[:, :, g, 0:64],
                   S["tmb"].t[:, 512 + g * 64:512 + (g + 1) * 64].rearrange("(n p) c -> p n c", p=128), [S["tmb"]], [VWX])
        cx.op("pool", lambda e: e.memset(VSX.t[:, :, :, 64:65], 1.0), [], [VSX])
        cx.op("pool", lambda e: e.memset(VWX.t[:, :, :, 64:65], 1.0), [], [VWX])
        if dstop <= 2:
            return
        SP = [cx.ps(f"dS{i}", [128, 512], F32) for i in range(4)]
        OP = [cx.ps(f"dO{i}", [128, 512], F32) for i in range(3)]
        tpt = cx.ps("dT", [128, 8, 128], BF16)
        _tb = Buf("dT", excl=True)
        TP = [Tile(tpt.t[:, 4 * i:4 * i + 4, :], _tb) for i in range(2)]
        pTs = [cx.sb(f"dpT{i}", [128, 512], BF16) for i in range(4)]
        OACC = [cx.sb(f"dOACC{i}", [128, 4, 384], F32) for i in range(2)]
        IMP = [cx.sb(f"dIMP{i}", [128, 4, 64], F32) for i in range(2)]
        recs = [cx.sb(f"drec{i}", [128, 2, 4], F32) for i in range(4)]
        sc1 = [cx.sb(f"dsc1{i}", [128, 64], F32) for i in range(2)]
        sc2 = [cx.sb(f"dsc2{i}", [128, 64], F32) for i in range(2)]
        m8 = [cx.sb(f"dm8{i}", [128, 16], F32) for i in range(2)]
        nm = [cx.sb(f"dnm{i}", [128, 128], BF16) for i in range(2)]
        for t_ in nm:
            cx.op("pool", lambda e, t_=t_: e.memset(t_.t[:], 0.0), [], [t_])
        obf = [cx.sb(f"dobf{i}", [128, 384], BF16) for i in range(2)]
        yst = [cx.sb(f"dyst{i}", [128, 3, 512], BF16) for i in range(2)]
        cnt = {"s": 0, "o": 0, "p": 0, "r": 0, "t": 0}

        def nxt(key, lst):
            x = lst[cnt[key] % len(lst)]
            cnt[key] += 1
            return x

        items = []

        def cmp_item(c, h):
            g, rr = divmod(h, 3)
            cs = slice(c * 512, (c + 1) * 512)
            oacc, imp = OACC[c % 2], IMP[c % 2]
            nts = [0] + ([1] if c >= 4 else [])
            st = {}

            def S_():
                st["pts"] = {}
                for nt in nts:
                    sp_ = nxt("s", SP)
                    need_mask = (c <= 4) if nt == 0 else True
                    cx.mm(sp_.t[:], KCT.t[0:64, g, nt * 128:(nt + 1) * 128], QA[h].t[0:64, cs], True, True,
                          [KCT, QA[h]], [sp_])
                    if need_mask:
                        cx.mm(sp_.t[:], identb.t[:], CMN.t[:, nt, cs], False, True, [identb, CMN], [sp_], skip=True)
                    pT = nxt("p", pTs)
                    cx.act(pT.t[:], sp_.t[:], AF.Exp, [sp_], [pT], scale=0.125)
                    st["pts"][nt] = pT

            def PV_():
                pts = st["pts"]
                for q4 in range(4):
                    qt = 4 * c + q4
                    ob_ = nxt("o", OP)
                    for j, nt in enumerate(nts):
                        cx.mm(ob_.t[:, 0:129], pts[nt].t[:, q4 * 128:(q4 + 1) * 128], VCX.t[:, g, nt, :],
                              j == 0, j == len(nts) - 1, [pts[nt], VCX], [ob_])
                    rc = nxt("r", recs)
                    cx.ts("dve", rc.t[:, 0, 0:1], ob_.t[:, 64:65], 1e-30, ALU.add, [ob_], [rc])
                    cx.op("dve", lambda e, rc=rc: e.reciprocal(out=rc.t[:, 0, 0:1], in_=rc.t[:, 0, 0:1]), [rc], [rc])
                    cx.tt("dve", rc.t[:, 1, 0:1], rc.t[:, 0, 0:1], GT.t[:, qt, 3 * h:3 * h + 1], ALU.mult, [rc, GT], [rc])
                    cx.ts("dve", oacc.t[:, q4, h * 64:(h + 1) * 64], ob_.t[:, 0:64], rc.t[:, 1, 0:1], ALU.mult,
                          [ob_, rc], [oacc])
                    if rr == 0:
                        cx.ts("dve", imp.t[:, q4, :], ob_.t[:, 65:129], rc.t[:, 0, 0:1], ALU.mult, [ob_, rc], [imp])
                    else:
                        cx.stt(imp.t[:, q4, :], ob_.t[:, 65:129], rc.t[:, 0, 0:1], imp.t[:, q4, :], ALU.mult, ALU.add,
                               [ob_, rc, imp], [imp])
                if rr == 2:
                    for q4 in range(4):
                        qt = 4 * c + q4
                        s1, s2, mm8, nm_ = sc1[q4 % 2], sc2[q4 % 2], m8[q4 % 2], nm[q4 % 2]
                        cx.tt("dve", s1.t[:], imp.t[:, q4, :], SB_.t[:, qt, :], ALU.add, [imp, SB_], [s1])
                        cx.op("dve", lambda e, mm8=mm8, s1=s1: e.max(out=mm8.t[:, 0:8], in_=s1.t[:]), [s1], [mm8])
                        cx.op("dve", lambda e, mm8=mm8, s1=s1, s2=s2: e.match_replace(
                            out=s2.t[:], in_to_replace=mm8.t[:, 0:8], in_values=s1.t[:], imm_value=-1e9), [s1, mm8], [s2])
                        cx.op("dve", lambda e, mm8=mm8, s2=s2: e.max(out=mm8.t[:, 8:16], in_=s2.t[:]), [s2], [mm8])
                        cx.ts("dve", mm8.t[:, 15:16], mm8.t[:, 15:16], 0.0, ALU.max, [mm8], [mm8])
                        cx.ts("dve", nm_.t[:, 64:128], s1.t[:], mm8.t[:, 15:16], ALU.is_lt, [s1, mm8], [nm_],
                              s2=NEG, op1=ALU.mult)
                        tp = nxt("t", TP)
                        cx.tr(tp.t[:, 0, :], nm_.t[:], ident.t[:], [nm_, ident], [tp])
                        for r3 in range(3):
                            hh = 3 * g + r3
                            cx.copy("dve",
                                    QA[hh].t[64:128, qt * 128:(qt + 1) * 128], tp.t[64:128, 0, :], [tp], [QA[hh]])
            return S_, PV_

        def att_items(c, branch, h):
            g = h // 3
            oacc = OACC[c % 2]
            kts = list(range(0, 4 * c + 4) if branch == 1 else range(max(4 * c - 4, 0), 4 * c + 4))
            shared = {"first": True}
            res = []
            for kt in kts:
                lo = max(kt - 4 * c, 0)
                hi = 3 if branch == 1 else min(kt + 4 - 4 * c, 3)
                n_ = (hi - lo + 1) * 128
                q0 = c * 512 + lo * 128
                ks_ = slice(kt * 128, (kt + 1) * 128)
                st = {}

                def S_(kt=kt, lo=lo, hi=hi, n_=n_, q0=q0, ks_=ks_, st=st):
                    sp_ = nxt("s", SP)
                    if branch == 1:
                        cx.mm(sp_.t[:, 0:n_], KSA[g].t[:, ks_], QA[h].t[:, q0:q0 + n_], True, True,
                              [KSA[g], QA[h]], [sp_])
                    else:
                        cx.mm(sp_.t[:, 0:n_], KW[g].t[0:64, ks_], QA[h].t[0:64, q0:q0 + n_], True, True,
                              [KW[g], QA[h]], [sp_])
                    if kt >= 4 * c:
                        cx.mm(sp_.t[:, 0:128], identb.t[:], causb.t[:], False, True, [identb, causb], [sp_], skip=True)
                    if branch == 2 and 4 * c <= kt + 4 <= 4 * c + 3:
                        cx.mm(sp_.t[:, n_ - 128:n_], identb.t[:], upperb.t[:], False, True, [identb, upperb], [sp_],
                              skip=True)
                    pT = nxt("p", pTs)
                    cx.act(pT.t[:, 0:n_], sp_.t[:, 0:n_], AF.Exp, [sp_], [pT], scale=0.125)
                    st["pT"] = pT

                def PV_(kt=kt, lo=lo, hi=hi, st=st, last=(kt == kts[-1])):
                    if shared["first"]:
                        shared["ob"] = nxt("o", OP)
                    ob_ = shared["ob"]
                    ov = ob_.t[:, 0:260].rearrange("p (a b) -> p a b", b=65)
                    pT = st["pT"]
                    vx = VSX if branch == 1 else VWX
                    for q4 in range(lo, hi + 1):
                        cx.mm(ov[:, q4, :], pT.t[:, (q4 - lo) * 128:(q4 - lo + 1) * 128], vx.t[:, kt, g, :],
                              shared["first"], True, [pT, vx], [ob_], skip=not shared["first"])
                        shared["first"] = False
                    if last:
                        rc = nxt("r", recs)
                        cx.ts("dve", rc.t[:, 0, :], ov[:, :, 64], 1e-30, ALU.add, [ob_], [rc])
                        cx.op("dve", lambda e, rc=rc: e.reciprocal(out=rc.t[:, 0, :], in_=rc.t[:, 0, :]), [rc], [rc])
                        col = 3 * h + branch
                        cx.tt("dve", rc.t[:, 1, :], rc.t[:, 0, :], GT.t[:, 4 * c:4 * c + 4, col], ALU.mult, [rc, GT], [rc])
                        for q4 in range(4):
                            av = oacc.t[:, q4, h * 64:(h + 1) * 64]
                            cx.stt(av, ov[:, q4, 0:64], rc.t[:, 1, q4:q4 + 1], av, ALU.mult, ALU.add,
                                   [ob_, rc, oacc], [oacc])
                res.append((S_, PV_))
            return res

        def out_item(c):
            cs = slice(c * 512, (c + 1) * 512)
            oacc = OACC[c % 2]

            def PV_():
                ys = yst[c % 2]
                for q4 in range(4):
                    ob2 = obf[q4 % 2]
                    cx.copy("pool", ob2.t[:], oacc.t[:, q4, :], [oacc], [ob2])
                    tp = nxt("t", TP)
                    for j in range(3):
                        cx.tr(tp.t[:, j, :], ob2.t[:, j * 128:(j + 1) * 128], ident.t[:], [ob2, ident], [tp])
                    cx.copy("act", ys.t[:, :, q4 * 128:(q4 + 1) * 128], tp.t[:, 0:3, :], [tp], [ys])
                for j in range(3):
                    cx.dma("sp", S["yt"].t[640 + j * 128:640 + (j + 1) * 128, cs], ys.t[:, j, :], [ys], [S["yt"]])
            return (lambda: None), PV_

        for c in range(8):
            for h in range(6):
                items.append(cmp_item(c, h))
            if dstop >= 5:
                for branch in ((1, 2) if dstop >= 6 else (1,)):
                    for h in range(6):
                        items.extend(att_items(c, branch, h))
            items.append(out_item(c))
        for i in range(len(items) + 1):
            if i < len(items):
                items[i][0]()
            if i >= 1:
                items[i - 1][1]()


def phase_E(cx, l, xd, x1d, W, G, S):
    with cx.phase() as es:
        wo = load_cast(cx, "wout", W["wout"], [128, 8, D])
        g_t = load_f32(cx, "ln1g", W["ln1_g"].t.broadcast_to([128, D]), W["ln1_g"], [128, D])
        b_t = load_f32(cx, "ln1b", W["ln1_b"].t.broadcast_to([128, D]), W["ln1_b"], [128, D])
        yts = [cx.sb(f"eyt{i}", [128, 8, 512], BF16) for i in range(2)]
        xrs = [cx.sb(f"exr{i}", [128, D], F32) for i in range(3)]
        pss = [cx.ps(f"eps{i}", [128, 512], F32) for i in range(6)]
        tmps = ln_tmps(cx, es, n=3)

        def load_y(tc):
            cx.dma("sp", yts[tc % 2].t[:], S["yt"].t[:, tc * 512:(tc + 1) * 512].rearrange("(k p) t -> p k t", p=128),
                   [S["yt"]], [yts[tc % 2]])
        stages = {}

        def mm_tile(tt):
            tc, q = divmod(tt, 4)
            if q == 0 and tc + 1 < 8:
                load_y(tc + 1)
            y_ = yts[tc % 2]
            xr = xrs[tt % 3]
            cx.dma("sp", xr.t[:], xd.t[tt * 128:(tt + 1) * 128, :], [xd], [xr])
            zp = pss[(tt % 3) * 2:(tt % 3) * 2 + 2]
            for hf in range(2):
                for k in range(8):
                    cx.mm(zp[hf].t[:], y_.t[:, k, q * 128:(q + 1) * 128], wo.t[:, k, hf * 512:(hf + 1) * 512],
                          k == 0, k == 7, [y_, wo], [zp[hf]])
            stages[tt] = ln_stages(cx, zp, xr, g_t, b_t, x1d, tt * 128, tmps[tt % 3])
        load_y(0)
        mm_tile(0)
        stages[0][0]()
        stages[0][1]()
        for tt in range(NT):
            if tt + 1 < NT:
                mm_tile(tt + 1)
                stages[tt + 1][0]()
            stages[tt][2]()
            stages[tt][3]()
            if tt + 1 < NT:
                stages[tt + 1][1]()
            if tt >= 1:
                stages[tt - 1][4]()
        stages[NT - 1][4]()


def phase_F(cx, l, x1d, x2d, W, G, S):
    TC = 1024
    ident = G["ident"]
    with cx.phase() as es:
        wd = cx.sb("wd_b", [128, NF, D], BF16)
        for f in range(NF):
            cx.dma("pool", wd.t[:, f, :], W["wd"].t[:, f, :], [W["wd"]], [wd])
        cw = load_f32(cx, "cw", W["cw"].t, W["cw"], [128, NF, 3])
        cb = load_f32(cx, "cb", W["cb"].t, W["cb"], [128, NF])
        g_t = load_f32(cx, "ln2g", W["ln2_g"].t.broadcast_to([128, D]), W["ln2_g"], [128, D])
        b_t = load_f32(cx, "ln2b", W["ln2_b"].t.broadcast_to([128, D]), W["ln2_b"], [128, D])
        carry = cx.sb("carry", [128, NF, 2], F32)
        cx.op("pool", lambda e: e.memset(carry.t[:], 0.0), [], [carry])
        xT = cx.sb("x1T", [128, 8, TC], BF16)
        act = cx.sb("ffact", [128, NF, TC], BF16)
        wb = [cx.sb(f"fwb{i}", [128, 2, 8, 128], BF16) for i in range(3)]
        hts = [cx.sb(f"fhb{i}", [128, 2 + TC], F32) for i in range(2)]
        hbA = [Tile(t.t, Buf(f"fhbA{i}")) for i, t in enumerate(hts)]
        hbB = [Tile(t.t, Buf(f"fhbB{i}")) for i, t in enumerate(hts)]
        hc = [cx.sb(f"fhc{i}", [128, 512], F32) for i in range(2)]
        gl = [cx.sb(f"fgl{i}", [128, 512], F32) for i in range(2)]
        xrs = [cx.sb(f"fxr{i}", [128, D], F32) for i in range(2)]
        tmps = ln_tmps(cx, es)
        pss = [cx.ps(f"fps{i}", [128, 512], F32) for i in range(6)]
        gi = 0
        nsteps = (T // TC) * NF

        def load_w(step):
            f = step % NF
            b_ = wb[step % 3]
            cx.dma("pool", b_.t[:, 0], W["wg"].t[f], [W["wg"]], [b_])
            cx.dma("pool", b_.t[:, 1], W["wu"].t[f], [W["wu"]], [b_])
        load_w(0)
        load_w(1)
        step = 0
        for tc in range(T // TC):
            build_xT(cx, x1d, xT, ident, TC // 128, tok0=tc * TC)
            for f in range(NF):
                b_ = wb[step % 3]
                if step + 2 < nsteps:
                    load_w(step + 2)
                step += 1
                hA, hB = hbA[f % 2], hbB[f % 2]
                ht = hts[f % 2].t
                cx.copy("act", ht[:, 0:2], carry.t[:, f, :], [carry], [hA])
                for hf in range(TC // 512):
                    ts_ = slice(hf * 512, (hf + 1) * 512)
                    pg, pu = pss[(gi % 2) * 2], pss[(gi % 2) * 2 + 1]
                    c_, g_ = hc[gi % 2], gl[gi % 2]
                    gi += 1
                    for k in range(8):
                        cx.mm(pg.t[:], b_.t[:, 0, k, :], xT.t[:, k, ts_], k == 0, k == 7, [b_, xT], [pg])
                    for k in range(8):
                        cx.mm(pu.t[:], b_.t[:, 1, k, :], xT.t[:, k, ts_], k == 0, k == 7, [b_, xT], [pu])
                    hw_ = [hA] if hf == 0 else [hB]
                    hr_ = [hA] if hf == 0 else [hA, hB]
                    o = hf * 512
                    cx.copy("act", ht[:, 2 + o:514 + o], pg.t[:], [pg], hw_)
                    cx.ts("dve", c_.t[:], ht[:, 2 + o:514 + o], cw.t[:, f, 2:3], ALU.mult, hr_ + [cw, cb], [c_],
                          s2=cb.t[:, f:f + 1], op1=ALU.add)
                    cx.stt(c_.t[:], ht[:, 1 + o:513 + o], cw.t[:, f, 1:2], c_.t[:], ALU.mult, ALU.add, hr_ + [cw, c_], [c_])
                    cx.stt(c_.t[:], ht[:, o:512 + o], cw.t[:, f, 0:1], c_.t[:], ALU.mult, ALU.add, hr_ + [cw, c_], [c_])
                    cx.act(g_.t[:], c_.t[:], AF.Gelu_apprx_tanh, [c_], [g_])
                    cx.tt("dve", act.t[:, f, ts_], g_.t[:], pu.t[:], ALU.mult, [g_, pu], [act])
                cx.copy("act", carry.t[:, f, :], ht[:, TC:TC + 2], [hB], [carry])
            for q in range(TC // 128):
                tt = tc * (TC // 128) + q
                xr = xrs[tt % 2]
                cx.dma("sp", xr.t[:], x1d.t[tt * 128:(tt + 1) * 128, :], [x1d], [xr])
                zp = pss[4:6]
                for hf in range(2):
                    for f in range(NF):
                        cx.mm(zp[hf].t[:], act.t[:, f, q * 128:(q + 1) * 128], wd.t[:, f, hf * 512:(hf + 1) * 512],
                              f == 0, f == NF - 1, [act, wd], [zp[hf]])
                layer_norm_store(cx, zp, xr, g_t, b_t, x2d, tt * 128, tmps[tt % 2])

SCRATCH = {
    "rope": ([2, 128, T], F32), "vp": ([256, T], F32),
    "qr": ([384, T], BF16), "kr": ([384, T], BF16), "qn": ([384, T], BF16),
    "ks": ([128, T], BF16), "kw": ([128, T], BF16), "kc": ([128, T], BF16), "vc": ([128, T], BF16),
    "tmb": ([T, 640], BF16), "gr": ([T, 384], F32), "gt": ([T, 18], F32),
    "yt": ([1024, T], BF16), "x1": ([T, D], F32), "xmid": ([T, D], F32),
}


def build_program(layers=(0, 1), phases="ABCDEF", ext_in=(), ext_out=(), prologue=True):
    nc = bass.Bass("TRN2", target_bir_lowering=False)
    cx = Ctx(nc, ext_in, ext_out)
    xd = cx.dr("x", [T, D], F32, kind="ExternalInput")
    posd = cx.dr("pos", [1, T], I32, kind="ExternalInput")
    Cd = {k: cx.dr(k, list(v.shape), F32, kind="ExternalInput") for k, v in CONSTS.items()}
    Wd = {l: {k: cx.dr(f"{k}_{l}", shp, F32, kind="ExternalInput") for k, shp in LAYER_SHAPES.items()} for l in layers}
    S = {k: cx.dr(k, shp, dt) for k, (shp, dt) in SCRATCH.items()}
    outd = cx.dr("y", [T, D], F32, kind="ExternalOutput")
    with contextlib.ExitStack() as gs:
        cx.stack = gs
        G = {"rope": S["rope"]}
        G["ident"] = load_cast(cx, "ident", Cd["c_ident"], [128, 128])
        G["inv"] = load_f32(cx, "inv", Cd["c_inv"].t, Cd["c_inv"], [128, 1])
        G["sgn"] = load_f32(cx, "sgn", Cd["c_sgn"].t, Cd["c_sgn"], [128, 1])
        G["pw"] = load_f32(cx, "pw", Cd["c_pw"].t, Cd["c_pw"], [128, 2])
        G["prc"] = load_f32(cx, "prc", Cd["c_prc"].t, Cd["c_prc"], [128, 2, 16])
        G["C"] = Cd
        if prologue:
            prologue_rope(cx, posd, G)
        cur = xd
        for li, l in enumerate(layers):
            nxt = outd if li == len(layers) - 1 else S["xmid"]
            W = Wd[l]
            if "A" in phases:
                phase_A(cx, l, cur, W, G, S)
            if "B" in phases:
                phase_B(cx, l, W, G, S)
            if "C" in phases:
                phase_C(cx, l, W, G, S)
            if "D" in phases:
                phase_D(cx, l, W, G, S)
            if "E" in phases:
                phase_E(cx, l, cur, S["x1"], W, G, S)
            if "F" in phases:
                phase_F(cx, l, S["x1"], nxt, W, G, S)
            cur = nxt
        cx.P.barrier()
        finals = [outd.b] + [cx.dram[n].b for n in cx.ext_out]
        cx.P.emit_all(final_bufs=finals)
    return nc, cx


def make_in_maps(inputs, layers=(0, 1), cores=range(8)):
    shared = dict(CONSTS)
    for l in layers:
        for k, v in _layer_arrays(inputs, l).items():
            assert list(v.shape) == LAYER_SHAPES[k], (k, v.shape)
            shared[f"{k}_{l}"] = v.astype(np.float32, copy=False)
    maps = []
    for b in cores:
        m = dict(shared)
        m["x"] = np.ascontiguousarray(inputs["x"][b])
        m["pos"] = np.ascontiguousarray(inputs["positions"][b].reshape(1, T).astype(np.int32))
        maps.append(m)
    return maps


def kernel(**inputs):
    inputs = {k: np.asarray(v) for k, v in inputs.items()}
    nc, cx = build_program()
    maps = make_in_maps(inputs)
    res = run_bass_kernel_spmd(nc, maps, core_ids=list(range(8)))
    return np.stack([np.asarray(r["y"]) for r in res.results], 0).astype(np.float32)
```

```python
import contextlib
import math
import numpy as np
import concourse.bass as bass
import concourse.mybir as mybir
from concourse.bass_utils import run_bass_kernel_spmd

F32 = mybir.dt.float32
BF16 = mybir.dt.bfloat16
I32 = mybir.dt.int32
AF = mybir.ActivationFunctionType
ALU = mybir.AluOpType
AX = mybir.AxisListType

T = 4096
D = 1024
DEPTH = 2
NT = T // 128
DFF = 2816
NF = DFF // 128
ALPHA = (2 * DEPTH) ** 0.25
NEG = -10000.0
DBG = {}

ENGS = ("pe", "act", "dve", "pool", "sp")


class Buf:
    __slots__ = ("name", "writers", "readers", "dsem", "dcount", "is_dram", "vsem", "excl")

    def __init__(self, name, is_dram=False, excl=False):
        self.name = name
        self.is_dram = is_dram
        self.excl = excl
        self.writers = []
        self.readers = []
        self.dsem = None
        self.dcount = 0
        self.vsem = None


class Op:
    __slots__ = ("eng", "emit", "waits", "is_dma", "dbuf", "dval", "needs_inc", "val", "vsem")

    def __init__(self, eng, emit, is_dma=False):
        self.eng = eng
        self.emit = emit
        self.waits = []
        self.is_dma = is_dma
        self.dbuf = None
        self.dval = 0
        self.needs_inc = False
        self.val = 0
        self.vsem = None


class Prog:
    def __init__(self, nc):
        self.nc = nc
        self.ops = {e: [] for e in ENGS}
        self.last = {e: None for e in ENGS}
        self.dma_bufs = {}
        self.pending_bar = {e: [] for e in ENGS}
        self.seq = 0
        self.bar_seq = 0
        self.vfree = {True: [], False: []}
        self.vkind = []
        self.vcount = []
        self.rsem = []

    def _dep(self, op, prod, force=False):
        if prod is op:
            return
        if (not force) and prod.val < self.bar_seq:
            return
        if not prod.is_dma and prod.eng == op.eng:
            if op.eng in ("pe", "sp"):
                return
        op.waits.append(prod)
        if not prod.is_dma:
            prod.needs_inc = True

    @staticmethod
    def _prune(lst):
        last = {}
        for r in lst:
            last[(r.eng, r.is_dma, r.vsem)] = r
        return list(last.values())

    def op(self, eng, emit, reads=(), writes=(), dma=False):
        o = Op(eng, emit, is_dma=dma)
        self.seq += 1
        o.val = self.seq
        if self.pending_bar[eng]:
            for p in self.pending_bar[eng]:
                self._dep(o, p, force=True)
            self.pending_bar[eng] = []
        for b in reads:
            for w in b.writers:
                self._dep(o, w)
            if b.excl:
                for r in b.readers:
                    if r.eng != eng:
                        self._dep(o, r)
        for b in writes:
            for r in b.readers:
                if (not r.is_dma) and r.eng == eng and not dma:
                    continue
                self._dep(o, r)
            if not b.readers:
                for w in b.writers:
                    if w.is_dma and dma:
                        continue
                    if (not w.is_dma) and w.eng == eng and not dma:
                        continue
                    self._dep(o, w)
        if dma:
            assert len(writes) == 1
            b = writes[0]
            if b.is_dram:
                b = [r for r in reads if not r.is_dram][0]
            if b.vsem is None:
                sw = (eng == "pool")
                if self.vfree[sw]:
                    b.vsem = self.vfree[sw].pop()
                else:
                    b.vsem = len(self.vcount)
                    self.vcount.append(0)
                    self.vkind.append(sw)
                b.dcount = self.vcount[b.vsem]
            b.dcount += 16
            self.vcount[b.vsem] = b.dcount
            o.dbuf = b
            o.dval = b.dcount
            o.vsem = b.vsem
            self.dma_bufs[id(b)] = b
        for b in writes:
            if b.readers:
                b.writers = [o]
                b.readers = []
            else:
                b.writers.append(o)
                if len(b.writers) > 6:
                    b.writers = self._prune(b.writers)
        for b in reads:
            b.readers.append(o)
            if len(b.readers) > 6:
                b.readers = self._prune(b.readers)
        self.ops[eng].append(o)
        if not dma:
            self.last[eng] = o
        return o

    def dma(self, eng, out, in_, reads, writes):
        return self.op(eng, lambda e: e.dma_start(out=out, in_=in_), reads, writes, dma=True)

    def barrier(self):
        targets = []
        for e in ENGS:
            if self.last[e] is not None:
                targets.append(self.last[e])
        for b in self.dma_bufs.values():
            if b.vsem is not None:
                p = Op("sp", None, is_dma=True)
                p.dbuf = b
                p.dval = b.dcount
                p.vsem = b.vsem
                targets.append(p)
                self.vfree[self.vkind[b.vsem]].append(b.vsem)
                b.vsem = None
        self.dma_bufs = {}
        self.seq += 1
        self.bar_seq = self.seq
        for e in ENGS:
            self.pending_bar[e] = self._prune(self.pending_bar[e] + targets)

    def emit_all(self, final_bufs=()):
        nc = self.nc
        esem = {e: nc.alloc_semaphore(name=f"es_{e}") for e in ENGS}
        self.rsem = [nc.alloc_semaphore(name=f"ds_{i}") for i in range(len(self.vcount))]
        for e in ENGS:
            c = 0
            for o in self.ops[e]:
                if (not o.is_dma) and o.needs_inc:
                    c += 1
                    o.val = c
        engobj = {"pe": "tensor", "act": "scalar", "dve": "vector", "pool": "gpsimd", "sp": "sync"}
        with nc.Block() as block:
            for e in ENGS:
                def body(eng, ops=self.ops[e], e=e):
                    waited = {}
                    for o in ops:
                        need = {}
                        for p in o.waits:
                            if p.is_dma:
                                sem, val = self.rsem[p.vsem], p.dval
                            else:
                                sem, val = esem[p.eng], p.val
                            if sem is None:
                                continue
                            if need.get(sem.num, (None, 0))[1] < val:
                                need[sem.num] = (sem, val)
                        for k, (sem, val) in need.items():
                            if waited.get(k, 0) >= val:
                                continue
                            waited[k] = val
                            eng.wait_ge(sem, val)
                        ins = o.emit(eng)
                        if o.is_dma:
                            ins.then_inc(self.rsem[o.vsem], 16)
                        elif o.needs_inc:
                            ins.then_inc(esem[e], 1)
                    if e == "sp":
                        for i, sem in enumerate(self.rsem):
                            if waited.get(sem.num, 0) < self.vcount[i]:
                                eng.wait_ge(sem, self.vcount[i])

                getattr(block, engobj[e])(body)


OFF = {}
_o = 0
for _n, _w in (("v_pool", 256), ("q_ret", 384), ("k_ret", 384), ("v_ret", 384), ("g_ret", 384),
               ("q_nsa", 384), ("k_cmp", 128), ("v_cmp", 128), ("k_slc", 128), ("v_slc", 128),
               ("k_win", 128), ("v_win", 128), ("gate", 18)):
    OFF[_n] = _o
    _o += _w


def _swap_cols(cols):
    cols = np.asarray(cols).reshape(-1, 64)
    return np.concatenate([cols[:, 32:], cols[:, :32]], axis=1).reshape(-1)


def _fm_cols():
    ch = []
    for c in range(2):
        ch.append(np.arange(OFF["v_pool"] + 128 * c, OFF["v_pool"] + 128 * (c + 1)))
    for name in ("q_ret", "k_ret", "q_nsa"):
        for c in range(3):
            ch.append(np.arange(OFF[name] + 128 * c, OFF[name] + 128 * (c + 1)))
    for name in ("k_slc", "k_win", "k_cmp", "v_cmp"):
        ch.append(np.arange(OFF[name], OFF[name] + 128))
    return ch


FM_COLS = _fm_cols()
NFM = len(FM_COLS)
TM_COLS = np.concatenate([np.arange(OFF["v_ret"], OFF["v_ret"] + 384),
                          np.arange(OFF["v_slc"], OFF["v_slc"] + 128),
                          np.arange(OFF["v_win"], OFF["v_win"] + 128),
                          np.arange(OFF["g_ret"], OFF["g_ret"] + 384),
                          np.arange(OFF["gate"], OFF["gate"] + 18)])
NTM = len(TM_COLS)


def _const_tables():
    c = {}
    p = np.arange(128)
    inv = (10000.0 ** (-np.arange(0, 64, 2, dtype=np.float32) / 64)).astype(np.float32)
    c["c_inv"] = inv[p % 32].reshape(128, 1).astype(np.float32)
    c["c_sgn"] = np.where((p % 64) < 32, -1.0, 1.0).reshape(128, 1).astype(np.float32)
    h = np.arange(6, dtype=np.float64)
    lg = np.log1p(-np.power(2.0, -5.0 - h))
    i = np.arange(128, dtype=np.float64)
    dm = np.zeros((128, 6, 128), np.float32)
    for hh in range(6):
        diff = i[None, :] - i[:, None]
        dm[:, hh, :] = np.where(diff >= 0, 0.125 * np.exp(lg[hh] * np.maximum(diff, 0)), 0.0)
    c["c_dm"] = dm
    xi = np.exp(lg[:, None] * (i[None, :] + 1.0))
    zeta = 0.125 * np.exp(lg[:, None] * (127.0 - i[None, :]))
    gam = np.exp(lg * 128.0)
    xir = np.zeros((128, 3, 128), np.float32)
    zt = np.zeros((128, 3, 128), np.float32)
    gc = np.zeros((128, 3), np.float32)
    for ck in range(3):
        for hh in range(2):
            xir[hh * 64:(hh + 1) * 64, ck, :] = xi[2 * ck + hh][None, :]
            zt[:, ck, hh * 64:(hh + 1) * 64] = zeta[2 * ck + hh][:, None]
            gc[hh * 64:(hh + 1) * 64, ck] = gam[2 * ck + hh]
    c["c_xir"] = xir
    c["c_zt"] = zt
    c["c_gc"] = gc
    win = np.zeros((128, 2), np.float32)
    rc = np.zeros((128, 2, 16), np.float32)
    for ck in range(2):
        for hh in range(2):
            w = (2, 4, 8, 16)[2 * ck + hh]
            win[hh * 64:(hh + 1) * 64, ck] = 1.0 / w
            rc[hh * 64:(hh + 1) * 64, ck, :] = 1.0 / np.minimum(np.arange(16) + 1, w)
    c["c_pw"] = win
    c["c_prc"] = rc
    kk = np.arange(128)[:, None]
    qq = np.arange(128)[None, :]
    c["c_caus"] = np.where(kk > qq, NEG, 0.0).astype(np.float32)
    c["c_upper"] = np.where(kk <= qq, NEG, 0.0).astype(np.float32)
    c["c_ident"] = np.eye(128, dtype=np.float32)
    pm = np.zeros((128, 128), np.float32)
    mm_ = np.arange(128)
    pm[(mm_ % 64 + 32) % 64 + 64 * (mm_ // 64), mm_] = 1.0
    c["c_perm"] = pm
    ex = np.zeros((64, T), np.float32)
    ex[np.arange(T) // 64, np.arange(T)] = 1.0
    c["c_expand"] = ex
    n = np.arange(256)
    ends = 16 * n + 31
    cm = np.where(ends[:, None] > np.arange(T)[None, :], NEG, 0.0).astype(np.float32)
    c["c_cmn"] = np.ascontiguousarray(cm.reshape(2, 128, T).transpose(1, 0, 2))
    ci = np.arange(256)[:, None]
    sj = np.arange(64)[None, :]
    ov = np.clip(np.minimum(ci * 16 + 32, (sj + 1) * 64) - np.maximum(ci * 16, sj * 64), 0, None) / 16.0
    ov[255, :] = 0.0
    c["c_ovl"] = np.ascontiguousarray(ov.astype(np.float32).reshape(2, 128, 64).transpose(1, 0, 2))
    tq = np.arange(T)
    cur = tq // 64
    blk = np.arange(64)[None, :]
    forced = (blk == 0) | (blk == cur[:, None]) | (blk == cur[:, None] - 1)
    valid = blk * 64 <= tq[:, None]
    bias = np.where(valid, np.where(forced, 1e6, 0.0), -100.0).astype(np.float32)
    c["c_sbias"] = np.ascontiguousarray(bias.reshape(32, 128, 64).transpose(1, 0, 2))
    return c


CONSTS = _const_tables()


def _layer_arrays(inp, l):
    a = {}
    w_in = inp["w_in"][l]
    wk = w_in.reshape(8, 128, -1)
    a["wfm"] = np.ascontiguousarray(
        np.stack([wk[:, :, cols].transpose(1, 0, 2) for cols in FM_COLS], 0))
    a["wtm"] = np.ascontiguousarray(wk[:, :, TM_COLS].transpose(1, 0, 2))
    a["wout"] = np.ascontiguousarray(inp["w_out"][l].reshape(8, 128, D).transpose(1, 0, 2))
    pw = inp["pool_w"][l]
    bd = np.zeros((2, 128, 128), np.float32)
    for ck in range(2):
        for hh in range(2):
            bd[ck, hh * 64:(hh + 1) * 64, hh * 64:(hh + 1) * 64] = pw[2 * ck + hh]
    a["bd"] = bd
    a["psc"] = np.ascontiguousarray(inp["pool_scale"][l].reshape(2, 128).T)
    a["gng"] = np.ascontiguousarray(inp["ret_gn_g"][l].reshape(1, 384))
    for kv in ("k", "v"):
        w1 = inp[f"cmp_w1_{kv}"][l].reshape(32, 64, 128)
        w1d = np.concatenate([w1, w1], axis=1).transpose(1, 0, 2)
        a[f"w1{kv}"] = np.ascontiguousarray(w1d)
        a[f"b1{kv}"] = np.ascontiguousarray(inp[f"cmp_b1_{kv}"][l].reshape(128, 1))
        a[f"pos{kv}"] = np.ascontiguousarray(inp[f"cmp_pos_{kv}"][l].T)
        a[f"w2{kv}"] = np.ascontiguousarray(inp[f"cmp_w2_{kv}"][l])
    a["w2ks"] = np.ascontiguousarray(inp["cmp_w2_k"][l][:, _swap_cols(np.arange(64))])
    a["wg"] = np.ascontiguousarray(inp["ffn_w_gate"][l].reshape(8, 128, NF, 128).transpose(2, 1, 0, 3))
    a["wu"] = np.ascontiguousarray(inp["ffn_w_up"][l].reshape(8, 128, NF, 128).transpose(2, 1, 0, 3))
    a["wd"] = np.ascontiguousarray(inp["ffn_w_down"][l].reshape(NF, 128, D).transpose(1, 0, 2))
    a["cw"] = np.ascontiguousarray(inp["ffn_conv_w"][l].reshape(3, NF, 128).transpose(2, 1, 0))
    a["cb"] = np.ascontiguousarray(inp["ffn_conv_b"][l].reshape(NF, 128).T)
    for nme in ("ln1_g", "ln1_b", "ln2_g", "ln2_b"):
        a[nme] = np.ascontiguousarray(inp[nme][l].reshape(1, D))
    return a


LAYER_SHAPES = {
    "wfm": [NFM, 128, 8, 128], "wtm": [128, 8, NTM], "wout": [128, 8, D], "bd": [2, 128, 128],
    "psc": [128, 2], "gng": [1, 384],
    "w1k": [128, 32, 128], "b1k": [128, 1], "posk": [64, 32], "w2k": [128, 64],
    "w1v": [128, 32, 128], "b1v": [128, 1], "posv": [64, 32], "w2v": [128, 64], "w2ks": [128, 64],
    "wg": [NF, 128, 8, 128], "wu": [NF, 128, 8, 128], "wd": [128, NF, D], "cw": [128, NF, 3], "cb": [128, NF],
    "ln1_g": [1, D], "ln1_b": [1, D], "ln2_g": [1, D], "ln2_b": [1, D],
}


class Tile:
    __slots__ = ("t", "b")

    def __init__(self, t, b):
        self.t = t
        self.b = b


class Ctx:
    def __init__(self, nc, ext_in=(), ext_out=()):
        self.nc = nc
        self.P = Prog(nc)
        self.ext_in = set(ext_in)
        self.ext_out = set(ext_out)
        self.dram = {}
        self.stack = None
        self.uid = 0

    def dr(self, name, shape, dt, kind=None):
        if kind is None:
            kind = "ExternalInput" if name in self.ext_in else ("ExternalOutput" if name in self.ext_out else "Internal")
        t = self.nc.dram_tensor(name, list(shape), dt, kind=kind).ap()
        tl = Tile(t, Buf(name, is_dram=True))
        self.dram[name] = tl
        return tl

    def sb(self, name, shape, dt, es=None):
        self.uid += 1
        t = (es or self.stack).enter_context(self.nc.sbuf_tensor(f"{name}_{self.uid}", list(shape), dt))
        return Tile(t, Buf(name))

    def ps(self, name, shape, dt, es=None):
        self.uid += 1
        t = (es or self.stack).enter_context(self.nc.psum_tensor(f"{name}_{self.uid}", list(shape), dt))
        return Tile(t, Buf(name, excl=True))

    @contextlib.contextmanager
    def phase(self):
        old = self.stack
        with contextlib.ExitStack() as es:
            self.stack = es
            yield es
            self.P.barrier()
        self.stack = old

    def dma(self, eng, out, in_, reads, writes):
        self.P.dma(eng, out, in_, [x.b for x in reads], [x.b for x in writes])

    def op(self, eng, fn, reads, writes):
        self.P.op(eng, fn, [x.b for x in reads], [x.b for x in writes])

    def mm(self, out, lhsT, rhs, start, stop, reads, writes, skip=False):
        kw = dict(start=start, stop=stop)
        if skip:
            kw["skip_group_check"] = True
        self.op("pe", lambda e: e.matmul(out, lhsT=lhsT, rhs=rhs, **kw), reads, writes)

    def tr(self, out, in_, ident, reads, writes):
        self.op("pe", lambda e: e.transpose(out, in_, ident), reads, writes)

    def copy(self, eng, out, in_, reads, writes):
        if eng == "act":
            self.op("act", lambda e: e.copy(out=out, in_=in_), reads, writes)
        else:
            self.op(eng, lambda e: e.tensor_copy(out=out, in_=in_), reads, writes)

    def act(self, out, in_, func, reads, writes, **kw):
        self.op("act", lambda e: e.activation(out=out, in_=in_, func=func, **kw), reads, writes)

    def tt(self, eng, out, in0, in1, op, reads, writes):
        self.op(eng, lambda e: e.tensor_tensor(out=out, in0=in0, in1=in1, op=op), reads, writes)

    def ts(self, eng, out, in0, s1, op0, reads, writes, s2=None, op1=None):
        if op1 is None:
            self.op(eng, lambda e: e.tensor_scalar(out=out, in0=in0, scalar1=s1, scalar2=None, op0=op0), reads, writes)
        else:
            self.op(eng, lambda e: e.tensor_scalar(out=out, in0=in0, scalar1=s1, scalar2=s2, op0=op0, op1=op1), reads, writes)

    def stt(self, out, in0, scalar, in1, op0, op1, reads, writes):
        self.op("dve", lambda e: e.scalar_tensor_tensor(out=out, in0=in0, scalar=scalar, in1=in1, op0=op0, op1=op1),
                reads, writes)


def load_cast(cx, name, src_tile, shape, es=None, eng=None, q=None):
    b = cx.sb(name + "_b", shape, BF16, es)
    cx.dma("pool", b.t[:], src_tile.t, [src_tile], [b])
    return b


def load_f32(cx, name, src_ap, src_tile, shape, es=None, q="sp"):
    f = cx.sb(name, shape, F32, es)
    cx.dma(q, f.t[:], src_ap, [src_tile], [f])
    return f


def build_xT(cx, xd, xT, ident, ntiles, tok0=0):
    with contextlib.ExitStack() as es:
        xb = [cx.sb(f"xb{i}", [128, D], BF16, es) for i in range(3)]
        pt = [cx.ps(f"xpt{i}", [128, 8, 128], BF16, es) for i in range(2)]
        for tt in range(ntiles):
            bb, p = xb[tt % 3], pt[tt % 2]
            r0 = tok0 + tt * 128
            cx.dma("pool", bb.t[:], xd.t[r0:r0 + 128, :], [xd], [bb])
            for k in range(8):
                cx.tr(p.t[:, k, :], bb.t[:, k * 128:(k + 1) * 128], ident.t[:], [bb, ident], [p])
            cx.copy("act" if tt % 2 == 0 else "dve", xT.t[:, :, tt * 128:(tt + 1) * 128], p.t[:], [p], [xT])
        cx.P.barrier()


def layer_norm_store(cx, zps, xres, g_t, b_t, outd, r0, tmp, eng_q="pool"):
    z, st, mv, rs, o = tmp
    for hf in range(2):
        cx.stt(z.t[:, hf * 512:(hf + 1) * 512], xres.t[:, hf * 512:(hf + 1) * 512], ALPHA, zps[hf].t[:],
               ALU.mult, ALU.add, [xres, zps[hf]], [z])
    for hf in range(2):
        cx.op("dve", lambda e, hf=hf: e.bn_stats(out=st.t[:, hf, :], in_=z.t[:, hf * 512:(hf + 1) * 512]), [z], [st])
    cx.op("dve", lambda e: e.bn_aggr(out=mv.t[:], in_=st.t[:]), [st], [mv])
    cx.ts("dve", rs.t[:, 0:1], mv.t[:, 1:2], 1e-5, ALU.add, [mv], [rs])
    cx.act(rs.t[:, 0:1], rs.t[:, 0:1], AF.Sqrt, [rs], [rs])
    cx.op("dve", lambda e: e.reciprocal(out=rs.t[:, 0:1], in_=rs.t[:, 0:1]), [rs], [rs])
    cx.stt(rs.t[:, 1:2], mv.t[:, 0:1], -1.0, rs.t[:, 0:1], ALU.mult, ALU.mult, [mv, rs], [rs])
    cx.act(o.t[:], z.t[:], AF.Identity, [z, rs], [o], scale=rs.t[:, 0:1], bias=rs.t[:, 1:2])
    cx.tt("dve", o.t[:], o.t[:], g_t.t[:], ALU.mult, [o, g_t], [o])
    cx.tt("pool", o.t[:], o.t[:], b_t.t[:], ALU.add, [o, b_t], [o])
    cx.dma(eng_q, outd.t[r0:r0 + 128, :], o.t[:], [o], [outd])


def ln_stages(cx, zps, xres, g_t, b_t, outd, r0, tmp, eng_q="pool"):
    z, st, mv, rs, o = tmp

    def s1():
        for hf in range(2):
            cx.stt(z.t[:, hf * 512:(hf + 1) * 512], xres.t[:, hf * 512:(hf + 1) * 512], ALPHA, zps[hf].t[:],
                   ALU.mult, ALU.add, [xres, zps[hf]], [z])
        for hf in range(2):
            cx.op("dve", lambda e, hf=hf: e.bn_stats(out=st.t[:, hf, :], in_=z.t[:, hf * 512:(hf + 1) * 512]), [z], [st])
        cx.op("dve", lambda e: e.bn_aggr(out=mv.t[:], in_=st.t[:]), [st], [mv])
        cx.ts("dve", rs.t[:, 0:1], mv.t[:, 1:2], 1e-5, ALU.add, [mv], [rs])

    def s2():
        cx.act(rs.t[:, 0:1], rs.t[:, 0:1], AF.Sqrt, [rs], [rs])

    def s3():
        cx.op("dve", lambda e: e.reciprocal(out=rs.t[:, 0:1], in_=rs.t[:, 0:1]), [rs], [rs])
        cx.stt(rs.t[:, 1:2], mv.t[:, 0:1], -1.0, rs.t[:, 0:1], ALU.mult, ALU.mult, [mv, rs], [rs])

    def s4():
        cx.act(o.t[:], z.t[:], AF.Identity, [z, rs], [o], scale=rs.t[:, 0:1], bias=rs.t[:, 1:2])

    def s5():
        cx.tt("dve", o.t[:], o.t[:], g_t.t[:], ALU.mult, [o, g_t], [o])
        cx.tt("pool", o.t[:], o.t[:], b_t.t[:], ALU.add, [o, b_t], [o])
        cx.dma(eng_q, outd.t[r0:r0 + 128, :], o.t[:], [o], [outd])
    return s1, s2, s3, s4, s5


def ln_tmps(cx, es, n=2):
    res = []
    for i in range(n):
        res.append((cx.sb(f"lnz{i}", [128, D], F32, es), cx.sb(f"lnst{i}", [128, 2, 6], F32, es),
                    cx.sb(f"lnmv{i}", [128, 2], F32, es), cx.sb(f"lnrs{i}", [128, 2], F32, es),
                    cx.sb(f"lno{i}", [128, D], F32, es)))
    return res


def prologue_rope(cx, posd, G):
    rope = G["rope"]
    with cx.phase() as es:
        pi_ = cx.sb("posi", [128, T], I32)
        ang = cx.sb("ang", [128, T], F32)
        m = cx.sb("rm", [128, T], F32)
        o = cx.sb("ro", [128, T], F32)
        cx.dma("sp", pi_.t[:], posd.t.broadcast_to([128, T]), [posd], [pi_])
        cx.copy("dve", ang.t[:], pi_.t[:], [pi_], [ang])
        cx.ts("dve", ang.t[:], ang.t[:], G["inv"].t[:, 0:1], ALU.mult, [ang, G["inv"]], [ang])
        ki = cx.sb("rki", [128, T], I32)
        C1 = 6.28125
        C2 = 2.0 * math.pi - 6.28125
        for which, shift in ((0, 0.25), (1, 0.0)):
            cx.ts("dve", m.t[:], ang.t[:], 1.0 / (2.0 * math.pi), ALU.mult, [ang], [m], s2=shift, op1=ALU.add)
            cx.copy("dve", ki.t[:], m.t[:], [m], [ki])
            cx.copy("dve", m.t[:], ki.t[:], [ki], [m])
            cx.stt(o.t[:], m.t[:], -C1, ang.t[:], ALU.mult, ALU.add, [m, ang], [o])
            cx.stt(o.t[:], m.t[:], -C2, o.t[:], ALU.mult, ALU.add, [m, o], [o])
            if which == 0:
                cx.ts("dve", o.t[:], o.t[:], 0.5 * math.pi, ALU.add, [o], [o])
            cx.ts("dve", o.t[:], o.t[:], math.pi, ALU.min, [o], [o], s2=-math.pi, op1=ALU.max)
            cx.act(o.t[:], o.t[:], AF.Sin, [o], [o])
            if which == 1:
                cx.ts("dve", o.t[:], o.t[:], G["sgn"].t[:, 0:1], ALU.mult, [o, G["sgn"]], [o])
            cx.dma("sp", rope.t[which], o.t[:], [o], [rope])


def phase_A(cx, l, xd, W, G, S):
    ident = G["ident"]
    rope = G["rope"]
    with cx.phase():
        xT = cx.sb("xT", [128, 8, T], BF16)
        build_xT(cx, xd, xT, ident, NT)
        with cx.phase() as es:
          if DBG.get("tm", True):
              wtm = load_cast(cx, "wtm", W["wtm"], [128, 8, NTM])
              pss = [cx.ps(f"tmps{i}", [128, 512], F32) for i in range(6)]
              ob = [cx.sb(f"tmob{i}", [128, 640], BF16) for i in range(2)]
              og = [cx.sb(f"tmog{i}", [128, 384], F32) for i in range(2)]
              ogt = [cx.sb(f"tmogt{i}", [128, 18], F32) for i in range(2)]
              for tt in range(NT):
                  p0, p1, p2 = pss[(tt % 2) * 3:(tt % 2) * 3 + 3]
                  for (pp, c0, c1) in ((p0, 0, 512), (p1, 512, 1024), (p2, 1024, NTM)):
                      for k in range(8):
                          cx.mm(pp.t[:, 0:c1 - c0], xT.t[:, k, tt * 128:(tt + 1) * 128], wtm.t[:, k, c0:c1],
                                k == 0, k == 7, [xT, wtm], [pp])
                  b_, g_, t_ = ob[tt % 2], og[tt % 2], ogt[tt % 2]
                  cx.copy("dve", b_.t[:, 0:512], p0.t[:], [p0], [b_])
                  cx.copy("dve", b_.t[:, 512:640], p1.t[:, 0:128], [p1], [b_])
                  cx.act(g_.t[:], p1.t[:, 128:512], AF.Silu, [p1], [g_])
                  cx.act(t_.t[:], p2.t[:, 0:18], AF.Sigmoid, [p2], [t_])
                  r0 = tt * 128
                  cx.dma("sp", S["tmb"].t[r0:r0 + 128, :], b_.t[:], [b_], [S["tmb"]])
                  cx.dma("sp", S["gr"].t[r0:r0 + 128, :], g_.t[:], [g_], [S["gr"]])
                  cx.dma("sp", S["gt"].t[r0:r0 + 128, :], t_.t[:], [t_], [S["gt"]])
        with cx.phase() as es:
            C = cx.sb("ropeC", [128, T], F32)
            Sn = cx.sb("ropeS", [128, T], F32)
            cx.dma("sp", C.t[:], rope.t[0], [rope], [C])
            cx.dma("sp", Sn.t[:], rope.t[1], [rope], [Sn])
            perm = load_cast(cx, "perm", G["C"]["c_perm"], [128, 128])
            wb = [cx.sb(f"wb{i}", [128, 8, 128], BF16) for i in range(3)]
            pss = [cx.ps(f"fmps{i}", [128, 512], F32) for i in range(5)]
            ps2 = [cx.ps(f"fmps2{i}", [128, 512], F32) for i in range(3)]
            ost = [cx.sb(f"fmo{i}", [128, T], BF16) for i in range(2)]
            vst = [cx.sb(f"fmv{i}", [128, 512], F32) for i in range(2)]
            qbs = [cx.sb(f"fmqb{i}", [128, 512], BF16) for i in range(3)]
            t1s = [cx.sb(f"fmt1{i}", [128, 512], F32) for i in range(2)]
            t2s = [cx.sb(f"fmt2{i}", [128, 512], F32) for i in range(2)]
            units = [(0, S["vp"], 0, False), (1, S["vp"], 128, False)]
            ci = 2
            for dest in ("qr", "kr", "qn"):
                for c in range(3):
                    units.append((ci, S[dest], 128 * c, True))
                    ci += 1
            units.append((ci, S["ks"], 0, True))
            units.append((ci + 1, S["kw"], 0, True))
            units.append((ci + 2, S["kc"], 0, False))
            units.append((ci + 3, S["vc"], 0, False))

            def load_w(ui):
                cx.dma("pool", wb[ui % 3].t[:], W["wfm"].t[units[ui][0]], [W["wfm"]], [wb[ui % 3]])
            load_w(0)
            load_w(1)
            seq = [(ui, tc) for ui in range(len(units)) for tc in range(8)]
            state = {}

            def stage1(idx):
                ui, tc = seq[idx]
                cid, dest, row0, is_rope = units[ui]
                if tc == 0 and ui + 2 < len(units):
                    load_w(ui + 2)
                ts_ = slice(tc * 512, (tc + 1) * 512)
                pp = pss[idx % 5]
                for k in range(8):
                    cx.mm(pp.t[:], wb[ui % 3].t[:, k, :], xT.t[:, k, ts_], k == 0, k == 7, [wb[ui % 3], xT], [pp])
                state[idx] = pp
                if is_rope:
                    qb = qbs[idx % 3]
                    cx.copy("act", qb.t[:], pp.t[:], [pp], [qb])

            def stage2(idx):
                ui, tc = seq[idx]
                cid, dest, row0, is_rope = units[ui]
                ts_ = slice(tc * 512, (tc + 1) * 512)
                pp = state.pop(idx)
                o_ = ost[ui % 2]
                if is_rope:
                    qb, p2 = qbs[idx % 3], ps2[idx % 3]
                    cx.mm(p2.t[:], perm.t[:], qb.t[:], True, True, [perm, qb], [p2])
                    t1, t2 = t1s[idx % 2], t2s[idx % 2]
                    cx.tt("dve", t1.t[:], pp.t[:], C.t[:, ts_], ALU.mult, [pp, C], [t1])
                    cx.tt("dve", t2.t[:], p2.t[:], Sn.t[:, ts_], ALU.mult, [p2, Sn], [t2])
                    cx.tt("pool", o_.t[:, ts_], t1.t[:], t2.t[:], ALU.add, [t1, t2], [o_])
                elif dest is S["vp"]:
                    v_ = vst[tc % 2]
                    cx.copy("act", v_.t[:], pp.t[:], [pp], [v_])
                    cx.dma("sp", dest.t[row0:row0 + 128, ts_], v_.t[:], [v_], [dest])
                else:
                    cx.copy("act", o_.t[:, ts_], pp.t[:], [pp], [o_])
                if tc == 7 and dest is not S["vp"]:
                    cx.dma("sp", dest.t[row0:row0 + 128, :], o_.t[:], [o_], [dest])
            for idx in range(len(seq) + 1):
                if idx < len(seq):
                    stage1(idx)
                if idx >= 1:
                    stage2(idx - 1)


def phase_B(cx, l, W, G, S):
    with cx.phase():
        psc = load_f32(cx, "psc", W["psc"].t, W["psc"], [128, 2])
        pw = G["pw"]
        prc = G["prc"]
        pss = [cx.ps(f"bps{i}", [128, 512], F32) for i in range(4)]
        for ck in range(2):
            bd = load_cast(cx, f"bd{ck}", Tile(W["bd"].t[ck], W["bd"].b), [128, 128])
            v = cx.sb(f"pv{ck}", [128, 16 + T], F32)
            s2 = cx.sb(f"ps2{ck}", [128, 16 + T], F32)
            s4 = cx.sb(f"ps4{ck}", [128, 16 + T], F32)
            mx = cx.sb(f"pmx{ck}", [128, T], BF16)
            o = cx.sb(f"pbo{ck}", [128, T], BF16)
            for t_ in (v, s2, s4):
                cx.op("pool", lambda e, t_=t_: e.memset(t_.t[:, 0:16], 0.0), [], [t_])
            cx.dma("sp", v.t[:, 16:], S["vp"].t[ck * 128:(ck + 1) * 128, :], [S["vp"]], [v])
            if ck == 0:
                cx.tt("dve", s2.t[:, 16:], v.t[:, 16:], v.t[:, 15:15 + T], ALU.add, [v], [s2])
                cx.tt("dve", s4.t[64:128, 16:], s2.t[64:128, 16:], s2.t[64:128, 14:14 + T], ALU.add, [s2], [s4])
                lo, hi = s2, s4
            else:
                cx.tt("dve", s2.t[:, 16:], v.t[:, 16:], v.t[:, 15:15 + T], ALU.add, [v], [s2])
                cx.tt("dve", s4.t[:, 16:], s2.t[:, 16:], s2.t[:, 14:14 + T], ALU.add, [s2], [s4])
                cx.tt("dve", s2.t[:, 16:], s4.t[:, 16:], s4.t[:, 12:12 + T], ALU.add, [s4], [s2])
                cx.tt("dve", s4.t[64:128, 16:], s2.t[64:128, 16:], s2.t[64:128, 8:8 + T], ALU.add, [s2], [s4])
                lo, hi = s2, s4
            for (src, r) in ((lo, slice(0, 64)), (hi, slice(64, 128))):
                cx.stt(mx.t[r, :], src.t[r, 16:], pw.t[r, ck:ck + 1], v.t[r, 16:], ALU.mult, ALU.subtract,
                       [src, pw, v], [mx])
                cx.tt("dve", src.t[r, 0:16], src.t[r, 16:32], prc.t[r, ck, :], ALU.mult, [src, prc, mx], [src])
                cx.tt("dve", mx.t[r, 0:16], src.t[r, 0:16], v.t[r, 16:32], ALU.subtract, [src, v], [mx])
            for tc in range(8):
                ts_ = slice(tc * 512, (tc + 1) * 512)
                pp = pss[tc % 4]
                cx.mm(pp.t[:], bd.t[:], mx.t[:, ts_], True, True, [bd, mx], [pp])
                cx.act(o.t[:, ts_], pp.t[:], AF.Copy, [pp, psc], [o], scale=psc.t[:, ck:ck + 1])
            cx.dma("sp", S["yt"].t[ck * 128:(ck + 1) * 128, :], o.t[:], [o], [S["yt"]])


def phase_C(cx, l, W, G, S):
    Cd = G["C"]
    ident = G["ident"]
    with cx.phase() as es:
        dm = load_f32(cx, "dm", Cd["c_dm"].t, Cd["c_dm"], [128, 6, 128])
        xir = load_f32(cx, "xir", Cd["c_xir"].t, Cd["c_xir"], [128, 3, 128])
        zt = load_f32(cx, "zt", Cd["c_zt"].t, Cd["c_zt"], [128, 3, 128])
        gc = load_f32(cx, "gc", Cd["c_gc"].t, Cd["c_gc"], [128, 3])
        gng = load_f32(cx, "gng", W["gng"].t.broadcast_to([128, 384]), W["gng"], [128, 384])
        bankA = [cx.ps(f"cA{i}", [128, 512], F32) for i in range(2)]
        bankO = [cx.ps(f"cO{i}", [128, 512], F32) for i in range(2)]
        bankV = [cx.ps(f"cV{i}", [128, 512], F32) for i in range(2)]
        bankY = cx.ps("cY", [128, 8, 128], BF16)
        bankK = cx.ps("cK", [128, 8, 128], BF16)
        ktp = [bankK.t[:, i, :] for i in range(3)]
        opv2 = [[b.t[:, ck * 128:(ck + 1) * 128] for ck in range(3)] for b in bankO]
        kvp2 = [[b.t[:, ck * 128:(ck + 1) * 128] for ck in range(3)] for b in bankV]
        ytp = [bankY.t[:, i, :] for i in range(3)]
        qT, kT, qx, v, vz, R, Rb = [], [], [], [], [], [], []
        for ck in range(3):
            rows = slice(ck * 128, (ck + 1) * 128)
            qT.append(cx.sb(f"cqT{ck}", [128, T], BF16))
            kT.append(cx.sb(f"ckT{ck}", [128, T], BF16))
            qx.append(cx.sb(f"cqx{ck}", [128, T], BF16))
            v.append(cx.sb(f"cv{ck}", [128, NT, 128], BF16))
            vz.append(cx.sb(f"cvz{ck}", [128, NT, 128], BF16))
            R.append(cx.sb(f"cR{ck}", [128, 64], F32))
            Rb.append(cx.sb(f"cRb{ck}", [128, 64], BF16))
            cx.dma("sp", qT[ck].t[:], S["qr"].t[rows, :], [S["qr"]], [qT[ck]])
            cx.dma("sp", kT[ck].t[:], S["kr"].t[rows, :], [S["kr"]], [kT[ck]])
            cx.dma("pool", v[ck].t[:], S["tmb"].t[:, ck * 128:(ck + 1) * 128].rearrange("(n p) c -> p n c", p=128),
                   [S["tmb"]], [v[ck]])
            cx.tt("pool", qx[ck].t[:].rearrange("p (n i) -> p n i", i=128), qT[ck].t[:].rearrange("p (n i) -> p n i", i=128),
                  xir.t[:, ck:ck + 1, :].broadcast_to([128, NT, 128]), ALU.mult, [qT[ck], xir], [qx[ck]])
            cx.tt("pool", vz[ck].t[:], v[ck].t[:], zt.t[:, ck:ck + 1, :].broadcast_to([128, NT, 128]), ALU.mult,
                  [v[ck], zt], [vz[ck]])
            cx.op("pool", lambda e, ck=ck: e.memset(R[ck].t[:], 0.0), [], [R[ck]])
            cx.op("pool", lambda e, ck=ck: e.memset(Rb[ck].t[:], 0.0), [], [Rb[ck]])
        NB = 2
        sgs = [[cx.sb(f"csg{ck}_{i}", [128, 8, 128], F32) for i in range(2)] for ck in range(3)]
        kts = [[cx.sb(f"ckt{ck}_{i}", [128, 128], BF16) for i in range(NB)] for ck in range(3)]
        sms = [[cx.sb(f"csm{hh}_{i}", [128, 3, 128], BF16) for i in range(NB)] for hh in range(2)]
        sts = [[cx.sb(f"cst{ck}_{i}", [128, 2, 6], F32) for i in range(NB)] for ck in range(3)]
        mvs = [[cx.sb(f"cmv{ck}_{i}", [128, 2, 2], F32) for i in range(NB)] for ck in range(3)]
        rss = [[cx.sb(f"crs{ck}_{i}", [128, 2, 2], F32) for i in range(NB)] for ck in range(3)]
        ons = [[cx.sb(f"con{ck}_{i}", [128, 128], F32) for i in range(NB)] for ck in range(3)]
        onb = [[cx.sb(f"conb{ck}_{i}", [128, 128], BF16) for i in range(NB)] for ck in range(3)]
        yts = [[cx.sb(f"cyt{ck}_{i}", [128, 128], BF16) for i in range(NB)] for ck in range(3)]

        def load_sg(ck, blk):
            cx.dma("pool", sgs[ck][blk % 2].t[:],
                   S["gr"].t[blk * 1024:(blk + 1) * 1024, ck * 128:(ck + 1) * 128].rearrange("(n p) c -> p n c", p=128),
                   [S["gr"]], [sgs[ck][blk % 2]])
        for ck in range(3):
            load_sg(ck, 0)
        def ctx(n):
            return slice(n * 128, (n + 1) * 128), n % NB, opv2[n % 2], kvp2[n % 2], bankO[n % 2], bankV[n % 2]

        def st_A(n):
            ns, i, opv, kvp, BO, BV = ctx(n)
            for ck in range(3):
                cx.tr(ktp[ck], kT[ck].t[:, ns], ident.t[:], [kT[ck], ident], [bankK])
                for hh in range(2):
                    r = slice(hh * 64, (hh + 1) * 64)
                    cx.mm(bankA[hh].t[:, ck * 128:(ck + 1) * 128], kT[ck].t[r, ns], qT[ck].t[r, ns], True, True,
                          [kT[ck], qT[ck]], [bankA[hh]])

        def st_1(n):
            ns, i, opv, kvp, BO, BV = ctx(n)
            for ck in range(3):
                cx.copy("act", kts[ck][i].t[:], ktp[ck], [bankK], [kts[ck][i]])
            for hh in range(2):
                cx.tt("dve", sms[hh][i].t[:], bankA[hh].t[:, 0:384].rearrange("p (a b) -> p a b", b=128),
                      dm.t[:, hh::2, :], ALU.mult, [bankA[hh], dm], [sms[hh][i]])

        def st_B(n):
            ns, i, opv, kvp, BO, BV = ctx(n)
            for ck in range(3):
                for hh in range(2):
                    r = slice(hh * 64, (hh + 1) * 64)
                    cx.mm(opv[ck][:, r], sms[hh][i].t[:, ck, :], v[ck].t[:, n, r], True, False,
                          [sms[hh][i], v[ck]], [BO])
                    cx.mm(opv[ck][:, r], qx[ck].t[r, ns], Rb[ck].t[r, :], False, True, [qx[ck], Rb[ck]], [BO])
                cx.mm(kvp[ck], kts[ck][i].t[:], vz[ck].t[:, n, :], True, True, [kts[ck][i], vz[ck]], [BV])

        def st_3a(n):
            ns, i, opv, kvp, BO, BV = ctx(n)
            for ck in range(3):
                for hh in range(2):
                    r = slice(hh * 64, (hh + 1) * 64)
                    cx.stt(R[ck].t[r, :], R[ck].t[r, :], gc.t[r, ck:ck + 1], kvp[ck][r, r], ALU.mult, ALU.add,
                           [R[ck], gc, BV], [R[ck]])
                cx.copy("pool", Rb[ck].t[:], R[ck].t[:], [R[ck]], [Rb[ck]])
            for ck in range(3):
                st, mv, rs = sts[ck][i], mvs[ck][i], rss[ck][i]
                for hh in range(2):
                    r = slice(hh * 64, (hh + 1) * 64)
                    cx.op("dve", lambda e, hh=hh, r=r, st=st, ck=ck, opv_=opv: e.bn_stats(out=st.t[:, hh, :], in_=opv_[ck][:, r]), [BO], [st])
                    cx.op("dve", lambda e, hh=hh, st=st, mv=mv: e.bn_aggr(out=mv.t[:, hh, :], in_=st.t[:, hh, :]), [st], [mv])
                cx.ts("dve", rs.t[:, 0, :], mv.t[:, :, 1], 1e-5, ALU.add, [mv], [rs])

        def st_sq(n):
            i = n % NB
            for ck in range(3):
                rs = rss[ck][i]
                cx.act(rs.t[:, 0, :], rs.t[:, 0, :], AF.Sqrt, [rs], [rs])

        def st_3b(n):
            i = n % NB
            for ck in range(3):
                rs, mv = rss[ck][i], mvs[ck][i]
                cx.op("dve", lambda e, rs=rs: e.reciprocal(out=rs.t[:, 0, :], in_=rs.t[:, 0, :]), [rs], [rs])
                cx.stt(rs.t[:, 1, :], mv.t[:, :, 0], -1.0, rs.t[:, 0, :], ALU.mult, ALU.mult, [mv, rs], [rs])

        def st_4(n):
            ns, i, opv, kvp, BO, BV = ctx(n)
            for ck in range(3):
                rs, on, ob = rss[ck][i], ons[ck][i], onb[ck][i]
                for hh in range(2):
                    r = slice(hh * 64, (hh + 1) * 64)
                    cx.act(on.t[:, r], opv[ck][:, r], AF.Identity, [BO, rs], [on], scale=rs.t[:, 0, hh:hh + 1],
                           bias=rs.t[:, 1, hh:hh + 1])
                cx.tt("pool", on.t[:], on.t[:], gng.t[:, ck * 128:(ck + 1) * 128], ALU.mult, [on, gng], [on])
                cx.tt("pool", ob.t[:], on.t[:], sgs[ck][(n // 8) % 2].t[:, n % 8, :], ALU.mult, [on, sgs[ck][(n // 8) % 2]], [ob])

        def st_C(n):
            ns, i, opv, kvp, BO, BV = ctx(n)
            for ck in range(3):
                cx.tr(ytp[ck], onb[ck][i].t[:], ident.t[:], [onb[ck][i], ident], [bankY])
            for ck in range(3):
                cx.copy("act", yts[ck][i].t[:], ytp[ck], [bankY], [yts[ck][i]])
                cx.dma("sp", S["yt"].t[256 + ck * 128:256 + (ck + 1) * 128, ns], yts[ck][i].t[:], [yts[ck][i]], [S["yt"]])

        st_A(0)
        st_1(0)
        for n in range(NT):
            st_B(n)
            if n >= 1:
                st_C(n - 1)
            st_3a(n)
            st_sq(n)
            if n + 1 < NT:
                st_A(n + 1)
                st_1(n + 1)
            st_3b(n)
            st_4(n)
            if n % 8 == 0 and n // 8 + 1 < NT // 8:
                for ck in range(3):
                    load_sg(ck, n // 8 + 1)
        st_C(NT - 1)


def phase_D(cx, l, W, G, S):
    Cd = G["C"]
    ident = G["ident"]
    rope = G["rope"]
    with cx.phase() as es:
        KCT = cx.sb("KCT", [64, 2, 256], BF16)
        VCX = cx.sb("VCX", [128, 2, 2, 129], BF16)
        with cx.phase():
            Cc = cx.sb("dCc", [64, T], F32)
            Sc = cx.sb("dSc", [64, T], F32)
            cx.dma("sp", Cc.t[:], rope.t[0][0:64, :], [rope], [Cc])
            cx.dma("sp", Sc.t[:], rope.t[1][0:64, :], [rope], [Sc])
            ovl = load_f32(cx, "ovl", Cd["c_ovl"].t, Cd["c_ovl"], [128, 2, 64])
            hps = [cx.ps(f"dhp{i}", [128, 512], F32) for i in range(2)]
            cps = cx.ps("dcp", [128, 512], F32)
            kp = cx.ps("dkp", [128, 2, 256], F32)
            ksp = cx.ps("dksp", [128, 2, 256], F32)
            vps = [cx.ps(f"dvp{i}", [128, 512], F32) for i in range(2)]
            cx.op("pool", lambda e: e.memset(KCT.t[:], 0.0), [], [KCT])
            cx.op("pool", lambda e: e.memset(VCX.t[:, :, :, 64:65], 1.0), [], [VCX])
            for g in range(2):
                cx.copy("pool", VCX.t[:, g, :, 65:129], ovl.t[:], [ovl], [VCX])
            for kv in ("k", "v"):
                src = S["kc"] if kv == "k" else S["vc"]
                kvT = cx.sb(f"dkvT{kv}", [128, T], BF16)
                cx.dma("sp", kvT.t[:], src.t, [src], [kvT])
                w1 = load_cast(cx, f"w1{kv}", W[f"w1{kv}"], [128, 32, 128])
                pos = load_cast(cx, f"pos{kv}", W[f"pos{kv}"], [64, 32], eng="dve")
                b1 = load_f32(cx, f"b1{kv}", W[f"b1{kv}"].t, W[f"b1{kv}"], [128, 1])
                w2 = load_cast(cx, f"w2{kv}", W[f"w2{kv}"], [128, 64], eng="dve")
                cb = cx.sb(f"dcb{kv}", [128, 1], F32)
                h1 = cx.sb(f"dh1{kv}", [128, 2, 256], BF16)
                cx.op("pool", lambda e, h1=h1: e.memset(h1.t[:], 0.0), [], [h1])
                for i in range(32):
                    cx.mm(cps.t[:, 0:1], w1.t[0:64, i, :], pos.t[0:64, i:i + 1], i == 0, i == 31, [w1, pos], [cps])
                cx.tt("dve", cb.t[:], cps.t[:, 0:1], b1.t[:], ALU.add, [cps, b1], [cb])
                for g in range(2):
                    r = slice(g * 64, (g + 1) * 64)
                    for i in range(32):
                        cx.mm(hps[g].t[:, 0:255], w1.t[r, i, :], kvT.t[r, i:i + 16 * 254 + 1:16], i == 0, i == 31,
                              [w1, kvT], [hps[g]])
                    cx.act(h1.t[:, g, 0:255], hps[g].t[:, 0:255], AF.Gelu_apprx_tanh, [hps[g], cb], [h1],
                           bias=cb.t[:, 0:1])
                if kv == "k":
                    w2s = load_cast(cx, "w2ks", W["w2ks"], [128, 64], eng="dve")
                    cx.mm(kp.t[0:64], w2.t[:], h1.t[:], True, True, [w2, h1], [kp])
                    cx.mm(ksp.t[0:64], w2s.t[:], h1.t[:], True, True, [w2s, h1], [ksp])
                    t1 = cx.sb("dkt1", [64, 2, 255], F32)
                    t2 = cx.sb("dkt2", [64, 2, 255], F32)
                    cview = Cc.t[:, 31::16].unsqueeze(1).broadcast_to([64, 2, 255])
                    sview = Sc.t[:, 31::16].unsqueeze(1).broadcast_to([64, 2, 255])
                    cx.tt("dve", t1.t[:], kp.t[0:64, :, 0:255], cview, ALU.mult, [kp, Cc], [t1])
                    cx.tt("dve", t2.t[:], ksp.t[0:64, :, 0:255], sview, ALU.mult, [ksp, Sc], [t2])
                    cx.tt("dve", KCT.t[:, :, 0:255], t1.t[:], t2.t[:], ALU.add, [t1, t2], [KCT])
                else:
                    for g in range(2):
                        for nt in range(2):
                            vp_ = vps[(g * 2 + nt) % 2]
                            cx.mm(vp_.t[:, 0:64], h1.t[:, g, nt * 128:(nt + 1) * 128], w2.t[:], True, True, [h1, w2], [vp_])
                            cx.copy("act", VCX.t[:, g, nt, 0:64], vp_.t[:, 0:64], [vp_], [VCX])
        dstop = DBG.get("d_stop", 9)
        if dstop <= 1:
            return
        identb = ident
        QA = [cx.sb(f"QA{h}", [128, T], BF16) for h in range(6)]
        KSA = [cx.sb(f"KSA{g}", [128, T], BF16) for g in range(2)]
        KW = [cx.sb(f"KW{g}", [64, T], BF16) for g in range(2)]
        VSX = cx.sb("VSX", [128, NT, 2, 65], BF16)
        VWX = cx.sb("VWX", [128, NT, 2, 65], BF16)
        CMN = cx.sb("CMN", [128, 2, T], BF16)
        SB_ = load_f32(cx, "sbias", Cd["c_sbias"].t, Cd["c_sbias"], [128, NT, 64])
        GT = cx.sb("GTs", [128, NT, 18], F32)
        cx.dma("sp", GT.t[:], S["gt"].t.rearrange("(n p) c -> p n c", p=128), [S["gt"]], [GT])
        causb = cx.sb("causb", [128, 128], BF16)
        upperb = cx.sb("upperb", [128, 128], BF16)
        cx.dma("pool", causb.t[:], Cd["c_caus"].t, [Cd["c_caus"]], [causb])
        cx.dma("pool", upperb.t[:], Cd["c_upper"].t, [Cd["c_upper"]], [upperb])
        for nt in range(2):
            cx.dma("pool", CMN.t[:, nt, :], Cd["c_cmn"].t[:, nt, :], [Cd["c_cmn"]], [CMN])
        for g in range(2):
            cx.dma("pool", KSA[g].t[64:128, :], Cd["c_expand"].t, [Cd["c_expand"]], [KSA[g]])
        for h in range(6):
            cx.dma("sp", QA[h].t[0:64, :], S["qn"].t[h * 64:(h + 1) * 64, :], [S["qn"]], [QA[h]])
            cx.op("pool", lambda e, h=h: e.memset(QA[h].t[64:128, :], 0.0), [], [QA[h]])
        for g in range(2):
            cx.dma("pool", KSA[g].t[0:64, :], S["ks"].t[g * 64:(g + 1) * 64, :], [S["ks"]], [KSA[g]])
            cx.dma("sp", KW[g].t[:], S["kw"].t[g * 64:(g + 1) * 64, :], [S["kw"]], [KW[g]])
            cx.dma("pool", VSX.t[:, :, g, 0:64],
                   S["tmb"].t[:, 384 + g * 64:384 + (g + 1) * 64].rearrange("(n p) c -> p n c", p=128), [S["tmb"]], [VSX])
            cx.dma("pool", VWX.t[:, :, g, 0:64],
                   S["tmb"].t[:, 512 + g * 64:512 + (g + 1) * 64].rearrange("(n p) c -> p n c", p=128), [S["tmb"]], [VWX])
        cx.op("pool", lambda e: e.memset(VSX.t[:, :, :, 64:65], 1.0), [], [VSX])
        cx.op("pool", lambda e: e.memset(VWX.t[:, :, :, 64:65], 1.0), [], [VWX])
        if dstop <= 2:
            return
        SP = [cx.ps(f"dS{i}", [128, 512], F32) for i in range(4)]
        OP = [cx.ps(f"dO{i}", [128, 512], F32) for i in range(3)]
        tpt = cx.ps("dT", [128, 8, 128], BF16)
        _tb = Buf("dT", excl=True)
        TP = [Tile(tpt.t[:, 4 * i:4 * i + 4, :], _tb) for i in range(2)]
        pTs = [cx.sb(f"dpT{i}", [128, 512], BF16) for i in range(4)]
        OACC = [cx.sb(f"dOACC{i}", [128, 4, 384], F32) for i in range(2)]
        IMP = [cx.sb(f"dIMP{i}", [128, 4, 64], F32) for i in range(2)]
        recs = [cx.sb(f"drec{i}", [128, 2, 4], F32) for i in range(4)]
        sc1 = [cx.sb(f"dsc1{i}", [128, 64], F32) for i in range(2)]
        sc2 = [cx.sb(f"dsc2{i}", [128, 64], F32) for i in range(2)]
        m8 = [cx.sb(f"dm8{i}", [128, 16], F32) for i in range(2)]
        nm = [cx.sb(f"dnm{i}", [128, 128], BF16) for i in range(2)]
        for t_ in nm:
            cx.op("pool", lambda e, t_=t_: e.memset(t_.t[:], 0.0), [], [t_])
        obf = [cx.sb(f"dobf{i}", [128, 384], BF16) for i in range(2)]
        yst = [cx.sb(f"dyst{i}", [128, 3, 512], BF16) for i in range(2)]
        cnt = {"s": 0, "o": 0, "p": 0, "r": 0, "t": 0}

        def nxt(key, lst):
            x = lst[cnt[key] % len(lst)]
            cnt[key] += 1
            return x

        items = []

        def cmp_item(c, h):
            g, rr = divmod(h, 3)
            cs = slice(c * 512, (c + 1) * 512)
            oacc, imp = OACC[c % 2], IMP[c % 2]
            nts = [0] + ([1] if c >= 4 else [])
            st = {}

            def S_():
                st["pts"] = {}
                for nt in nts:
                    sp_ = nxt("s", SP)
                    need_mask = (c <= 4) if nt == 0 else True
                    cx.mm(sp_.t[:], KCT.t[0:64, g, nt * 128:(nt + 1) * 128], QA[h].t[0:64, cs], True, True,
                          [KCT, QA[h]], [sp_])
                    if need_mask:
                        cx.mm(sp_.t[:], identb.t[:], CMN.t[:, nt, cs], False, True, [identb, CMN], [sp_], skip=True)
                    pT = nxt("p", pTs)
                    cx.act(pT.t[:], sp_.t[:], AF.Exp, [sp_], [pT], scale=0.125)
                    st["pts"][nt] = pT

            def PV_():
                pts = st["pts"]
                for q4 in range(4):
                    qt = 4 * c + q4
                    ob_ = nxt("o", OP)
                    for j, nt in enumerate(nts):
                        cx.mm(ob_.t[:, 0:129], pts[nt].t[:, q4 * 128:(q4 + 1) * 128], VCX.t[:, g, nt, :],
                              j == 0, j == len(nts) - 1, [pts[nt], VCX], [ob_])
                    rc = nxt("r", recs)
                    cx.ts("dve", rc.t[:, 0, 0:1], ob_.t[:, 64:65], 1e-30, ALU.add, [ob_], [rc])
                    cx.op("dve", lambda e, rc=rc: e.reciprocal(out=rc.t[:, 0, 0:1], in_=rc.t[:, 0, 0:1]), [rc], [rc])
                    cx.tt("dve", rc.t[:, 1, 0:1], rc.t[:, 0, 0:1], GT.t[:, qt, 3 * h:3 * h + 1], ALU.mult, [rc, GT], [rc])
                    cx.ts("dve", oacc.t[:, q4, h * 64:(h + 1) * 64], ob_.t[:, 0:64], rc.t[:, 1, 0:1], ALU.mult,
                          [ob_, rc], [oacc])
                    if rr == 0:
                        cx.ts("dve", imp.t[:, q4, :], ob_.t[:, 65:129], rc.t[:, 0, 0:1], ALU.mult, [ob_, rc], [imp])
                    else:
                        cx.stt(imp.t[:, q4, :], ob_.t[:, 65:129], rc.t[:, 0, 0:1], imp.t[:, q4, :], ALU.mult, ALU.add,
                               [ob_, rc, imp], [imp])
                if rr == 2:
                    for q4 in range(4):
                        qt = 4 * c + q4
                        s1, s2, mm8, nm_ = sc1[q4 % 2], sc2[q4 % 2], m8[q4 % 2], nm[q4 % 2]
                        cx.tt("dve", s1.t[:], imp.t[:, q4, :], SB_.t[:, qt, :], ALU.add, [imp, SB_], [s1])
                        cx.op("dve", lambda e, mm8=mm8, s1=s1: e.max(out=mm8.t[:, 0:8], in_=s1.t[:]), [s1], [mm8])
                        cx.op("dve", lambda e, mm8=mm8, s1=s1, s2=s2: e.match_replace(
                            out=s2.t[:], in_to_replace=mm8.t[:, 0:8], in_values=s1.t[:], imm_value=-1e9), [s1, mm8], [s2])
                        cx.op("dve", lambda e, mm8=mm8, s2=s2: e.max(out=mm8.t[:, 8:16], in_=s2.t[:]), [s2], [mm8])
                        cx.ts("dve", mm8.t[:, 15:16], mm8.t[:, 15:16], 0.0, ALU.max, [mm8], [mm8])
                        cx.ts("dve", nm_.t[:, 64:128], s1.t[:], mm8.t[:, 15:16], ALU.is_lt, [s1, mm8], [nm_],
                              s2=NEG, op1=ALU.mult)
                        tp = nxt("t", TP)
                        cx.tr(tp.t[:, 0, :], nm_.t[:], ident.t[:], [nm_, ident], [tp])
                        for r3 in range(3):
                            hh = 3 * g + r3
                            cx.copy("dve",
                                    QA[hh].t[64:128, qt * 128:(qt + 1) * 128], tp.t[64:128, 0, :], [tp], [QA[hh]])
            return S_, PV_

        def att_items(c, branch, h):
            g = h // 3
            oacc = OACC[c % 2]
            kts = list(range(0, 4 * c + 4) if branch == 1 else range(max(4 * c - 4, 0), 4 * c + 4))
            shared = {"first": True}
            res = []
            for kt in kts:
                lo = max(kt - 4 * c, 0)
                hi = 3 if branch == 1 else min(kt + 4 - 4 * c, 3)
                n_ = (hi - lo + 1) * 128
                q0 = c * 512 + lo * 128
                ks_ = slice(kt * 128, (kt + 1) * 128)
                st = {}

                def S_(kt=kt, lo=lo, hi=hi, n_=n_, q0=q0, ks_=ks_, st=st):
                    sp_ = nxt("s", SP)
                    if branch == 1:
                        cx.mm(sp_.t[:, 0:n_], KSA[g].t[:, ks_], QA[h].t[:, q0:q0 + n_], True, True,
                              [KSA[g], QA[h]], [sp_])
                    else:
                        cx.mm(sp_.t[:, 0:n_], KW[g].t[0:64, ks_], QA[h].t[0:64, q0:q0 + n_], True, True,
                              [KW[g], QA[h]], [sp_])
                    if kt >= 4 * c:
                        cx.mm(sp_.t[:, 0:128], identb.t[:], causb.t[:], False, True, [identb, causb], [sp_], skip=True)
                    if branch == 2 and 4 * c <= kt + 4 <= 4 * c + 3:
                        cx.mm(sp_.t[:, n_ - 128:n_], identb.t[:], upperb.t[:], False, True, [identb, upperb], [sp_],
                              skip=True)
                    pT = nxt("p", pTs)
                    cx.act(pT.t[:, 0:n_], sp_.t[:, 0:n_], AF.Exp, [sp_], [pT], scale=0.125)
                    st["pT"] = pT

                def PV_(kt=kt, lo=lo, hi=hi, st=st, last=(kt == kts[-1])):
                    if shared["first"]:
                        shared["ob"] = nxt("o", OP)
                    ob_ = shared["ob"]
                    ov = ob_.t[:, 0:260].rearrange("p (a b) -> p a b", b=65)
                    pT = st["pT"]
                    vx = VSX if branch == 1 else VWX
                    for q4 in range(lo, hi + 1):
                        cx.mm(ov[:, q4, :], pT.t[:, (q4 - lo) * 128:(q4 - lo + 1) * 128], vx.t[:, kt, g, :],
                              shared["first"], True, [pT, vx], [ob_], skip=not shared["first"])
                        shared["first"] = False
                    if last:
                        rc = nxt("r", recs)
                        cx.ts("dve", rc.t[:, 0, :], ov[:, :, 64], 1e-30, ALU.add, [ob_], [rc])
                        cx.op("dve", lambda e, rc=rc: e.reciprocal(out=rc.t[:, 0, :], in_=rc.t[:, 0, :]), [rc], [rc])
                        col = 3 * h + branch
                        cx.tt("dve", rc.t[:, 1, :], rc.t[:, 0, :], GT.t[:, 4 * c:4 * c + 4, col], ALU.mult, [rc, GT], [rc])
                        for q4 in range(4):
                            av = oacc.t[:, q4, h * 64:(h + 1) * 64]
                            cx.stt(av, ov[:, q4, 0:64], rc.t[:, 1, q4:q4 + 1], av, ALU.mult, ALU.add,
                                   [ob_, rc, oacc], [oacc])
                res.append((S_, PV_))
            return res

        def out_item(c):
            cs = slice(c * 512, (c + 1) * 512)
            oacc = OACC[c % 2]

            def PV_():
                ys = yst[c % 2]
                for q4 in range(4):
                    ob2 = obf[q4 % 2]
                    cx.copy("pool", ob2.t[:], oacc.t[:, q4, :], [oacc], [ob2])
                    tp = nxt("t", TP)
                    for j in range(3):
                        cx.tr(tp.t[:, j, :], ob2.t[:, j * 128:(j + 1) * 128], ident.t[:], [ob2, ident], [tp])
                    cx.copy("act", ys.t[:, :, q4 * 128:(q4 + 1) * 128], tp.t[:, 0:3, :], [tp], [ys])
                for j in range(3):
                    cx.dma("sp", S["yt"].t[640 + j * 128:640 + (j + 1) * 128, cs], ys.t[:, j, :], [ys], [S["yt"]])
            return (lambda: None), PV_

        for c in range(8):
            for h in range(6):
                items.append(cmp_item(c, h))
            if dstop >= 5:
                for branch in ((1, 2) if dstop >= 6 else (1,)):
                    for h in range(6):
                        items.extend(att_items(c, branch, h))
            items.append(out_item(c))
        for i in range(len(items) + 1):
            if i < len(items):
                items[i][0]()
            if i >= 1:
                items[i - 1][1]()


def phase_E(cx, l, xd, x1d, W, G, S):
    with cx.phase() as es:
        wo = load_cast(cx, "wout", W["wout"], [128, 8, D])
        g_t = load_f32(cx, "ln1g", W["ln1_g"].t.broadcast_to([128, D]), W["ln1_g"], [128, D])
        b_t = load_f32(cx, "ln1b", W["ln1_b"].t.broadcast_to([128, D]), W["ln1_b"], [128, D])
        yts = [cx.sb(f"eyt{i}", [128, 8, 512], BF16) for i in range(2)]
        xrs = [cx.sb(f"exr{i}", [128, D], F32) for i in range(3)]
        pss = [cx.ps(f"eps{i}", [128, 512], F32) for i in range(6)]
        tmps = ln_tmps(cx, es, n=3)

        def load_y(tc):
            cx.dma("sp", yts[tc % 2].t[:], S["yt"].t[:, tc * 512:(tc + 1) * 512].rearrange("(k p) t -> p k t", p=128),
                   [S["yt"]], [yts[tc % 2]])
        stages = {}

        def mm_tile(tt):
            tc, q = divmod(tt, 4)
            if q == 0 and tc + 1 < 8:
                load_y(tc + 1)
            y_ = yts[tc % 2]
            xr = xrs[tt % 3]
            cx.dma("sp", xr.t[:], xd.t[tt * 128:(tt + 1) * 128, :], [xd], [xr])
            zp = pss[(tt % 3) * 2:(tt % 3) * 2 + 2]
            for hf in range(2):
                for k in range(8):
                    cx.mm(zp[hf].t[:], y_.t[:, k, q * 128:(q + 1) * 128], wo.t[:, k, hf * 512:(hf + 1) * 512],
                          k == 0, k == 7, [y_, wo], [zp[hf]])
            stages[tt] = ln_stages(cx, zp, xr, g_t, b_t, x1d, tt * 128, tmps[tt % 3])
        load_y(0)
        mm_tile(0)
        stages[0][0]()
        stages[0][1]()
        for tt in range(NT):
            if tt + 1 < NT:
                mm_tile(tt + 1)
                stages[tt + 1][0]()
            stages[tt][2]()
            stages[tt][3]()
            if tt + 1 < NT:
                stages[tt + 1][1]()
            if tt >= 1:
                stages[tt - 1][4]()
        stages[NT - 1][4]()


def phase_F(cx, l, x1d, x2d, W, G, S):
    TC = 1024
    ident = G["ident"]
    with cx.phase() as es:
        wd = cx.sb("wd_b", [128, NF, D], BF16)
        for f in range(NF):
            cx.dma("pool", wd.t[:, f, :], W["wd"].t[:, f, :], [W["wd"]], [wd])
        cw = load_f32(cx, "cw", W["cw"].t, W["cw"], [128, NF, 3])
        cb = load_f32(cx, "cb", W["cb"].t, W["cb"], [128, NF])
        g_t = load_f32(cx, "ln2g", W["ln2_g"].t.broadcast_to([128, D]), W["ln2_g"], [128, D])
        b_t = load_f32(cx, "ln2b", W["ln2_b"].t.broadcast_to([128, D]), W["ln2_b"], [128, D])
        carry = cx.sb("carry", [128, NF, 2], F32)
        cx.op("pool", lambda e: e.memset(carry.t[:], 0.0), [], [carry])
        xTs = [cx.sb(f"x1T{i}", [128, 8, TC], BF16) for i in range(2)]
        xbs = [cx.sb(f"fxb{i}", [128, D], BF16) for i in range(3)]
        xpt = [cx.ps(f"fxpt{i}", [128, 8, 128], BF16) for i in range(2)]
        xcnt = [0]

        def emit_xT(tc):
            for q in range(TC // 128):
                j = xcnt[0]
                xcnt[0] += 1
                bb, p = xbs[j % 3], xpt[j % 2]
                r0 = tc * TC + q * 128
                cx.dma("pool", bb.t[:], x1d.t[r0:r0 + 128, :], [x1d], [bb])
                for k in range(8):
                    cx.tr(p.t[:, k, :], bb.t[:, k * 128:(k + 1) * 128], ident.t[:], [bb, ident], [p])
                cx.copy("act" if j % 2 == 0 else "dve", xTs[tc % 2].t[:, :, q * 128:(q + 1) * 128], p.t[:], [p], [xTs[tc % 2]])
        act = cx.sb("ffact", [128, NF, TC], BF16)
        wb = [cx.sb(f"fwb{i}", [128, 2, 8, 128], BF16) for i in range(3)]
        hts = [cx.sb(f"fhb{i}", [128, 2 + TC], F32) for i in range(2)]
        hbA = [Tile(t.t, Buf(f"fhbA{i}")) for i, t in enumerate(hts)]
        hbB = [Tile(t.t, Buf(f"fhbB{i}")) for i, t in enumerate(hts)]
        hc = [cx.sb(f"fhc{i}", [128, 512], F32) for i in range(2)]
        gl = [cx.sb(f"fgl{i}", [128, 512], F32) for i in range(2)]
        xrs = [cx.sb(f"fxr{i}", [128, D], F32) for i in range(2)]
        tmps = ln_tmps(cx, es)
        pss = [cx.ps(f"fps{i}", [128, 512], F32) for i in range(6)]
        gi = 0
        nsteps = (T // TC) * NF

        def load_w(step):
            f = step % NF
            b_ = wb[step % 3]
            cx.dma("pool", b_.t[:, 0], W["wg"].t[f], [W["wg"]], [b_])
            cx.dma("pool", b_.t[:, 1], W["wu"].t[f], [W["wu"]], [b_])
        load_w(0)
        load_w(1)
        step = 0
        emit_xT(0)
        for tc in range(T // TC):
            xT = xTs[tc % 2]
            for f in range(NF):
                b_ = wb[step % 3]
                if step + 2 < nsteps:
                    load_w(step + 2)
                step += 1
                hA, hB = hbA[f % 2], hbB[f % 2]
                ht = hts[f % 2].t
                cx.copy("act", ht[:, 0:2], carry.t[:, f, :], [carry], [hA])
                for hf in range(TC // 512):
                    ts_ = slice(hf * 512, (hf + 1) * 512)
                    pg, pu = pss[(gi % 2) * 2], pss[(gi % 2) * 2 + 1]
                    c_, g_ = hc[gi % 2], gl[gi % 2]
                    gi += 1
                    for k in range(8):
                        cx.mm(pg.t[:], b_.t[:, 0, k, :], xT.t[:, k, ts_], k == 0, k == 7, [b_, xT], [pg])
                    for k in range(8):
                        cx.mm(pu.t[:], b_.t[:, 1, k, :], xT.t[:, k, ts_], k == 0, k == 7, [b_, xT], [pu])
                    hw_ = [hA] if hf == 0 else [hB]
                    hr_ = [hA] if hf == 0 else [hA, hB]
                    o = hf * 512
                    cx.copy("act", ht[:, 2 + o:514 + o], pg.t[:], [pg], hw_)
                    cx.ts("dve", c_.t[:], ht[:, 2 + o:514 + o], cw.t[:, f, 2:3], ALU.mult, hr_ + [cw, cb], [c_],
                          s2=cb.t[:, f:f + 1], op1=ALU.add)
                    cx.stt(c_.t[:], ht[:, 1 + o:513 + o], cw.t[:, f, 1:2], c_.t[:], ALU.mult, ALU.add, hr_ + [cw, c_], [c_])
                    cx.stt(c_.t[:], ht[:, o:512 + o], cw.t[:, f, 0:1], c_.t[:], ALU.mult, ALU.add, hr_ + [cw, c_], [c_])
                    cx.act(g_.t[:], c_.t[:], AF.Gelu_apprx_tanh, [c_], [g_])
                    cx.tt("dve", act.t[:, f, ts_], g_.t[:], pu.t[:], ALU.mult, [g_, pu], [act])
                cx.copy("act", carry.t[:, f, :], ht[:, TC:TC + 2], [hB], [carry])
            if tc + 1 < T // TC:
                emit_xT(tc + 1)
            for q in range(TC // 128):
                tt = tc * (TC // 128) + q
                xr = xrs[tt % 2]
                cx.dma("sp", xr.t[:], x1d.t[tt * 128:(tt + 1) * 128, :], [x1d], [xr])
                zp = pss[4:6]
                for hf in range(2):
                    for f in range(NF):
                        cx.mm(zp[hf].t[:], act.t[:, f, q * 128:(q + 1) * 128], wd.t[:, f, hf * 512:(hf + 1) * 512],
                              f == 0, f == NF - 1, [act, wd], [zp[hf]])
                layer_norm_store(cx, zp, xr, g_t, b_t, x2d, tt * 128, tmps[tt % 2])

SCRATCH = {
    "rope": ([2, 128, T], F32), "vp": ([256, T], F32),
    "qr": ([384, T], BF16), "kr": ([384, T], BF16), "qn": ([384, T], BF16),
    "ks": ([128, T], BF16), "kw": ([128, T], BF16), "kc": ([128, T], BF16), "vc": ([128, T], BF16),
    "tmb": ([T, 640], BF16), "gr": ([T, 384], F32), "gt": ([T, 18], F32),
    "yt": ([1024, T], BF16), "x1": ([T, D], F32), "xmid": ([T, D], F32),
}


def build_program(layers=(0, 1), phases="ABCDEF", ext_in=(), ext_out=(), prologue=True):
    nc = bass.Bass("TRN2", target_bir_lowering=False)
    cx = Ctx(nc, ext_in, ext_out)
    xd = cx.dr("x", [T, D], F32, kind="ExternalInput")
    posd = cx.dr("pos", [1, T], I32, kind="ExternalInput")
    Cd = {k: cx.dr(k, list(v.shape), F32, kind="ExternalInput") for k, v in CONSTS.items()}
    Wd = {l: {k: cx.dr(f"{k}_{l}", shp, F32, kind="ExternalInput") for k, shp in LAYER_SHAPES.items()} for l in layers}
    S = {k: cx.dr(k, shp, dt) for k, (shp, dt) in SCRATCH.items()}
    outd = cx.dr("y", [T, D], F32, kind="ExternalOutput")
    with contextlib.ExitStack() as gs:
        cx.stack = gs
        G = {"rope": S["rope"]}
        G["ident"] = load_cast(cx, "ident", Cd["c_ident"], [128, 128])
        G["inv"] = load_f32(cx, "inv", Cd["c_inv"].t, Cd["c_inv"], [128, 1])
        G["sgn"] = load_f32(cx, "sgn", Cd["c_sgn"].t, Cd["c_sgn"], [128, 1])
        G["pw"] = load_f32(cx, "pw", Cd["c_pw"].t, Cd["c_pw"], [128, 2])
        G["prc"] = load_f32(cx, "prc", Cd["c_prc"].t, Cd["c_prc"], [128, 2, 16])
        G["C"] = Cd
        if prologue:
            prologue_rope(cx, posd, G)
        cur = xd
        for li, l in enumerate(layers):
            nxt = outd if li == len(layers) - 1 else S["xmid"]
            W = Wd[l]
            if "A" in phases:
                phase_A(cx, l, cur, W, G, S)
            if "B" in phases:
                phase_B(cx, l, W, G, S)
            if "C" in phases:
                phase_C(cx, l, W, G, S)
            if "D" in phases:
                phase_D(cx, l, W, G, S)
            if "E" in phases:
                phase_E(cx, l, cur, S["x1"], W, G, S)
            if "F" in phases:
                phase_F(cx, l, S["x1"], nxt, W, G, S)
            cur = nxt
        cx.P.barrier()
        finals = [outd.b] + [cx.dram[n].b for n in cx.ext_out]
        cx.P.emit_all(final_bufs=finals)
    return nc, cx


def make_in_maps(inputs, layers=(0, 1), cores=range(8)):
    shared = dict(CONSTS)
    for l in layers:
        for k, v in _layer_arrays(inputs, l).items():
            assert list(v.shape) == LAYER_SHAPES[k], (k, v.shape)
            shared[f"{k}_{l}"] = v.astype(np.float32, copy=False)
    maps = []
    for b in cores:
        m = dict(shared)
        m["x"] = np.ascontiguousarray(inputs["x"][b])
        m["pos"] = np.ascontiguousarray(inputs["positions"][b].reshape(1, T).astype(np.int32))
        maps.append(m)
    return maps


def kernel(**inputs):
    inputs = {k: np.asarray(v) for k, v in inputs.items()}
    nc, cx = build_program()
    maps = make_in_maps(inputs)
    res = run_bass_kernel_spmd(nc, maps, core_ids=list(range(8)))
    return np.stack([np.asarray(r["y"]) for r in res.results], 0).astype(np.float32)
```

```python
import contextlib
import math
import numpy as np
import concourse.bass as bass
import concourse.mybir as mybir
from concourse.bass_utils import run_bass_kernel_spmd

F32 = mybir.dt.float32
BF16 = mybir.dt.bfloat16
I32 = mybir.dt.int32
AF = mybir.ActivationFunctionType
ALU = mybir.AluOpType
AX = mybir.AxisListType

T = 4096
D = 1024
DEPTH = 2
NT = T // 128
DFF = 2816
NF = DFF // 128
ALPHA = (2 * DEPTH) ** 0.25
NEG = -10000.0
DBG = {}

ENGS = ("pe", "act", "dve", "pool", "sp")


class Buf:
    __slots__ = ("name", "writers", "readers", "dsem", "dcount", "is_dram", "vsem", "excl")

    def __init__(self, name, is_dram=False, excl=False):
        self.name = name
        self.is_dram = is_dram
        self.excl = excl
        self.writers = []
        self.readers = []
        self.dsem = None
        self.dcount = 0
        self.vsem = None


class Op:
    __slots__ = ("eng", "emit", "waits", "is_dma", "dbuf", "dval", "needs_inc", "val", "vsem")

    def __init__(self, eng, emit, is_dma=False):
        self.eng = eng
        self.emit = emit
        self.waits = []
        self.is_dma = is_dma
        self.dbuf = None
        self.dval = 0
        self.needs_inc = False
        self.val = 0
        self.vsem = None


class Prog:
    def __init__(self, nc):
        self.nc = nc
        self.ops = {e: [] for e in ENGS}
        self.last = {e: None for e in ENGS}
        self.dma_bufs = {}
        self.pending_bar = {e: [] for e in ENGS}
        self.seq = 0
        self.bar_seq = 0
        self.vfree = {True: [], False: []}
        self.vkind = []
        self.vcount = []
        self.rsem = []

    def _dep(self, op, prod, force=False):
        if prod is op:
            return
        if (not force) and prod.val < self.bar_seq:
            return
        if not prod.is_dma and prod.eng == op.eng:
            if op.eng in ("pe", "sp"):
                return
        op.waits.append(prod)
        if not prod.is_dma:
            prod.needs_inc = True

    @staticmethod
    def _prune(lst):
        last = {}
        for r in lst:
            last[(r.eng, r.is_dma, r.vsem)] = r
        return list(last.values())

    def op(self, eng, emit, reads=(), writes=(), dma=False):
        o = Op(eng, emit, is_dma=dma)
        self.seq += 1
        o.val = self.seq
        if self.pending_bar[eng]:
            for p in self.pending_bar[eng]:
                self._dep(o, p, force=True)
            self.pending_bar[eng] = []
        for b in reads:
            for w in b.writers:
                self._dep(o, w)
            if b.excl:
                for r in b.readers:
                    if r.eng != eng:
                        self._dep(o, r)
        for b in writes:
            for r in b.readers:
                if (not r.is_dma) and r.eng == eng and not dma:
                    continue
                self._dep(o, r)
            if not b.readers:
                for w in b.writers:
                    if w.is_dma and dma:
                        continue
                    if (not w.is_dma) and w.eng == eng and not dma:
                        continue
                    self._dep(o, w)
        if dma:
            assert len(writes) == 1
            b = writes[0]
            if b.is_dram:
                b = [r for r in reads if not r.is_dram][0]
            if b.vsem is None:
                sw = (eng == "pool")
                if self.vfree[sw]:
                    b.vsem = self.vfree[sw].pop()
                else:
                    b.vsem = len(self.vcount)
                    self.vcount.append(0)
                    self.vkind.append(sw)
                b.dcount = self.vcount[b.vsem]
            b.dcount += 16
            self.vcount[b.vsem] = b.dcount
            o.dbuf = b
            o.dval = b.dcount
            o.vsem = b.vsem
            self.dma_bufs[id(b)] = b
        for b in writes:
            if b.readers:
                b.writers = [o]
                b.readers = []
            else:
                b.writers.append(o)
                if len(b.writers) > 6:
                    b.writers = self._prune(b.writers)
        for b in reads:
            b.readers.append(o)
            if len(b.readers) > 6:
                b.readers = self._prune(b.readers)
        self.ops[eng].append(o)
        if not dma:
            self.last[eng] = o
        return o

    def dma(self, eng, out, in_, reads, writes):
        return self.op(eng, lambda e: e.dma_start(out=out, in_=in_), reads, writes, dma=True)

    def barrier(self):
        targets = []
        for e in ENGS:
            if self.last[e] is not None:
                targets.append(self.last[e])
        for b in self.dma_bufs.values():
            if b.vsem is not None:
                p = Op("sp", None, is_dma=True)
                p.dbuf = b
                p.dval = b.dcount
                p.vsem = b.vsem
                targets.append(p)
                self.vfree[self.vkind[b.vsem]].append(b.vsem)
                b.vsem = None
        self.dma_bufs = {}
        self.seq += 1
        self.bar_seq = self.seq
        for e in ENGS:
            self.pending_bar[e] = self._prune(self.pending_bar[e] + targets)

    def emit_all(self, final_bufs=()):
        nc = self.nc
        esem = {e: nc.alloc_semaphore(name=f"es_{e}") for e in ENGS}
        self.rsem = [nc.alloc_semaphore(name=f"ds_{i}") for i in range(len(self.vcount))]
        for e in ENGS:
            c = 0
            for o in self.ops[e]:
                if (not o.is_dma) and o.needs_inc:
                    c += 1
                    o.val = c
        engobj = {"pe": "tensor", "act": "scalar", "dve": "vector", "pool": "gpsimd", "sp": "sync"}
        with nc.Block() as block:
            for e in ENGS:
                def body(eng, ops=self.ops[e], e=e):
                    waited = {}
                    for o in ops:
                        need = {}
                        for p in o.waits:
                            if p.is_dma:
                                sem, val = self.rsem[p.vsem], p.dval
                            else:
                                sem, val = esem[p.eng], p.val
                            if sem is None:
                                continue
                            if need.get(sem.num, (None, 0))[1] < val:
                                need[sem.num] = (sem, val)
                        for k, (sem, val) in need.items():
                            if waited.get(k, 0) >= val:
                                continue
                            waited[k] = val
                            eng.wait_ge(sem, val)
                        ins = o.emit(eng)
                        if o.is_dma:
                            ins.then_inc(self.rsem[o.vsem], 16)
                        elif o.needs_inc:
                            ins.then_inc(esem[e], 1)
                    if e == "sp":
                        for i, sem in enumerate(self.rsem):
                            if waited.get(sem.num, 0) < self.vcount[i]:
                                eng.wait_ge(sem, self.vcount[i])

                getattr(block, engobj[e])(body)


OFF = {}
_o = 0
for _n, _w in (("v_pool", 256), ("q_ret", 384), ("k_ret", 384), ("v_ret", 384), ("g_ret", 384),
               ("q_nsa", 384), ("k_cmp", 128), ("v_cmp", 128), ("k_slc", 128), ("v_slc", 128),
               ("k_win", 128), ("v_win", 128), ("gate", 18)):
    OFF[_n] = _o
    _o += _w


def _swap_cols(cols):
    cols = np.asarray(cols).reshape(-1, 64)
    return np.concatenate([cols[:, 32:], cols[:, :32]], axis=1).reshape(-1)


def _fm_cols():
    ch = []
    for c in range(2):
        ch.append(np.arange(OFF["v_pool"] + 128 * c, OFF["v_pool"] + 128 * (c + 1)))
    for name in ("q_ret", "k_ret", "q_nsa"):
        for c in range(3):
            ch.append(np.arange(OFF[name] + 128 * c, OFF[name] + 128 * (c + 1)))
    for name in ("k_slc", "k_win", "k_cmp", "v_cmp"):
        ch.append(np.arange(OFF[name], OFF[name] + 128))
    return ch


FM_COLS = _fm_cols()
NFM = len(FM_COLS)
TM_COLS = np.concatenate([np.arange(OFF["v_ret"], OFF["v_ret"] + 384),
                          np.arange(OFF["v_slc"], OFF["v_slc"] + 128),
                          np.arange(OFF["v_win"], OFF["v_win"] + 128),
                          np.arange(OFF["g_ret"], OFF["g_ret"] + 384),
                          np.arange(OFF["gate"], OFF["gate"] + 18)])
NTM = len(TM_COLS)


def _const_tables():
    c = {}
    p = np.arange(128)
    inv = (10000.0 ** (-np.arange(0, 64, 2, dtype=np.float32) / 64)).astype(np.float32)
    c["c_inv"] = inv[p % 32].reshape(128, 1).astype(np.float32)
    c["c_sgn"] = np.where((p % 64) < 32, -1.0, 1.0).reshape(128, 1).astype(np.float32)
    h = np.arange(6, dtype=np.float64)
    lg = np.log1p(-np.power(2.0, -5.0 - h))
    i = np.arange(128, dtype=np.float64)
    dm = np.zeros((128, 6, 128), np.float32)
    for hh in range(6):
        diff = i[None, :] - i[:, None]
        dm[:, hh, :] = np.where(diff >= 0, 0.125 * np.exp(lg[hh] * np.maximum(diff, 0)), 0.0)
    c["c_dm"] = dm
    xi = np.exp(lg[:, None] * (i[None, :] + 1.0))
    zeta = 0.125 * np.exp(lg[:, None] * (127.0 - i[None, :]))
    gam = np.exp(lg * 128.0)
    xir = np.zeros((128, 3, 128), np.float32)
    zt = np.zeros((128, 3, 128), np.float32)
    gc = np.zeros((128, 3), np.float32)
    for ck in range(3):
        for hh in range(2):
            xir[hh * 64:(hh + 1) * 64, ck, :] = xi[2 * ck + hh][None, :]
            zt[:, ck, hh * 64:(hh + 1) * 64] = zeta[2 * ck + hh][:, None]
            gc[hh * 64:(hh + 1) * 64, ck] = gam[2 * ck + hh]
    c["c_xir"] = xir
    c["c_zt"] = zt
    c["c_gc"] = gc
    win = np.zeros((128, 2), np.float32)
    rc = np.zeros((128, 2, 16), np.float32)
    for ck in range(2):
        for hh in range(2):
            w = (2, 4, 8, 16)[2 * ck + hh]
            win[hh * 64:(hh + 1) * 64, ck] = 1.0 / w
            rc[hh * 64:(hh + 1) * 64, ck, :] = 1.0 / np.minimum(np.arange(16) + 1, w)
    c["c_pw"] = win
    c["c_prc"] = rc
    kk = np.arange(128)[:, None]
    qq = np.arange(128)[None, :]
    c["c_caus"] = np.where(kk > qq, NEG, 0.0).astype(np.float32)
    c["c_upper"] = np.where(kk <= qq, NEG, 0.0).astype(np.float32)
    c["c_ident"] = np.eye(128, dtype=np.float32)
    pm = np.zeros((128, 128), np.float32)
    mm_ = np.arange(128)
    pm[(mm_ % 64 + 32) % 64 + 64 * (mm_ // 64), mm_] = 1.0
    c["c_perm"] = pm
    ex = np.zeros((64, T), np.float32)
    ex[np.arange(T) // 64, np.arange(T)] = 1.0
    c["c_expand"] = ex
    n = np.arange(256)
    ends = 16 * n + 31
    cm = np.where(ends[:, None] > np.arange(T)[None, :], NEG, 0.0).astype(np.float32)
    c["c_cmn"] = np.ascontiguousarray(cm.reshape(2, 128, T).transpose(1, 0, 2))
    ci = np.arange(256)[:, None]
    sj = np.arange(64)[None, :]
    ov = np.clip(np.minimum(ci * 16 + 32, (sj + 1) * 64) - np.maximum(ci * 16, sj * 64), 0, None) / 16.0
    ov[255, :] = 0.0
    c["c_ovl"] = np.ascontiguousarray(ov.astype(np.float32).reshape(2, 128, 64).transpose(1, 0, 2))
    tq = np.arange(T)
    cur = tq // 64
    blk = np.arange(64)[None, :]
    forced = (blk == 0) | (blk == cur[:, None]) | (blk == cur[:, None] - 1)
    valid = blk * 64 <= tq[:, None]
    bias = np.where(valid, np.where(forced, 1e6, 0.0), -100.0).astype(np.float32)
    c["c_sbias"] = np.ascontiguousarray(bias.reshape(32, 128, 64).transpose(1, 0, 2))
    return c


CONSTS = _const_tables()


def _layer_arrays(inp, l):
    a = {}
    w_in = inp["w_in"][l]
    wk = w_in.reshape(8, 128, -1)
    a["wfm"] = np.ascontiguousarray(
        np.stack([wk[:, :, cols].transpose(1, 0, 2) for cols in FM_COLS], 0))
    a["wtm"] = np.ascontiguousarray(wk[:, :, TM_COLS].transpose(1, 0, 2))
    a["wout"] = np.ascontiguousarray(inp["w_out"][l].reshape(8, 128, D).transpose(1, 0, 2))
    pw = inp["pool_w"][l]
    bd = np.zeros((2, 128, 128), np.float32)
    for ck in range(2):
        for hh in range(2):
            bd[ck, hh * 64:(hh + 1) * 64, hh * 64:(hh + 1) * 64] = pw[2 * ck + hh]
    a["bd"] = bd
    a["psc"] = np.ascontiguousarray(inp["pool_scale"][l].reshape(2, 128).T)
    a["gng"] = np.ascontiguousarray(inp["ret_gn_g"][l].reshape(1, 384))
    for kv in ("k", "v"):
        w1 = inp[f"cmp_w1_{kv}"][l].reshape(32, 64, 128)
        w1d = np.concatenate([w1, w1], axis=1).transpose(1, 0, 2)
        a[f"w1{kv}"] = np.ascontiguousarray(w1d)
        a[f"b1{kv}"] = np.ascontiguousarray(inp[f"cmp_b1_{kv}"][l].reshape(128, 1))
        a[f"pos{kv}"] = np.ascontiguousarray(inp[f"cmp_pos_{kv}"][l].T)
        a[f"w2{kv}"] = np.ascontiguousarray(inp[f"cmp_w2_{kv}"][l])
    a["w2ks"] = np.ascontiguousarray(inp["cmp_w2_k"][l][:, _swap_cols(np.arange(64))])
    a["wg"] = np.ascontiguousarray(inp["ffn_w_gate"][l].reshape(8, 128, NF, 128).transpose(2, 1, 0, 3))
    a["wu"] = np.ascontiguousarray(inp["ffn_w_up"][l].reshape(8, 128, NF, 128).transpose(2, 1, 0, 3))
    a["wd"] = np.ascontiguousarray(inp["ffn_w_down"][l].reshape(NF, 128, D).transpose(1, 0, 2))
    a["cw"] = np.ascontiguousarray(inp["ffn_conv_w"][l].reshape(3, NF, 128).transpose(2, 1, 0))
    a["cb"] = np.ascontiguousarray(inp["ffn_conv_b"][l].reshape(NF, 128).T)
    for nme in ("ln1_g", "ln1_b", "ln2_g", "ln2_b"):
        a[nme] = np.ascontiguousarray(inp[nme][l].reshape(1, D))
    return a


LAYER_SHAPES = {
    "wfm": [NFM, 128, 8, 128], "wtm": [128, 8, NTM], "wout": [128, 8, D], "bd": [2, 128, 128],
    "psc": [128, 2], "gng": [1, 384],
    "w1k": [128, 32, 128], "b1k": [128, 1], "posk": [64, 32], "w2k": [128, 64],
    "w1v": [128, 32, 128], "b1v": [128, 1], "posv": [64, 32], "w2v": [128, 64], "w2ks": [128, 64],
    "wg": [NF, 128, 8, 128], "wu": [NF, 128, 8, 128], "wd": [128, NF, D], "cw": [128, NF, 3], "cb": [128, NF],
    "ln1_g": [1, D], "ln1_b": [1, D], "ln2_g": [1, D], "ln2_b": [1, D],
}


class Tile:
    __slots__ = ("t", "b")

    def __init__(self, t, b):
        self.t = t
        self.b = b


class Ctx:
    def __init__(self, nc, ext_in=(), ext_out=()):
        self.nc = nc
        self.P = Prog(nc)
        self.ext_in = set(ext_in)
        self.ext_out = set(ext_out)
        self.dram = {}
        self.stack = None
        self.uid = 0

    def dr(self, name, shape, dt, kind=None):
        if kind is None:
            kind = "ExternalInput" if name in self.ext_in else ("ExternalOutput" if name in self.ext_out else "Internal")
        t = self.nc.dram_tensor(name, list(shape), dt, kind=kind).ap()
        tl = Tile(t, Buf(name, is_dram=True))
        self.dram[name] = tl
        return tl

    def sb(self, name, shape, dt, es=None):
        self.uid += 1
        t = (es or self.stack).enter_context(self.nc.sbuf_tensor(f"{name}_{self.uid}", list(shape), dt))
        return Tile(t, Buf(name))

    def ps(self, name, shape, dt, es=None):
        self.uid += 1
        t = (es or self.stack).enter_context(self.nc.psum_tensor(f"{name}_{self.uid}", list(shape), dt))
        return Tile(t, Buf(name, excl=True))

    @contextlib.contextmanager
    def phase(self):
        old = self.stack
        with contextlib.ExitStack() as es:
            self.stack = es
            yield es
            self.P.barrier()
        self.stack = old

    def dma(self, eng, out, in_, reads, writes):
        self.P.dma(eng, out, in_, [x.b for x in reads], [x.b for x in writes])

    def op(self, eng, fn, reads, writes):
        self.P.op(eng, fn, [x.b for x in reads], [x.b for x in writes])

    def mm(self, out, lhsT, rhs, start, stop, reads, writes, skip=False):
        kw = dict(start=start, stop=stop)
        if skip:
            kw["skip_group_check"] = True
        self.op("pe", lambda e: e.matmul(out, lhsT=lhsT, rhs=rhs, **kw), reads, writes)

    def tr(self, out, in_, ident, reads, writes):
        self.op("pe", lambda e: e.transpose(out, in_, ident), reads, writes)

    def copy(self, eng, out, in_, reads, writes):
        if eng == "act":
            self.op("act", lambda e: e.copy(out=out, in_=in_), reads, writes)
        else:
            self.op(eng, lambda e: e.tensor_copy(out=out, in_=in_), reads, writes)

    def act(self, out, in_, func, reads, writes, **kw):
        self.op("act", lambda e: e.activation(out=out, in_=in_, func=func, **kw), reads, writes)

    def tt(self, eng, out, in0, in1, op, reads, writes):
        self.op(eng, lambda e: e.tensor_tensor(out=out, in0=in0, in1=in1, op=op), reads, writes)

    def ts(self, eng, out, in0, s1, op0, reads, writes, s2=None, op1=None):
        if op1 is None:
            self.op(eng, lambda e: e.tensor_scalar(out=out, in0=in0, scalar1=s1, scalar2=None, op0=op0), reads, writes)
        else:
            self.op(eng, lambda e: e.tensor_scalar(out=out, in0=in0, scalar1=s1, scalar2=s2, op0=op0, op1=op1), reads, writes)

    def stt(self, out, in0, scalar, in1, op0, op1, reads, writes):
        self.op("dve", lambda e: e.scalar_tensor_tensor(out=out, in0=in0, scalar=scalar, in1=in1, op0=op0, op1=op1),
                reads, writes)


def load_cast(cx, name, src_tile, shape, es=None, eng=None, q=None):
    b = cx.sb(name + "_b", shape, BF16, es)
    cx.dma("pool", b.t[:], src_tile.t, [src_tile], [b])
    return b


def load_f32(cx, name, src_ap, src_tile, shape, es=None, q="sp"):
    f = cx.sb(name, shape, F32, es)
    cx.dma(q, f.t[:], src_ap, [src_tile], [f])
    return f


def build_xT(cx, xd, xT, ident, ntiles, tok0=0):
    with contextlib.ExitStack() as es:
        xb = [cx.sb(f"xb{i}", [128, D], BF16, es) for i in range(3)]
        pt = [cx.ps(f"xpt{i}", [128, 8, 128], BF16, es) for i in range(2)]
        for tt in range(ntiles):
            bb, p = xb[tt % 3], pt[tt % 2]
            r0 = tok0 + tt * 128
            cx.dma("pool", bb.t[:], xd.t[r0:r0 + 128, :], [xd], [bb])
            for k in range(8):
                cx.tr(p.t[:, k, :], bb.t[:, k * 128:(k + 1) * 128], ident.t[:], [bb, ident], [p])
            cx.copy("act" if tt % 2 == 0 else "dve", xT.t[:, :, tt * 128:(tt + 1) * 128], p.t[:], [p], [xT])
        cx.P.barrier()


def layer_norm_store(cx, zps, xres, g_t, b_t, outd, r0, tmp, eng_q="pool"):
    z, st, mv, rs, o = tmp
    for hf in range(2):
        cx.stt(z.t[:, hf * 512:(hf + 1) * 512], xres.t[:, hf * 512:(hf + 1) * 512], ALPHA, zps[hf].t[:],
               ALU.mult, ALU.add, [xres, zps[hf]], [z])
    for hf in range(2):
        cx.op("dve", lambda e, hf=hf: e.bn_stats(out=st.t[:, hf, :], in_=z.t[:, hf * 512:(hf + 1) * 512]), [z], [st])
    cx.op("dve", lambda e: e.bn_aggr(out=mv.t[:], in_=st.t[:]), [st], [mv])
    cx.ts("dve", rs.t[:, 0:1], mv.t[:, 1:2], 1e-5, ALU.add, [mv], [rs])
    cx.act(rs.t[:, 0:1], rs.t[:, 0:1], AF.Sqrt, [rs], [rs])
    cx.op("dve", lambda e: e.reciprocal(out=rs.t[:, 0:1], in_=rs.t[:, 0:1]), [rs], [rs])
    cx.stt(rs.t[:, 1:2], mv.t[:, 0:1], -1.0, rs.t[:, 0:1], ALU.mult, ALU.mult, [mv, rs], [rs])
    cx.act(o.t[:], z.t[:], AF.Identity, [z, rs], [o], scale=rs.t[:, 0:1], bias=rs.t[:, 1:2])
    cx.tt("dve", o.t[:], o.t[:], g_t.t[:], ALU.mult, [o, g_t], [o])
    cx.tt("pool", o.t[:], o.t[:], b_t.t[:], ALU.add, [o, b_t], [o])
    cx.dma(eng_q, outd.t[r0:r0 + 128, :], o.t[:], [o], [outd])


def ln_stages(cx, zps, xres, g_t, b_t, outd, r0, tmp, eng_q="pool"):
    z, st, mv, rs, o = tmp

    def s1():
        for hf in range(2):
            cx.stt(z.t[:, hf * 512:(hf + 1) * 512], xres.t[:, hf * 512:(hf + 1) * 512], ALPHA, zps[hf].t[:],
                   ALU.mult, ALU.add, [xres, zps[hf]], [z])
        for hf in range(2):
            cx.op("dve", lambda e, hf=hf: e.bn_stats(out=st.t[:, hf, :], in_=z.t[:, hf * 512:(hf + 1) * 512]), [z], [st])
        cx.op("dve", lambda e: e.bn_aggr(out=mv.t[:], in_=st.t[:]), [st], [mv])
        cx.ts("dve", rs.t[:, 0:1], mv.t[:, 1:2], 1e-5, ALU.add, [mv], [rs])

    def s2():
        cx.act(rs.t[:, 0:1], rs.t[:, 0:1], AF.Sqrt, [rs], [rs])

    def s3():
        cx.op("dve", lambda e: e.reciprocal(out=rs.t[:, 0:1], in_=rs.t[:, 0:1]), [rs], [rs])
        cx.stt(rs.t[:, 1:2], mv.t[:, 0:1], -1.0, rs.t[:, 0:1], ALU.mult, ALU.mult, [mv, rs], [rs])

    def s4():
        cx.act(o.t[:], z.t[:], AF.Identity, [z, rs], [o], scale=rs.t[:, 0:1], bias=rs.t[:, 1:2])

    def s5():
        cx.tt("dve", o.t[:], o.t[:], g_t.t[:], ALU.mult, [o, g_t], [o])
        cx.tt("pool", o.t[:], o.t[:], b_t.t[:], ALU.add, [o, b_t], [o])
        cx.dma(eng_q, outd.t[r0:r0 + 128, :], o.t[:], [o], [outd])
    return s1, s2, s3, s4, s5


def ln_tmps(cx, es, n=2):
    res = []
    for i in range(n):
        res.append((cx.sb(f"lnz{i}", [128, D], F32, es), cx.sb(f"lnst{i}", [128, 2, 6], F32, es),
                    cx.sb(f"lnmv{i}", [128, 2], F32, es), cx.sb(f"lnrs{i}", [128, 2], F32, es),
                    cx.sb(f"lno{i}", [128, D], F32, es)))
    return res


def prologue_rope(cx, posd, G):
    rope = G["rope"]
    with cx.phase() as es:
        pi_ = cx.sb("posi", [128, T], I32)
        ang = cx.sb("ang", [128, T], F32)
        m = cx.sb("rm", [128, T], F32)
        o = cx.sb("ro", [128, T], F32)
        cx.dma("sp", pi_.t[:], posd.t.broadcast_to([128, T]), [posd], [pi_])
        cx.copy("dve", ang.t[:], pi_.t[:], [pi_], [ang])
        cx.ts("dve", ang.t[:], ang.t[:], G["inv"].t[:, 0:1], ALU.mult, [ang, G["inv"]], [ang])
        ki = cx.sb("rki", [128, T], I32)
        C1 = 6.28125
        C2 = 2.0 * math.pi - 6.28125
        for which, shift in ((0, 0.25), (1, 0.0)):
            cx.ts("dve", m.t[:], ang.t[:], 1.0 / (2.0 * math.pi), ALU.mult, [ang], [m], s2=shift, op1=ALU.add)
            cx.copy("dve", ki.t[:], m.t[:], [m], [ki])
            cx.copy("dve", m.t[:], ki.t[:], [ki], [m])
            cx.stt(o.t[:], m.t[:], -C1, ang.t[:], ALU.mult, ALU.add, [m, ang], [o])
            cx.stt(o.t[:], m.t[:], -C2, o.t[:], ALU.mult, ALU.add, [m, o], [o])
            if which == 0:
                cx.ts("dve", o.t[:], o.t[:], 0.5 * math.pi, ALU.add, [o], [o])
            cx.ts("dve", o.t[:], o.t[:], math.pi, ALU.min, [o], [o], s2=-math.pi, op1=ALU.max)
            cx.act(o.t[:], o.t[:], AF.Sin, [o], [o])
            if which == 1:
                cx.ts("dve", o.t[:], o.t[:], G["sgn"].t[:, 0:1], ALU.mult, [o, G["sgn"]], [o])
            cx.dma("sp", rope.t[which], o.t[:], [o], [rope])


def phase_A(cx, l, xd, W, G, S):
    ident = G["ident"]
    rope = G["rope"]
    with cx.phase():
        xT = cx.sb("xT", [128, 8, T], BF16)
        build_xT(cx, xd, xT, ident, NT)
        with cx.phase() as es:
          if DBG.get("tm", True):
              wtm = load_cast(cx, "wtm", W["wtm"], [128, 8, NTM])
              pss = [cx.ps(f"tmps{i}", [128, 512], F32) for i in range(6)]
              ob = [cx.sb(f"tmob{i}", [128, 640], BF16) for i in range(2)]
              og = [cx.sb(f"tmog{i}", [128, 384], F32) for i in range(2)]
              ogt = [cx.sb(f"tmogt{i}", [128, 18], F32) for i in range(2)]
              for tt in range(NT):
                  p0, p1, p2 = pss[(tt % 2) * 3:(tt % 2) * 3 + 3]
                  for (pp, c0, c1) in ((p0, 0, 512), (p1, 512, 1024), (p2, 1024, NTM)):
                      for k in range(8):
                          cx.mm(pp.t[:, 0:c1 - c0], xT.t[:, k, tt * 128:(tt + 1) * 128], wtm.t[:, k, c0:c1],
                                k == 0, k == 7, [xT, wtm], [pp])
                  b_, g_, t_ = ob[tt % 2], og[tt % 2], ogt[tt % 2]
                  cx.copy("dve", b_.t[:, 0:512], p0.t[:], [p0], [b_])
                  cx.copy("dve", b_.t[:, 512:640], p1.t[:, 0:128], [p1], [b_])
                  cx.act(g_.t[:], p1.t[:, 128:512], AF.Silu, [p1], [g_])
                  cx.act(t_.t[:], p2.t[:, 0:18], AF.Sigmoid, [p2], [t_])
                  r0 = tt * 128
                  cx.dma("sp", S["tmb"].t[r0:r0 + 128, :], b_.t[:], [b_], [S["tmb"]])
                  cx.dma("sp", S["gr"].t[r0:r0 + 128, :], g_.t[:], [g_], [S["gr"]])
                  cx.dma("sp", S["gt"].t[r0:r0 + 128, :], t_.t[:], [t_], [S["gt"]])
        with cx.phase() as es:
            C = cx.sb("ropeC", [128, T], F32)
            Sn = cx.sb("ropeS", [128, T], F32)
            cx.dma("sp", C.t[:], rope.t[0], [rope], [C])
            cx.dma("sp", Sn.t[:], rope.t[1], [rope], [Sn])
            perm = load_cast(cx, "perm", G["C"]["c_perm"], [128, 128])
            wb = [cx.sb(f"wb{i}", [128, 8, 128], BF16) for i in range(3)]
            pss = [cx.ps(f"fmps{i}", [128, 512], F32) for i in range(5)]
            ps2 = [cx.ps(f"fmps2{i}", [128, 512], F32) for i in range(3)]
            ost = [cx.sb(f"fmo{i}", [128, T], BF16) for i in range(2)]
            vst = [cx.sb(f"fmv{i}", [128, 512], F32) for i in range(2)]
            qbs = [cx.sb(f"fmqb{i}", [128, 512], BF16) for i in range(3)]
            t1s = [cx.sb(f"fmt1{i}", [128, 512], F32) for i in range(2)]
            t2s = [cx.sb(f"fmt2{i}", [128, 512], F32) for i in range(2)]
            units = [(0, S["vp"], 0, False), (1, S["vp"], 128, False)]
            ci = 2
            for dest in ("qr", "kr", "qn"):
                for c in range(3):
                    units.append((ci, S[dest], 128 * c, True))
                    ci += 1
            units.append((ci, S["ks"], 0, True))
            units.append((ci + 1, S["kw"], 0, True))
            units.append((ci + 2, S["kc"], 0, False))
            units.append((ci + 3, S["vc"], 0, False))

            def load_w(ui):
                cx.dma("pool", wb[ui % 3].t[:], W["wfm"].t[units[ui][0]], [W["wfm"]], [wb[ui % 3]])
            load_w(0)
            load_w(1)
            seq = [(ui, tc) for ui in range(len(units)) for tc in range(8)]
            state = {}

            def stage1(idx):
                ui, tc = seq[idx]
                cid, dest, row0, is_rope = units[ui]
                if tc == 0 and ui + 2 < len(units):
                    load_w(ui + 2)
                ts_ = slice(tc * 512, (tc + 1) * 512)
                pp = pss[idx % 5]
                for k in range(8):
                    cx.mm(pp.t[:], wb[ui % 3].t[:, k, :], xT.t[:, k, ts_], k == 0, k == 7, [wb[ui % 3], xT], [pp])
                state[idx] = pp
                if is_rope:
                    qb = qbs[idx % 3]
                    cx.copy("act", qb.t[:], pp.t[:], [pp], [qb])

            def stage2(idx):
                ui, tc = seq[idx]
                cid, dest, row0, is_rope = units[ui]
                ts_ = slice(tc * 512, (tc + 1) * 512)
                pp = state.pop(idx)
                o_ = ost[ui % 2]
                if is_rope:
                    qb, p2 = qbs[idx % 3], ps2[idx % 3]
                    cx.mm(p2.t[:], perm.t[:], qb.t[:], True, True, [perm, qb], [p2])
                    t1, t2 = t1s[idx % 2], t2s[idx % 2]
                    cx.tt("dve", t1.t[:], pp.t[:], C.t[:, ts_], ALU.mult, [pp, C], [t1])
                    cx.tt("dve", t2.t[:], p2.t[:], Sn.t[:, ts_], ALU.mult, [p2, Sn], [t2])
                    cx.tt("pool", o_.t[:, ts_], t1.t[:], t2.t[:], ALU.add, [t1, t2], [o_])
                elif dest is S["vp"]:
                    v_ = vst[tc % 2]
                    cx.copy("act", v_.t[:], pp.t[:], [pp], [v_])
                    cx.dma("sp", dest.t[row0:row0 + 128, ts_], v_.t[:], [v_], [dest])
                else:
                    cx.copy("act", o_.t[:, ts_], pp.t[:], [pp], [o_])
                if tc == 7 and dest is not S["vp"]:
                    cx.dma("sp", dest.t[row0:row0 + 128, :], o_.t[:], [o_], [dest])
            for idx in range(len(seq) + 1):
                if idx < len(seq):
                    stage1(idx)
                if idx >= 1:
                    stage2(idx - 1)


def phase_B(cx, l, W, G, S):
    with cx.phase():
        psc = load_f32(cx, "psc", W["psc"].t, W["psc"], [128, 2])
        pw = G["pw"]
        prc = G["prc"]
        pss = [cx.ps(f"bps{i}", [128, 512], F32) for i in range(4)]
        for ck in range(2):
            bd = load_cast(cx, f"bd{ck}", Tile(W["bd"].t[ck], W["bd"].b), [128, 128])
            v = cx.sb(f"pv{ck}", [128, 16 + T], F32)
            s2 = cx.sb(f"ps2{ck}", [128, 16 + T], F32)
            s4 = cx.sb(f"ps4{ck}", [128, 16 + T], F32)
            mx = cx.sb(f"pmx{ck}", [128, T], BF16)
            o = cx.sb(f"pbo{ck}", [128, T], BF16)
            for t_ in (v, s2, s4):
                cx.op("pool", lambda e, t_=t_: e.memset(t_.t[:, 0:16], 0.0), [], [t_])
            cx.dma("sp", v.t[:, 16:], S["vp"].t[ck * 128:(ck + 1) * 128, :], [S["vp"]], [v])
            if ck == 0:
                cx.tt("dve", s2.t[:, 16:], v.t[:, 16:], v.t[:, 15:15 + T], ALU.add, [v], [s2])
                cx.tt("dve", s4.t[64:128, 16:], s2.t[64:128, 16:], s2.t[64:128, 14:14 + T], ALU.add, [s2], [s4])
                lo, hi = s2, s4
            else:
                cx.tt("dve", s2.t[:, 16:], v.t[:, 16:], v.t[:, 15:15 + T], ALU.add, [v], [s2])
                cx.tt("dve", s4.t[:, 16:], s2.t[:, 16:], s2.t[:, 14:14 + T], ALU.add, [s2], [s4])
                cx.tt("dve", s2.t[:, 16:], s4.t[:, 16:], s4.t[:, 12:12 + T], ALU.add, [s4], [s2])
                cx.tt("dve", s4.t[64:128, 16:], s2.t[64:128, 16:], s2.t[64:128, 8:8 + T], ALU.add, [s2], [s4])
                lo, hi = s2, s4
            for (src, r) in ((lo, slice(0, 64)), (hi, slice(64, 128))):
                cx.stt(mx.t[r, :], src.t[r, 16:], pw.t[r, ck:ck + 1], v.t[r, 16:], ALU.mult, ALU.subtract,
                       [src, pw, v], [mx])
                cx.tt("dve", src.t[r, 0:16], src.t[r, 16:32], prc.t[r, ck, :], ALU.mult, [src, prc, mx], [src])
                cx.tt("dve", mx.t[r, 0:16], src.t[r, 0:16], v.t[r, 16:32], ALU.subtract, [src, v], [mx])
            for tc in range(8):
                ts_ = slice(tc * 512, (tc + 1) * 512)
                pp = pss[tc % 4]
                cx.mm(pp.t[:], bd.t[:], mx.t[:, ts_], True, True, [bd, mx], [pp])
                cx.act(o.t[:, ts_], pp.t[:], AF.Copy, [pp, psc], [o], scale=psc.t[:, ck:ck + 1])
            cx.dma("sp", S["yt"].t[ck * 128:(ck + 1) * 128, :], o.t[:], [o], [S["yt"]])


def phase_C(cx, l, W, G, S):
    Cd = G["C"]
    ident = G["ident"]
    with cx.phase() as es:
        dm = load_f32(cx, "dm", Cd["c_dm"].t, Cd["c_dm"], [128, 6, 128])
        xir = load_f32(cx, "xir", Cd["c_xir"].t, Cd["c_xir"], [128, 3, 128])
        zt = load_f32(cx, "zt", Cd["c_zt"].t, Cd["c_zt"], [128, 3, 128])
        gc = load_f32(cx, "gc", Cd["c_gc"].t, Cd["c_gc"], [128, 3])
        gng = load_f32(cx, "gng", W["gng"].t.broadcast_to([128, 384]), W["gng"], [128, 384])
        bankA = [cx.ps(f"cA{i}", [128, 512], F32) for i in range(2)]
        bankO = [cx.ps(f"cO{i}", [128, 512], F32) for i in range(2)]
        bankV = [cx.ps(f"cV{i}", [128, 512], F32) for i in range(2)]
        bankY = cx.ps("cY", [128, 8, 128], BF16)
        bankK = cx.ps("cK", [128, 8, 128], BF16)
        ktp = [bankK.t[:, i, :] for i in range(3)]
        opv2 = [[b.t[:, ck * 128:(ck + 1) * 128] for ck in range(3)] for b in bankO]
        kvp2 = [[b.t[:, ck * 128:(ck + 1) * 128] for ck in range(3)] for b in bankV]
        ytp = [bankY.t[:, i, :] for i in range(3)]
        qT, kT, qx, v, vz, R, Rb = [], [], [], [], [], [], []
        for ck in range(3):
            rows = slice(ck * 128, (ck + 1) * 128)
            qT.append(cx.sb(f"cqT{ck}", [128, T], BF16))
            kT.append(cx.sb(f"ckT{ck}", [128, T], BF16))
            qx.append(cx.sb(f"cqx{ck}", [128, T], BF16))
            v.append(cx.sb(f"cv{ck}", [128, NT, 128], BF16))
            vz.append(cx.sb(f"cvz{ck}", [128, NT, 128], BF16))
            R.append(cx.sb(f"cR{ck}", [128, 64], F32))
            Rb.append(cx.sb(f"cRb{ck}", [128, 64], BF16))
            cx.dma("sp", qT[ck].t[:], S["qr"].t[rows, :], [S["qr"]], [qT[ck]])
            cx.dma("sp", kT[ck].t[:], S["kr"].t[rows, :], [S["kr"]], [kT[ck]])
            cx.dma("pool", v[ck].t[:], S["tmb"].t[:, ck * 128:(ck + 1) * 128].rearrange("(n p) c -> p n c", p=128),
                   [S["tmb"]], [v[ck]])
            cx.tt("pool", qx[ck].t[:].rearrange("p (n i) -> p n i", i=128), qT[ck].t[:].rearrange("p (n i) -> p n i", i=128),
                  xir.t[:, ck:ck + 1, :].broadcast_to([128, NT, 128]), ALU.mult, [qT[ck], xir], [qx[ck]])
            cx.tt("pool", vz[ck].t[:], v[ck].t[:], zt.t[:, ck:ck + 1, :].broadcast_to([128, NT, 128]), ALU.mult,
                  [v[ck], zt], [vz[ck]])
            cx.op("pool", lambda e, ck=ck: e.memset(R[ck].t[:], 0.0), [], [R[ck]])
            cx.op("pool", lambda e, ck=ck: e.memset(Rb[ck].t[:], 0.0), [], [Rb[ck]])
        NB = 2
        sgs = [[cx.sb(f"csg{ck}_{i}", [128, 8, 128], F32) for i in range(2)] for ck in range(3)]
        kts = [[cx.sb(f"ckt{ck}_{i}", [128, 128], BF16) for i in range(NB)] for ck in range(3)]
        sms = [[cx.sb(f"csm{hh}_{i}", [128, 3, 128], BF16) for i in range(NB)] for hh in range(2)]
        sts = [[cx.sb(f"cst{ck}_{i}", [128, 2, 6], F32) for i in range(NB)] for ck in range(3)]
        mvs = [[cx.sb(f"cmv{ck}_{i}", [128, 2, 2], F32) for i in range(NB)] for ck in range(3)]
        rss = [[cx.sb(f"crs{ck}_{i}", [128, 2, 2], F32) for i in range(NB)] for ck in range(3)]
        ons = [[cx.sb(f"con{ck}_{i}", [128, 128], F32) for i in range(NB)] for ck in range(3)]
        onb = [[cx.sb(f"conb{ck}_{i}", [128, 128], BF16) for i in range(NB)] for ck in range(3)]
        yts = [[cx.sb(f"cyt{ck}_{i}", [128, 128], BF16) for i in range(NB)] for ck in range(3)]

        def load_sg(ck, blk):
            cx.dma("pool", sgs[ck][blk % 2].t[:],
                   S["gr"].t[blk * 1024:(blk + 1) * 1024, ck * 128:(ck + 1) * 128].rearrange("(n p) c -> p n c", p=128),
                   [S["gr"]], [sgs[ck][blk % 2]])
        for ck in range(3):
            load_sg(ck, 0)
        def ctx(n):
            return slice(n * 128, (n + 1) * 128), n % NB, opv2[n % 2], kvp2[n % 2], bankO[n % 2], bankV[n % 2]

        def st_A(n):
            ns, i, opv, kvp, BO, BV = ctx(n)
            for ck in range(3):
                cx.tr(ktp[ck], kT[ck].t[:, ns], ident.t[:], [kT[ck], ident], [bankK])
                for hh in range(2):
                    r = slice(hh * 64, (hh + 1) * 64)
                    cx.mm(bankA[hh].t[:, ck * 128:(ck + 1) * 128], kT[ck].t[r, ns], qT[ck].t[r, ns], True, True,
                          [kT[ck], qT[ck]], [bankA[hh]])

        def st_1(n):
            ns, i, opv, kvp, BO, BV = ctx(n)
            for ck in range(3):
                cx.copy("act", kts[ck][i].t[:], ktp[ck], [bankK], [kts[ck][i]])
            for hh in range(2):
                cx.tt("dve", sms[hh][i].t[:], bankA[hh].t[:, 0:384].rearrange("p (a b) -> p a b", b=128),
                      dm.t[:, hh::2, :], ALU.mult, [bankA[hh], dm], [sms[hh][i]])

        def st_B(n):
            ns, i, opv, kvp, BO, BV = ctx(n)
            for ck in range(3):
                for hh in range(2):
                    r = slice(hh * 64, (hh + 1) * 64)
                    cx.mm(opv[ck][:, r], sms[hh][i].t[:, ck, :], v[ck].t[:, n, r], True, False,
                          [sms[hh][i], v[ck]], [BO])
                    cx.mm(opv[ck][:, r], qx[ck].t[r, ns], Rb[ck].t[r, :], False, True, [qx[ck], Rb[ck]], [BO])
                cx.mm(kvp[ck], kts[ck][i].t[:], vz[ck].t[:, n, :], True, True, [kts[ck][i], vz[ck]], [BV])

        def st_3a(n):
            ns, i, opv, kvp, BO, BV = ctx(n)
            for ck in range(3):
                for hh in range(2):
                    r = slice(hh * 64, (hh + 1) * 64)
                    cx.stt(R[ck].t[r, :], R[ck].t[r, :], gc.t[r, ck:ck + 1], kvp[ck][r, r], ALU.mult, ALU.add,
                           [R[ck], gc, BV], [R[ck]])
                cx.copy("pool", Rb[ck].t[:], R[ck].t[:], [R[ck]], [Rb[ck]])
            for ck in range(3):
                st, mv, rs = sts[ck][i], mvs[ck][i], rss[ck][i]
                for hh in range(2):
                    r = slice(hh * 64, (hh + 1) * 64)
                    cx.op("dve", lambda e, hh=hh, r=r, st=st, ck=ck, opv_=opv: e.bn_stats(out=st.t[:, hh, :], in_=opv_[ck][:, r]), [BO], [st])
                    cx.op("dve", lambda e, hh=hh, st=st, mv=mv: e.bn_aggr(out=mv.t[:, hh, :], in_=st.t[:, hh, :]), [st], [mv])
                cx.ts("dve", rs.t[:, 0, :], mv.t[:, :, 1], 1e-5, ALU.add, [mv], [rs])

        def st_sq(n):
            i = n % NB
            for ck in range(3):
                rs = rss[ck][i]
                cx.act(rs.t[:, 0, :], rs.t[:, 0, :], AF.Sqrt, [rs], [rs])

        def st_3b(n):
            i = n % NB
            for ck in range(3):
                rs, mv = rss[ck][i], mvs[ck][i]
                cx.op("dve", lambda e, rs=rs: e.reciprocal(out=rs.t[:, 0, :], in_=rs.t[:, 0, :]), [rs], [rs])
                cx.stt(rs.t[:, 1, :], mv.t[:, :, 0], -1.0, rs.t[:, 0, :], ALU.mult, ALU.mult, [mv, rs], [rs])

        def st_4(n):
            ns, i, opv, kvp, BO, BV = ctx(n)
            for ck in range(3):
                rs, on, ob = rss[ck][i], ons[ck][i], onb[ck][i]
                for hh in range(2):
                    r = slice(hh * 64, (hh + 1) * 64)
                    cx.act(on.t[:, r], opv[ck][:, r], AF.Identity, [BO, rs], [on], scale=rs.t[:, 0, hh:hh + 1],
                           bias=rs.t[:, 1, hh:hh + 1])
                cx.tt("pool", on.t[:], on.t[:], gng.t[:, ck * 128:(ck + 1) * 128], ALU.mult, [on, gng], [on])
                cx.tt("pool", ob.t[:], on.t[:], sgs[ck][(n // 8) % 2].t[:, n % 8, :], ALU.mult, [on, sgs[ck][(n // 8) % 2]], [ob])

        def st_C(n):
            ns, i, opv, kvp, BO, BV = ctx(n)
            for ck in range(3):
                cx.tr(ytp[ck], onb[ck][i].t[:], ident.t[:], [onb[ck][i], ident], [bankY])
            for ck in range(3):
                cx.copy("act", yts[ck][i].t[:], ytp[ck], [bankY], [yts[ck][i]])
                cx.dma("sp", S["yt"].t[256 + ck * 128:256 + (ck + 1) * 128, ns], yts[ck][i].t[:], [yts[ck][i]], [S["yt"]])

        st_A(0)
        st_1(0)
        for n in range(NT):
            st_B(n)
            if n >= 1:
                st_C(n - 1)
            st_3a(n)
            st_sq(n)
            if n + 1 < NT:
                st_A(n + 1)
                st_1(n + 1)
            st_3b(n)
            st_4(n)
            if n % 8 == 0 and n // 8 + 1 < NT // 8:
                for ck in range(3):
                    load_sg(ck, n // 8 + 1)
        st_C(NT - 1)


def phase_D(cx, l, W, G, S):
    Cd = G["C"]
    ident = G["ident"]
    rope = G["rope"]
    with cx.phase() as es:
        KCT = cx.sb("KCT", [64, 2, 256], BF16)
        VCX = cx.sb("VCX", [128, 2, 2, 129], BF16)
        with cx.phase():
            Cc = cx.sb("dCc", [64, T], F32)
            Sc = cx.sb("dSc", [64, T], F32)
            cx.dma("sp", Cc.t[:], rope.t[0][0:64, :], [rope], [Cc])
            cx.dma("sp", Sc.t[:], rope.t[1][0:64, :], [rope], [Sc])
            ovl = load_f32(cx, "ovl", Cd["c_ovl"].t, Cd["c_ovl"], [128, 2, 64])
            hps = [cx.ps(f"dhp{i}", [128, 512], F32) for i in range(2)]
            cps = cx.ps("dcp", [128, 512], F32)
            kp = cx.ps("dkp", [128, 2, 256], F32)
            ksp = cx.ps("dksp", [128, 2, 256], F32)
            vps = [cx.ps(f"dvp{i}", [128, 512], F32) for i in range(2)]
            cx.op("pool", lambda e: e.memset(KCT.t[:], 0.0), [], [KCT])
            cx.op("pool", lambda e: e.memset(VCX.t[:, :, :, 64:65], 1.0), [], [VCX])
            for g in range(2):
                cx.copy("pool", VCX.t[:, g, :, 65:129], ovl.t[:], [ovl], [VCX])
            for kv in ("k", "v"):
                src = S["kc"] if kv == "k" else S["vc"]
                kvT = cx.sb(f"dkvT{kv}", [128, T], BF16)
                cx.dma("sp", kvT.t[:], src.t, [src], [kvT])
                w1 = load_cast(cx, f"w1{kv}", W[f"w1{kv}"], [128, 32, 128])
                pos = load_cast(cx, f"pos{kv}", W[f"pos{kv}"], [64, 32], eng="dve")
                b1 = load_f32(cx, f"b1{kv}", W[f"b1{kv}"].t, W[f"b1{kv}"], [128, 1])
                w2 = load_cast(cx, f"w2{kv}", W[f"w2{kv}"], [128, 64], eng="dve")
                cb = cx.sb(f"dcb{kv}", [128, 1], F32)
                h1 = cx.sb(f"dh1{kv}", [128, 2, 256], BF16)
                cx.op("pool", lambda e, h1=h1: e.memset(h1.t[:], 0.0), [], [h1])
                for i in range(32):
                    cx.mm(cps.t[:, 0:1], w1.t[0:64, i, :], pos.t[0:64, i:i + 1], i == 0, i == 31, [w1, pos], [cps])
                cx.tt("dve", cb.t[:], cps.t[:, 0:1], b1.t[:], ALU.add, [cps, b1], [cb])
                for g in range(2):
                    r = slice(g * 64, (g + 1) * 64)
                    for i in range(32):
                        cx.mm(hps[g].t[:, 0:255], w1.t[r, i, :], kvT.t[r, i:i + 16 * 254 + 1:16], i == 0, i == 31,
                              [w1, kvT], [hps[g]])
                    cx.act(h1.t[:, g, 0:255], hps[g].t[:, 0:255], AF.Gelu_apprx_tanh, [hps[g], cb], [h1],
                           bias=cb.t[:, 0:1])
                if kv == "k":
                    w2s = load_cast(cx, "w2ks", W["w2ks"], [128, 64], eng="dve")
                    cx.mm(kp.t[0:64], w2.t[:], h1.t[:], True, True, [w2, h1], [kp])
                    cx.mm(ksp.t[0:64], w2s.t[:], h1.t[:], True, True, [w2s, h1], [ksp])
                    t1 = cx.sb("dkt1", [64, 2, 255], F32)
                    t2 = cx.sb("dkt2", [64, 2, 255], F32)
                    cview = Cc.t[:, 31::16].unsqueeze(1).broadcast_to([64, 2, 255])
                    sview = Sc.t[:, 31::16].unsqueeze(1).broadcast_to([64, 2, 255])
                    cx.tt("dve", t1.t[:], kp.t[0:64, :, 0:255], cview, ALU.mult, [kp, Cc], [t1])
                    cx.tt("dve", t2.t[:], ksp.t[0:64, :, 0:255], sview, ALU.mult, [ksp, Sc], [t2])
                    cx.tt("dve", KCT.t[:, :, 0:255], t1.t[:], t2.t[:], ALU.add, [t1, t2], [KCT])
                else:
                    for g in range(2):
                        for nt in range(2):
                            vp_ = vps[(g * 2 + nt) % 2]
                            cx.mm(vp_.t[:, 0:64], h1.t[:, g, nt * 128:(nt + 1) * 128], w2.t[:], True, True, [h1, w2], [vp_])
                            cx.copy("act", VCX.t[:, g, nt, 0:64], vp_.t[:, 0:64], [vp_], [VCX])
        dstop = DBG.get("d_stop", 9)
        if dstop <= 1:
            return
        identb = ident
        QA = [cx.sb(f"QA{h}", [128, T], BF16) for h in range(6)]
        KSA = [cx.sb(f"KSA{g}", [128, T], BF16) for g in range(2)]
        KW = [cx.sb(f"KW{g}", [64, T], BF16) for g in range(2)]
        VSX = cx.sb("VSX", [128, NT, 2, 65], BF16)
        VWX = cx.sb("VWX", [128, NT, 2, 65], BF16)
        CMN = cx.sb("CMN", [128, 2, T], BF16)
        SB_ = load_f32(cx, "sbias", Cd["c_sbias"].t, Cd["c_sbias"], [128, NT, 64])
        GT = cx.sb("GTs", [128, NT, 18], F32)
        cx.dma("sp", GT.t[:], S["gt"].t.rearrange("(n p) c -> p n c", p=128), [S["gt"]], [GT])
        causb = cx.sb("causb", [128, 128], BF16)
        upperb = cx.sb("upperb", [128, 128], BF16)
        cx.dma("pool", causb.t[:], Cd["c_caus"].t, [Cd["c_caus"]], [causb])
        cx.dma("pool", upperb.t[:], Cd["c_upper"].t, [Cd["c_upper"]], [upperb])
        for nt in range(2):
            cx.dma("pool", CMN.t[:, nt, :], Cd["c_cmn"].t[:, nt, :], [Cd["c_cmn"]], [CMN])
        for g in range(2):
            cx.dma("pool", KSA[g].t[64:128, :], Cd["c_expand"].t, [Cd["c_expand"]], [KSA[g]])
        for h in range(6):
            cx.dma("sp", QA[h].t[0:64, :], S["qn"].t[h * 64:(h + 1) * 64, :], [S["qn"]], [QA[h]])
            cx.op("pool", lambda e, h=h: e.memset(QA[h].t[64:128, :], 0.0), [], [QA[h]])
        for g in range(2):
            cx.dma("pool", KSA[g].t[0:64, :], S["ks"].t[g * 64:(g + 1) * 64, :], [S["ks"]], [KSA[g]])
            cx.dma("sp", KW[g].t[:], S["kw"].t[g * 64:(g + 1) * 64, :], [S["kw"]], [KW[g]])
            cx.dma("pool", VSX.t[:, :, g, 0:64],
                   S["tmb"].t[:, 384 + g * 64:384 + (g + 1) * 64].rearrange("(n p) c -> p n c", p=128), [S["tmb"]], [VSX])
            cx.dma("pool", VWX.t[:, :, g, 0:64],
                   S["tmb"].t[:, 512 + g * 64:512 + (g + 1) * 64].rearrange("(n p) c -> p n c", p=128), [S["tmb"]], [VWX])
        cx.op("pool", lambda e: e.memset(VSX.t[:, :, :, 64:65], 1.0), [], [VSX])
        cx.op("pool", lambda e: e.memset(VWX.t[:, :, :, 64:65], 1.0), [], [VWX])
        if dstop <= 2:
            return
        SP = [cx.ps(f"dS{i}", [128, 512], F32) for i in range(4)]
        OP = [cx.ps(f"dO{i}", [128, 512], F32) for i in range(3)]
        tpt = cx.ps("dT", [128, 8, 128], BF16)
        _tb = Buf("dT", excl=True)
        TP = [Tile(tpt.t[:, 4 * i:4 * i + 4, :], _tb) for i in range(2)]
        pTs = [cx.sb(f"dpT{i}", [128, 512], BF16) for i in range(8)]
        OACC = [cx.sb(f"dOACC{i}", [128, 4, 384], F32) for i in range(2)]
        IMP = [cx.sb(f"dIMP{i}", [128, 4, 64], F32) for i in range(2)]
        recs = [cx.sb(f"drec{i}", [128, 2, 4], F32) for i in range(4)]
        sc1 = [cx.sb(f"dsc1{i}", [128, 64], F32) for i in range(2)]
        sc2 = [cx.sb(f"dsc2{i}", [128, 64], F32) for i in range(2)]
        m8 = [cx.sb(f"dm8{i}", [128, 16], F32) for i in range(2)]
        nm = [cx.sb(f"dnm{i}", [128, 128], BF16) for i in range(2)]
        for t_ in nm:
            cx.op("pool", lambda e, t_=t_: e.memset(t_.t[:], 0.0), [], [t_])
        obf = [cx.sb(f"dobf{i}", [128, 384], BF16) for i in range(2)]
        yst = [cx.sb(f"dyst{i}", [128, 3, 512], BF16) for i in range(2)]
        cnt = {"s": 0, "o": 0, "p": 0, "r": 0, "t": 0}

        def nxt(key, lst):
            x = lst[cnt[key] % len(lst)]
            cnt[key] += 1
            return x

        items = []

        def cmp_item(c, h):
            g, rr = divmod(h, 3)
            cs = slice(c * 512, (c + 1) * 512)
            oacc, imp = OACC[c % 2], IMP[c % 2]
            nts = [0] + ([1] if c >= 4 else [])
            st = {}

            def S_():
                st["pts"] = {}
                for nt in nts:
                    sp_ = nxt("s", SP)
                    need_mask = (c <= 4) if nt == 0 else True
                    cx.mm(sp_.t[:], KCT.t[0:64, g, nt * 128:(nt + 1) * 128], QA[h].t[0:64, cs], True, True,
                          [KCT, QA[h]], [sp_])
                    if need_mask:
                        cx.mm(sp_.t[:], identb.t[:], CMN.t[:, nt, cs], False, True, [identb, CMN], [sp_], skip=True)
                    pT = nxt("p", pTs)
                    cx.act(pT.t[:], sp_.t[:], AF.Exp, [sp_], [pT], scale=0.125)
                    st["pts"][nt] = pT

            def PV_():
                pts = st["pts"]
                for q4 in range(4):
                    qt = 4 * c + q4
                    ob_ = nxt("o", OP)
                    for j, nt in enumerate(nts):
                        cx.mm(ob_.t[:, 0:129], pts[nt].t[:, q4 * 128:(q4 + 1) * 128], VCX.t[:, g, nt, :],
                              j == 0, j == len(nts) - 1, [pts[nt], VCX], [ob_])
                    rc = nxt("r", recs)
                    cx.ts("dve", rc.t[:, 0, 0:1], ob_.t[:, 64:65], 1e-30, ALU.add, [ob_], [rc])
                    cx.op("dve", lambda e, rc=rc: e.reciprocal(out=rc.t[:, 0, 0:1], in_=rc.t[:, 0, 0:1]), [rc], [rc])
                    cx.tt("dve", rc.t[:, 1, 0:1], rc.t[:, 0, 0:1], GT.t[:, qt, 3 * h:3 * h + 1], ALU.mult, [rc, GT], [rc])
                    cx.ts("dve", oacc.t[:, q4, h * 64:(h + 1) * 64], ob_.t[:, 0:64], rc.t[:, 1, 0:1], ALU.mult,
                          [ob_, rc], [oacc])
                    if rr == 0:
                        cx.ts("dve", imp.t[:, q4, :], ob_.t[:, 65:129], rc.t[:, 0, 0:1], ALU.mult, [ob_, rc], [imp])
                    else:
                        cx.stt(imp.t[:, q4, :], ob_.t[:, 65:129], rc.t[:, 0, 0:1], imp.t[:, q4, :], ALU.mult, ALU.add,
                               [ob_, rc, imp], [imp])
                if rr == 2:
                    for q4 in range(4):
                        qt = 4 * c + q4
                        s1, s2, mm8, nm_ = sc1[q4 % 2], sc2[q4 % 2], m8[q4 % 2], nm[q4 % 2]
                        cx.tt("dve", s1.t[:], imp.t[:, q4, :], SB_.t[:, qt, :], ALU.add, [imp, SB_], [s1])
                        cx.op("dve", lambda e, mm8=mm8, s1=s1: e.max(out=mm8.t[:, 0:8], in_=s1.t[:]), [s1], [mm8])
                        cx.op("dve", lambda e, mm8=mm8, s1=s1, s2=s2: e.match_replace(
                            out=s2.t[:], in_to_replace=mm8.t[:, 0:8], in_values=s1.t[:], imm_value=-1e9), [s1, mm8], [s2])
                        cx.op("dve", lambda e, mm8=mm8, s2=s2: e.max(out=mm8.t[:, 8:16], in_=s2.t[:]), [s2], [mm8])
                        cx.ts("dve", mm8.t[:, 15:16], mm8.t[:, 15:16], 0.0, ALU.max, [mm8], [mm8])
                        cx.ts("dve", nm_.t[:, 64:128], s1.t[:], mm8.t[:, 15:16], ALU.is_lt, [s1, mm8], [nm_],
                              s2=NEG, op1=ALU.mult)
                        tp = nxt("t", TP)
                        cx.tr(tp.t[:, 0, :], nm_.t[:], ident.t[:], [nm_, ident], [tp])
                        for r3 in range(3):
                            hh = 3 * g + r3
                            cx.copy("dve",
                                    QA[hh].t[64:128, qt * 128:(qt + 1) * 128], tp.t[64:128, 0, :], [tp], [QA[hh]])
            return S_, PV_

        def att_items(c, branch, h):
            g = h // 3
            oacc = OACC[c % 2]
            kts = list(range(0, 4 * c + 4) if branch == 1 else range(max(4 * c - 4, 0), 4 * c + 4))
            shared = {"first": True}
            res = []
            for kt in kts:
                lo = max(kt - 4 * c, 0)
                hi = 3 if branch == 1 else min(kt + 4 - 4 * c, 3)
                n_ = (hi - lo + 1) * 128
                q0 = c * 512 + lo * 128
                ks_ = slice(kt * 128, (kt + 1) * 128)
                st = {}

                def S_(kt=kt, lo=lo, hi=hi, n_=n_, q0=q0, ks_=ks_, st=st):
                    sp_ = nxt("s", SP)
                    if branch == 1:
                        cx.mm(sp_.t[:, 0:n_], KSA[g].t[:, ks_], QA[h].t[:, q0:q0 + n_], True, True,
                              [KSA[g], QA[h]], [sp_])
                    else:
                        cx.mm(sp_.t[:, 0:n_], KW[g].t[0:64, ks_], QA[h].t[0:64, q0:q0 + n_], True, True,
                              [KW[g], QA[h]], [sp_])
                    if kt >= 4 * c:
                        cx.mm(sp_.t[:, 0:128], identb.t[:], causb.t[:], False, True, [identb, causb], [sp_], skip=True)
                    if branch == 2 and 4 * c <= kt + 4 <= 4 * c + 3:
                        cx.mm(sp_.t[:, n_ - 128:n_], identb.t[:], upperb.t[:], False, True, [identb, upperb], [sp_],
                              skip=True)
                    pT = nxt("p", pTs)
                    cx.act(pT.t[:, 0:n_], sp_.t[:, 0:n_], AF.Exp, [sp_], [pT], scale=0.125)
                    st["pT"] = pT

                def PV_(kt=kt, lo=lo, hi=hi, st=st, last=(kt == kts[-1])):
                    if shared["first"]:
                        shared["ob"] = nxt("o", OP)
                    ob_ = shared["ob"]
                    ov = ob_.t[:, 0:260].rearrange("p (a b) -> p a b", b=65)
                    pT = st["pT"]
                    vx = VSX if branch == 1 else VWX
                    for q4 in range(lo, hi + 1):
                        cx.mm(ov[:, q4, :], pT.t[:, (q4 - lo) * 128:(q4 - lo + 1) * 128], vx.t[:, kt, g, :],
                              shared["first"], True, [pT, vx], [ob_], skip=not shared["first"])
                        shared["first"] = False
                    if last:
                        rc = nxt("r", recs)
                        cx.ts("dve", rc.t[:, 0, :], ov[:, :, 64], 1e-30, ALU.add, [ob_], [rc])
                        cx.op("dve", lambda e, rc=rc: e.reciprocal(out=rc.t[:, 0, :], in_=rc.t[:, 0, :]), [rc], [rc])
                        col = 3 * h + branch
                        cx.tt("dve", rc.t[:, 1, :], rc.t[:, 0, :], GT.t[:, 4 * c:4 * c + 4, col], ALU.mult, [rc, GT], [rc])
                        for q4 in range(4):
                            av = oacc.t[:, q4, h * 64:(h + 1) * 64]
                            cx.stt(av, ov[:, q4, 0:64], rc.t[:, 1, q4:q4 + 1], av, ALU.mult, ALU.add,
                                   [ob_, rc, oacc], [oacc])
                res.append((S_, PV_))
            return res

        def out_item(c):
            cs = slice(c * 512, (c + 1) * 512)
            oacc = OACC[c % 2]

            def PV_():
                ys = yst[c % 2]
                for q4 in range(4):
                    ob2 = obf[q4 % 2]
                    cx.copy("pool", ob2.t[:], oacc.t[:, q4, :], [oacc], [ob2])
                    tp = nxt("t", TP)
                    for j in range(3):
                        cx.tr(tp.t[:, j, :], ob2.t[:, j * 128:(j + 1) * 128], ident.t[:], [ob2, ident], [tp])
                    cx.copy("act", ys.t[:, :, q4 * 128:(q4 + 1) * 128], tp.t[:, 0:3, :], [tp], [ys])
                for j in range(3):
                    cx.dma("sp", S["yt"].t[640 + j * 128:640 + (j + 1) * 128, cs], ys.t[:, j, :], [ys], [S["yt"]])
            return (lambda: None), PV_

        for c in range(8):
            for h in range(6):
                items.append(cmp_item(c, h))
            if dstop >= 5:
                for branch in ((1, 2) if dstop >= 6 else (1,)):
                    for h in range(6):
                        items.extend(att_items(c, branch, h))
            items.append(out_item(c))
        SKEW = 2
        for i in range(len(items) + SKEW):
            if i < len(items):
                items[i][0]()
            if i >= SKEW:
                items[i - SKEW][1]()


def phase_E(cx, l, xd, x1d, W, G, S):
    with cx.phase() as es:
        wo = load_cast(cx, "wout", W["wout"], [128, 8, D])
        g_t = load_f32(cx, "ln1g", W["ln1_g"].t.broadcast_to([128, D]), W["ln1_g"], [128, D])
        b_t = load_f32(cx, "ln1b", W["ln1_b"].t.broadcast_to([128, D]), W["ln1_b"], [128, D])
        yts = [cx.sb(f"eyt{i}", [128, 8, 512], BF16) for i in range(2)]
        xrs = [cx.sb(f"exr{i}", [128, D], F32) for i in range(3)]
        pss = [cx.ps(f"eps{i}", [128, 512], F32) for i in range(6)]
        tmps = ln_tmps(cx, es, n=3)

        def load_y(tc):
            cx.dma("sp", yts[tc % 2].t[:], S["yt"].t[:, tc * 512:(tc + 1) * 512].rearrange("(k p) t -> p k t", p=128),
                   [S["yt"]], [yts[tc % 2]])
        stages = {}

        def mm_tile(tt):
            tc, q = divmod(tt, 4)
            if q == 0 and tc + 1 < 8:
                load_y(tc + 1)
            y_ = yts[tc % 2]
            xr = xrs[tt % 3]
            cx.dma("sp", xr.t[:], xd.t[tt * 128:(tt + 1) * 128, :], [xd], [xr])
            zp = pss[(tt % 3) * 2:(tt % 3) * 2 + 2]
            for hf in range(2):
                for k in range(8):
                    cx.mm(zp[hf].t[:], y_.t[:, k, q * 128:(q + 1) * 128], wo.t[:, k, hf * 512:(hf + 1) * 512],
                          k == 0, k == 7, [y_, wo], [zp[hf]])
            stages[tt] = ln_stages(cx, zp, xr, g_t, b_t, x1d, tt * 128, tmps[tt % 3])
        load_y(0)
        mm_tile(0)
        stages[0][0]()
        stages[0][1]()
        for tt in range(NT):
            if tt + 1 < NT:
                mm_tile(tt + 1)
                stages[tt + 1][0]()
            stages[tt][2]()
            stages[tt][3]()
            if tt + 1 < NT:
                stages[tt + 1][1]()
            if tt >= 1:
                stages[tt - 1][4]()
        stages[NT - 1][4]()


def phase_F(cx, l, x1d, x2d, W, G, S):
    TC = 1024
    ident = G["ident"]
    with cx.phase() as es:
        wd = cx.sb("wd_b", [128, NF, D], BF16)
        for f in range(NF):
            cx.dma("pool", wd.t[:, f, :], W["wd"].t[:, f, :], [W["wd"]], [wd])
        cw = load_f32(cx, "cw", W["cw"].t, W["cw"], [128, NF, 3])
        cb = load_f32(cx, "cb", W["cb"].t, W["cb"], [128, NF])
        g_t = load_f32(cx, "ln2g", W["ln2_g"].t.broadcast_to([128, D]), W["ln2_g"], [128, D])
        b_t = load_f32(cx, "ln2b", W["ln2_b"].t.broadcast_to([128, D]), W["ln2_b"], [128, D])
        carry = cx.sb("carry", [128, NF, 2], F32)
        cx.op("pool", lambda e: e.memset(carry.t[:], 0.0), [], [carry])
        xTs = [cx.sb(f"x1T{i}", [128, 8, TC], BF16) for i in range(2)]
        xbs = [cx.sb(f"fxb{i}", [128, D], BF16) for i in range(3)]
        xpt = [cx.ps(f"fxpt{i}", [128, 8, 128], BF16) for i in range(2)]
        xcnt = [0]

        def emit_xT(tc):
            for q in range(TC // 128):
                j = xcnt[0]
                xcnt[0] += 1
                bb, p = xbs[j % 3], xpt[j % 2]
                r0 = tc * TC + q * 128
                cx.dma("pool", bb.t[:], x1d.t[r0:r0 + 128, :], [x1d], [bb])
                for k in range(8):
                    cx.tr(p.t[:, k, :], bb.t[:, k * 128:(k + 1) * 128], ident.t[:], [bb, ident], [p])
                cx.copy("act" if j % 2 == 0 else "dve", xTs[tc % 2].t[:, :, q * 128:(q + 1) * 128], p.t[:], [p], [xTs[tc % 2]])
        act = cx.sb("ffact", [128, NF, TC], BF16)
        wb = [cx.sb(f"fwb{i}", [128, 2, 8, 128], BF16) for i in range(3)]
        hts = [cx.sb(f"fhb{i}", [128, 2 + TC], F32) for i in range(2)]
        hbA = [Tile(t.t, Buf(f"fhbA{i}")) for i, t in enumerate(hts)]
        hbB = [Tile(t.t, Buf(f"fhbB{i}")) for i, t in enumerate(hts)]
        hc = [cx.sb(f"fhc{i}", [128, 512], F32) for i in range(2)]
        gl = [cx.sb(f"fgl{i}", [128, 512], F32) for i in range(2)]
        xrs = [cx.sb(f"fxr{i}", [128, D], F32) for i in range(2)]
        tmps = ln_tmps(cx, es)
        pss = [cx.ps(f"fps{i}", [128, 512], F32) for i in range(6)]
        gi = 0
        nsteps = (T // TC) * NF

        def load_w(step):
            f = step % NF
            b_ = wb[step % 3]
            cx.dma("pool", b_.t[:, 0], W["wg"].t[f], [W["wg"]], [b_])
            cx.dma("pool", b_.t[:, 1], W["wu"].t[f], [W["wu"]], [b_])
        load_w(0)
        load_w(1)
        step = 0
        emit_xT(0)
        for tc in range(T // TC):
            xT = xTs[tc % 2]
            for f in range(NF):
                b_ = wb[step % 3]
                if step + 2 < nsteps:
                    load_w(step + 2)
                step += 1
                hA, hB = hbA[f % 2], hbB[f % 2]
                ht = hts[f % 2].t
                cx.copy("act", ht[:, 0:2], carry.t[:, f, :], [carry], [hA])
                for hf in range(TC // 512):
                    ts_ = slice(hf * 512, (hf + 1) * 512)
                    pg, pu = pss[(gi % 2) * 2], pss[(gi % 2) * 2 + 1]
                    c_, g_ = hc[gi % 2], gl[gi % 2]
                    gi += 1
                    for k in range(8):
                        cx.mm(pg.t[:], b_.t[:, 0, k, :], xT.t[:, k, ts_], k == 0, k == 7, [b_, xT], [pg])
                    for k in range(8):
                        cx.mm(pu.t[:], b_.t[:, 1, k, :], xT.t[:, k, ts_], k == 0, k == 7, [b_, xT], [pu])
                    hw_ = [hA] if hf == 0 else [hB]
                    hr_ = [hA] if hf == 0 else [hA, hB]
                    o = hf * 512
                    cx.copy("act", ht[:, 2 + o:514 + o], pg.t[:], [pg], hw_)
                    cx.ts("dve", c_.t[:], ht[:, 2 + o:514 + o], cw.t[:, f, 2:3], ALU.mult, hr_ + [cw, cb], [c_],
                          s2=cb.t[:, f:f + 1], op1=ALU.add)
                    cx.stt(c_.t[:], ht[:, 1 + o:513 + o], cw.t[:, f, 1:2], c_.t[:], ALU.mult, ALU.add, hr_ + [cw, c_], [c_])
                    cx.stt(c_.t[:], ht[:, o:512 + o], cw.t[:, f, 0:1], c_.t[:], ALU.mult, ALU.add, hr_ + [cw, c_], [c_])
                    cx.act(g_.t[:], c_.t[:], AF.Gelu_apprx_tanh, [c_], [g_])
                    cx.tt("dve", act.t[:, f, ts_], g_.t[:], pu.t[:], ALU.mult, [g_, pu], [act])
                cx.copy("act", carry.t[:, f, :], ht[:, TC:TC + 2], [hB], [carry])
            if tc + 1 < T // TC:
                emit_xT(tc + 1)
            for q in range(TC // 128):
                tt = tc * (TC // 128) + q
                xr = xrs[tt % 2]
                cx.dma("sp", xr.t[:], x1d.t[tt * 128:(tt + 1) * 128, :], [x1d], [xr])
                zp = pss[4:6]
                for hf in range(2):
                    for f in range(NF):
                        cx.mm(zp[hf].t[:], act.t[:, f, q * 128:(q + 1) * 128], wd.t[:, f, hf * 512:(hf + 1) * 512],
                              f == 0, f == NF - 1, [act, wd], [zp[hf]])
                layer_norm_store(cx, zp, xr, g_t, b_t, x2d, tt * 128, tmps[tt % 2])

SCRATCH = {
    "rope": ([2, 128, T], F32), "vp": ([256, T], F32),
    "qr": ([384, T], BF16), "kr": ([384, T], BF16), "qn": ([384, T], BF16),
    "ks": ([128, T], BF16), "kw": ([128, T], BF16), "kc": ([128, T], BF16), "vc": ([128, T], BF16),
    "tmb": ([T, 640], BF16), "gr": ([T, 384], F32), "gt": ([T, 18], F32),
    "yt": ([1024, T], BF16), "x1": ([T, D], F32), "xmid": ([T, D], F32),
}


def build_program(layers=(0, 1), phases="ABCDEF", ext_in=(), ext_out=(), prologue=True):
    nc = bass.Bass("TRN2", target_bir_lowering=False)
    cx = Ctx(nc, ext_in, ext_out)
    xd = cx.dr("x", [T, D], F32, kind="ExternalInput")
    posd = cx.dr("pos", [1, T], I32, kind="ExternalInput")
    Cd = {k: cx.dr(k, list(v.shape), F32, kind="ExternalInput") for k, v in CONSTS.items()}
    Wd = {l: {k: cx.dr(f"{k}_{l}", shp, F32, kind="ExternalInput") for k, shp in LAYER_SHAPES.items()} for l in layers}
    S = {k: cx.dr(k, shp, dt) for k, (shp, dt) in SCRATCH.items()}
    outd = cx.dr("y", [T, D], F32, kind="ExternalOutput")
    with contextlib.ExitStack() as gs:
        cx.stack = gs
        G = {"rope": S["rope"]}
        G["ident"] = load_cast(cx, "ident", Cd["c_ident"], [128, 128])
        G["inv"] = load_f32(cx, "inv", Cd["c_inv"].t, Cd["c_inv"], [128, 1])
        G["sgn"] = load_f32(cx, "sgn", Cd["c_sgn"].t, Cd["c_sgn"], [128, 1])
        G["pw"] = load_f32(cx, "pw", Cd["c_pw"].t, Cd["c_pw"], [128, 2])
        G["prc"] = load_f32(cx, "prc", Cd["c_prc"].t, Cd["c_prc"], [128, 2, 16])
        G["C"] = Cd
        if prologue:
            prologue_rope(cx, posd, G)
        cur = xd
        for li, l in enumerate(layers):
            nxt = outd if li == len(layers) - 1 else S["xmid"]
            W = Wd[l]
            if "A" in phases:
                phase_A(cx, l, cur, W, G, S)
            if "B" in phases:
                phase_B(cx, l, W, G, S)
            if "C" in phases:
                phase_C(cx, l, W, G, S)
            if "D" in phases:
                phase_D(cx, l, W, G, S)
            if "E" in phases:
                phase_E(cx, l, cur, S["x1"], W, G, S)
            if "F" in phases:
                phase_F(cx, l, S["x1"], nxt, W, G, S)
            cur = nxt
        cx.P.barrier()
        finals = [outd.b] + [cx.dram[n].b for n in cx.ext_out]
        cx.P.emit_all(final_bufs=finals)
    return nc, cx


def make_in_maps(inputs, layers=(0, 1), cores=range(8)):
    shared = dict(CONSTS)
    for l in layers:
        for k, v in _layer_arrays(inputs, l).items():
            assert list(v.shape) == LAYER_SHAPES[k], (k, v.shape)
            shared[f"{k}_{l}"] = v.astype(np.float32, copy=False)
    maps = []
    for b in cores:
        m = dict(shared)
        m["x"] = np.ascontiguousarray(inputs["x"][b])
        m["pos"] = np.ascontiguousarray(inputs["positions"][b].reshape(1, T).astype(np.int32))
        maps.append(m)
    return maps


def kernel(**inputs):
    inputs = {k: np.asarray(v) for k, v in inputs.items()}
    nc, cx = build_program()
    maps = make_in_maps(inputs)
    res = run_bass_kernel_spmd(nc, maps, core_ids=list(range(8)))
    return np.stack([np.asarray(r["y"]) for r in res.results], 0).astype(np.float32)
```

```python
import contextlib
import math
import numpy as np
import concourse.bass as bass
import concourse.mybir as mybir
from concourse.bass_utils import run_bass_kernel_spmd

F32 = mybir.dt.float32
BF16 = mybir.dt.bfloat16
I32 = mybir.dt.int32
AF = mybir.ActivationFunctionType
ALU = mybir.AluOpType
AX = mybir.AxisListType

T = 4096
D = 1024
DEPTH = 2
NT = T // 128
DFF = 2816
NF = DFF // 128
ALPHA = (2 * DEPTH) ** 0.25
NEG = -10000.0
DBG = {}

ENGS = ("pe", "act", "dve", "pool", "sp")


class Buf:
    __slots__ = ("name", "writers", "readers", "dsem", "dcount", "is_dram", "vsem", "excl")

    def __init__(self, name, is_dram=False, excl=False):
        self.name = name
        self.is_dram = is_dram
        self.excl = excl
        self.writers = []
        self.readers = []
        self.dsem = None
        self.dcount = 0
        self.vsem = None


class Op:
    __slots__ = ("eng", "emit", "waits", "is_dma", "dbuf", "dval", "needs_inc", "val", "vsem")

    def __init__(self, eng, emit, is_dma=False):
        self.eng = eng
        self.emit = emit
        self.waits = []
        self.is_dma = is_dma
        self.dbuf = None
        self.dval = 0
        self.needs_inc = False
        self.val = 0
        self.vsem = None


class Prog:
    def __init__(self, nc):
        self.nc = nc
        self.ops = {e: [] for e in ENGS}
        self.last = {e: None for e in ENGS}
        self.dma_bufs = {}
        self.pending_bar = {e: [] for e in ENGS}
        self.seq = 0
        self.bar_seq = 0
        self.vfree = {True: [], False: []}
        self.vkind = []
        self.vcount = []
        self.rsem = []

    def _dep(self, op, prod, force=False):
        if prod is op:
            return
        if (not force) and prod.val < self.bar_seq:
            return
        if not prod.is_dma and prod.eng == op.eng:
            if op.eng in ("pe", "sp"):
                return
        op.waits.append(prod)
        if not prod.is_dma:
            prod.needs_inc = True

    @staticmethod
    def _prune(lst):
        last = {}
        for r in lst:
            last[(r.eng, r.is_dma, r.vsem)] = r
        return list(last.values())

    def op(self, eng, emit, reads=(), writes=(), dma=False):
        o = Op(eng, emit, is_dma=dma)
        self.seq += 1
        o.val = self.seq
        if self.pending_bar[eng]:
            for p in self.pending_bar[eng]:
                self._dep(o, p, force=True)
            self.pending_bar[eng] = []
        for b in reads:
            for w in b.writers:
                self._dep(o, w)
            if b.excl:
                for r in b.readers:
                    if r.eng != eng:
                        self._dep(o, r)
        for b in writes:
            for r in b.readers:
                if (not r.is_dma) and r.eng == eng and not dma:
                    continue
                self._dep(o, r)
            if not b.readers:
                for w in b.writers:
                    if w.is_dma and dma:
                        continue
                    if (not w.is_dma) and w.eng == eng and not dma:
                        continue
                    self._dep(o, w)
        if dma:
            assert len(writes) == 1
            b = writes[0]
            if b.is_dram:
                b = [r for r in reads if not r.is_dram][0]
            if b.vsem is None:
                sw = (eng == "pool")
                if self.vfree[sw]:
                    b.vsem = self.vfree[sw].pop()
                else:
                    b.vsem = len(self.vcount)
                    self.vcount.append(0)
                    self.vkind.append(sw)
                b.dcount = self.vcount[b.vsem]
            b.dcount += 16
            self.vcount[b.vsem] = b.dcount
            o.dbuf = b
            o.dval = b.dcount
            o.vsem = b.vsem
            self.dma_bufs[id(b)] = b
        for b in writes:
            if b.readers:
                b.writers = [o]
                b.readers = []
            else:
                b.writers.append(o)
                if len(b.writers) > 6:
                    b.writers = self._prune(b.writers)
        for b in reads:
            b.readers.append(o)
            if len(b.readers) > 6:
                b.readers = self._prune(b.readers)
        self.ops[eng].append(o)
        if not dma:
            self.last[eng] = o
        return o

    def dma(self, eng, out, in_, reads, writes):
        return self.op(eng, lambda e: e.dma_start(out=out, in_=in_), reads, writes, dma=True)

    def barrier(self):
        targets = []
        for e in ENGS:
            if self.last[e] is not None:
                targets.append(self.last[e])
        for b in self.dma_bufs.values():
            if b.vsem is not None:
                p = Op("sp", None, is_dma=True)
                p.dbuf = b
                p.dval = b.dcount
                p.vsem = b.vsem
                targets.append(p)
                self.vfree[self.vkind[b.vsem]].append(b.vsem)
                b.vsem = None
        self.dma_bufs = {}
        self.seq += 1
        self.bar_seq = self.seq
        for e in ENGS:
            self.pending_bar[e] = self._prune(self.pending_bar[e] + targets)

    def emit_all(self, final_bufs=()):
        nc = self.nc
        esem = {e: nc.alloc_semaphore(name=f"es_{e}") for e in ENGS}
        self.rsem = [nc.alloc_semaphore(name=f"ds_{i}") for i in range(len(self.vcount))]
        for e in ENGS:
            c = 0
            for o in self.ops[e]:
                if (not o.is_dma) and o.needs_inc:
                    c += 1
                    o.val = c
        engobj = {"pe": "tensor", "act": "scalar", "dve": "vector", "pool": "gpsimd", "sp": "sync"}
        with nc.Block() as block:
            for e in ENGS:
                def body(eng, ops=self.ops[e], e=e):
                    waited = {}
                    for o in ops:
                        need = {}
                        for p in o.waits:
                            if p.is_dma:
                                sem, val = self.rsem[p.vsem], p.dval
                            else:
                                sem, val = esem[p.eng], p.val
                            if sem is None:
                                continue
                            if need.get(sem.num, (None, 0))[1] < val:
                                need[sem.num] = (sem, val)
                        for k, (sem, val) in need.items():
                            if waited.get(k, 0) >= val:
                                continue
                            waited[k] = val
                            eng.wait_ge(sem, val)
                        ins = o.emit(eng)
                        if o.is_dma:
                            ins.then_inc(self.rsem[o.vsem], 16)
                        elif o.needs_inc:
                            ins.then_inc(esem[e], 1)
                    if e == "sp":
                        for i, sem in enumerate(self.rsem):
                            if waited.get(sem.num, 0) < self.vcount[i]:
                                eng.wait_ge(sem, self.vcount[i])

                getattr(block, engobj[e])(body)


OFF = {}
_o = 0
for _n, _w in (("v_pool", 256), ("q_ret", 384), ("k_ret", 384), ("v_ret", 384), ("g_ret", 384),
               ("q_nsa", 384), ("k_cmp", 128), ("v_cmp", 128), ("k_slc", 128), ("v_slc", 128),
               ("k_win", 128), ("v_win", 128), ("gate", 18)):
    OFF[_n] = _o
    _o += _w


def _swap_cols(cols):
    cols = np.asarray(cols).reshape(-1, 64)
    return np.concatenate([cols[:, 32:], cols[:, :32]], axis=1).reshape(-1)


def _fm_cols():
    ch = []
    for c in range(2):
        ch.append(np.arange(OFF["v_pool"] + 128 * c, OFF["v_pool"] + 128 * (c + 1)))
    for name in ("q_ret", "k_ret", "q_nsa"):
        for c in range(3):
            ch.append(np.arange(OFF[name] + 128 * c, OFF[name] + 128 * (c + 1)))
    for name in ("k_slc", "k_win", "k_cmp", "v_cmp"):
        ch.append(np.arange(OFF[name], OFF[name] + 128))
    return ch


FM_COLS = _fm_cols()
NFM = len(FM_COLS)
TM_COLS = np.concatenate([np.arange(OFF["v_ret"], OFF["v_ret"] + 384),
                          np.arange(OFF["v_slc"], OFF["v_slc"] + 128),
                          np.arange(OFF["v_win"], OFF["v_win"] + 128),
                          np.arange(OFF["g_ret"], OFF["g_ret"] + 384),
                          np.arange(OFF["gate"], OFF["gate"] + 18)])
NTM = len(TM_COLS)


def _const_tables():
    c = {}
    p = np.arange(128)
    inv = (10000.0 ** (-np.arange(0, 64, 2, dtype=np.float32) / 64)).astype(np.float32)
    c["c_inv"] = inv[p % 32].reshape(128, 1).astype(np.float32)
    c["c_sgn"] = np.where((p % 64) < 32, -1.0, 1.0).reshape(128, 1).astype(np.float32)
    h = np.arange(6, dtype=np.float64)
    lg = np.log1p(-np.power(2.0, -5.0 - h))
    i = np.arange(128, dtype=np.float64)
    dm = np.zeros((128, 6, 128), np.float32)
    for hh in range(6):
        diff = i[None, :] - i[:, None]
        dm[:, hh, :] = np.where(diff >= 0, 0.125 * np.exp(lg[hh] * np.maximum(diff, 0)), 0.0)
    c["c_dm"] = dm
    xi = np.exp(lg[:, None] * (i[None, :] + 1.0))
    zeta = 0.125 * np.exp(lg[:, None] * (127.0 - i[None, :]))
    gam = np.exp(lg * 128.0)
    xir = np.zeros((128, 3, 128), np.float32)
    zt = np.zeros((128, 3, 128), np.float32)
    gc = np.zeros((128, 3), np.float32)
    for ck in range(3):
        for hh in range(2):
            xir[hh * 64:(hh + 1) * 64, ck, :] = xi[2 * ck + hh][None, :]
            zt[:, ck, hh * 64:(hh + 1) * 64] = zeta[2 * ck + hh][:, None]
            gc[hh * 64:(hh + 1) * 64, ck] = gam[2 * ck + hh]
    c["c_xir"] = xir
    c["c_zt"] = zt
    c["c_gc"] = gc
    win = np.zeros((128, 2), np.float32)
    rc = np.zeros((128, 2, 16), np.float32)
    for ck in range(2):
        for hh in range(2):
            w = (2, 4, 8, 16)[2 * ck + hh]
            win[hh * 64:(hh + 1) * 64, ck] = 1.0 / w
            rc[hh * 64:(hh + 1) * 64, ck, :] = 1.0 / np.minimum(np.arange(16) + 1, w)
    c["c_pw"] = win
    c["c_prc"] = rc
    kk = np.arange(128)[:, None]
    qq = np.arange(128)[None, :]
    c["c_caus"] = np.where(kk > qq, NEG, 0.0).astype(np.float32)
    c["c_upper"] = np.where(kk <= qq, NEG, 0.0).astype(np.float32)
    c["c_ident"] = np.eye(128, dtype=np.float32)
    pm = np.zeros((128, 128), np.float32)
    mm_ = np.arange(128)
    pm[(mm_ % 64 + 32) % 64 + 64 * (mm_ // 64), mm_] = 1.0
    c["c_perm"] = pm
    ex = np.zeros((64, T), np.float32)
    ex[np.arange(T) // 64, np.arange(T)] = 1.0
    c["c_expand"] = ex
    n = np.arange(256)
    ends = 16 * n + 31
    cm = np.where(ends[:, None] > np.arange(T)[None, :], NEG, 0.0).astype(np.float32)
    c["c_cmn"] = np.ascontiguousarray(cm.reshape(2, 128, T).transpose(1, 0, 2))
    ci = np.arange(256)[:, None]
    sj = np.arange(64)[None, :]
    ov = np.clip(np.minimum(ci * 16 + 32, (sj + 1) * 64) - np.maximum(ci * 16, sj * 64), 0, None) / 16.0
    ov[255, :] = 0.0
    c["c_ovl"] = np.ascontiguousarray(ov.astype(np.float32).reshape(2, 128, 64).transpose(1, 0, 2))
    tq = np.arange(T)
    cur = tq // 64
    blk = np.arange(64)[None, :]
    forced = (blk == 0) | (blk == cur[:, None]) | (blk == cur[:, None] - 1)
    valid = blk * 64 <= tq[:, None]
    bias = np.where(valid, np.where(forced, 1e6, 0.0), -100.0).astype(np.float32)
    c["c_sbias"] = np.ascontiguousarray(bias.reshape(32, 128, 64).transpose(1, 0, 2))
    return c


CONSTS = _const_tables()


def _layer_arrays(inp, l):
    a = {}
    w_in = inp["w_in"][l]
    wk = w_in.reshape(8, 128, -1)
    a["wfm"] = np.ascontiguousarray(
        np.stack([wk[:, :, cols].transpose(1, 0, 2) for cols in FM_COLS], 0))
    a["wtm"] = np.ascontiguousarray(wk[:, :, TM_COLS].transpose(1, 0, 2))
    a["wout"] = np.ascontiguousarray(inp["w_out"][l].reshape(8, 128, D).transpose(1, 0, 2))
    pw = inp["pool_w"][l]
    bd = np.zeros((2, 128, 128), np.float32)
    for ck in range(2):
        for hh in range(2):
            bd[ck, hh * 64:(hh + 1) * 64, hh * 64:(hh + 1) * 64] = pw[2 * ck + hh]
    a["bd"] = bd
    a["psc"] = np.ascontiguousarray(inp["pool_scale"][l].reshape(2, 128).T)
    a["gng"] = np.ascontiguousarray(inp["ret_gn_g"][l].reshape(1, 384))
    for kv in ("k", "v"):
        w1 = inp[f"cmp_w1_{kv}"][l].reshape(32, 64, 128)
        w1d = np.concatenate([w1, w1], axis=1).transpose(1, 0, 2)
        a[f"w1{kv}"] = np.ascontiguousarray(w1d)
        a[f"b1{kv}"] = np.ascontiguousarray(inp[f"cmp_b1_{kv}"][l].reshape(128, 1))
        a[f"pos{kv}"] = np.ascontiguousarray(inp[f"cmp_pos_{kv}"][l].T)
        a[f"w2{kv}"] = np.ascontiguousarray(inp[f"cmp_w2_{kv}"][l])
    a["w2ks"] = np.ascontiguousarray(inp["cmp_w2_k"][l][:, _swap_cols(np.arange(64))])
    a["wg"] = np.ascontiguousarray(inp["ffn_w_gate"][l].reshape(8, 128, NF, 128).transpose(2, 1, 0, 3))
    a["wu"] = np.ascontiguousarray(inp["ffn_w_up"][l].reshape(8, 128, NF, 128).transpose(2, 1, 0, 3))
    a["wd"] = np.ascontiguousarray(inp["ffn_w_down"][l].reshape(NF, 128, D).transpose(1, 0, 2))
    a["cw"] = np.ascontiguousarray(inp["ffn_conv_w"][l].reshape(3, NF, 128).transpose(2, 1, 0))
    a["cb"] = np.ascontiguousarray(inp["ffn_conv_b"][l].reshape(NF, 128).T)
    for nme in ("ln1_g", "ln1_b", "ln2_g", "ln2_b"):
        a[nme] = np.ascontiguousarray(inp[nme][l].reshape(1, D))
    return a


LAYER_SHAPES = {
    "wfm": [NFM, 128, 8, 128], "wtm": [128, 8, NTM], "wout": [128, 8, D], "bd": [2, 128, 128],
    "psc": [128, 2], "gng": [1, 384],
    "w1k": [128, 32, 128], "b1k": [128, 1], "posk": [64, 32], "w2k": [128, 64],
    "w1v": [128, 32, 128], "b1v": [128, 1], "posv": [64, 32], "w2v": [128, 64], "w2ks": [128, 64],
    "wg": [NF, 128, 8, 128], "wu": [NF, 128, 8, 128], "wd": [128, NF, D], "cw": [128, NF, 3], "cb": [128, NF],
    "ln1_g": [1, D], "ln1_b": [1, D], "ln2_g": [1, D], "ln2_b": [1, D],
}


class Tile:
    __slots__ = ("t", "b")

    def __init__(self, t, b):
        self.t = t
        self.b = b


class Ctx:
    def __init__(self, nc, ext_in=(), ext_out=()):
        self.nc = nc
        self.P = Prog(nc)
        self.ext_in = set(ext_in)
        self.ext_out = set(ext_out)
        self.dram = {}
        self.stack = None
        self.uid = 0

    def dr(self, name, shape, dt, kind=None):
        if kind is None:
            kind = "ExternalInput" if name in self.ext_in else ("ExternalOutput" if name in self.ext_out else "Internal")
        t = self.nc.dram_tensor(name, list(shape), dt, kind=kind).ap()
        tl = Tile(t, Buf(name, is_dram=True))
        self.dram[name] = tl
        return tl

    def sb(self, name, shape, dt, es=None):
        self.uid += 1
        t = (es or self.stack).enter_context(self.nc.sbuf_tensor(f"{name}_{self.uid}", list(shape), dt))
        return Tile(t, Buf(name))

    def ps(self, name, shape, dt, es=None):
        self.uid += 1
        t = (es or self.stack).enter_context(self.nc.psum_tensor(f"{name}_{self.uid}", list(shape), dt))
        return Tile(t, Buf(name, excl=True))

    @contextlib.contextmanager
    def phase(self):
        old = self.stack
        with contextlib.ExitStack() as es:
            self.stack = es
            yield es
            self.P.barrier()
        self.stack = old

    def dma(self, eng, out, in_, reads, writes):
        self.P.dma(eng, out, in_, [x.b for x in reads], [x.b for x in writes])

    def op(self, eng, fn, reads, writes):
        self.P.op(eng, fn, [x.b for x in reads], [x.b for x in writes])

    def mm(self, out, lhsT, rhs, start, stop, reads, writes, skip=False):
        kw = dict(start=start, stop=stop)
        if skip:
            kw["skip_group_check"] = True
        self.op("pe", lambda e: e.matmul(out, lhsT=lhsT, rhs=rhs, **kw), reads, writes)

    def tr(self, out, in_, ident, reads, writes):
        self.op("pe", lambda e: e.transpose(out, in_, ident), reads, writes)

    def copy(self, eng, out, in_, reads, writes):
        if eng == "act":
            self.op("act", lambda e: e.copy(out=out, in_=in_), reads, writes)
        else:
            self.op(eng, lambda e: e.tensor_copy(out=out, in_=in_), reads, writes)

    def act(self, out, in_, func, reads, writes, **kw):
        self.op("act", lambda e: e.activation(out=out, in_=in_, func=func, **kw), reads, writes)

    def tt(self, eng, out, in0, in1, op, reads, writes):
        self.op(eng, lambda e: e.tensor_tensor(out=out, in0=in0, in1=in1, op=op), reads, writes)

    def ts(self, eng, out, in0, s1, op0, reads, writes, s2=None, op1=None):
        if op1 is None:
            self.op(eng, lambda e: e.tensor_scalar(out=out, in0=in0, scalar1=s1, scalar2=None, op0=op0), reads, writes)
        else:
            self.op(eng, lambda e: e.tensor_scalar(out=out, in0=in0, scalar1=s1, scalar2=s2, op0=op0, op1=op1), reads, writes)

    def stt(self, out, in0, scalar, in1, op0, op1, reads, writes):
        self.op("dve", lambda e: e.scalar_tensor_tensor(out=out, in0=in0, scalar=scalar, in1=in1, op0=op0, op1=op1),
                reads, writes)


def load_cast(cx, name, src_tile, shape, es=None, eng=None, q=None):
    b = cx.sb(name + "_b", shape, BF16, es)
    cx.dma("pool", b.t[:], src_tile.t, [src_tile], [b])
    return b


def load_f32(cx, name, src_ap, src_tile, shape, es=None, q="sp"):
    f = cx.sb(name, shape, F32, es)
    cx.dma(q, f.t[:], src_ap, [src_tile], [f])
    return f


def build_xT(cx, xd, xT, ident, ntiles, tok0=0):
    with contextlib.ExitStack() as es:
        xb = [cx.sb(f"xb{i}", [128, D], BF16, es) for i in range(3)]
        pt = [cx.ps(f"xpt{i}", [128, 8, 128], BF16, es) for i in range(2)]
        for tt in range(ntiles):
            bb, p = xb[tt % 3], pt[tt % 2]
            r0 = tok0 + tt * 128
            cx.dma("pool", bb.t[:], xd.t[r0:r0 + 128, :], [xd], [bb])
            for k in range(8):
                cx.tr(p.t[:, k, :], bb.t[:, k * 128:(k + 1) * 128], ident.t[:], [bb, ident], [p])
            cx.copy("act" if tt % 2 == 0 else "dve", xT.t[:, :, tt * 128:(tt + 1) * 128], p.t[:], [p], [xT])
        cx.P.barrier()


def layer_norm_store(cx, zps, xres, g_t, b_t, outd, r0, tmp, eng_q="pool"):
    z, st, mv, rs, o = tmp
    for hf in range(2):
        cx.stt(z.t[:, hf * 512:(hf + 1) * 512], xres.t[:, hf * 512:(hf + 1) * 512], ALPHA, zps[hf].t[:],
               ALU.mult, ALU.add, [xres, zps[hf]], [z])
    for hf in range(2):
        cx.op("dve", lambda e, hf=hf: e.bn_stats(out=st.t[:, hf, :], in_=z.t[:, hf * 512:(hf + 1) * 512]), [z], [st])
    cx.op("dve", lambda e: e.bn_aggr(out=mv.t[:], in_=st.t[:]), [st], [mv])
    cx.ts("dve", rs.t[:, 0:1], mv.t[:, 1:2], 1e-5, ALU.add, [mv], [rs])
    cx.act(rs.t[:, 0:1], rs.t[:, 0:1], AF.Sqrt, [rs], [rs])
    cx.op("dve", lambda e: e.reciprocal(out=rs.t[:, 0:1], in_=rs.t[:, 0:1]), [rs], [rs])
    cx.stt(rs.t[:, 1:2], mv.t[:, 0:1], -1.0, rs.t[:, 0:1], ALU.mult, ALU.mult, [mv, rs], [rs])
    cx.act(o.t[:], z.t[:], AF.Identity, [z, rs], [o], scale=rs.t[:, 0:1], bias=rs.t[:, 1:2])
    cx.tt("dve", o.t[:], o.t[:], g_t.t[:], ALU.mult, [o, g_t], [o])
    cx.tt("pool", o.t[:], o.t[:], b_t.t[:], ALU.add, [o, b_t], [o])
    cx.dma(eng_q, outd.t[r0:r0 + 128, :], o.t[:], [o], [outd])


def ln_stages(cx, zps, xres, g_t, b_t, outd, r0, tmp, eng_q="pool"):
    z, st, mv, rs, o = tmp

    def s1():
        for hf in range(2):
            cx.stt(z.t[:, hf * 512:(hf + 1) * 512], xres.t[:, hf * 512:(hf + 1) * 512], ALPHA, zps[hf].t[:],
                   ALU.mult, ALU.add, [xres, zps[hf]], [z])
        for hf in range(2):
            cx.op("dve", lambda e, hf=hf: e.bn_stats(out=st.t[:, hf, :], in_=z.t[:, hf * 512:(hf + 1) * 512]), [z], [st])
        cx.op("dve", lambda e: e.bn_aggr(out=mv.t[:], in_=st.t[:]), [st], [mv])
        cx.ts("dve", rs.t[:, 0:1], mv.t[:, 1:2], 1e-5, ALU.add, [mv], [rs])

    def s2():
        cx.act(rs.t[:, 0:1], rs.t[:, 0:1], AF.Sqrt, [rs], [rs])

    def s3():
        cx.op("dve", lambda e: e.reciprocal(out=rs.t[:, 0:1], in_=rs.t[:, 0:1]), [rs], [rs])
        cx.stt(rs.t[:, 1:2], mv.t[:, 0:1], -1.0, rs.t[:, 0:1], ALU.mult, ALU.mult, [mv, rs], [rs])

    def s4():
        cx.act(o.t[:], z.t[:], AF.Identity, [z, rs], [o], scale=rs.t[:, 0:1], bias=rs.t[:, 1:2])

    def s5():
        cx.tt("dve", o.t[:], o.t[:], g_t.t[:], ALU.mult, [o, g_t], [o])
        cx.tt("pool", o.t[:], o.t[:], b_t.t[:], ALU.add, [o, b_t], [o])
        cx.dma(eng_q, outd.t[r0:r0 + 128, :], o.t[:], [o], [outd])
    return s1, s2, s3, s4, s5


def ln_tmps(cx, es, n=2):
    res = []
    for i in range(n):
        res.append((cx.sb(f"lnz{i}", [128, D], F32, es), cx.sb(f"lnst{i}", [128, 2, 6], F32, es),
                    cx.sb(f"lnmv{i}", [128, 2], F32, es), cx.sb(f"lnrs{i}", [128, 2], F32, es),
                    cx.sb(f"lno{i}", [128, D], F32, es)))
    return res


def prologue_rope(cx, posd, G):
    rope = G["rope"]
    with cx.phase() as es:
        pi_ = cx.sb("posi", [128, T], I32)
        ang = cx.sb("ang", [128, T], F32)
        m = cx.sb("rm", [128, T], F32)
        o = cx.sb("ro", [128, T], F32)
        cx.dma("sp", pi_.t[:], posd.t.broadcast_to([128, T]), [posd], [pi_])
        cx.copy("dve", ang.t[:], pi_.t[:], [pi_], [ang])
        cx.ts("dve", ang.t[:], ang.t[:], G["inv"].t[:, 0:1], ALU.mult, [ang, G["inv"]], [ang])
        ki = cx.sb("rki", [128, T], I32)
        C1 = 6.28125
        C2 = 2.0 * math.pi - 6.28125
        for which, shift in ((0, 0.25), (1, 0.0)):
            cx.ts("dve", m.t[:], ang.t[:], 1.0 / (2.0 * math.pi), ALU.mult, [ang], [m], s2=shift, op1=ALU.add)
            cx.copy("dve", ki.t[:], m.t[:], [m], [ki])
            cx.copy("dve", m.t[:], ki.t[:], [ki], [m])
            cx.stt(o.t[:], m.t[:], -C1, ang.t[:], ALU.mult, ALU.add, [m, ang], [o])
            cx.stt(o.t[:], m.t[:], -C2, o.t[:], ALU.mult, ALU.add, [m, o], [o])
            if which == 0:
                cx.ts("dve", o.t[:], o.t[:], 0.5 * math.pi, ALU.add, [o], [o])
            cx.ts("dve", o.t[:], o.t[:], math.pi, ALU.min, [o], [o], s2=-math.pi, op1=ALU.max)
            cx.act(o.t[:], o.t[:], AF.Sin, [o], [o])
            if which == 1:
                cx.ts("dve", o.t[:], o.t[:], G["sgn"].t[:, 0:1], ALU.mult, [o, G["sgn"]], [o])
            cx.dma("sp", rope.t[which], o.t[:], [o], [rope])


def phase_A(cx, l, xd, W, G, S):
    ident = G["ident"]
    rope = G["rope"]
    with cx.phase():
        xT = cx.sb("xT", [128, 8, T], BF16)
        build_xT(cx, xd, xT, ident, NT)
        with cx.phase() as es:
          if DBG.get("tm", True):
              wtm = load_cast(cx, "wtm", W["wtm"], [128, 8, NTM])
              pss = [cx.ps(f"tmps{i}", [128, 512], F32) for i in range(6)]
              ob = [cx.sb(f"tmob{i}", [128, 640], BF16) for i in range(2)]
              og = [cx.sb(f"tmog{i}", [128, 384], F32) for i in range(2)]
              ogt = [cx.sb(f"tmogt{i}", [128, 18], F32) for i in range(2)]
              for tt in range(NT):
                  p0, p1, p2 = pss[(tt % 2) * 3:(tt % 2) * 3 + 3]
                  for (pp, c0, c1) in ((p0, 0, 512), (p1, 512, 1024), (p2, 1024, NTM)):
                      for k in range(8):
                          cx.mm(pp.t[:, 0:c1 - c0], xT.t[:, k, tt * 128:(tt + 1) * 128], wtm.t[:, k, c0:c1],
                                k == 0, k == 7, [xT, wtm], [pp])
                  b_, g_, t_ = ob[tt % 2], og[tt % 2], ogt[tt % 2]
                  cx.copy("dve", b_.t[:, 0:512], p0.t[:], [p0], [b_])
                  cx.copy("dve", b_.t[:, 512:640], p1.t[:, 0:128], [p1], [b_])
                  cx.act(g_.t[:], p1.t[:, 128:512], AF.Silu, [p1], [g_])
                  cx.act(t_.t[:], p2.t[:, 0:18], AF.Sigmoid, [p2], [t_])
                  r0 = tt * 128
                  cx.dma("sp", S["tmb"].t[r0:r0 + 128, :], b_.t[:], [b_], [S["tmb"]])
                  cx.dma("sp", S["gr"].t[r0:r0 + 128, :], g_.t[:], [g_], [S["gr"]])
                  cx.dma("sp", S["gt"].t[r0:r0 + 128, :], t_.t[:], [t_], [S["gt"]])
        with cx.phase() as es:
            C = cx.sb("ropeC", [128, T], F32)
            Sn = cx.sb("ropeS", [128, T], F32)
            cx.dma("sp", C.t[:], rope.t[0], [rope], [C])
            cx.dma("sp", Sn.t[:], rope.t[1], [rope], [Sn])
            perm = load_cast(cx, "perm", G["C"]["c_perm"], [128, 128])
            wb = [cx.sb(f"wb{i}", [128, 8, 128], BF16) for i in range(3)]
            pss = [cx.ps(f"fmps{i}", [128, 512], F32) for i in range(5)]
            ps2 = [cx.ps(f"fmps2{i}", [128, 512], F32) for i in range(3)]
            ost = [cx.sb(f"fmo{i}", [128, T], BF16) for i in range(2)]
            vst = [cx.sb(f"fmv{i}", [128, 512], F32) for i in range(2)]
            qbs = [cx.sb(f"fmqb{i}", [128, 512], BF16) for i in range(3)]
            t1s = [cx.sb(f"fmt1{i}", [128, 512], F32) for i in range(2)]
            t2s = [cx.sb(f"fmt2{i}", [128, 512], F32) for i in range(2)]
            units = [(0, S["vp"], 0, False), (1, S["vp"], 128, False)]
            ci = 2
            for dest in ("qr", "kr", "qn"):
                for c in range(3):
                    units.append((ci, S[dest], 128 * c, True))
                    ci += 1
            units.append((ci, S["ks"], 0, True))
            units.append((ci + 1, S["kw"], 0, True))
            units.append((ci + 2, S["kc"], 0, False))
            units.append((ci + 3, S["vc"], 0, False))

            def load_w(ui):
                cx.dma("pool", wb[ui % 3].t[:], W["wfm"].t[units[ui][0]], [W["wfm"]], [wb[ui % 3]])
            load_w(0)
            load_w(1)
            seq = [(ui, tc) for ui in range(len(units)) for tc in range(8)]
            state = {}

            def stage1(idx):
                ui, tc = seq[idx]
                cid, dest, row0, is_rope = units[ui]
                if tc == 0 and ui + 2 < len(units):
                    load_w(ui + 2)
                ts_ = slice(tc * 512, (tc + 1) * 512)
                pp = pss[idx % 5]
                for k in range(8):
                    cx.mm(pp.t[:], wb[ui % 3].t[:, k, :], xT.t[:, k, ts_], k == 0, k == 7, [wb[ui % 3], xT], [pp])
                state[idx] = pp
                if is_rope:
                    qb = qbs[idx % 3]
                    cx.copy("act", qb.t[:], pp.t[:], [pp], [qb])

            def stage2(idx):
                ui, tc = seq[idx]
                cid, dest, row0, is_rope = units[ui]
                ts_ = slice(tc * 512, (tc + 1) * 512)
                pp = state.pop(idx)
                o_ = ost[ui % 2]
                if is_rope:
                    qb, p2 = qbs[idx % 3], ps2[idx % 3]
                    cx.mm(p2.t[:], perm.t[:], qb.t[:], True, True, [perm, qb], [p2])
                    t1, t2 = t1s[idx % 2], t2s[idx % 2]
                    cx.tt("dve", t1.t[:], pp.t[:], C.t[:, ts_], ALU.mult, [pp, C], [t1])
                    cx.tt("dve", t2.t[:], p2.t[:], Sn.t[:, ts_], ALU.mult, [p2, Sn], [t2])
                    cx.tt("pool", o_.t[:, ts_], t1.t[:], t2.t[:], ALU.add, [t1, t2], [o_])
                elif dest is S["vp"]:
                    v_ = vst[tc % 2]
                    cx.copy("act", v_.t[:], pp.t[:], [pp], [v_])
                    cx.dma("sp", dest.t[row0:row0 + 128, ts_], v_.t[:], [v_], [dest])
                else:
                    cx.copy("act", o_.t[:, ts_], pp.t[:], [pp], [o_])
                if tc == 7 and dest is not S["vp"]:
                    cx.dma("sp", dest.t[row0:row0 + 128, :], o_.t[:], [o_], [dest])
            for idx in range(len(seq) + 1):
                if idx < len(seq):
                    stage1(idx)
                if idx >= 1:
                    stage2(idx - 1)


def phase_B(cx, l, W, G, S):
    with cx.phase():
        psc = load_f32(cx, "psc", W["psc"].t, W["psc"], [128, 2])
        pw = G["pw"]
        prc = G["prc"]
        pss = [cx.ps(f"bps{i}", [128, 512], F32) for i in range(4)]
        for ck in range(2):
            bd = load_cast(cx, f"bd{ck}", Tile(W["bd"].t[ck], W["bd"].b), [128, 128])
            v = cx.sb(f"pv{ck}", [128, 16 + T], F32)
            s2 = cx.sb(f"ps2{ck}", [128, 16 + T], F32)
            s4 = cx.sb(f"ps4{ck}", [128, 16 + T], F32)
            mx = cx.sb(f"pmx{ck}", [128, T], BF16)
            o = cx.sb(f"pbo{ck}", [128, T], BF16)
            for t_ in (v, s2, s4):
                cx.op("pool", lambda e, t_=t_: e.memset(t_.t[:, 0:16], 0.0), [], [t_])
            cx.dma("sp", v.t[:, 16:], S["vp"].t[ck * 128:(ck + 1) * 128, :], [S["vp"]], [v])
            if ck == 0:
                cx.tt("dve", s2.t[:, 16:], v.t[:, 16:], v.t[:, 15:15 + T], ALU.add, [v], [s2])
                cx.tt("dve", s4.t[64:128, 16:], s2.t[64:128, 16:], s2.t[64:128, 14:14 + T], ALU.add, [s2], [s4])
                lo, hi = s2, s4
            else:
                cx.tt("dve", s2.t[:, 16:], v.t[:, 16:], v.t[:, 15:15 + T], ALU.add, [v], [s2])
                cx.tt("dve", s4.t[:, 16:], s2.t[:, 16:], s2.t[:, 14:14 + T], ALU.add, [s2], [s4])
                cx.tt("dve", s2.t[:, 16:], s4.t[:, 16:], s4.t[:, 12:12 + T], ALU.add, [s4], [s2])
                cx.tt("dve", s4.t[64:128, 16:], s2.t[64:128, 16:], s2.t[64:128, 8:8 + T], ALU.add, [s2], [s4])
                lo, hi = s2, s4
            for (src, r) in ((lo, slice(0, 64)), (hi, slice(64, 128))):
                cx.stt(mx.t[r, :], src.t[r, 16:], pw.t[r, ck:ck + 1], v.t[r, 16:], ALU.mult, ALU.subtract,
                       [src, pw, v], [mx])
                cx.tt("dve", src.t[r, 0:16], src.t[r, 16:32], prc.t[r, ck, :], ALU.mult, [src, prc, mx], [src])
                cx.tt("dve", mx.t[r, 0:16], src.t[r, 0:16], v.t[r, 16:32], ALU.subtract, [src, v], [mx])
            for tc in range(8):
                ts_ = slice(tc * 512, (tc + 1) * 512)
                pp = pss[tc % 4]
                cx.mm(pp.t[:], bd.t[:], mx.t[:, ts_], True, True, [bd, mx], [pp])
                cx.act(o.t[:, ts_], pp.t[:], AF.Copy, [pp, psc], [o], scale=psc.t[:, ck:ck + 1])
            cx.dma("sp", S["yt"].t[ck * 128:(ck + 1) * 128, :], o.t[:], [o], [S["yt"]])


def phase_C(cx, l, W, G, S):
    Cd = G["C"]
    ident = G["ident"]
    with cx.phase() as es:
        dm = load_f32(cx, "dm", Cd["c_dm"].t, Cd["c_dm"], [128, 6, 128])
        xir = load_f32(cx, "xir", Cd["c_xir"].t, Cd["c_xir"], [128, 3, 128])
        zt = load_f32(cx, "zt", Cd["c_zt"].t, Cd["c_zt"], [128, 3, 128])
        gc = load_f32(cx, "gc", Cd["c_gc"].t, Cd["c_gc"], [128, 3])
        gng = load_f32(cx, "gng", W["gng"].t.broadcast_to([128, 384]), W["gng"], [128, 384])
        bankA = [cx.ps(f"cA{i}", [128, 512], F32) for i in range(2)]
        bankO = [cx.ps(f"cO{i}", [128, 512], F32) for i in range(2)]
        bankV = [cx.ps(f"cV{i}", [128, 512], F32) for i in range(2)]
        bankY = cx.ps("cY", [128, 8, 128], BF16)
        bankK = cx.ps("cK", [128, 8, 128], BF16)
        ktp = [bankK.t[:, i, :] for i in range(3)]
        opv2 = [[b.t[:, ck * 128:(ck + 1) * 128] for ck in range(3)] for b in bankO]
        kvp2 = [[b.t[:, ck * 128:(ck + 1) * 128] for ck in range(3)] for b in bankV]
        ytp = [bankY.t[:, i, :] for i in range(3)]
        qT, kT, qx, v, vz, R, Rb = [], [], [], [], [], [], []
        for ck in range(3):
            rows = slice(ck * 128, (ck + 1) * 128)
            qT.append(cx.sb(f"cqT{ck}", [128, T], BF16))
            kT.append(cx.sb(f"ckT{ck}", [128, T], BF16))
            qx.append(cx.sb(f"cqx{ck}", [128, T], BF16))
            v.append(cx.sb(f"cv{ck}", [128, NT, 128], BF16))
            vz.append(cx.sb(f"cvz{ck}", [128, NT, 128], BF16))
            R.append(cx.sb(f"cR{ck}", [128, 64], F32))
            Rb.append(cx.sb(f"cRb{ck}", [128, 64], BF16))
            cx.dma("sp", qT[ck].t[:], S["qr"].t[rows, :], [S["qr"]], [qT[ck]])
            cx.dma("sp", kT[ck].t[:], S["kr"].t[rows, :], [S["kr"]], [kT[ck]])
            cx.dma("pool", v[ck].t[:], S["tmb"].t[:, ck * 128:(ck + 1) * 128].rearrange("(n p) c -> p n c", p=128),
                   [S["tmb"]], [v[ck]])
            cx.tt("dve", qx[ck].t[:].rearrange("p (n i) -> p n i", i=128), qT[ck].t[:].rearrange("p (n i) -> p n i", i=128),
                  xir.t[:, ck:ck + 1, :].broadcast_to([128, NT, 128]), ALU.mult, [qT[ck], xir], [qx[ck]])
            cx.tt("dve", vz[ck].t[:], v[ck].t[:], zt.t[:, ck:ck + 1, :].broadcast_to([128, NT, 128]), ALU.mult,
                  [v[ck], zt], [vz[ck]])
            cx.op("pool", lambda e, ck=ck: e.memset(R[ck].t[:], 0.0), [], [R[ck]])
            cx.op("pool", lambda e, ck=ck: e.memset(Rb[ck].t[:], 0.0), [], [Rb[ck]])
        NB = 2
        sgs = [[cx.sb(f"csg{ck}_{i}", [128, 8, 128], F32) for i in range(2)] for ck in range(3)]
        kts = [[cx.sb(f"ckt{ck}_{i}", [128, 128], BF16) for i in range(NB)] for ck in range(3)]
        sms = [[cx.sb(f"csm{hh}_{i}", [128, 3, 128], BF16) for i in range(NB)] for hh in range(2)]
        sts = [[cx.sb(f"cst{ck}_{i}", [128, 2, 6], F32) for i in range(NB)] for ck in range(3)]
        mvs = [[cx.sb(f"cmv{ck}_{i}", [128, 2, 2], F32) for i in range(NB)] for ck in range(3)]
        rss = [[cx.sb(f"crs{ck}_{i}", [128, 2, 2], F32) for i in range(NB)] for ck in range(3)]
        ons = [[cx.sb(f"con{ck}_{i}", [128, 128], F32) for i in range(NB)] for ck in range(3)]
        onb = [[cx.sb(f"conb{ck}_{i}", [128, 128], BF16) for i in range(NB)] for ck in range(3)]
        yts = [[cx.sb(f"cyt{ck}_{i}", [128, 128], BF16) for i in range(NB)] for ck in range(3)]

        def load_sg(ck, blk):
            cx.dma("pool", sgs[ck][blk % 2].t[:],
                   S["gr"].t[blk * 1024:(blk + 1) * 1024, ck * 128:(ck + 1) * 128].rearrange("(n p) c -> p n c", p=128),
                   [S["gr"]], [sgs[ck][blk % 2]])
        for ck in range(3):
            load_sg(ck, 0)
        def ctx(n):
            return slice(n * 128, (n + 1) * 128), n % NB, opv2[n % 2], kvp2[n % 2], bankO[n % 2], bankV[n % 2]

        def st_A(n):
            ns, i, opv, kvp, BO, BV = ctx(n)
            for ck in range(3):
                cx.tr(ktp[ck], kT[ck].t[:, ns], ident.t[:], [kT[ck], ident], [bankK])
                for hh in range(2):
                    r = slice(hh * 64, (hh + 1) * 64)
                    cx.mm(bankA[hh].t[:, ck * 128:(ck + 1) * 128], kT[ck].t[r, ns], qT[ck].t[r, ns], True, True,
                          [kT[ck], qT[ck]], [bankA[hh]])

        def st_1(n):
            ns, i, opv, kvp, BO, BV = ctx(n)
            for ck in range(3):
                cx.copy("act", kts[ck][i].t[:], ktp[ck], [bankK], [kts[ck][i]])
            for hh in range(2):
                cx.tt("dve", sms[hh][i].t[:], bankA[hh].t[:, 0:384].rearrange("p (a b) -> p a b", b=128),
                      dm.t[:, hh::2, :], ALU.mult, [bankA[hh], dm], [sms[hh][i]])

        def st_B(n):
            ns, i, opv, kvp, BO, BV = ctx(n)
            for ck in range(3):
                for hh in range(2):
                    r = slice(hh * 64, (hh + 1) * 64)
                    cx.mm(opv[ck][:, r], sms[hh][i].t[:, ck, :], v[ck].t[:, n, r], True, False,
                          [sms[hh][i], v[ck]], [BO])
                    cx.mm(opv[ck][:, r], qx[ck].t[r, ns], Rb[ck].t[r, :], False, True, [qx[ck], Rb[ck]], [BO])
                cx.mm(kvp[ck], kts[ck][i].t[:], vz[ck].t[:, n, :], True, True, [kts[ck][i], vz[ck]], [BV])

        def st_3a(n):
            ns, i, opv, kvp, BO, BV = ctx(n)
            for ck in range(3):
                for hh in range(2):
                    r = slice(hh * 64, (hh + 1) * 64)
                    cx.stt(R[ck].t[r, :], R[ck].t[r, :], gc.t[r, ck:ck + 1], kvp[ck][r, r], ALU.mult, ALU.add,
                           [R[ck], gc, BV], [R[ck]])
                cx.copy("pool", Rb[ck].t[:], R[ck].t[:], [R[ck]], [Rb[ck]])
            for ck in range(3):
                st, mv, rs = sts[ck][i], mvs[ck][i], rss[ck][i]
                for hh in range(2):
                    r = slice(hh * 64, (hh + 1) * 64)
                    cx.op("dve", lambda e, hh=hh, r=r, st=st, ck=ck, opv_=opv: e.bn_stats(out=st.t[:, hh, :], in_=opv_[ck][:, r]), [BO], [st])
                    cx.op("dve", lambda e, hh=hh, st=st, mv=mv: e.bn_aggr(out=mv.t[:, hh, :], in_=st.t[:, hh, :]), [st], [mv])
                cx.ts("dve", rs.t[:, 0, :], mv.t[:, :, 1], 1e-5, ALU.add, [mv], [rs])

        def st_sq(n):
            i = n % NB
            for ck in range(3):
                rs = rss[ck][i]
                cx.act(rs.t[:, 0, :], rs.t[:, 0, :], AF.Sqrt, [rs], [rs])

        def st_3b(n):
            i = n % NB
            for ck in range(3):
                rs, mv = rss[ck][i], mvs[ck][i]
                cx.op("dve", lambda e, rs=rs: e.reciprocal(out=rs.t[:, 0, :], in_=rs.t[:, 0, :]), [rs], [rs])
                cx.stt(rs.t[:, 1, :], mv.t[:, :, 0], -1.0, rs.t[:, 0, :], ALU.mult, ALU.mult, [mv, rs], [rs])

        def st_4(n):
            ns, i, opv, kvp, BO, BV = ctx(n)
            for ck in range(3):
                rs, on, ob = rss[ck][i], ons[ck][i], onb[ck][i]
                for hh in range(2):
                    r = slice(hh * 64, (hh + 1) * 64)
                    cx.act(on.t[:, r], opv[ck][:, r], AF.Identity, [BO, rs], [on], scale=rs.t[:, 0, hh:hh + 1],
                           bias=rs.t[:, 1, hh:hh + 1])
                cx.tt("pool", on.t[:], on.t[:], gng.t[:, ck * 128:(ck + 1) * 128], ALU.mult, [on, gng], [on])
                cx.tt("pool", ob.t[:], on.t[:], sgs[ck][(n // 8) % 2].t[:, n % 8, :], ALU.mult, [on, sgs[ck][(n // 8) % 2]], [ob])

        def st_C(n):
            ns, i, opv, kvp, BO, BV = ctx(n)
            for ck in range(3):
                cx.tr(ytp[ck], onb[ck][i].t[:], ident.t[:], [onb[ck][i], ident], [bankY])
            for ck in range(3):
                cx.copy("act", yts[ck][i].t[:], ytp[ck], [bankY], [yts[ck][i]])
                cx.dma("sp", S["yt"].t[256 + ck * 128:256 + (ck + 1) * 128, ns], yts[ck][i].t[:], [yts[ck][i]], [S["yt"]])

        st_A(0)
        st_1(0)
        for n in range(NT):
            st_B(n)
            if n >= 1:
                st_C(n - 1)
            st_3a(n)
            st_sq(n)
            if n + 1 < NT:
                st_A(n + 1)
                st_1(n + 1)
            st_3b(n)
            st_4(n)
            if n % 8 == 0 and n // 8 + 1 < NT // 8:
                for ck in range(3):
                    load_sg(ck, n // 8 + 1)
        st_C(NT - 1)


def phase_D(cx, l, W, G, S):
    Cd = G["C"]
    ident = G["ident"]
    rope = G["rope"]
    with cx.phase() as es:
        KCT = cx.sb("KCT", [64, 2, 256], BF16)
        VCX = cx.sb("VCX", [128, 2, 2, 129], BF16)
        with cx.phase():
            Cc = cx.sb("dCc", [64, T], F32)
            Sc = cx.sb("dSc", [64, T], F32)
            cx.dma("sp", Cc.t[:], rope.t[0][0:64, :], [rope], [Cc])
            cx.dma("sp", Sc.t[:], rope.t[1][0:64, :], [rope], [Sc])
            ovl = load_f32(cx, "ovl", Cd["c_ovl"].t, Cd["c_ovl"], [128, 2, 64])
            hps = [cx.ps(f"dhp{i}", [128, 512], F32) for i in range(2)]
            cps = cx.ps("dcp", [128, 512], F32)
            kp = cx.ps("dkp", [128, 2, 256], F32)
            ksp = cx.ps("dksp", [128, 2, 256], F32)
            vps = [cx.ps(f"dvp{i}", [128, 512], F32) for i in range(2)]
            cx.op("pool", lambda e: e.memset(KCT.t[:], 0.0), [], [KCT])
            cx.op("pool", lambda e: e.memset(VCX.t[:, :, :, 64:65], 1.0), [], [VCX])
            for g in range(2):
                cx.copy("pool", VCX.t[:, g, :, 65:129], ovl.t[:], [ovl], [VCX])
            for kv in ("k", "v"):
                src = S["kc"] if kv == "k" else S["vc"]
                kvT = cx.sb(f"dkvT{kv}", [128, T], BF16)
                cx.dma("sp", kvT.t[:], src.t, [src], [kvT])
                w1 = load_cast(cx, f"w1{kv}", W[f"w1{kv}"], [128, 32, 128])
                pos = load_cast(cx, f"pos{kv}", W[f"pos{kv}"], [64, 32], eng="dve")
                b1 = load_f32(cx, f"b1{kv}", W[f"b1{kv}"].t, W[f"b1{kv}"], [128, 1])
                w2 = load_cast(cx, f"w2{kv}", W[f"w2{kv}"], [128, 64], eng="dve")
                cb = cx.sb(f"dcb{kv}", [128, 1], F32)
                h1 = cx.sb(f"dh1{kv}", [128, 2, 256], BF16)
                cx.op("pool", lambda e, h1=h1: e.memset(h1.t[:], 0.0), [], [h1])
                for i in range(32):
                    cx.mm(cps.t[:, 0:1], w1.t[0:64, i, :], pos.t[0:64, i:i + 1], i == 0, i == 31, [w1, pos], [cps])
                cx.tt("dve", cb.t[:], cps.t[:, 0:1], b1.t[:], ALU.add, [cps, b1], [cb])
                for g in range(2):
                    r = slice(g * 64, (g + 1) * 64)
                    for i in range(32):
                        cx.mm(hps[g].t[:, 0:255], w1.t[r, i, :], kvT.t[r, i:i + 16 * 254 + 1:16], i == 0, i == 31,
                              [w1, kvT], [hps[g]])
                    cx.act(h1.t[:, g, 0:255], hps[g].t[:, 0:255], AF.Gelu_apprx_tanh, [hps[g], cb], [h1],
                           bias=cb.t[:, 0:1])
                if kv == "k":
                    w2s = load_cast(cx, "w2ks", W["w2ks"], [128, 64], eng="dve")
                    cx.mm(kp.t[0:64], w2.t[:], h1.t[:], True, True, [w2, h1], [kp])
                    cx.mm(ksp.t[0:64], w2s.t[:], h1.t[:], True, True, [w2s, h1], [ksp])
                    t1 = cx.sb("dkt1", [64, 2, 255], F32)
                    t2 = cx.sb("dkt2", [64, 2, 255], F32)
                    cview = Cc.t[:, 31::16].unsqueeze(1).broadcast_to([64, 2, 255])
                    sview = Sc.t[:, 31::16].unsqueeze(1).broadcast_to([64, 2, 255])
                    cx.tt("dve", t1.t[:], kp.t[0:64, :, 0:255], cview, ALU.mult, [kp, Cc], [t1])
                    cx.tt("dve", t2.t[:], ksp.t[0:64, :, 0:255], sview, ALU.mult, [ksp, Sc], [t2])
                    cx.tt("dve", KCT.t[:, :, 0:255], t1.t[:], t2.t[:], ALU.add, [t1, t2], [KCT])
                else:
                    for g in range(2):
                        for nt in range(2):
                            vp_ = vps[(g * 2 + nt) % 2]
                            cx.mm(vp_.t[:, 0:64], h1.t[:, g, nt * 128:(nt + 1) * 128], w2.t[:], True, True, [h1, w2], [vp_])
                            cx.copy("act", VCX.t[:, g, nt, 0:64], vp_.t[:, 0:64], [vp_], [VCX])
        dstop = DBG.get("d_stop", 9)
        if dstop <= 1:
            return
        identb = ident
        QA = [cx.sb(f"QA{h}", [128, T], BF16) for h in range(6)]
        KSA = [cx.sb(f"KSA{g}", [128, T], BF16) for g in range(2)]
        KW = [cx.sb(f"KW{g}", [64, T], BF16) for g in range(2)]
        VSX = cx.sb("VSX", [128, NT, 2, 65], BF16)
        VWX = cx.sb("VWX", [128, NT, 2, 65], BF16)
        CMN = cx.sb("CMN", [128, 2, T], BF16)
        SB_ = load_f32(cx, "sbias", Cd["c_sbias"].t, Cd["c_sbias"], [128, NT, 64])
        GT = cx.sb("GTs", [128, NT, 18], F32)
        cx.dma("sp", GT.t[:], S["gt"].t.rearrange("(n p) c -> p n c", p=128), [S["gt"]], [GT])
        causb = cx.sb("causb", [128, 128], BF16)
        upperb = cx.sb("upperb", [128, 128], BF16)
        cx.dma("pool", causb.t[:], Cd["c_caus"].t, [Cd["c_caus"]], [causb])
        cx.dma("pool", upperb.t[:], Cd["c_upper"].t, [Cd["c_upper"]], [upperb])
        for nt in range(2):
            cx.dma("pool", CMN.t[:, nt, :], Cd["c_cmn"].t[:, nt, :], [Cd["c_cmn"]], [CMN])
        for g in range(2):
            cx.dma("pool", KSA[g].t[64:128, :], Cd["c_expand"].t, [Cd["c_expand"]], [KSA[g]])
        for h in range(6):
            cx.dma("sp", QA[h].t[0:64, :], S["qn"].t[h * 64:(h + 1) * 64, :], [S["qn"]], [QA[h]])
            cx.op("pool", lambda e, h=h: e.memset(QA[h].t[64:128, :], 0.0), [], [QA[h]])
        for g in range(2):
            cx.dma("pool", KSA[g].t[0:64, :], S["ks"].t[g * 64:(g + 1) * 64, :], [S["ks"]], [KSA[g]])
            cx.dma("sp", KW[g].t[:], S["kw"].t[g * 64:(g + 1) * 64, :], [S["kw"]], [KW[g]])
            cx.dma("pool", VSX.t[:, :, g, 0:64],
                   S["tmb"].t[:, 384 + g * 64:384 + (g + 1) * 64].rearrange("(n p) c -> p n c", p=128), [S["tmb"]], [VSX])
            cx.dma("pool", VWX.t[:, :, g, 0:64],
                   S["tmb"].t[:, 512 + g * 64:512 + (g + 1) * 64].rearrange("(n p) c -> p n c", p=128), [S["tmb"]], [VWX])
        cx.op("pool", lambda e: e.memset(VSX.t[:, :, :, 64:65], 1.0), [], [VSX])
        cx.op("pool", lambda e: e.memset(VWX.t[:, :, :, 64:65], 1.0), [], [VWX])
        if dstop <= 2:
            return
        SP = [cx.ps(f"dS{i}", [128, 512], F32) for i in range(4)]
        OP = [cx.ps(f"dO{i}", [128, 512], F32) for i in range(3)]
        tpt = cx.ps("dT", [128, 8, 128], BF16)
        _tb = Buf("dT", excl=True)
        TP = [Tile(tpt.t[:, 4 * i:4 * i + 4, :], _tb) for i in range(2)]
        pTs = [cx.sb(f"dpT{i}", [128, 512], BF16) for i in range(8)]
        OACC = [cx.sb(f"dOACC{i}", [128, 4, 384], F32) for i in range(2)]
        IMP = [cx.sb(f"dIMP{i}", [128, 4, 64], F32) for i in range(2)]
        recs = [cx.sb(f"drec{i}", [128, 2, 4], F32) for i in range(4)]
        sc1 = [cx.sb(f"dsc1{i}", [128, 64], F32) for i in range(2)]
        sc2 = [cx.sb(f"dsc2{i}", [128, 64], F32) for i in range(2)]
        m8 = [cx.sb(f"dm8{i}", [128, 16], F32) for i in range(2)]
        nm = [cx.sb(f"dnm{i}", [128, 128], BF16) for i in range(2)]
        for t_ in nm:
            cx.op("pool", lambda e, t_=t_: e.memset(t_.t[:], 0.0), [], [t_])
        obf = [cx.sb(f"dobf{i}", [128, 384], BF16) for i in range(2)]
        yst = [cx.sb(f"dyst{i}", [128, 3, 512], BF16) for i in range(2)]
        cnt = {"s": 0, "o": 0, "p": 0, "r": 0, "t": 0}

        def nxt(key, lst):
            x = lst[cnt[key] % len(lst)]
            cnt[key] += 1
            return x

        items = []

        def cmp_item(c, h):
            g, rr = divmod(h, 3)
            cs = slice(c * 512, (c + 1) * 512)
            oacc, imp = OACC[c % 2], IMP[c % 2]
            nts = [0] + ([1] if c >= 4 else [])
            st = {}

            def S_():
                st["pts"] = {}
                for nt in nts:
                    sp_ = nxt("s", SP)
                    need_mask = (c <= 4) if nt == 0 else True
                    cx.mm(sp_.t[:], KCT.t[0:64, g, nt * 128:(nt + 1) * 128], QA[h].t[0:64, cs], True, True,
                          [KCT, QA[h]], [sp_])
                    if need_mask:
                        cx.mm(sp_.t[:], identb.t[:], CMN.t[:, nt, cs], False, True, [identb, CMN], [sp_], skip=True)
                    pT = nxt("p", pTs)
                    cx.act(pT.t[:], sp_.t[:], AF.Exp, [sp_], [pT], scale=0.125)
                    st["pts"][nt] = pT

            def PV_():
                pts = st["pts"]
                for q4 in range(4):
                    qt = 4 * c + q4
                    ob_ = nxt("o", OP)
                    for j, nt in enumerate(nts):
                        cx.mm(ob_.t[:, 0:129], pts[nt].t[:, q4 * 128:(q4 + 1) * 128], VCX.t[:, g, nt, :],
                              j == 0, j == len(nts) - 1, [pts[nt], VCX], [ob_])
                    rc = nxt("r", recs)
                    cx.ts("dve", rc.t[:, 0, 0:1], ob_.t[:, 64:65], 1e-30, ALU.add, [ob_], [rc])
                    cx.op("dve", lambda e, rc=rc: e.reciprocal(out=rc.t[:, 0, 0:1], in_=rc.t[:, 0, 0:1]), [rc], [rc])
                    cx.tt("dve", rc.t[:, 1, 0:1], rc.t[:, 0, 0:1], GT.t[:, qt, 3 * h:3 * h + 1], ALU.mult, [rc, GT], [rc])
                    cx.ts("dve", oacc.t[:, q4, h * 64:(h + 1) * 64], ob_.t[:, 0:64], rc.t[:, 1, 0:1], ALU.mult,
                          [ob_, rc], [oacc])
                    if rr == 0:
                        cx.ts("dve", imp.t[:, q4, :], ob_.t[:, 65:129], rc.t[:, 0, 0:1], ALU.mult, [ob_, rc], [imp])
                    else:
                        cx.stt(imp.t[:, q4, :], ob_.t[:, 65:129], rc.t[:, 0, 0:1], imp.t[:, q4, :], ALU.mult, ALU.add,
                               [ob_, rc, imp], [imp])
                if rr == 2:
                    for q4 in range(4):
                        qt = 4 * c + q4
                        s1, s2, mm8, nm_ = sc1[q4 % 2], sc2[q4 % 2], m8[q4 % 2], nm[q4 % 2]
                        cx.tt("dve", s1.t[:], imp.t[:, q4, :], SB_.t[:, qt, :], ALU.add, [imp, SB_], [s1])
                        cx.op("dve", lambda e, mm8=mm8, s1=s1: e.max(out=mm8.t[:, 0:8], in_=s1.t[:]), [s1], [mm8])
                        cx.op("dve", lambda e, mm8=mm8, s1=s1, s2=s2: e.match_replace(
                            out=s2.t[:], in_to_replace=mm8.t[:, 0:8], in_values=s1.t[:], imm_value=-1e9), [s1, mm8], [s2])
                        cx.op("dve", lambda e, mm8=mm8, s2=s2: e.max(out=mm8.t[:, 8:16], in_=s2.t[:]), [s2], [mm8])
                        cx.ts("dve", mm8.t[:, 15:16], mm8.t[:, 15:16], 0.0, ALU.max, [mm8], [mm8])
                        cx.ts("dve", nm_.t[:, 64:128], s1.t[:], mm8.t[:, 15:16], ALU.is_lt, [s1, mm8], [nm_],
                              s2=NEG, op1=ALU.mult)
                        tp = nxt("t", TP)
                        cx.tr(tp.t[:, 0, :], nm_.t[:], ident.t[:], [nm_, ident], [tp])
                        for r3 in range(3):
                            hh = 3 * g + r3
                            cx.copy("dve",
                                    QA[hh].t[64:128, qt * 128:(qt + 1) * 128], tp.t[64:128, 0, :], [tp], [QA[hh]])
            return S_, PV_

        def att_items(c, branch, h):
            g = h // 3
            oacc = OACC[c % 2]
            kts = list(range(0, 4 * c + 4) if branch == 1 else range(max(4 * c - 4, 0), 4 * c + 4))
            shared = {"first": True}
            res = []
            for kt in kts:
                lo = max(kt - 4 * c, 0)
                hi = 3 if branch == 1 else min(kt + 4 - 4 * c, 3)
                n_ = (hi - lo + 1) * 128
                q0 = c * 512 + lo * 128
                ks_ = slice(kt * 128, (kt + 1) * 128)
                st = {}

                def S_(kt=kt, lo=lo, hi=hi, n_=n_, q0=q0, ks_=ks_, st=st):
                    sp_ = nxt("s", SP)
                    if branch == 1:
                        cx.mm(sp_.t[:, 0:n_], KSA[g].t[:, ks_], QA[h].t[:, q0:q0 + n_], True, True,
                              [KSA[g], QA[h]], [sp_])
                    else:
                        cx.mm(sp_.t[:, 0:n_], KW[g].t[0:64, ks_], QA[h].t[0:64, q0:q0 + n_], True, True,
                              [KW[g], QA[h]], [sp_])
                    if kt >= 4 * c:
                        cx.mm(sp_.t[:, 0:128], identb.t[:], causb.t[:], False, True, [identb, causb], [sp_], skip=True)
                    if branch == 2 and 4 * c <= kt + 4 <= 4 * c + 3:
                        cx.mm(sp_.t[:, n_ - 128:n_], identb.t[:], upperb.t[:], False, True, [identb, upperb], [sp_],
                              skip=True)
                    pT = nxt("p", pTs)
                    cx.act(pT.t[:, 0:n_], sp_.t[:, 0:n_], AF.Exp, [sp_], [pT], scale=0.125)
                    st["pT"] = pT

                def PV_(kt=kt, lo=lo, hi=hi, st=st, last=(kt == kts[-1])):
                    if shared["first"]:
                        shared["ob"] = nxt("o", OP)
                    ob_ = shared["ob"]
                    ov = ob_.t[:, 0:260].rearrange("p (a b) -> p a b", b=65)
                    pT = st["pT"]
                    vx = VSX if branch == 1 else VWX
                    for q4 in range(lo, hi + 1):
                        cx.mm(ov[:, q4, :], pT.t[:, (q4 - lo) * 128:(q4 - lo + 1) * 128], vx.t[:, kt, g, :],
                              shared["first"], True, [pT, vx], [ob_], skip=not shared["first"])
                        shared["first"] = False
                    if last:
                        rc = nxt("r", recs)
                        cx.ts("dve", rc.t[:, 0, :], ov[:, :, 64], 1e-30, ALU.add, [ob_], [rc])
                        cx.op("dve", lambda e, rc=rc: e.reciprocal(out=rc.t[:, 0, :], in_=rc.t[:, 0, :]), [rc], [rc])
                        col = 3 * h + branch
                        cx.tt("dve", rc.t[:, 1, :], rc.t[:, 0, :], GT.t[:, 4 * c:4 * c + 4, col], ALU.mult, [rc, GT], [rc])
                        for q4 in range(4):
                            av = oacc.t[:, q4, h * 64:(h + 1) * 64]
                            cx.stt(av, ov[:, q4, 0:64], rc.t[:, 1, q4:q4 + 1], av, ALU.mult, ALU.add,
                                   [ob_, rc, oacc], [oacc])
                res.append((S_, PV_))
            return res

        def out_item(c):
            cs = slice(c * 512, (c + 1) * 512)
            oacc = OACC[c % 2]

            def PV_():
                ys = yst[c % 2]
                for q4 in range(4):
                    ob2 = obf[q4 % 2]
                    cx.copy("pool", ob2.t[:], oacc.t[:, q4, :], [oacc], [ob2])
                    tp = nxt("t", TP)
                    for j in range(3):
                        cx.tr(tp.t[:, j, :], ob2.t[:, j * 128:(j + 1) * 128], ident.t[:], [ob2, ident], [tp])
                    cx.copy("act", ys.t[:, :, q4 * 128:(q4 + 1) * 128], tp.t[:, 0:3, :], [tp], [ys])
                for j in range(3):
                    cx.dma("sp", S["yt"].t[640 + j * 128:640 + (j + 1) * 128, cs], ys.t[:, j, :], [ys], [S["yt"]])
            return (lambda: None), PV_

        for c in range(8):
            for h in range(6):
                items.append(cmp_item(c, h))
            if dstop >= 5:
                for branch in ((1, 2) if dstop >= 6 else (1,)):
                    for h in range(6):
                        items.extend(att_items(c, branch, h))
            items.append(out_item(c))
        SKEW = 2
        for i in range(len(items) + SKEW):
            if i < len(items):
                items[i][0]()
            if i >= SKEW:
                items[i - SKEW][1]()


def phase_E(cx, l, xd, x1d, W, G, S):
    with cx.phase() as es:
        wo = load_cast(cx, "wout", W["wout"], [128, 8, D])
        g_t = load_f32(cx, "ln1g", W["ln1_g"].t.broadcast_to([128, D]), W["ln1_g"], [128, D])
        b_t = load_f32(cx, "ln1b", W["ln1_b"].t.broadcast_to([128, D]), W["ln1_b"], [128, D])
        yts = [cx.sb(f"eyt{i}", [128, 8, 512], BF16) for i in range(2)]
        xrs = [cx.sb(f"exr{i}", [128, D], F32) for i in range(3)]
        pss = [cx.ps(f"eps{i}", [128, 512], F32) for i in range(6)]
        tmps = ln_tmps(cx, es, n=3)

        def load_y(tc):
            cx.dma("sp", yts[tc % 2].t[:], S["yt"].t[:, tc * 512:(tc + 1) * 512].rearrange("(k p) t -> p k t", p=128),
                   [S["yt"]], [yts[tc % 2]])
        stages = {}

        def mm_tile(tt):
            tc, q = divmod(tt, 4)
            if q == 0 and tc + 1 < 8:
                load_y(tc + 1)
            y_ = yts[tc % 2]
            xr = xrs[tt % 3]
            cx.dma("sp", xr.t[:], xd.t[tt * 128:(tt + 1) * 128, :], [xd], [xr])
            zp = pss[(tt % 3) * 2:(tt % 3) * 2 + 2]
            for hf in range(2):
                for k in range(8):
                    cx.mm(zp[hf].t[:], y_.t[:, k, q * 128:(q + 1) * 128], wo.t[:, k, hf * 512:(hf + 1) * 512],
                          k == 0, k == 7, [y_, wo], [zp[hf]])
            stages[tt] = ln_stages(cx, zp, xr, g_t, b_t, x1d, tt * 128, tmps[tt % 3])
        load_y(0)
        mm_tile(0)
        stages[0][0]()
        stages[0][1]()
        for tt in range(NT):
            if tt + 1 < NT:
                mm_tile(tt + 1)
                stages[tt + 1][0]()
            stages[tt][2]()
            stages[tt][3]()
            if tt + 1 < NT:
                stages[tt + 1][1]()
            if tt >= 1:
                stages[tt - 1][4]()
        stages[NT - 1][4]()


def phase_F(cx, l, x1d, x2d, W, G, S):
    TC = 1024
    ident = G["ident"]
    with cx.phase() as es:
        wd = cx.sb("wd_b", [128, NF, D], BF16)
        for f in range(NF):
            cx.dma("pool", wd.t[:, f, :], W["wd"].t[:, f, :], [W["wd"]], [wd])
        cw = load_f32(cx, "cw", W["cw"].t, W["cw"], [128, NF, 3])
        cb = load_f32(cx, "cb", W["cb"].t, W["cb"], [128, NF])
        g_t = load_f32(cx, "ln2g", W["ln2_g"].t.broadcast_to([128, D]), W["ln2_g"], [128, D])
        b_t = load_f32(cx, "ln2b", W["ln2_b"].t.broadcast_to([128, D]), W["ln2_b"], [128, D])
        carry = cx.sb("carry", [128, NF, 2], F32)
        cx.op("pool", lambda e: e.memset(carry.t[:], 0.0), [], [carry])
        xTs = [cx.sb(f"x1T{i}", [128, 8, TC], BF16) for i in range(2)]
        xbs = [cx.sb(f"fxb{i}", [128, D], BF16) for i in range(3)]
        xpt = [cx.ps(f"fxpt{i}", [128, 8, 128], BF16) for i in range(2)]
        xcnt = [0]

        def emit_xT(tc):
            for q in range(TC // 128):
                j = xcnt[0]
                xcnt[0] += 1
                bb, p = xbs[j % 3], xpt[j % 2]
                r0 = tc * TC + q * 128
                cx.dma("pool", bb.t[:], x1d.t[r0:r0 + 128, :], [x1d], [bb])
                for k in range(8):
                    cx.tr(p.t[:, k, :], bb.t[:, k * 128:(k + 1) * 128], ident.t[:], [bb, ident], [p])
                cx.copy("act" if j % 2 == 0 else "dve", xTs[tc % 2].t[:, :, q * 128:(q + 1) * 128], p.t[:], [p], [xTs[tc % 2]])
        act = cx.sb("ffact", [128, NF, TC], BF16)
        wb = [cx.sb(f"fwb{i}", [128, 2, 8, 128], BF16) for i in range(3)]
        hts = [cx.sb(f"fhb{i}", [128, 2 + TC], F32) for i in range(2)]
        hbA = [Tile(t.t, Buf(f"fhbA{i}")) for i, t in enumerate(hts)]
        hbB = [Tile(t.t, Buf(f"fhbB{i}")) for i, t in enumerate(hts)]
        hc = [cx.sb(f"fhc{i}", [128, 512], F32) for i in range(2)]
        gl = [cx.sb(f"fgl{i}", [128, 512], F32) for i in range(2)]
        xrs = [cx.sb(f"fxr{i}", [128, D], F32) for i in range(2)]
        tmps = ln_tmps(cx, es)
        pss = [cx.ps(f"fps{i}", [128, 512], F32) for i in range(6)]
        gi = 0
        nsteps = (T // TC) * NF

        def load_w(step):
            f = step % NF
            b_ = wb[step % 3]
            cx.dma("pool", b_.t[:, 0], W["wg"].t[f], [W["wg"]], [b_])
            cx.dma("pool", b_.t[:, 1], W["wu"].t[f], [W["wu"]], [b_])
        load_w(0)
        load_w(1)
        step = 0
        emit_xT(0)
        for tc in range(T // TC):
            xT = xTs[tc % 2]
            for f in range(NF):
                b_ = wb[step % 3]
                if step + 2 < nsteps:
                    load_w(step + 2)
                step += 1
                hA, hB = hbA[f % 2], hbB[f % 2]
                ht = hts[f % 2].t
                cx.copy("act", ht[:, 0:2], carry.t[:, f, :], [carry], [hA])
                for hf in range(TC // 512):
                    ts_ = slice(hf * 512, (hf + 1) * 512)
                    pg, pu = pss[(gi % 2) * 2], pss[(gi % 2) * 2 + 1]
                    c_, g_ = hc[gi % 2], gl[gi % 2]
                    gi += 1
                    for k in range(8):
                        cx.mm(pg.t[:], b_.t[:, 0, k, :], xT.t[:, k, ts_], k == 0, k == 7, [b_, xT], [pg])
                    for k in range(8):
                        cx.mm(pu.t[:], b_.t[:, 1, k, :], xT.t[:, k, ts_], k == 0, k == 7, [b_, xT], [pu])
                    hw_ = [hA] if hf == 0 else [hB]
                    hr_ = [hA] if hf == 0 else [hA, hB]
                    o = hf * 512
                    cx.copy("act", ht[:, 2 + o:514 + o], pg.t[:], [pg], hw_)
                    cx.ts("dve", c_.t[:], ht[:, 2 + o:514 + o], cw.t[:, f, 2:3], ALU.mult, hr_ + [cw, cb], [c_],
                          s2=cb.t[:, f:f + 1], op1=ALU.add)
                    cx.stt(c_.t[:], ht[:, 1 + o:513 + o], cw.t[:, f, 1:2], c_.t[:], ALU.mult, ALU.add, hr_ + [cw, c_], [c_])
                    cx.stt(c_.t[:], ht[:, o:512 + o], cw.t[:, f, 0:1], c_.t[:], ALU.mult, ALU.add, hr_ + [cw, c_], [c_])
                    cx.act(g_.t[:], c_.t[:], AF.Gelu_apprx_tanh, [c_], [g_])
                    cx.tt("dve", act.t[:, f, ts_], g_.t[:], pu.t[:], ALU.mult, [g_, pu], [act])
                cx.copy("act", carry.t[:, f, :], ht[:, TC:TC + 2], [hB], [carry])
            if tc + 1 < T // TC:
                emit_xT(tc + 1)
            for q in range(TC // 128):
                tt = tc * (TC // 128) + q
                xr = xrs[tt % 2]
                cx.dma("sp", xr.t[:], x1d.t[tt * 128:(tt + 1) * 128, :], [x1d], [xr])
                zp = pss[4:6]
                for hf in range(2):
                    for f in range(NF):
                        cx.mm(zp[hf].t[:], act.t[:, f, q * 128:(q + 1) * 128], wd.t[:, f, hf * 512:(hf + 1) * 512],
                              f == 0, f == NF - 1, [act, wd], [zp[hf]])
                layer_norm_store(cx, zp, xr, g_t, b_t, x2d, tt * 128, tmps[tt % 2])

SCRATCH = {
    "rope": ([2, 128, T], F32), "vp": ([256, T], F32),
    "qr": ([384, T], BF16), "kr": ([384, T], BF16), "qn": ([384, T], BF16),
    "ks": ([128, T], BF16), "kw": ([128, T], BF16), "kc": ([128, T], BF16), "vc": ([128, T], BF16),
    "tmb": ([T, 640], BF16), "gr": ([T, 384], F32), "gt": ([T, 18], F32),
    "yt": ([1024, T], BF16), "x1": ([T, D], F32), "xmid": ([T, D], F32),
}


def build_program(layers=(0, 1), phases="ABCDEF", ext_in=(), ext_out=(), prologue=True):
    nc = bass.Bass("TRN2", target_bir_lowering=False)
    cx = Ctx(nc, ext_in, ext_out)
    xd = cx.dr("x", [T, D], F32, kind="ExternalInput")
    posd = cx.dr("pos", [1, T], I32, kind="ExternalInput")
    Cd = {k: cx.dr(k, list(v.shape), F32, kind="ExternalInput") for k, v in CONSTS.items()}
    Wd = {l: {k: cx.dr(f"{k}_{l}", shp, F32, kind="ExternalInput") for k, shp in LAYER_SHAPES.items()} for l in layers}
    S = {k: cx.dr(k, shp, dt) for k, (shp, dt) in SCRATCH.items()}
    outd = cx.dr("y", [T, D], F32, kind="ExternalOutput")
    with contextlib.ExitStack() as gs:
        cx.stack = gs
        G = {"rope": S["rope"]}
        G["ident"] = load_cast(cx, "ident", Cd["c_ident"], [128, 128])
        G["inv"] = load_f32(cx, "inv", Cd["c_inv"].t, Cd["c_inv"], [128, 1])
        G["sgn"] = load_f32(cx, "sgn", Cd["c_sgn"].t, Cd["c_sgn"], [128, 1])
        G["pw"] = load_f32(cx, "pw", Cd["c_pw"].t, Cd["c_pw"], [128, 2])
        G["prc"] = load_f32(cx, "prc", Cd["c_prc"].t, Cd["c_prc"], [128, 2, 16])
        G["C"] = Cd
        if prologue:
            prologue_rope(cx, posd, G)
        cur = xd
        for li, l in enumerate(layers):
            nxt = outd if li == len(layers) - 1 else S["xmid"]
            W = Wd[l]
            if "A" in phases:
                phase_A(cx, l, cur, W, G, S)
            if "B" in phases:
                phase_B(cx, l, W, G, S)
            if "C" in phases:
                phase_C(cx, l, W, G, S)
            if "D" in phases:
                phase_D(cx, l, W, G, S)
            if "E" in phases:
                phase_E(cx, l, cur, S["x1"], W, G, S)
            if "F" in phases:
                phase_F(cx, l, S["x1"], nxt, W, G, S)
            cur = nxt
        cx.P.barrier()
        finals = [outd.b] + [cx.dram[n].b for n in cx.ext_out]
        cx.P.emit_all(final_bufs=finals)
    return nc, cx


def make_in_maps(inputs, layers=(0, 1), cores=range(8)):
    shared = dict(CONSTS)
    for l in layers:
        for k, v in _layer_arrays(inputs, l).items():
            assert list(v.shape) == LAYER_SHAPES[k], (k, v.shape)
            shared[f"{k}_{l}"] = v.astype(np.float32, copy=False)
    maps = []
    for b in cores:
        m = dict(shared)
        m["x"] = np.ascontiguousarray(inputs["x"][b])
        m["pos"] = np.ascontiguousarray(inputs["positions"][b].reshape(1, T).astype(np.int32))
        maps.append(m)
    return maps


def kernel(**inputs):
    inputs = {k: np.asarray(v) for k, v in inputs.items()}
    nc, cx = build_program()
    maps = make_in_maps(inputs)
    res = run_bass_kernel_spmd(nc, maps, core_ids=list(range(8)))
    return np.stack([np.asarray(r["y"]) for r in res.results], 0).astype(np.float32)
```
